# Optimizing a Trainium2 kernel written in Bass

```python
import jax, jax.numpy as jnp
from jax import lax
import numpy as np

D_MODEL = 1024
BATCH = 2
SEQ = 8192
DEPTH = 4

GRID_W = 64
CTX_LEN = 256

DEEPNORM_ALPHA = (2 * DEPTH) ** 0.25
DEEPNORM_BETA = (8 * DEPTH) ** -0.25
LN_EPS = 1e-6

N_EVEN = (DEPTH + 1) // 2
N_ODD = DEPTH // 2

GLA_HEADS = 4
GLA_DK = 64
GLA_DV = 128
GLA_QK_W = GLA_HEADS * GLA_DK
GLA_V_W = GLA_HEADS * GLA_DV
GLA_GATE_RANK = 16
GLA_TAU = 16.0
GLA_CHUNK = 64

MLA_HEADS = 8
MLA_NOPE = 64
MLA_ROPE = 32
MLA_V = 64
MLA_Q_RANK = 256
MLA_KV_RANK = 128
MLA_V_W = MLA_HEADS * MLA_V
MLA_SCALE = (MLA_NOPE + MLA_ROPE) ** -0.5
ROPE_BASE = 10000.0
Q_BLOCK = 128

FNET_GROUPS = 4
FNET_GROUP_CH = 128
FNET_W = FNET_GROUPS * FNET_GROUP_CH

SGU_GROUPS = 4
SGU_GROUP_CH = 128
SGU_W = SGU_GROUPS * SGU_GROUP_CH
SGU_CHUNK = 128

EVEN_IN_SIZES = (GLA_QK_W, GLA_QK_W, GLA_V_W, 2 * GLA_GATE_RANK, GLA_V_W,
                 MLA_Q_RANK, MLA_KV_RANK, MLA_ROPE, MLA_V_W)
ODD_IN_SIZES = (FNET_W, FNET_W, SGU_W, SGU_W, SGU_W)
EVEN_OUT_IN = GLA_V_W + MLA_V_W
ODD_OUT_IN = FNET_W + SGU_W

kernel_name = "hybrid_gla_mla_fnet_sgu_prefix_dit"


def split_cols(z, sizes):
    idx = np.cumsum(sizes)[:-1].tolist()
    return jnp.split(z, idx, axis=-1)


def layer_norm(x, g=None, b=None):
    xf = x.astype(jnp.float32)
    xc = xf - jnp.mean(xf, -1, keepdims=True)
    y = xc * lax.rsqrt(jnp.mean(xc * xc, -1, keepdims=True) + LN_EPS)
    if g is not None:
        y = y * g.astype(jnp.float32) + b.astype(jnp.float32)
    return y.astype(x.dtype)


def rms_norm(x, g):
    xf = x.astype(jnp.float32)
    y = xf * lax.rsqrt(jnp.mean(xf * xf, -1, keepdims=True) + LN_EPS)
    return (y * g.astype(jnp.float32)).astype(x.dtype)


def ada_mod(cond, w, b):
    m = jnp.dot(jax.nn.silu(cond), w) + b
    return jnp.split(m, 3, axis=-1)


def modulate(x, shift, scale):
    return layer_norm(x) * (1 + scale[..., None, :]) + shift[..., None, :]


def axial_rope_angles(n):
    rows = n // GRID_W
    row = jnp.repeat(jnp.arange(rows, dtype=jnp.float32), GRID_W)
    col = (jnp.arange(n) % GRID_W).astype(jnp.float32)
    n_freq = MLA_ROPE // 4
    inv = ROPE_BASE ** (-jnp.arange(n_freq, dtype=jnp.float32) / n_freq)
    ang = jnp.concatenate([row[:, None] * inv, col[:, None] * inv], -1)
    return jnp.cos(ang), jnp.sin(ang)


def apply_rope(x, cos, sin):
    xf = x.astype(jnp.float32)
    x1, x2 = jnp.split(xf, 2, axis=-1)
    return jnp.concatenate([x1 * cos - x2 * sin, x1 * sin + x2 * cos], -1).astype(x.dtype)


def gla_q(q):
    B, n, _ = q.shape
    return q.astype(jnp.float32).reshape(B, n, GLA_HEADS, GLA_DK) * GLA_DK ** -0.5


def gla_kvg(k, v, lr, w2, b):
    B, n, _ = k.shape
    heads = lambda a, d: a.astype(jnp.float32).reshape(B, n, GLA_HEADS, d)
    lr_dirs = jnp.split(lr.astype(jnp.float32), 2, axis=-1)
    g = tuple(
        heads(jax.nn.log_sigmoid(jnp.einsum('bnr,re->bne', lr_dirs[d], w2[d].astype(jnp.float32))
                                 + b[d].astype(jnp.float32)) / GLA_TAU, GLA_DK)
        for d in range(2))
    return heads(k, GLA_DK), heads(v, GLA_DV), g


def gla_chunk_scan(q, k, v, g, s0):
    B, n, H, _ = q.shape
    nc = n // GLA_CHUNK
    chunk = lambda a: a.reshape(B, nc, GLA_CHUNK, H, a.shape[-1])
    q, k, v, g = chunk(q), chunk(k), chunk(v), chunk(g)
    b = jnp.cumsum(g, axis=2)
    b_mid = b[:, :, GLA_CHUNK // 2 - 1:GLA_CHUNK // 2]
    b_end = b[:, :, -1:]
    att = jnp.einsum('bcthk,bcshk->bchts', q * jnp.exp(b - b_mid), k * jnp.exp(b_mid - b))
    tri = jnp.tril(jnp.ones((GLA_CHUNK, GLA_CHUNK), dtype=bool))
    o = jnp.einsum('bchts,bcshv->bcthv', jnp.where(tri, att, 0.0), v)
    ds = jnp.einsum('bcshk,bcshv->bchkv', k * jnp.exp(b_end - b), v)
    decay = jnp.exp(b_end[:, :, 0])

    def step(s, inp):
        d, dsc = inp
        return d[..., None] * s + dsc, s

    s_last, s_start = lax.scan(step, s0, (jnp.moveaxis(decay, 1, 0), jnp.moveaxis(ds, 1, 0)))
    o = o + jnp.einsum('bcthk,cbhkv->bcthv', q * jnp.exp(b), s_start)
    return o.reshape(B, n, H, v.shape[-1]), s_last


def gla_final_state(k, v, g):
    bc = jnp.cumsum(g, axis=1)
    return jnp.einsum('bshk,bshv->bhkv', k * jnp.exp(bc[:, -1:] - bc), v)


def gla_bidir(q_l, k_l, v_l, g_l, k_c, v_c, g_c, q_c):
    B, _, H, _ = k_c.shape
    ident = lambda a: a
    rev = lambda a: a[:, ::-1]
    o_l, o_c = 0.0, 0.0
    for d, order in ((0, ident), (1, rev)):
        if q_c is None:
            s_c = gla_final_state(order(k_c), order(v_c), order(g_c[d]))
        else:
            s0 = jnp.zeros((B, H, GLA_DK, GLA_DV), jnp.float32)
            oc, s_c = gla_chunk_scan(order(q_c), order(k_c), order(v_c), order(g_c[d]), s0)
            o_c = o_c + order(oc)
        ol, _ = gla_chunk_scan(order(q_l), order(k_l), order(v_l), order(g_l[d]), s_c)
        o_l = o_l + order(ol)
    return o_l, (o_c if q_c is not None else None)


def mla_q(cq, q_norm_g, w_uq, rope):
    B, n, _ = cq.shape
    q = jnp.einsum('bnr,re->bne', rms_norm(cq, q_norm_g), w_uq).reshape(B, n, MLA_HEADS, MLA_NOPE + MLA_ROPE)
    q_nope, q_rope = q[..., :MLA_NOPE], q[..., MLA_NOPE:]
    if rope is not None:
        q_rope = apply_rope(q_rope, rope[0][:, None, :], rope[1][:, None, :])
    return q_nope, q_rope


def mla_kv(ckv, kr, kv_norm_g, w_ukv, rope):
    B, n, _ = ckv.shape
    kv = jnp.einsum('bnr,re->bne', rms_norm(ckv, kv_norm_g), w_ukv).reshape(B, n, MLA_HEADS, MLA_NOPE + MLA_V)
    if rope is not None:
        kr = apply_rope(kr, rope[0], rope[1])
    return kv[..., :MLA_NOPE], kr, kv[..., MLA_NOPE:]


def mla_attend(q_nope, q_rope, k_nope, k_rope, v):
    s = (jnp.einsum('bqhd,bkhd->bhqk', q_nope, k_nope, preferred_element_type=jnp.float32)
         + jnp.einsum('bqhr,bkr->bhqk', q_rope, k_rope, preferred_element_type=jnp.float32)) * MLA_SCALE
    p = jax.nn.softmax(s, axis=-1).astype(v.dtype)
    return jnp.einsum('bhqk,bkhd->bqhd', p, v)


def mla_latent_blocks(q_nope, q_rope, k_nope, k_rope, v):
    B, n = q_nope.shape[:2]
    nb = n // Q_BLOCK
    blocks = lambda a: jnp.moveaxis(a.reshape(B, nb, Q_BLOCK, *a.shape[2:]), 1, 0)
    o = lax.map(lambda qs: mla_attend(qs[0], qs[1], k_nope, k_rope, v), (blocks(q_nope), blocks(q_rope)))
    return jnp.moveaxis(o, 0, 1).reshape(B, n, MLA_V_W)


def even_mixer(h_lat, h_ctx, need_ctx_out, rope, w_in, gla_w2, gla_b, gla_norm_g,
               q_norm_g, w_uq, kv_norm_g, w_ukv, w_out):
    B, n, _ = h_lat.shape
    L = h_ctx.shape[1]
    proj = lambda h: split_cols(jnp.einsum('bnd,de->bne', h, w_in), EVEN_IN_SIZES)
    gq_l, gk_l, gv_l, glr_l, gg_l, cq_l, ckv_l, kr_l, mg_l = proj(h_lat)
    gq_c, gk_c, gv_c, glr_c, gg_c, cq_c, ckv_c, kr_c, mg_c = proj(h_ctx)

    k_l, v_l, g_l = gla_kvg(gk_l, gv_l, glr_l, gla_w2, gla_b)
    k_c, v_c, g_c = gla_kvg(gk_c, gv_c, glr_c, gla_w2, gla_b)
    o_gl, o_gc = gla_bidir(gla_q(gq_l), k_l, v_l, g_l, k_c, v_c, g_c,
                           gla_q(gq_c) if need_ctx_out else None)
    y_gla = rms_norm(o_gl, gla_norm_g).reshape(B, n, GLA_V_W).astype(h_lat.dtype) * jax.nn.silu(gg_l)

    kn_c, krot_c, vm_c = mla_kv(ckv_c, kr_c, kv_norm_g, w_ukv, None)
    kn_l, krot_l, vm_l = mla_kv(ckv_l, kr_l, kv_norm_g, w_ukv, rope)
    qn_l, qr_l = mla_q(cq_l, q_norm_g, w_uq, rope)
    o_ml = mla_latent_blocks(qn_l, qr_l,
                             jnp.concatenate([kn_c, kn_l], 1),
                             jnp.concatenate([krot_c, krot_l], 1),
                             jnp.concatenate([vm_c, vm_l], 1))
    y_mla = o_ml * jax.nn.silu(mg_l)
    y_lat = jnp.einsum('bne,ed->bnd', jnp.concatenate([y_gla, y_mla], -1), w_out)

    y_ctx = None
    if need_ctx_out:
        yg_c = rms_norm(o_gc, gla_norm_g).reshape(B, L, GLA_V_W).astype(h_ctx.dtype) * jax.nn.silu(gg_c)
        qn_c, qr_c = mla_q(cq_c, q_norm_g, w_uq, None)
        ym_c = mla_attend(qn_c, qr_c, kn_c, krot_c, vm_c).reshape(B, L, MLA_V_W) * jax.nn.silu(mg_c)
        y_ctx = jnp.einsum('bne,ed->bnd', jnp.concatenate([yg_c, ym_c], -1), w_out)
    return y_lat, y_ctx


def odd_mixer(h, w_in, sgu_w, sgu_b, w_out):
    B, n, _ = h.shape
    f, f_gate, u, v, s_gate = split_cols(jnp.einsum('bnd,de->bne', h, w_in), ODD_IN_SIZES)
    fg = f.astype(jnp.float32).reshape(B, n, FNET_GROUPS, FNET_GROUP_CH)
    fr = jnp.fft.fft2(fg, axes=(1, 3), norm='ortho').real.reshape(B, n, FNET_W).astype(h.dtype)
    y_f = fr * jax.nn.silu(f_gate)
    u = jax.nn.gelu(u, approximate=False)
    vg = layer_norm(jax.nn.gelu(v, approximate=False).reshape(B, n // SGU_CHUNK, SGU_CHUNK, SGU_GROUPS, SGU_GROUP_CH))
    sv = jnp.einsum('gts,bcsgd->bctgd', sgu_w, vg) + sgu_b.T[:, :, None]
    y_s = u * sv.reshape(B, n, SGU_W) * jax.nn.silu(s_gate)
    return jnp.einsum('bne,ed->bnd', jnp.concatenate([y_f, y_s], -1), w_out)


def setup_inputs(seed: int = 0) -> dict:
    key = jax.random.key(seed)
    ks = iter(jax.random.split(key, 24))
    nrm = lambda shape, scale: jax.random.normal(next(ks), shape, jnp.float32) * scale
    D = D_MODEL
    return {
        "x": nrm((BATCH, SEQ, D), 1.0),
        "c": nrm((BATCH, D), 1.0),
        "ctx": nrm((BATCH, CTX_LEN, D), 1.0),
        "c_ctx": nrm((D,), 1.0),
        "ada_w": nrm((DEPTH, D, 3 * D), D ** -0.5),
        "ada_b": nrm((DEPTH, 3 * D), 0.02),
        "post_ln_g": 1.0 + nrm((DEPTH, D), 0.02),
        "post_ln_b": nrm((DEPTH, D), 0.02),
        "even_w_in": nrm((N_EVEN, D, sum(EVEN_IN_SIZES)), D ** -0.5),
        "gla_w2": nrm((N_EVEN, 2, GLA_GATE_RANK, GLA_QK_W), GLA_GATE_RANK ** -0.5),
        "gla_b": nrm((N_EVEN, 2, GLA_QK_W), 0.1),
        "gla_norm_g": 1.0 + nrm((N_EVEN, GLA_DV), 0.02),
        "mla_q_norm_g": 1.0 + nrm((N_EVEN, MLA_Q_RANK), 0.02),
        "mla_w_uq": nrm((N_EVEN, MLA_Q_RANK, MLA_HEADS * (MLA_NOPE + MLA_ROPE)), MLA_Q_RANK ** -0.5),
        "mla_kv_norm_g": 1.0 + nrm((N_EVEN, MLA_KV_RANK), 0.02),
        "mla_w_ukv": nrm((N_EVEN, MLA_KV_RANK, MLA_HEADS * (MLA_NOPE + MLA_V)), MLA_KV_RANK ** -0.5),
        "even_w_out": nrm((N_EVEN, EVEN_OUT_IN, D), EVEN_OUT_IN ** -0.5 * DEEPNORM_BETA),
        "odd_w_in": nrm((N_ODD, D, sum(ODD_IN_SIZES)), D ** -0.5),
        "sgu_w": nrm((N_ODD, SGU_GROUPS, SGU_CHUNK, SGU_CHUNK), SGU_CHUNK ** -0.5),
        "sgu_b": 1.0 + nrm((N_ODD, SGU_GROUPS, SGU_CHUNK), 0.1),
        "odd_w_out": nrm((N_ODD, ODD_OUT_IN, D), ODD_OUT_IN ** -0.5 * DEEPNORM_BETA),
    }


def reference(x, c, ctx, c_ctx, ada_w, ada_b, post_ln_g, post_ln_b, even_w_in, gla_w2, gla_b, gla_norm_g,
              mla_q_norm_g, mla_w_uq, mla_kv_norm_g, mla_w_ukv, even_w_out, odd_w_in, sgu_w, sgu_b, odd_w_out):
    rope = axial_rope_angles(x.shape[1])
    for l in range(DEPTH):
        need_ctx_out = any(j % 2 == 0 for j in range(l + 1, DEPTH))
        i = l // 2
        shift, scale, gate = ada_mod(c, ada_w[l], ada_b[l])
        h_lat = modulate(x, shift, scale)
        if l % 2 == 0 or need_ctx_out:
            shift_c, scale_c, gate_c = ada_mod(c_ctx, ada_w[l], ada_b[l])
            h_ctx = modulate(ctx, shift_c, scale_c)
        if l % 2 == 0:
            y_lat, y_ctx = even_mixer(h_lat, h_ctx, need_ctx_out, rope, even_w_in[i], gla_w2[i], gla_b[i],
                                      gla_norm_g[i], mla_q_norm_g[i], mla_w_uq[i], mla_kv_norm_g[i],
                                      mla_w_ukv[i], even_w_out[i])
        else:
            y_lat = odd_mixer(h_lat, odd_w_in[i], sgu_w[i], sgu_b[i], odd_w_out[i])
            y_ctx = odd_mixer(h_ctx, odd_w_in[i], sgu_w[i], sgu_b[i], odd_w_out[i]) if need_ctx_out else None
        x = layer_norm(DEEPNORM_ALPHA * x + gate[:, None, :] * y_lat, post_ln_g[l], post_ln_b[l])
        if need_ctx_out:
            ctx = layer_norm(DEEPNORM_ALPHA * ctx + gate_c[None, :] * y_ctx, post_ln_g[l], post_ln_b[l])
    return x
```

```python
import contextlib
import numpy as np
import concourse.bass as bass
import concourse.mybir as mybir
from concourse.bass_utils import run_bass_kernel_spmd

F32 = mybir.dt.float32
BF16 = mybir.dt.bfloat16
AF = mybir.ActivationFunctionType
ALU = mybir.AluOpType
AX = mybir.AxisListType

SAME_ENG_SYNC = True


class Sched:
    def __init__(self, nc):
        self.nc = nc
        self.ops = []
        self.last_w = {}
        self.readers = {}
        self.stack = contextlib.ExitStack()
        self.n_sb = 0

    def sb(self, shape, dtype=F32, name=None):
        self.n_sb += 1
        name = name or f"sb{self.n_sb}"
        return self.stack.enter_context(self.nc.sbuf_tensor(name, list(shape), dtype))

    def ps(self, shape, dtype=F32, name=None):
        self.n_sb += 1
        name = name or f"ps{self.n_sb}"
        return self.stack.enter_context(self.nc.psum_tensor(name, list(shape), dtype))

    def add(self, eng, fn, r=(), w=(), dma=None):
        idx = len(self.ops)
        deps = set()
        for k in r:
            if k in self.last_w:
                deps.add(self.last_w[k])
        for k in w:
            if k in self.last_w:
                deps.add(self.last_w[k])
            for x in self.readers.get(k, ()):
                deps.add(x)
        deps.discard(idx)
        self.ops.append(dict(eng=eng, fn=fn, deps=deps, dma=dma))
        for k in r:
            self.readers.setdefault(k, []).append(idx)
        for k in w:
            self.last_w[k] = idx
            self.readers[k] = []
        return idx

    def pe(self, fn, r=(), w=()):
        return self.add("pe", fn, r, w)

    def act(self, fn, r=(), w=()):
        return self.add("act", fn, r, w)

    def dve(self, fn, r=(), w=()):
        return self.add("dve", fn, r, w)

    def pool(self, fn, r=(), w=()):
        return self.add("pool", fn, r, w)

    def dma(self, fn, r=(), w=(), group=None, eng="sp"):
        assert group is not None
        return self.add(eng, fn, r, w, dma=group)

    def emit(self):
        nc = self.nc
        ops = self.ops
        n = len(ops)
        needs_signal = [False] * n
        for i, op in enumerate(ops):
            keep = set()
            for d in op["deps"]:
                dop = ops[d]
                if dop["dma"] is None and dop["eng"] == op["eng"]:
                    if op["eng"] == "pe" or not SAME_ENG_SYNC:
                        continue
                keep.add(d)
                needs_signal[d] = True
            op["deps"] = keep
        dma_groups = {}
        engs = ["pe", "act", "dve", "pool", "sp"]
        sem_names = {e: f"s_{e}" for e in engs}
        sems = {}
        for e in engs:
            sems[e] = self.stack.enter_context(nc.semaphore(sem_names[e]))
        cnt = {e: 0 for e in engs}
        gcnt = {}
        for i, op in enumerate(ops):
            if op["dma"] is not None:
                g = op["dma"]
                if g not in dma_groups:
                    dma_groups[g] = self.stack.enter_context(nc.semaphore(f"d_{len(dma_groups)}"))
                    gcnt[g] = 0
                op["sem"] = dma_groups[g]
                nd = op.get("ndma")
            elif needs_signal[i]:
                cnt[op["eng"]] += 1
                op["sem"] = sems[op["eng"]]
                op["val"] = cnt[op["eng"]]
        for i, op in enumerate(ops):
            if op["dma"] is not None:
                g = op["dma"]
                nd = getattr(op["fn"], "ndma", 1)
                gcnt[g] += 16 * nd
                op["val"] = gcnt[g]
        self.final_dma = dict((g, (dma_groups[g], gcnt[g])) for g in dma_groups)

        engobj = {"pe": "tensor", "act": "scalar", "dve": "vector", "pool": "gpsimd", "sp": "sync"}

        def stream(ename):
            def body(eng):
                known = {}
                for i, op in enumerate(ops):
                    if op["eng"] != ename:
                        continue
                    for d in sorted(op["deps"]):
                        dop = ops[d]
                        s, v = dop["sem"], dop["val"]
                        key = id(s)
                        if known.get(key, 0) < v:
                            eng.wait_ge(s, v)
                            known[key] = v
                    ins = op["fn"](eng)
                    if op["dma"] is not None:
                        if not isinstance(ins, (list, tuple)):
                            ins = [ins]
                        assert len(ins) == getattr(op["fn"], "ndma", 1), (len(ins),)
                        for x in ins:
                            x.then_inc(op["sem"], 16)
                    elif needs_signal[i]:
                        ins.then_inc(op["sem"], 1)
                if ename == "sp":
                    for g, (s, v) in self.final_dma.items():
                        if known.get(id(s), 0) < v:
                            eng.wait_ge(s, v)
            return body

        with nc.Block() as block:
            block.tensor(stream("pe"))
            block.scalar(stream("act"))
            block.vector(stream("dve"))
            block.gpsimd(stream("pool"))
            block.sync(stream("sp"))

    def close(self):
        self.stack.close()


def ndma(n):
    def deco(f):
        f.ndma = n
        return f
    return deco


def build_k1(T, K, N, mode, n_lat_tiles=None, eps=1e-6):
    nc = bass.Bass("TRN2", target_bir_lowering=False)
    x = nc.dram_tensor("x", [T, K], F32, kind="ExternalInput").ap()
    w = nc.dram_tensor("w", [K, N], F32, kind="ExternalInput").ap()
    if mode == "ln":
        mod = nc.dram_tensor("mod", [4, K], F32, kind="ExternalInput").ap()
    elif mode == "rms":
        g = nc.dram_tensor("g", [1, K], F32, kind="ExternalInput").ap()
    else:
        bias = nc.dram_tensor("bias", [1, N], F32, kind="ExternalInput").ap()
    ident_d = nc.dram_tensor("ident", [128, 128], F32, kind="ExternalInput").ap()
    z = nc.dram_tensor("z", [T, N], F32, kind="ExternalOutput").ap()
    nt = T // 128
    kc = K // 128
    nch = (N + 511) // 512
    S = Sched(nc)
    wbf = S.sb([128, kc, N], BF16, "wbf")
    wst = [S.sb([128, N], F32, f"wst{i}") for i in range(2)]
    ident = S.sb([128, 128], BF16, "identb")
    identf = S.sb([128, 128], F32, "identf")
    if mode == "ln":
        modt = S.sb([128, 4, K], F32, "modt")
    elif mode == "rms":
        gt = S.sb([128, K], F32, "gt")
    else:
        bt = S.sb([128, N], F32, "bt")
    NX = 2
    xt = [S.sb([128, K], F32, f"xt{i}") for i in range(NX)]
    xn = [S.sb([128, K], F32, f"xn{i}") for i in range(NX)]
    hb = [S.sb([128, K], BF16, f"hb{i}") for i in range(NX)]
    hT = [S.sb([128, kc, 128], BF16, f"hT{i}") for i in range(NX)]
    st = [S.sb([128, 8, 6], F32, f"st{i}") for i in range(NX)]
    mv = [S.sb([128, 4], F32, f"mv{i}") for i in range(NX)]
    zt = [S.sb([128, N], F32, f"zt{i}") for i in range(NX)]
    tp = [S.ps([128, kc, 128], BF16, f"tp{i}") for i in range(2)]
    zp = [S.ps([128, 512], F32, f"zp{i}") for i in range(4)]

    S.dma(lambda e: e.dma_start(out=identf[:], in_=ident_d), w=["identf"], group="identf")
    S.dve(lambda e: e.tensor_copy(ident[:], identf[:]), r=["identf"], w=["ident"])
    if mode == "ln":
        S.dma(lambda e: e.dma_start(out=modt[:].rearrange("p a k -> p (a k)"),
                                    in_=mod.rearrange("a k -> (a k)").partition_broadcast(128)),
              w=["modt"], group="modt")
        for a in (1, 3):
            S.dve(lambda e, a=a: e.tensor_scalar_add(modt[:, a, :], modt[:, a, :], 1.0), r=["modt"], w=["modt"])
    elif mode == "rms":
        S.dma(lambda e: e.dma_start(out=gt[:], in_=g[0].partition_broadcast(128)), w=["gt"], group="gt")
    else:
        S.dma(lambda e: e.dma_start(out=bt[:], in_=bias[0].partition_broadcast(128)), w=["bt"], group="bt")
    for k in range(kc):
        s = k % 2
        S.dma(lambda e, k=k, s=s: e.dma_start(out=wst[s][:], in_=w[k * 128:(k + 1) * 128, :]),
              w=[f"wst{s}"], group=f"wst{s}")
        if k % 2 == 0:
            S.act(lambda e, k=k, s=s: e.copy(wbf[:, k, :], wst[s][:]), r=[f"wst{s}"], w=[f"wbf{k}"])
        else:
            S.dve(lambda e, k=k, s=s: e.tensor_copy(wbf[:, k, :], wst[s][:]), r=[f"wst{s}"], w=[f"wbf{k}"])
    wkeys = [f"wbf{k}" for k in range(kc)]
    zpi = 0
    for i in range(nt):
        s = i % NX
        S.dma(lambda e, i=i, s=s: e.dma_start(out=xt[s][:], in_=x[i * 128:(i + 1) * 128, :]),
              w=[f"xt{s}"], group=f"xt{s}")
        if mode == "ln":
            nsub = max(1, K // 512)
            fs = K // nsub
            for j in range(nsub):
                S.dve(lambda e, s=s, j=j, fs=fs: e.bn_stats(st[s][:, j, :], xt[s][:, j * fs:(j + 1) * fs]),
                      r=[f"xt{s}"], w=[f"st{s}"])
            S.dve(lambda e, s=s, nsub=nsub: e.bn_aggr(mv[s][:, 0:2], st[s][:, 0:nsub, :]), r=[f"st{s}"], w=[f"mv{s}"])
            S.dve(lambda e, s=s: e.tensor_scalar_add(mv[s][:, 3:4], mv[s][:, 1:2], eps), r=[f"mv{s}"], w=[f"mv{s}"])
            S.act(lambda e, s=s: e.sqrt(mv[s][:, 3:4], mv[s][:, 3:4]), r=[f"mv{s}"], w=[f"mv{s}"])
            S.dve(lambda e, s=s: e.reciprocal(mv[s][:, 2:3], mv[s][:, 3:4]), r=[f"mv{s}"], w=[f"mv{s}"])
            S.dve(lambda e, s=s: e.tensor_scalar(xn[s][:], xt[s][:], mv[s][:, 0:1], mv[s][:, 2:3],
                                                 ALU.subtract, ALU.mult),
                  r=[f"xt{s}", f"mv{s}"], w=[f"xn{s}"])
            a = 0 if (n_lat_tiles is None or i < n_lat_tiles) else 2
            S.pool(lambda e, s=s, a=a: e.tensor_tensor(xn[s][:], xn[s][:], modt[:, a + 1, :], ALU.mult),
                   r=[f"xn{s}", "modt"], w=[f"xn{s}"])
            S.pool(lambda e, s=s, a=a: e.tensor_tensor(hb[s][:], xn[s][:], modt[:, a, :], ALU.add),
                   r=[f"xn{s}", "modt"], w=[f"hb{s}"])
        elif mode == "silu":
            S.act(lambda e, s=s: e.activation(hb[s][:], xt[s][:], AF.Silu), r=[f"xt{s}"], w=[f"hb{s}"])
        else:
            S.act(lambda e, s=s: e.activation(xn[s][:], xt[s][:], AF.Square, accum_out=mv[s][:, 0:1]),
                  r=[f"xt{s}"], w=[f"xn{s}", f"mv{s}"])
            S.dve(lambda e, s=s: e.tensor_scalar(mv[s][:, 1:2], mv[s][:, 0:1], 1.0 / K, eps, ALU.mult, ALU.add),
                  r=[f"mv{s}"], w=[f"mv{s}"])
            S.act(lambda e, s=s: e.sqrt(mv[s][:, 3:4], mv[s][:, 1:2]), r=[f"mv{s}"], w=[f"mv{s}"])
            S.dve(lambda e, s=s: e.reciprocal(mv[s][:, 2:3], mv[s][:, 3:4]), r=[f"mv{s}"], w=[f"mv{s}"])
            S.dve(lambda e, s=s: e.scalar_tensor_tensor(hb[s][:], xt[s][:], mv[s][:, 2:3], gt[:],
                                                        ALU.mult, ALU.mult),
                  r=[f"xt{s}", f"mv{s}", "gt"], w=[f"hb{s}"])
        t = i % 2
        for k in range(kc):
            S.pe(lambda e, s=s, t=t, k=k: e.transpose(tp[t][:, k, :], hb[s][:, k * 128:(k + 1) * 128], ident[:]),
                 r=[f"hb{s}", "ident"], w=[f"tp{t}"])
        S.act(lambda e, s=s, t=t: e.copy(hT[s][:], tp[t][:]), r=[f"tp{t}"], w=[f"hT{s}"])
        for c in range(nch):
            c0, c1 = c * 512, min(N, (c + 1) * 512)
            p = zpi % 4
            zpi += 1
            for k in range(kc):
                S.pe(lambda e, s=s, p=p, k=k, c0=c0, c1=c1: e.matmul(
                    zp[p][:, 0:c1 - c0], hT[s][:, k, :], wbf[:, k, c0:c1], start=(k == 0), stop=(k == kc - 1)),
                    r=[f"hT{s}", wkeys[k]], w=[f"zp{p}"])
            if mode == "silu":
                S.dve(lambda e, s=s, p=p, c0=c0, c1=c1: e.tensor_tensor(zt[s][:, c0:c1], zp[p][:, 0:c1 - c0], bt[:, c0:c1], ALU.add),
                      r=[f"zp{p}", "bt"], w=[f"zt{s}"])
            elif c % 2 == 0:
                S.dve(lambda e, s=s, p=p, c0=c0, c1=c1: e.tensor_copy(zt[s][:, c0:c1], zp[p][:, 0:c1 - c0]),
                      r=[f"zp{p}"], w=[f"zt{s}"])
            else:
                S.act(lambda e, s=s, p=p, c0=c0, c1=c1: e.copy(zt[s][:, c0:c1], zp[p][:, 0:c1 - c0]),
                      r=[f"zp{p}"], w=[f"zt{s}"])
        S.dma(lambda e, i=i, s=s: e.dma_start(out=z[i * 128:(i + 1) * 128, :], in_=zt[s][:]),
              r=[f"zt{s}"], w=[f"zout{s}"], group=f"zo{s}")
    S.emit()
    S.close()
    return nc


MLA_SCALE = 96 ** -0.5


def build_k3(NQL, NQC, NK, nheads=8):
    nc = bass.Bass("TRN2", target_bir_lowering=False)
    NQ = NQL + NQC
    dt = lambda name, shape, k="ExternalInput": nc.dram_tensor(name, list(shape), F32, kind=k).ap()
    q = dt("q", [NQ, 768]); kv = dt("kv", [NK, 1024]); kr = dt("kr", [NK, 32])
    csq = dt("csq", [NQ, 32])
    csk = dt("csk", [NK, 32])
    ident_d = dt("ident", [128, 128])
    out = dt("out", [NQ, 512], "ExternalOutput")
    nqt = NQ // 128
    nkt = NK // 128
    S = Sched(nc)
    ident = S.sb([128, 128], BF16, "identb")
    identf = S.sb([128, 128], F32, "identf")
    S.dma(lambda e: e.dma_start(out=identf[:], in_=ident_d), w=["identf"], group="identf")
    S.dve(lambda e: e.tensor_copy(ident[:], identf[:]), r=["identf"], w=["ident"])
    KH = (nkt + 1) // 2
    kvh = S.sb([128, KH, 128], F32, "kvh")
    kpad = S.sb([128, nkt, 128], BF16, "kpad")
    vx = [S.sb([128, nkt, 65], BF16, f"vx{i}") for i in range(2)]
    kT = [S.sb([128, nkt * 128], BF16, f"kT{i}") for i in range(2)]
    qT = [S.sb([128, nqt * 128], BF16, f"qT{i}") for i in range(2)]
    qh = S.sb([128, nqt, 96], F32, "qh")
    qpad = S.sb([128, nqt, 128], BF16, "qpad")
    krl = S.sb([128, nkt, 32], F32, "krl")
    cskt = S.sb([128, nkt, 32], F32, "cskt")
    csqt = S.sb([128, nqt, 32], F32, "csqt")
    tk = [S.sb([128, nkt, 16], F32, f"tk{i}") for i in range(2)]
    tq = [S.sb([128, nqt, 16], F32, f"tq{i}") for i in range(2)]
    pT = [S.sb([128, 512], BF16, f"pT{i}") for i in range(3)]
    oTs = [S.sb([65, 512], F32, f"oTs{i}") for i in range(2)]
    ost = [S.sb([128, 64], F32, f"ost{i}") for i in range(4)]
    rc = [S.sb([128, 1], F32, f"rc{i}") for i in range(4)]
    sTp = [S.ps([128, 512], F32, f"sTp{i}") for i in range(3)]
    oTp = [S.ps([65, 512], F32, f"oTp{i}") for i in range(2)]
    tpk = S.ps([128, 8, 128], BF16, "tpk")
    tpo = S.ps([128, 65], F32, "tpo")

    S.pool(lambda e: e.memset(kpad[:], 0.0), w=["kpad"])
    S.pool(lambda e: e.memset(qpad[:], 0.0), w=["qpad"])
    for i in range(2):
        S.pool(lambda e, i=i: e.memset(vx[i][:], 1.0), w=[f"vx{i}"])
    S.dma(lambda e: e.dma_start(out=krl[:], in_=kr.rearrange("(t p) c -> p t c", p=128)), w=["krl"], group="krl")
    S.dma(lambda e: e.dma_start(out=cskt[:], in_=csk.rearrange("(t p) c -> p t c", p=128)), w=["cskt"], group="cskt")
    S.dma(lambda e: e.dma_start(out=csqt[:], in_=csq.rearrange("(t p) c -> p t c", p=128)), w=["csqt"], group="csqt")

    def rope(eng_a, eng_b, src, cs, tmp, dst, keys_r, key_tmp, key_dst, xo):
        x1 = lambda: src[:, :, xo:xo + 16]
        x2 = lambda: src[:, :, xo + 16:xo + 32]
        c = lambda: cs[:, :, 0:16]
        sn = lambda: cs[:, :, 16:32]
        S.add(eng_a, lambda e: e.tensor_tensor(tmp[0][:], x1(), c(), ALU.mult), r=keys_r, w=[key_tmp + "0"])
        S.add(eng_b, lambda e: e.tensor_tensor(tmp[1][:], x2(), sn(), ALU.mult), r=keys_r, w=[key_tmp + "1"])
        S.add(eng_a, lambda e: e.tensor_tensor(dst[:, :, 64:80], tmp[0][:], tmp[1][:], ALU.subtract),
              r=[key_tmp + "0", key_tmp + "1"], w=[key_dst])
        S.add(eng_a, lambda e: e.tensor_tensor(tmp[0][:], x1(), sn(), ALU.mult), r=keys_r, w=[key_tmp + "0"])
        S.add(eng_b, lambda e: e.tensor_tensor(tmp[1][:], x2(), c(), ALU.mult), r=keys_r, w=[key_tmp + "1"])
        S.add(eng_a, lambda e: e.tensor_tensor(dst[:, :, 96:112], tmp[0][:], tmp[1][:], ALU.add),
              r=[key_tmp + "0", key_tmp + "1"], w=[key_dst])

    rope("dve", "pool", krl, cskt, tk, kpad, ["krl", "cskt"], "tk", "kpad", 0)

    chunks = []
    for c0 in range(0, NQL, 512):
        chunks.append((c0, min(512, NQL - c0), 0, nkt))
    if NQC:
        chunks.append((NQL, NQC, 0, 2))
    sti = 0
    oti = 0
    osti = 0
    for h in range(nheads):
        hb = h % 2
        for half in range(2):
            t0 = half * KH
            t1 = min(nkt, t0 + KH)
            if t1 <= t0:
                continue
            S.dma(lambda e, h=h, t0=t0, t1=t1: e.dma_start(
                out=kvh[:, 0:t1 - t0, :],
                in_=kv[t0 * 128:t1 * 128, h * 128:(h + 1) * 128].rearrange("(t p) c -> p t c", p=128)),
                w=["kvh"], group="kvh")
            S.dve(lambda e, t0=t0, t1=t1: e.tensor_copy(kpad[:, t0:t1, 0:64], kvh[:, 0:t1 - t0, 0:64]),
                  r=["kvh"], w=["kpad"])
            S.pool(lambda e, t0=t0, t1=t1, hb=hb: e.tensor_copy(vx[hb][:, t0:t1, 0:64], kvh[:, 0:t1 - t0, 64:128]),
                   r=["kvh"], w=[f"vx{hb}"])
        for g0 in range(0, nkt, 8):
            g1 = min(nkt, g0 + 8)
            for t in range(g0, g1):
                S.pe(lambda e, t=t, g0=g0: e.transpose(tpk[:, t - g0, :], kpad[:, t, :], ident[:]),
                     r=["kpad", "ident"], w=["tpk"])
            S.dve(lambda e, g0=g0, g1=g1, hb=hb: e.tensor_copy(
                kT[hb][:, g0 * 128:g1 * 128], tpk[:, 0:g1 - g0, :].rearrange("p a b -> p (a b)")),
                r=["tpk"], w=[f"kT{hb}"])
        S.dma(lambda e, h=h: e.dma_start(out=qh[:], in_=q[:, h * 96:(h + 1) * 96].rearrange("(t p) c -> p t c", p=128)),
              w=["qh"], group="qh")
        S.pool(lambda e: e.tensor_copy(qpad[:, :, 0:64], qh[:, :, 0:64]), r=["qh"], w=["qpad"])
        rope("pool", "dve", qh, csqt, tq, qpad, ["qh", "csqt"], "tq", "qpad", 64)
        for g0 in range(0, nqt, 8):
            g1 = min(nqt, g0 + 8)
            for t in range(g0, g1):
                S.pe(lambda e, t=t, g0=g0: e.transpose(tpk[:, t - g0, :], qpad[:, t, :], ident[:]),
                     r=["qpad", "ident"], w=["tpk"])
            S.dve(lambda e, g0=g0, g1=g1, hb=hb: e.tensor_copy(
                qT[hb][:, g0 * 128:g1 * 128], tpk[:, 0:g1 - g0, :].rearrange("p a b -> p (a b)")),
                r=["tpk"], w=[f"qT{hb}"])
        for (q0, qn, k0, k1) in chunks:
            op = oti % 2
            oti += 1
            for kt in range(k0, k1):
                sp = sti % 3
                sti += 1
                S.pe(lambda e, sp=sp, hb=hb, kt=kt, q0=q0, qn=qn: e.matmul(
                    sTp[sp][:, 0:qn], kT[hb][:, kt * 128:(kt + 1) * 128], qT[hb][:, q0:q0 + qn],
                    start=True, stop=True), r=[f"kT{hb}", f"qT{hb}"], w=[f"sTp{sp}"])
                S.act(lambda e, sp=sp, qn=qn: e.activation(pT[sp][:, 0:qn], sTp[sp][:, 0:qn], AF.Exp, scale=MLA_SCALE),
                      r=[f"sTp{sp}"], w=[f"pT{sp}"])
                S.pe(lambda e, sp=sp, hb=hb, kt=kt, qn=qn, op=op, k0=k0, k1=k1: e.matmul(
                    oTp[op][:, 0:qn], vx[hb][:, kt, :], pT[sp][:, 0:qn], start=(kt == k0), stop=(kt == k1 - 1)),
                    r=[f"vx{hb}", f"pT{sp}"], w=[f"oTp{op}"])
            S.dve(lambda e, op=op, qn=qn: e.tensor_copy(oTs[op][:, 0:qn], oTp[op][:, 0:qn]),
                  r=[f"oTp{op}"], w=[f"oTs{op}"])
            for j in range(qn // 128):
                os_ = osti % 4
                osti += 1
                S.pe(lambda e, op=op, j=j: e.transpose(tpo[:], oTs[op][:, j * 128:(j + 1) * 128], identf[0:65, 0:65]),
                     r=[f"oTs{op}", "identf"], w=["tpo"])
                S.dve(lambda e, os_=os_: e.reciprocal(rc[os_][:], tpo[:, 64:65]), r=["tpo"], w=[f"rc{os_}"])
                S.dve(lambda e, os_=os_: e.tensor_scalar(ost[os_][:], tpo[:, 0:64], rc[os_][:, 0:1], None, ALU.mult),
                      r=["tpo", f"rc{os_}"], w=[f"ost{os_}"])
                r0 = q0 + j * 128
                S.dma(lambda e, os_=os_, r0=r0, h=h: e.dma_start(out=out[r0:r0 + 128, h * 64:(h + 1) * 64], in_=ost[os_][:]),
                      r=[f"ost{os_}"], w=[f"oo{os_}"], group=f"oo{os_}")
    S.emit()
    S.close()
    return nc


def rope_tables(pos):
    pos = np.asarray(pos)
    row = (pos // 64).astype(np.float32)
    col = (pos % 64).astype(np.float32)
    inv = (10000.0 ** (-np.arange(8, dtype=np.float32) / 8)).astype(np.float32)
    ang = np.concatenate([row[:, None] * inv, col[:, None] * inv], -1).astype(np.float32)
    c = np.cos(ang).astype(np.float32)
    s = np.sin(ang).astype(np.float32)
    ident = pos < 0
    c[ident] = 1.0
    s[ident] = 0.0
    return np.concatenate([c, s], -1).astype(np.float32)


import os
STEPS = int(os.environ.get('K4_STEPS', '99'))
SUB = os.environ.get('K4_SUB', 'abcde')

BLK = 768


def gla_consts():
    s = np.arange(64)[:, None]
    t = np.arange(64)[None, :]
    U = (s <= t).astype(np.float32)
    Umid = U - (s <= 31).astype(np.float32)
    M2T = (s > t).astype(np.float32)
    c = np.zeros((64, 320), np.float32)
    c[:, 0:64] = Umid
    c[:, 64:128] = U
    c[:, 128:192] = M2T
    c[:, 192:256] = U
    c[0, 256:320] = 1.0
    return c


def build_k4(NT, nscan=2):
    nc = bass.Bass("TRN2", target_bir_lowering=False)
    dt = lambda name, shape, k="ExternalInput": nc.dram_tensor(name, list(shape), F32, kind=k).ap()
    qT = dt("qT", [nscan, 64, NT]); kT = dt("kT", [nscan, 64, NT])
    ktok = dt("ktok", [nscan, NT, 64]); vtok = dt("vtok", [nscan, NT, 128])
    glrT = dt("glrT", [nscan, 16, NT]); w2 = dt("w2", [nscan, 16, 64]); bb = dt("bb", [nscan, 1, 64])
    cst_d = dt("cst", [64, 320])
    out = dt("out", [nscan, NT, 128], "ExternalOutput")
    nblk = NT // BLK
    cpb = BLK // 64
    S = Sched(nc)
    cst = S.sb([64, 320], F32, "cst_sb")
    maskb = S.sb([64, 64], BF16, "maskb")
    S.dma(lambda e: e.dma_start(out=cst[:], in_=cst_d), w=["cst"], group="cst")
    S.dve(lambda e: e.tensor_copy(maskb[:], cst[:, 192:256]), r=["cst"], w=["maskb"])
    w2t = S.sb([16, 64], F32, "w2t")
    bbt = S.sb([1, 64], F32, "bbt")
    NB = 2
    qTb = [S.sb([64, BLK], F32, f"qTb{i}") for i in range(NB)]
    kTb = [S.sb([64, BLK], F32, f"kTb{i}") for i in range(NB)]
    ktb = [S.sb([64, cpb, 64], F32, f"ktb{i}") for i in range(NB)]
    vtb = [S.sb([64, cpb, 128], F32, f"vtb{i}") for i in range(NB)]
    vbb = [S.sb([64, cpb, 128], BF16, f"vbb{i}") for i in range(NB)]
    glb = [S.sb([16, BLK], F32, f"glb{i}") for i in range(NB)]
    NP = 4
    Et = [S.sb([64, 64], F32, f"Et{i}") for i in range(NP)]
    Lt = [S.sb([64, 64], F32, f"Lt{i}") for i in range(NP)]
    e12 = [S.sb([64, 192], F32, f"e12{i}") for i in range(NP)]
    e4 = [S.sb([64, 64], F32, f"e4{i}") for i in range(NP)]
    dec = [S.sb([64, 1], F32, f"dec{i}") for i in range(NP)]
    qe = [S.sb([64, 64], BF16, f"qe{i}") for i in range(NP)]
    ke = [S.sb([64, 64], BF16, f"ke{i}") for i in range(NP)]
    qb = [S.sb([64, 64], BF16, f"qb{i}") for i in range(NP)]
    kd = [S.sb([64, 64], BF16, f"kd{i}") for i in range(NP)]
    attm = [S.sb([64, 64], BF16, f"attm{i}") for i in range(NP)]
    ot = [S.sb([64, 128], F32, f"ot{i}") for i in range(NP)]
    Sf = [S.sb([64, 128], F32, f"Sf{i}") for i in range(2)]
    Sb = [S.sb([64, 128], BF16, f"Sb{i}") for i in range(2)]
    pA = [S.ps([64, 512], F32, f"pA{i}") for i in range(NP)]
    pB = [S.ps([64, 512], F32, f"pB{i}") for i in range(NP)]

    ci = 0
    for d in range(nscan):
        S.dma(lambda e, d=d: e.dma_start(out=w2t[:], in_=w2[d]), w=["w2t"], group="w2t")
        S.dma(lambda e, d=d: e.dma_start(out=bbt[:], in_=bb[d]), w=["bbt"], group="bbt")
        S.dve(lambda e: e.memset(Sf[0][:], 0.0), w=["Sf0"])
        S.pool(lambda e: e.memset(Sb[0][:], 0.0), w=["Sb0"])
        si = 0
        for blk in range(nblk):
            b = blk % NB
            cols = slice(blk * BLK, (blk + 1) * BLK)
            S.dma(lambda e, d=d, b=b, cols=cols: e.dma_start(out=qTb[b][:], in_=qT[d][:, cols]), w=[f"qTb{b}"], group=f"qTb{b}")
            S.dma(lambda e, d=d, b=b, cols=cols: e.dma_start(out=kTb[b][:], in_=kT[d][:, cols]), w=[f"kTb{b}"], group=f"kTb{b}")
            S.dma(lambda e, d=d, b=b, cols=cols: e.dma_start(
                out=ktb[b][:], in_=ktok[d][cols, :].rearrange("(c p) k -> p c k", p=64)), w=[f"ktb{b}"], group=f"ktb{b}")
            S.dma(lambda e, d=d, b=b, cols=cols: e.dma_start(
                out=vtb[b][:], in_=vtok[d][cols, :].rearrange("(c p) k -> p c k", p=64)), w=[f"vtb{b}"], group=f"vtb{b}")
            S.dma(lambda e, d=d, b=b, cols=cols: e.dma_start(out=glb[b][:], in_=glrT[d][:, cols]), w=[f"glb{b}"], group=f"glb{b}")
            S.pool(lambda e, b=b: e.tensor_copy(vbb[b][:], vtb[b][:]), r=[f"vtb{b}"], w=[f"vbb{b}"])
            S.pool(lambda e, b=b: e.tensor_scalar(qTb[b][:], qTb[b][:], 0.125, None, ALU.mult), r=[f"qTb{b}"], w=[f"qTb{b}"])
            for cc in range(cpb):
                p = ci % NP
                ci += 1
                cs = slice(cc * 64, (cc + 1) * 64)
                S.pe(lambda e, p=p, b=b, cs=cs: e.matmul(pA[p][:, 0:64], glb[b][:, cs], w2t[:], start=True, stop=False),
                     r=[f"glb{b}", "w2t"], w=[f"pA{p}"])
                S.pe(lambda e, p=p: e.matmul(pA[p][:, 0:64], cst[0:1, 256:320], bbt[:], start=False, stop=True),
                     r=["cst", "bbt"], w=[f"pA{p}"])
                S.act(lambda e, p=p: e.activation(Et[p][:], pA[p][:, 0:64], AF.Exp, scale=-1.0), r=[f"pA{p}"], w=[f"Et{p}"])
                S.act(lambda e, p=p: e.activation(Lt[p][:], Et[p][:], AF.Ln, bias=1.0), r=[f"Et{p}"], w=[f"Lt{p}"])
                if STEPS < 3: continue
                S.pe(lambda e, p=p: e.matmul(pA[p][:, 64:128], Lt[p][:], cst[:, 0:64], start=True, stop=True),
                     r=[f"Lt{p}", "cst"], w=[f"pA{p}"])
                S.pe(lambda e, p=p: e.matmul(pA[p][:, 128:192], Lt[p][:], cst[:, 64:128], start=True, stop=True),
                     r=[f"Lt{p}", "cst"], w=[f"pA{p}"])
                S.pe(lambda e, p=p: e.matmul(pA[p][:, 192:256], cst[:, 128:192], Lt[p][:], start=True, stop=True),
                     r=[f"Lt{p}", "cst"], w=[f"pA{p}"])
                if STEPS < 4: continue
                if 'a' in SUB: S.act(lambda e, p=p: e.activation(e12[p][:, 0:64], pA[p][:, 64:128], AF.Exp, scale=-1.0 / 16),
                      r=[f"pA{p}"], w=[f"e12{p}"])
                if 'e' in SUB: S.act(lambda e, p=p: e.activation(e12[p][:, 64:128], pA[p][:, 128:192], AF.Exp, scale=-1.0 / 16),
                      r=[f"pA{p}"], w=[f"e12{p}"])
                if 'b' in SUB: S.act(lambda e, p=p: e.activation(e12[p][:, 128:192], pA[p][:, 64:128], AF.Exp, scale=1.0 / 16),
                      r=[f"pA{p}"], w=[f"e12{p}"])
                if 'c' in SUB: S.act(lambda e, p=p: e.activation(e4[p][:], pA[p][:, 192:256], AF.Exp, scale=-1.0 / 16),
                      r=[f"pA{p}"], w=[f"e4{p}"])
                if 'f' in SUB: S.act(lambda e, p=p: e.activation(e4[p][:], pA[p][:, 64:128], AF.Exp, scale=-1.0 / 16),
                      r=[f"pA{p}"], w=[f"e4{p}"])
                if 'g' in SUB: S.act(lambda e, p=p: e.activation(e12[p][:, 0:64], pA[p][:, 192:256], AF.Exp, scale=-1.0 / 16),
                      r=[f"pA{p}"], w=[f"e12{p}"])
                if 'd' in SUB: S.act(lambda e, p=p: e.copy(dec[p][:], e12[p][:, 127:128]), r=[f"e12{p}"], w=[f"dec{p}"])
                if STEPS < 5: continue
                S.dve(lambda e, p=p, b=b, cs=cs: e.tensor_tensor(qe[p][:], qTb[b][:, cs], e12[p][:, 0:64], ALU.mult),
                      r=[f"qTb{b}", f"e12{p}"], w=[f"qe{p}"])
                S.dve(lambda e, p=p, b=b, cs=cs: e.tensor_tensor(ke[p][:], kTb[b][:, cs], e12[p][:, 128:192], ALU.mult),
                      r=[f"kTb{b}", f"e12{p}"], w=[f"ke{p}"])
                S.pool(lambda e, p=p, b=b, cs=cs: e.tensor_tensor(qb[p][:], qTb[b][:, cs], e12[p][:, 64:128], ALU.mult),
                       r=[f"qTb{b}", f"e12{p}"], w=[f"qb{p}"])
                S.pool(lambda e, p=p, b=b, cc=cc: e.tensor_tensor(kd[p][:], ktb[b][:, cc, :], e4[p][:], ALU.mult),
                       r=[f"ktb{b}", f"e4{p}"], w=[f"kd{p}"])
                if STEPS < 6: continue
                S.pe(lambda e, p=p: e.matmul(pA[p][:, 256:320], ke[p][:], qe[p][:], start=True, stop=True),
                     r=[f"ke{p}", f"qe{p}"], w=[f"pA{p}"])
                S.dve(lambda e, p=p: e.tensor_tensor(attm[p][:], pA[p][:, 256:320], maskb[:], ALU.mult),
                      r=[f"pA{p}", "maskb"], w=[f"attm{p}"])
                if STEPS < 8: continue
                s0 = si % 2
                s1 = (si + 1) % 2
                si += 1
                S.pe(lambda e, p=p, b=b, cc=cc: e.matmul(pB[p][:, 0:128], attm[p][:], vbb[b][:, cc, :], start=True, stop=False),
                     r=[f"attm{p}", f"vbb{b}"], w=[f"pB{p}"])
                S.pe(lambda e, p=p, s0=s0: e.matmul(pB[p][:, 0:128], qb[p][:], Sb[s0][:], start=False, stop=True),
                     r=[f"qb{p}", f"Sb{s0}"], w=[f"pB{p}"])
                S.pe(lambda e, p=p, b=b, cc=cc: e.matmul(pB[p][:, 128:256], kd[p][:], vbb[b][:, cc, :], start=True, stop=True),
                     r=[f"kd{p}", f"vbb{b}"], w=[f"pB{p}"])
                S.dve(lambda e, p=p, s0=s0, s1=s1: e.scalar_tensor_tensor(
                    Sf[s1][:], Sf[s0][:], dec[p][:, 0:1], pB[p][:, 128:256], ALU.mult, ALU.add),
                    r=[f"Sf{s0}", f"dec{p}", f"pB{p}"], w=[f"Sf{s1}"])
                S.pool(lambda e, s1=s1: e.tensor_copy(Sb[s1][:], Sf[s1][:]), r=[f"Sf{s1}"], w=[f"Sb{s1}"])
                S.dve(lambda e, p=p: e.tensor_copy(ot[p][:], pB[p][:, 0:128]), r=[f"pB{p}"], w=[f"ot{p}"])
                r0 = blk * BLK + cc * 64
                S.dma(lambda e, p=p, d=d, r0=r0: e.dma_start(out=out[d][r0:r0 + 64, :], in_=ot[p][:]),
                      r=[f"ot{p}"], w=[f"oo{p}"], group=f"oo{p}")
    S.emit()
    S.close()
    return nc


def gla_ref(q, k, v, g):
    n = q.shape[0]
    Sx = np.zeros((64, 128))
    o = np.zeros((n, 128))
    for t in range(n):
        Sx = np.exp(g[t])[:, None] * Sx + np.outer(k[t], v[t])
        o[t] = q[t] @ Sx
    return o


ALPHA = 8 ** 0.25


def build_k5(T, kind, n_lat_tiles, eps=1e-6):
    nc = bass.Bass("TRN2", target_bir_lowering=False)
    D = 1024
    dt = lambda name, shape, k="ExternalInput": nc.dram_tensor(name, list(shape), F32, kind=k).ap()
    x = dt("x", [T, D])
    w = dt("w", [D, D])
    vecs = dt("vecs", [4, D])
    ident_d = dt("ident", [128, 128])
    if kind == "even":
        ogf = dt("ogf", [T, 512]); ogb = dt("ogb", [T, 512]); oml = dt("oml", [T, 512])
        z = dt("z", [T, 2496])
        gng = dt("gng", [1, 128])
    else:
        fr = dt("fr", [T, 512])
        z = dt("z", [T, 2560])
        swT = dt("swT", [4, 128, 128])
        sbT = dt("sbT", [128, 4])
    out = dt("out", [T, D], "ExternalOutput")
    nt = T // 128
    S = Sched(nc)
    wbf = S.sb([128, 8, D], BF16, "wbf")
    wst = [S.sb([128, D], F32, f"wst{i}") for i in range(2)]
    ident = S.sb([128, 128], BF16, "identb")
    identf = S.sb([128, 128], F32, "identf")
    vt = S.sb([128, 4, D], F32, "vt")
    S.dma(lambda e: e.dma_start(out=identf[:], in_=ident_d), w=["identf"], group="identf")
    S.dve(lambda e: e.tensor_copy(ident[:], identf[:]), r=["identf"], w=["ident"])
    S.dma(lambda e: e.dma_start(out=vt[:].rearrange("p a k -> p (a k)"),
                                in_=vecs.rearrange("a k -> (a k)").partition_broadcast(128)),
          w=["vt"], group="vt")
    if kind == "even":
        gn = S.sb([128, 128], F32, "gn")
        S.dma(lambda e: e.dma_start(out=gn[:], in_=gng[0].partition_broadcast(128)), w=["gn"], group="gn")
    else:
        swf = S.sb([128, 4, 128], F32, "swf")
        swb = S.sb([128, 4, 128], BF16, "swb")
        sbt = S.sb([128, 4], F32, "sbt")
        S.dma(lambda e: e.dma_start(out=swf[:], in_=swT.rearrange("g s t -> s g t")), w=["swf"], group="swf")
        S.dve(lambda e: e.tensor_copy(swb[:], swf[:]), r=["swf"], w=["swb"])
        S.dma(lambda e: e.dma_start(out=sbt[:], in_=sbT), w=["sbt"], group="sbt")
    for k in range(8):
        s = k % 2
        S.dma(lambda e, k=k, s=s: e.dma_start(out=wst[s][:], in_=w[k * 128:(k + 1) * 128, :]),
              w=[f"wst{s}"], group=f"wst{s}")
        if k % 2 == 0:
            S.act(lambda e, k=k, s=s: e.copy(wbf[:, k, :], wst[s][:]), r=[f"wst{s}"], w=[f"wbf{k}"])
        else:
            S.dve(lambda e, k=k, s=s: e.tensor_copy(wbf[:, k, :], wst[s][:]), r=[f"wst{s}"], w=[f"wbf{k}"])
    NX = 2
    xt = [S.sb([128, D], F32, f"xt{i}") for i in range(NX)]
    ab = [S.sb([128, D], BF16, f"ab{i}") for i in range(NX)]
    aT = [S.sb([128, 8, 128], BF16, f"aT{i}") for i in range(NX)]
    rr = [S.sb([128, D], F32, f"rr{i}") for i in range(NX)]
    ot = [S.sb([128, D], F32, f"ot{i}") for i in range(NX)]
    st = [S.sb([128, 8, 6], F32, f"st{i}") for i in range(NX)]
    mv = [S.sb([128, 16], F32, f"mv{i}") for i in range(NX)]
    if kind == "even":
        i1 = [S.sb([128, 512], F32, f"i1{i}") for i in range(NX)]
        i2 = [S.sb([128, 512], F32, f"i2{i}") for i in range(NX)]
        i3 = [S.sb([128, 512], F32, f"i3{i}") for i in range(NX)]
        i4 = [S.sb([128, 512], F32, f"i4{i}") for i in range(NX)]
        i5 = [S.sb([128, 512], F32, f"i5{i}") for i in range(NX)]
    else:
        i1 = [S.sb([128, 512], F32, f"i1{i}") for i in range(NX)]
        zz = [S.sb([128, 2048], F32, f"zz{i}") for i in range(NX)]
        vgb = [S.sb([128, 512], BF16, f"vgb{i}") for i in range(NX)]
        svp = [S.ps([128, 512], F32, f"svp{i}") for i in range(1)]
    tp = [S.ps([128, 8, 128], BF16, f"tp{i}") for i in range(2)]
    yp = [S.ps([128, 512], F32, f"yp{i}") for i in range(4)]
    wkeys = [f"wbf{k}" for k in range(8)]

    def rstd_chain(s, col_var, col_out, ncol=1):
        S.dve(lambda e: e.tensor_scalar_add(mv[s][:, col_var:col_var + ncol], mv[s][:, col_var:col_var + ncol], eps),
              r=[f"mv{s}"], w=[f"mv{s}"])
        S.act(lambda e: e.sqrt(mv[s][:, col_var:col_var + ncol], mv[s][:, col_var:col_var + ncol]),
              r=[f"mv{s}"], w=[f"mv{s}"])
        S.dve(lambda e: e.reciprocal(mv[s][:, col_out:col_out + ncol], mv[s][:, col_var:col_var + ncol]),
              r=[f"mv{s}"], w=[f"mv{s}"])

    for i in range(nt):
        s = i % NX
        rows = slice(i * 128, (i + 1) * 128)
        S.dma(lambda e, s=s, rows=rows: e.dma_start(out=xt[s][:], in_=x[rows, :]), w=[f"xt{s}"], group=f"xt{s}")
        if kind == "even":
            S.dma(lambda e, s=s, rows=rows: e.dma_start(out=i1[s][:], in_=ogf[rows, :]), w=[f"i1{s}"], group=f"i1{s}")
            S.dma(lambda e, s=s, rows=rows: e.dma_start(out=i2[s][:], in_=ogb[rows, :]), w=[f"i2{s}"], group=f"i2{s}")
            S.dma(lambda e, s=s, rows=rows: e.dma_start(out=i3[s][:], in_=z[rows, 1056:1568]), w=[f"i3{s}"], group=f"i3{s}")
            S.dma(lambda e, s=s, rows=rows: e.dma_start(out=i4[s][:], in_=oml[rows, :]), w=[f"i4{s}"], group=f"i4{s}")
            S.dma(lambda e, s=s, rows=rows: e.dma_start(out=i5[s][:], in_=z[rows, 1984:2496]), w=[f"i5{s}"], group=f"i5{s}")
            S.dve(lambda e, s=s: e.tensor_tensor(i1[s][:], i1[s][:], i2[s][:], ALU.add), r=[f"i1{s}", f"i2{s}"], w=[f"i1{s}"])
            S.pool(lambda e, s=s: e.tensor_tensor(i2[s][:], i1[s][:], i1[s][:], ALU.mult), r=[f"i1{s}"], w=[f"i2{s}"])
            S.dve(lambda e, s=s: e.reduce_sum(mv[s][:, 0:4], i2[s][:].rearrange("p (h d) -> p h d", h=4), AX.X),
                  r=[f"i2{s}"], w=[f"mv{s}"])
            S.dve(lambda e, s=s: e.tensor_scalar(mv[s][:, 0:4], mv[s][:, 0:4], 1.0 / 128, None, ALU.mult),
                  r=[f"mv{s}"], w=[f"mv{s}"])
            rstd_chain(s, 0, 4, 4)
            S.act(lambda e, s=s: e.activation(i3[s][:], i3[s][:], AF.Silu), r=[f"i3{s}"], w=[f"i3{s}"])
            S.act(lambda e, s=s: e.activation(i5[s][:], i5[s][:], AF.Silu), r=[f"i5{s}"], w=[f"i5{s}"])
            for h in range(4):
                hs = slice(h * 128, (h + 1) * 128)
                S.dve(lambda e, s=s, h=h, hs=hs: e.scalar_tensor_tensor(
                    i1[s][:, hs], i1[s][:, hs], mv[s][:, 4 + h:5 + h], gn[:], ALU.mult, ALU.mult),
                    r=[f"i1{s}", f"mv{s}", "gn"], w=[f"i1{s}"])
            S.pool(lambda e, s=s: e.tensor_tensor(ab[s][:, 0:512], i1[s][:], i3[s][:], ALU.mult),
                   r=[f"i1{s}", f"i3{s}"], w=[f"ab{s}"])
            S.pool(lambda e, s=s: e.tensor_tensor(ab[s][:, 512:1024], i4[s][:], i5[s][:], ALU.mult),
                   r=[f"i4{s}", f"i5{s}"], w=[f"ab{s}"])
        else:
            S.dma(lambda e, s=s, rows=rows: e.dma_start(out=i1[s][:], in_=fr[rows, :]), w=[f"i1{s}"], group=f"i1{s}")
            S.dma(lambda e, s=s, rows=rows: e.dma_start(out=zz[s][:], in_=z[rows, 512:2560]), w=[f"zz{s}"], group=f"zz{s}")
            S.act(lambda e, s=s: e.activation(zz[s][:, 0:512], zz[s][:, 0:512], AF.Silu), r=[f"zz{s}"], w=[f"zz{s}"])
            S.act(lambda e, s=s: e.activation(zz[s][:, 1536:2048], zz[s][:, 1536:2048], AF.Silu), r=[f"zz{s}"], w=[f"zz{s}"])
            S.act(lambda e, s=s: e.activation(zz[s][:, 512:1536], zz[s][:, 512:1536], AF.Gelu), r=[f"zz{s}"], w=[f"zz{s}"])
            S.pool(lambda e, s=s: e.tensor_tensor(ab[s][:, 0:512], i1[s][:], zz[s][:, 0:512], ALU.mult),
                   r=[f"i1{s}", f"zz{s}"], w=[f"ab{s}"])
            for g in range(4):
                S.dve(lambda e, s=s, g=g: e.bn_stats(st[s][:, g, :], zz[s][:, 1024 + g * 128:1024 + (g + 1) * 128]),
                      r=[f"zz{s}"], w=[f"st{s}"])
                S.dve(lambda e, s=s, g=g: e.bn_aggr(mv[s][:, 2 * g:2 * g + 2], st[s][:, g:g + 1, :]),
                      r=[f"st{s}"], w=[f"mv{s}"])
            for g in range(4):
                rstd_chain(s, 2 * g + 1, 8 + g, 1)
            for g in range(4):
                S.dve(lambda e, s=s, g=g: e.tensor_scalar(
                    vgb[s][:, g * 128:(g + 1) * 128], zz[s][:, 1024 + g * 128:1024 + (g + 1) * 128],
                    mv[s][:, 2 * g:2 * g + 1], mv[s][:, 8 + g:9 + g], ALU.subtract, ALU.mult),
                    r=[f"zz{s}", f"mv{s}"], w=[f"vgb{s}"])
            for g in range(4):
                S.pe(lambda e, s=s, g=g: e.matmul(svp[0][:, g * 128:(g + 1) * 128], swb[:, g, :],
                                                  vgb[s][:, g * 128:(g + 1) * 128], start=True, stop=True),
                     r=[f"vgb{s}", "swb"], w=["svp0"])
            for g in range(4):
                gs = slice(g * 128, (g + 1) * 128)
                S.dve(lambda e, s=s, g=g, gs=gs: e.scalar_tensor_tensor(
                    zz[s][:, 512 + g * 128:512 + (g + 1) * 128], svp[0][:, gs], sbt[:, g:g + 1],
                    zz[s][:, 512 + g * 128:512 + (g + 1) * 128], ALU.add, ALU.mult),
                    r=["svp0", "sbt", f"zz{s}"], w=[f"zz{s}"])
            S.pool(lambda e, s=s: e.tensor_tensor(ab[s][:, 512:1024], zz[s][:, 512:1024], zz[s][:, 1536:2048], ALU.mult),
                   r=[f"zz{s}"], w=[f"ab{s}"])
        t = i % 2
        for k in range(8):
            S.pe(lambda e, s=s, t=t, k=k: e.transpose(tp[t][:, k, :], ab[s][:, k * 128:(k + 1) * 128], ident[:]),
                 r=[f"ab{s}", "ident"], w=[f"tp{t}"])
        S.act(lambda e, s=s, t=t: e.copy(aT[s][:], tp[t][:]), r=[f"tp{t}"], w=[f"aT{s}"])
        gi = 0 if i < n_lat_tiles else 1
        for c in range(2):
            p = (2 * i + c) % 4
            cs = slice(c * 512, (c + 1) * 512)
            for k in range(8):
                S.pe(lambda e, s=s, p=p, k=k, cs=cs: e.matmul(yp[p][:], aT[s][:, k, :], wbf[:, k, cs],
                                                             start=(k == 0), stop=(k == 7)),
                     r=[f"aT{s}", wkeys[k]], w=[f"yp{p}"])
            S.dve(lambda e, s=s, p=p, cs=cs, gi=gi: e.tensor_tensor(rr[s][:, cs], yp[p][:], vt[:, gi, cs], ALU.mult),
                  r=[f"yp{p}", "vt"], w=[f"rr{s}"])
        S.dve(lambda e, s=s: e.scalar_tensor_tensor(rr[s][:], xt[s][:], ALPHA, rr[s][:], ALU.mult, ALU.add),
               r=[f"xt{s}", f"rr{s}"], w=[f"rr{s}"])
        for j in range(2):
            S.dve(lambda e, s=s, j=j: e.bn_stats(st[s][:, 4 + j, :], rr[s][:, j * 512:(j + 1) * 512]),
                  r=[f"rr{s}"], w=[f"st{s}"])
        S.dve(lambda e, s=s: e.bn_aggr(mv[s][:, 12:14], st[s][:, 4:6, :]), r=[f"st{s}"], w=[f"mv{s}"])
        rstd_chain(s, 13, 14, 1)
        S.dve(lambda e, s=s: e.tensor_scalar(rr[s][:], rr[s][:], mv[s][:, 12:13], mv[s][:, 14:15],
                                             ALU.subtract, ALU.mult), r=[f"rr{s}", f"mv{s}"], w=[f"rr{s}"])
        S.pool(lambda e, s=s: e.tensor_tensor(rr[s][:], rr[s][:], vt[:, 2, :], ALU.mult), r=[f"rr{s}", "vt"], w=[f"rr{s}"])
        S.pool(lambda e, s=s: e.tensor_tensor(ot[s][:], rr[s][:], vt[:, 3, :], ALU.add), r=[f"rr{s}", "vt"], w=[f"ot{s}"])
        S.dma(lambda e, s=s, rows=rows: e.dma_start(out=out[rows, :], in_=ot[s][:]), r=[f"ot{s}"], w=[f"oo{s}"],
              group=f"oo{s}")
    S.emit()
    S.close()
    return nc


def fnet_consts():
    n1 = np.arange(128)[:, None, None].astype(np.float64)
    n2 = np.arange(64)[None, :, None].astype(np.float64)
    k1 = np.arange(128)[None, None, :].astype(np.float64)
    ang = 2 * np.pi * k1 * (64 * n1 + n2) / 8192.0
    TW = np.concatenate([np.cos(ang), -np.sin(ang)], -1).astype(np.float32)
    c = np.arange(128)[:, None].astype(np.float64)
    cp = np.arange(128)[None, :].astype(np.float64)
    a = 2 * np.pi * c * cp / 128.0
    Cc, Sc = np.cos(a), np.sin(a)
    FC = np.concatenate([Cc, -Sc, Sc, Cc], -1).astype(np.float32)
    W3 = np.zeros((64, 2, 2, 2, 64), np.float64)
    n2v = np.arange(64)[:, None]
    k2v = np.arange(64)[None, :]
    a3 = 2 * np.pi * n2v * k2v / 64.0
    for aa in range(2):
        W3[:, aa, 0, aa, :] = np.cos(a3)
        W3[:, aa, 1, aa, :] = np.sin(a3)
    W3 = W3.reshape(128, 256).astype(np.float32)
    n = np.arange(256)[:, None].astype(np.float64)
    k = np.arange(256)[None, :].astype(np.float64)
    ac = 2 * np.pi * n * k / 256.0
    TWc = np.concatenate([np.cos(ac), -np.sin(ac)], -1).reshape(2, 128, 512).transpose(1, 0, 2)
    TWc = np.ascontiguousarray(TWc).astype(np.float32)
    return dict(TW=TW, FC=FC, W3=W3, TWc=TWc)


def build_k7(with_ctx):
    nc = bass.Bass("TRN2", target_bir_lowering=False)
    dt = lambda name, shape, k="ExternalInput": nc.dram_tensor(name, list(shape), F32, kind=k).ap()
    f = dt("f", [8192, 128])
    TW = dt("TW", [128, 64, 256]); FC = dt("FC", [128, 512]); W3 = dt("W3", [128, 256])
    out = dt("out", [128, 8192], "ExternalOutput")
    if with_ctx:
        fc = dt("fc", [256, 128]); TWc = dt("TWc", [128, 2, 512])
        outc = dt("outc", [256, 128], "ExternalOutput")
    S = Sched(nc)
    st = [S.sb([128, 8, 256], F32, f"st{i}") for i in range(2)]
    TWb = S.sb([128, 64, 256], BF16, "TWb")
    fb = S.sb([128, 64, 128], BF16, "fb")
    Y = S.sb([128, 2, 64, 128], BF16, "Y")
    U = S.sb([128, 64, 256], BF16, "U")
    FCb = S.sb([128, 512], BF16, "FCb")
    W3b = S.sb([128, 256], BF16, "W3b")
    frT = S.sb([128, 8192], F32, "frT")
    ps = [S.ps([128, 512], F32, f"ps{i}") for i in range(4)]
    for i in range(8):
        s = i % 2
        S.dma(lambda e, i=i, s=s: e.dma_start(out=st[s][:], in_=TW[:, i * 8:(i + 1) * 8, :]), w=[f"st{s}"], group=f"st{s}")
        S.add("dve" if i % 2 == 0 else "pool",
              lambda e, i=i, s=s: e.tensor_copy(TWb[:, i * 8:(i + 1) * 8, :], st[s][:]), r=[f"st{s}"], w=["TWb"])
    for i in range(4):
        s = i % 2
        S.dma(lambda e, i=i, s=s: e.dma_start(
            out=st[s][:].rearrange("p a b -> p (a b)"),
            in_=f.rearrange("(n1 n2) c -> n1 (n2 c)", n2=64)[:, i * 2048:(i + 1) * 2048]), w=[f"st{s}"], group=f"st{s}")
        S.add("dve" if i % 2 == 0 else "pool",
              lambda e, i=i, s=s: e.tensor_copy(fb[:, i * 16:(i + 1) * 16, :].rearrange("p a b -> p (a b)"),
                                                st[s][:].rearrange("p a b -> p (a b)")), r=[f"st{s}"], w=["fb"])
    S.dma(lambda e: e.dma_start(out=st[0][:, 0:2, :].rearrange("p a b -> p (a b)"), in_=FC), w=["st0"], group="st0")
    S.dve(lambda e: e.tensor_copy(FCb[:], st[0][:, 0:2, :].rearrange("p a b -> p (a b)")), r=["st0"], w=["FCb"])
    S.dma(lambda e: e.dma_start(out=st[1][:, 0, :], in_=W3), w=["st1"], group="st1")
    S.dve(lambda e: e.tensor_copy(W3b[:], st[1][:, 0, :]), r=["st1"], w=["W3b"])
    pi = 0
    for n2 in range(0, 64, 2):
        p = pi % 4; pi += 1
        for d in range(2):
            S.pe(lambda e, p=p, n2=n2, d=d: e.matmul(ps[p][:, d * 256:(d + 1) * 256], fb[:, n2 + d, :], TWb[:, n2 + d, :],
                                                     start=True, stop=True), r=["fb", "TWb"], w=[f"ps{p}"])
        for d in range(2):
            src = lambda p=p, d=d: ps[p][:, d * 256:(d + 1) * 256].rearrange("p (r j a) -> p r j a", r=2, a=2)
            dst = lambda n2=n2, d=d: Y[:, :, :, 2 * (n2 + d):2 * (n2 + d) + 2]
            if d == 0:
                S.act(lambda e, src=src, dst=dst: e.copy(dst(), src()), r=[f"ps{p}"], w=["Y"])
            else:
                S.dve(lambda e, src=src, dst=dst: e.tensor_copy(dst(), src()), r=[f"ps{p}"], w=["Y"])
    for j in range(0, 64, 2):
        p = pi % 4; pi += 1
        for d in range(2):
            jj = j + d
            S.pe(lambda e, p=p, jj=jj, d=d: e.matmul(ps[p][:, d * 256:(d + 1) * 256], Y[:, 0, jj, :],
                                                     FCb[:, 0:256], start=True, stop=False), r=["Y", "FCb"], w=[f"ps{p}"])
            S.pe(lambda e, p=p, jj=jj, d=d: e.matmul(ps[p][:, d * 256:(d + 1) * 256], Y[:, 1, jj, :],
                                                     FCb[:, 256:512], start=False, stop=True), r=["Y", "FCb"], w=[f"ps{p}"])
        if (j // 2) % 2 == 0:
            S.act(lambda e, p=p, j=j: e.copy(U[:, j:j + 2, :].rearrange("p a b -> p (a b)"), ps[p][:]), r=[f"ps{p}"], w=["U"])
        else:
            S.dve(lambda e, p=p, j=j: e.tensor_copy(U[:, j:j + 2, :].rearrange("p a b -> p (a b)"), ps[p][:]), r=[f"ps{p}"], w=["U"])
    scale = 1.0 / 1024.0
    frv = frT[:].rearrange("p (k2 m) -> p k2 m", m=128)
    for j0 in range(0, 64, 4):
        p = pi % 4; pi += 1
        for d in range(4):
            jj = j0 + d
            S.pe(lambda e, p=p, jj=jj, d=d: e.matmul(ps[p][:, d * 128:(d + 1) * 128], U[:, jj, 0:128], W3b[:, 0:128],
                                                     start=True, stop=False), r=["U", "W3b"], w=[f"ps{p}"])
            S.pe(lambda e, p=p, jj=jj, d=d: e.matmul(ps[p][:, d * 128:(d + 1) * 128], U[:, jj, 128:256], W3b[:, 128:256],
                                                     start=False, stop=True), r=["U", "W3b"], w=[f"ps{p}"])
        S.dve(lambda e, p=p, j0=j0: e.tensor_scalar(
            frv[:, :, 2 * j0:2 * j0 + 8], ps[p][:].rearrange("p (m k2) -> p k2 m", k2=64), scale, None, ALU.mult),
            r=[f"ps{p}"], w=["frT"])
    for i in range(4):
        S.dma(lambda e, i=i: e.dma_start(out=out[:, i * 2048:(i + 1) * 2048], in_=frT[:, i * 2048:(i + 1) * 2048]),
              r=["frT"], w=[f"fo{i}"], group=f"fo{i}")
    if with_ctx:
        fcf = S.sb([128, 2, 128], F32, "fcf")
        fcb = S.sb([128, 2, 128], BF16, "fcb")
        TWcb = S.sb([128, 2, 512], BF16, "TWcb")
        Yc = S.sb([128, 512], BF16, "Yc")
        oc = S.sb([128, 2, 128], F32, "oc")
        S.dma(lambda e: e.dma_start(out=fcf[:], in_=fc.rearrange("(t p) c -> p t c", p=128)), w=["fcf"], group="fcf")
        S.dve(lambda e: e.tensor_copy(fcb[:], fcf[:]), r=["fcf"], w=["fcb"])
        for t in range(2):
            S.dma(lambda e, t=t: e.dma_start(out=st[t][:, 0:2, :].rearrange("p a b -> p (a b)"), in_=TWc[:, t, :]),
                  w=[f"st{t}"], group=f"st{t}")
            S.dve(lambda e, t=t: e.tensor_copy(TWcb[:, t, :], st[t][:, 0:2, :].rearrange("p a b -> p (a b)")),
                  r=[f"st{t}"], w=["TWcb"])
        p = pi % 4; pi += 1
        for t in range(2):
            S.pe(lambda e, p=p, t=t: e.matmul(ps[p][:], fcb[:, t, :], TWcb[:, t, :], start=(t == 0), stop=(t == 1)),
                 r=["fcb", "TWcb"], w=[f"ps{p}"])
        S.dve(lambda e, p=p: e.tensor_copy(Yc[:], ps[p][:]), r=[f"ps{p}"], w=["Yc"])
        p = pi % 4; pi += 1
        for kt in range(2):
            S.pe(lambda e, p=p, kt=kt: e.matmul(ps[p][:, kt * 128:(kt + 1) * 128], Yc[:, kt * 128:(kt + 1) * 128],
                                                FCb[:, 0:128], start=True, stop=False), r=["Yc", "FCb"], w=[f"ps{p}"])
            S.pe(lambda e, p=p, kt=kt: e.matmul(ps[p][:, kt * 128:(kt + 1) * 128], Yc[:, 256 + kt * 128:256 + (kt + 1) * 128],
                                                FCb[:, 256:384], start=False, stop=True), r=["Yc", "FCb"], w=[f"ps{p}"])
        S.dve(lambda e, p=p: e.tensor_scalar(oc[:].rearrange("p a b -> p (a b)"), ps[p][:, 0:256],
                                             1.0 / np.sqrt(256.0 * 128.0), None, ALU.mult), r=[f"ps{p}"], w=["oc"])
        S.dma(lambda e: e.dma_start(out=outc.rearrange("(t p) c -> p t c", p=128), in_=oc[:]), r=["oc"], w=["oco"], group="oco")
    S.emit()
    S.close()
    return nc


_PROG = {}
NCORES = 8
EYE = np.eye(128, dtype=np.float32)


def _prog(key, builder):
    if key not in _PROG:
        _PROG[key] = builder()
    return _PROG[key]


def _run(key, builder, in_maps):
    nc = _prog(key, builder)
    res = run_bass_kernel_spmd(nc, in_maps, core_ids=list(range(NCORES)))
    return res.results


def _c(a):
    return np.ascontiguousarray(a, dtype=np.float32)


def kernel(x, c, ctx, c_ctx, ada_w, ada_b, post_ln_g, post_ln_b, even_w_in, gla_w2, gla_b, gla_norm_g,
           mla_q_norm_g, mla_w_uq, mla_kv_norm_g, mla_w_ukv, even_w_out, odd_w_in, sgu_w, sgu_b, odd_w_out):
    f32 = np.float32
    x_cur = np.asarray(x, f32)
    ctx_cur = np.asarray(ctx, f32)
    B, SEQ, D = x_cur.shape
    L = ctx_cur.shape[1]
    Q = SEQ // 4
    T = Q + L
    DEPTH = ada_w.shape[0]
    cin = np.zeros((128, D), f32)
    cin[0] = c[0]; cin[1] = c[1]; cin[2] = c_ctx
    HN = 3 * D // 2
    maps = [dict(x=cin, w=_c(ada_w[j // 2][:, (j % 2) * HN:(j % 2 + 1) * HN]),
                 bias=_c(ada_b[j // 2][None, (j % 2) * HN:(j % 2 + 1) * HN]), ident=EYE) for j in range(NCORES)]
    r = _run("k0", lambda: build_k1(128, D, HN, "silu"), maps)
    mods = [np.concatenate([r[2 * l]["z"][0:3], r[2 * l + 1]["z"][0:3]], 1).reshape(3, 3, D)
            for l in range(DEPTH)]
    qpos = [np.concatenate([np.arange(Q) + Q * i, -np.ones(L, int)]) for i in range(4)]
    csq = [rope_tables(p) for p in qpos]
    csk = rope_tables(np.concatenate([-np.ones(L, int), np.arange(SEQ)]))
    gcst = gla_consts()
    fcst = fnet_consts()
    for l in range(DEPTH):
        li = l // 2
        even = (l % 2 == 0)
        need_ctx = any(jj % 2 == 0 for jj in range(l + 1, DEPTH))
        X = [np.concatenate([x_cur[j // 4, (j % 4) * Q:(j % 4 + 1) * Q], ctx_cur[j // 4]], 0) for j in range(NCORES)]
        modv = [_c(np.stack([mods[l][j // 4][0], mods[l][j // 4][1], mods[l][2][0], mods[l][2][1]])) for j in range(NCORES)]
        w_in = _c(even_w_in[li] if even else odd_w_in[li])
        N = w_in.shape[1]
        maps = [dict(x=X[j], w=w_in, mod=modv[j], ident=EYE) for j in range(NCORES)]
        r = _run(("k1ln", N), lambda: build_k1(T, D, N, "ln", n_lat_tiles=Q // 128), maps)
        Z = [r[j]["z"] for j in range(NCORES)]
        vecs = [_c(np.stack([mods[l][j // 4][2], mods[l][2][2], post_ln_g[l], post_ln_b[l]])) for j in range(NCORES)]

        def gather(cols):
            out = []
            for b in range(B):
                out.append(np.concatenate([Z[4 * b][Q:, cols]] + [Z[4 * b + i][:Q, cols] for i in range(4)], 0))
            return out
        if even:
            maps = [dict(x=_c(Z[j][:, 1568:1824]), w=_c(mla_w_uq[li]), g=_c(mla_q_norm_g[li][None]), ident=EYE)
                    for j in range(NCORES)]
            r = _run("k1q", lambda: build_k1(T, 256, 768, "rms"), maps)
            Qp = [r[j]["z"] for j in range(NCORES)]
            maps = [dict(x=_c(Z[j][:, 1824:1952]), w=_c(mla_w_ukv[li]), g=_c(mla_kv_norm_g[li][None]), ident=EYE)
                    for j in range(NCORES)]
            r = _run("k1kv", lambda: build_k1(T, 128, 1024, "rms"), maps)
            KV = [r[j]["z"] for j in range(NCORES)]
            kv_all = [np.concatenate([KV[4 * b][Q:]] + [KV[4 * b + i][:Q] for i in range(4)], 0) for b in range(B)]
            kr_all = gather(slice(1952, 1984))
            maps = [dict(q=Qp[j], kv=_c(kv_all[j // 4]), kr=_c(kr_all[j // 4]), csq=csq[j % 4], csk=csk, ident=EYE)
                    for j in range(NCORES)]
            r = _run("k3", lambda: build_k3(Q, L, SEQ + L), maps)
            OML = [r[j]["out"] for j in range(NCORES)]
            gq = gather(slice(0, 256)); gk = gather(slice(256, 512)); gv = gather(slice(512, 1024))
            glr = gather(slice(1024, 1056))

            def scan_order(a, d):
                if d == 0:
                    return a
                return np.concatenate([a[:L][::-1], a[L:][::-1]], 0)
            maps = []
            for j in range(NCORES):
                b, h = j // 4, j % 4
                qh = gq[b][:, h * 64:(h + 1) * 64]; kh = gk[b][:, h * 64:(h + 1) * 64]
                vh = gv[b][:, h * 128:(h + 1) * 128]
                m = dict(cst=gcst)
                m["qT"] = _c(np.stack([scan_order(qh, d).T for d in range(2)]))
                m["kT"] = _c(np.stack([scan_order(kh, d).T for d in range(2)]))
                m["ktok"] = _c(np.stack([scan_order(kh, d) for d in range(2)]))
                m["vtok"] = _c(np.stack([scan_order(vh, d) for d in range(2)]))
                m["glrT"] = _c(np.stack([scan_order(glr[b][:, 16 * d:16 * (d + 1)], d).T for d in range(2)]))
                m["w2"] = _c(np.stack([gla_w2[li][d][:, h * 64:(h + 1) * 64] for d in range(2)]))
                m["bb"] = _c(np.stack([gla_b[li][d][None, h * 64:(h + 1) * 64] for d in range(2)]))
                maps.append(m)
            r = _run("k4", lambda: build_k4(SEQ + L), maps)
            ogf_all = [np.zeros((SEQ + L, 512), f32) for _ in range(B)]
            ogb_all = [np.zeros((SEQ + L, 512), f32) for _ in range(B)]
            for j in range(NCORES):
                b, h = j // 4, j % 4
                o = r[j]["out"]
                ogf_all[b][:, h * 128:(h + 1) * 128] = o[0]
                ogb_all[b][:, h * 128:(h + 1) * 128] = scan_order(o[1], 1)

            def tok(a_all, j):
                b, i = j // 4, j % 4
                return _c(np.concatenate([a_all[b][L + Q * i:L + Q * (i + 1)], a_all[b][:L]], 0))
            maps = [dict(x=X[j], w=_c(even_w_out[li]), vecs=vecs[j], ident=EYE, ogf=tok(ogf_all, j), ogb=tok(ogb_all, j),
                         oml=OML[j], z=Z[j], gng=_c(gla_norm_g[li][None])) for j in range(NCORES)]
            r = _run("k5e", lambda: build_k5(T, "even", Q // 128), maps)
        else:
            maps = []
            for j in range(NCORES):
                b, g = j // 4, j % 4
                cols = slice(g * 128, (g + 1) * 128)
                fl = np.concatenate([Z[4 * b + i][:Q, cols] for i in range(4)], 0)
                maps.append(dict(f=_c(fl), fc=_c(Z[4 * b][Q:, cols]), **fcst))
            r = _run("k7", lambda: build_k7(True), maps)
            fr_all = [np.zeros((SEQ, 512), f32) for _ in range(B)]
            frc = [np.zeros((L, 512), f32) for _ in range(B)]
            for j in range(NCORES):
                b, g = j // 4, j % 4
                fr_all[b][:, g * 128:(g + 1) * 128] = r[j]["out"].T
                frc[b][:, g * 128:(g + 1) * 128] = r[j]["outc"]
            maps = [dict(x=X[j], w=_c(odd_w_out[li]), vecs=vecs[j], ident=EYE,
                         fr=_c(np.concatenate([fr_all[j // 4][Q * (j % 4):Q * (j % 4 + 1)], frc[j // 4]], 0)),
                         z=Z[j], swT=_c(np.transpose(sgu_w[li], (0, 2, 1))), sbT=_c(np.asarray(sgu_b[li]).T))
                    for j in range(NCORES)]
            r = _run("k5o", lambda: build_k5(T, "odd", Q // 128), maps)
        x_new = np.empty_like(x_cur)
        for j in range(NCORES):
            x_new[j // 4, (j % 4) * Q:(j % 4 + 1) * Q] = r[j]["out"][:Q]
        x_cur = x_new
        if need_ctx:
            ctx_cur = np.stack([r[4 * b]["out"][Q:] for b in range(B)])
    return x_cur.astype(np.float32)
```

```python
import contextlib
import numpy as np
import concourse.bass as bass
import concourse.mybir as mybir
from concourse.bass_utils import run_bass_kernel_spmd

F32 = mybir.dt.float32
BF16 = mybir.dt.bfloat16
AF = mybir.ActivationFunctionType
ALU = mybir.AluOpType
AX = mybir.AxisListType
ALPHA = 8 ** 0.25
MLA_SCALE = 96 ** -0.5
ARENA_F32 = 52900


class Sched:
    def __init__(self, nc):
        self.nc = nc
        self.ops = []
        self.last_w = {}
        self.readers = {}
        self.stack = contextlib.ExitStack()
        self.bar = set()
        self.pending = {}

    def barrier(self):
        last = {}
        for i, op in enumerate(self.ops):
            k = ("dma", op["dma"]) if op["dma"] is not None else ("eng", op["eng"])
            last[k] = i
        self.bar = set(last.values())
        self.pending = {e: True for e in ["pe", "act", "dve", "pool", "sp"]}
        self.last_w = {}
        self.readers = {}

    muted = False

    def add(self, eng, fn, r=(), w=(), dma=None, inc=16):
        if self.muted:
            return -1
        idx = len(self.ops)
        deps = set()
        for k in r:
            if k in self.last_w:
                deps.add(self.last_w[k])
        for k in w:
            if k in self.last_w:
                deps.add(self.last_w[k])
            for x in self.readers.get(k, ()):
                deps.add(x)
        if self.pending.get(eng):
            deps |= self.bar
            self.pending[eng] = False
        deps.discard(idx)
        self.ops.append(dict(eng=eng, fn=fn, deps=deps, dma=dma, inc=inc))
        for k in r:
            self.readers.setdefault(k, []).append(idx)
        for k in w:
            self.last_w[k] = idx
            self.readers[k] = []
        return idx

    def pe(self, fn, r=(), w=()):
        return self.add("pe", fn, r, w)

    def act(self, fn, r=(), w=()):
        return self.add("act", fn, r, w)

    def dve(self, fn, r=(), w=()):
        return self.add("dve", fn, r, w)

    def pool(self, fn, r=(), w=()):
        return self.add("pool", fn, r, w)

    def dma(self, fn, r=(), w=(), group=None, eng="sp", inc=16):
        assert group is not None
        return self.add(eng, fn, r, w, dma=group, inc=inc)

    def emit(self):
        nc = self.nc
        ops = self.ops
        n = len(ops)
        needs_signal = [False] * n
        for i, op in enumerate(ops):
            keep = set()
            for d in op["deps"]:
                dop = ops[d]
                if dop["dma"] is None and dop["eng"] == op["eng"] and op["eng"] == "pe":
                    continue
                keep.add(d)
                needs_signal[d] = True
            op["deps"] = keep
        engs = ["pe", "act", "dve", "pool", "sp"]
        sems = {e: self.stack.enter_context(nc.semaphore(f"s_{e}")) for e in engs}
        cnt = {e: 0 for e in engs}
        groups = {}
        gcnt = {}
        for i, op in enumerate(ops):
            if op["dma"] is not None:
                g = op["dma"]
                if g not in groups:
                    groups[g] = self.stack.enter_context(nc.semaphore(f"d_{len(groups)}"))
                    gcnt[g] = 0
                op["sem"] = groups[g]
                gcnt[g] += op["inc"] * getattr(op["fn"], "ndma", 1)
                op["val"] = gcnt[g]
            elif needs_signal[i]:
                cnt[op["eng"]] += 1
                op["sem"] = sems[op["eng"]]
                op["val"] = cnt[op["eng"]]
        print("sched: ops", n, "dma groups", len(groups), "sem counts", cnt, flush=True)
        final = dict((g, (groups[g], gcnt[g])) for g in groups)

        def stream(ename):
            def body(eng):
                known = {}
                for i, op in enumerate(ops):
                    if op["eng"] != ename:
                        continue
                    for d in sorted(op["deps"]):
                        dop = ops[d]
                        s, v = dop["sem"], dop["val"]
                        if known.get(id(s), 0) < v:
                            eng.wait_ge(s, v)
                            known[id(s)] = v
                    ins = op["fn"](eng)
                    if op["dma"] is not None:
                        if not isinstance(ins, (list, tuple)):
                            ins = [ins]
                        assert len(ins) == getattr(op["fn"], "ndma", 1)
                        for x in ins:
                            x.then_inc(op["sem"], op["inc"])
                    elif needs_signal[i]:
                        ins.then_inc(op["sem"], 1)
                if ename == "sp":
                    for g, (s, v) in final.items():
                        if known.get(id(s), 0) < v:
                            eng.wait_ge(s, v)
            return body

        with nc.Block() as block:
            block.tensor(stream("pe"))
            block.scalar(stream("act"))
            block.vector(stream("dve"))
            block.gpsimd(stream("pool"))
            block.sync(stream("sp"))

    def close(self):
        self.stack.close()


class G:
    def __init__(self, nc):
        self.nc = nc
        self.S = Sched(nc)
        self.arena = self.S.stack.enter_context(nc.sbuf_tensor("arena", [128, ARENA_F32], F32))
        self.banks = [self.S.stack.enter_context(nc.psum_tensor(f"bank{i}", [128, 512], F32)) for i in range(8)]
        self.off = 0
        self.nd = 0

    nphase = 0
    max_phase = 10 ** 9

    def phase(self):
        self.nphase += 1
        if self.nphase > self.max_phase:
            self.S.muted = True
        self.S.barrier()
        self.off = 0

    def sb(self, shape, dtype=F32, name=None):
        P = shape[0]
        n = int(np.prod(shape[1:]))
        words = n if dtype == F32 else (n + 1) // 2
        o = self.off
        self.off += words
        assert self.off <= ARENA_F32, ("SBUF arena overflow", self.off)
        ap = self.arena[0:P, o:o + words]
        if dtype != F32:
            ap = ap.bitcast(dtype)[:, 0:n]
        if len(shape) == 3:
            ap = ap.rearrange("p (a b) -> p a b", a=shape[1], b=shape[2])
        elif len(shape) == 4:
            ap = ap.rearrange("p (a b c) -> p a b c", a=shape[1], b=shape[2], c=shape[3])
        return ap

    def ps(self, bank, shape, dtype=F32):
        P = shape[0]
        n = int(np.prod(shape[1:]))
        ap = self.banks[bank][0:P, :]
        if dtype != F32:
            ap = ap.bitcast(dtype)
        ap = ap[:, 0:n]
        if len(shape) == 3:
            ap = ap.rearrange("p (a b) -> p a b", a=shape[1], b=shape[2])
        return ap

    def dram(self, name, shape, kind="Internal"):
        return self.nc.dram_tensor(name, list(shape), F32, kind=kind)

    def finish(self):
        self.S.emit()
        self.S.close()


def load_ident(g, tag="id"):
    S = g.S
    identf = g.sb([128, 128], F32)
    ident = g.sb([128, 128], BF16)
    S.dma(lambda e: e.dma_start(out=identf[:], in_=g.ident_d), w=["identf"], group="identf")
    S.dve(lambda e: e.tensor_copy(ident[:], identf[:]), r=["identf"], w=["ident"])
    return identf, ident


def dcopy(g, dst, src, name, grp="dcopy"):
    g.S.dma(lambda e: e.dma_start(out=dst, in_=src), w=[name], group=grp)


def allgather(g, dst, src, name, r=()):
    groups = [[0, 1, 2, 3], [4, 5, 6, 7]]
    g.S.dma(lambda e: e.collective_compute("AllGather", ALU.bypass, replica_groups=groups,
                                            ins=[src.ap().opt()], outs=[dst.ap().opt()]),
            r=list(r), w=[name], group="cc", eng="pool", inc=1)


def emit_k1(g, xsrc, nt, K, N, w, z, mode, n_lat_tiles=None, mods=None, gvec=None, bias=None, eps=1e-6):
    S = g.S
    g.phase()
    kc = K // 128
    nch = (N + 511) // 512
    identf, ident = load_ident(g)
    wbf = g.sb([128, kc, N], BF16)
    wst = [g.sb([128, N], F32) for i in range(2)]
    if mode == "ln":
        modt = g.sb([128, 4, K], F32)
        for a in range(4):
            S.dma(lambda e, a=a: e.dma_start(out=modt[:, a, :], in_=mods[a].partition_broadcast(128)),
                  w=["modt"], group="modt")
        for a in (1, 3):
            S.dve(lambda e, a=a: e.tensor_scalar_add(modt[:, a, :], modt[:, a, :], 1.0), r=["modt"], w=["modt"])
    elif mode == "rms":
        gt = g.sb([128, K], F32)
        S.dma(lambda e: e.dma_start(out=gt[:], in_=gvec.partition_broadcast(128)), w=["gt"], group="gt")
    else:
        bt = g.sb([128, N], F32)
        S.dma(lambda e: e.dma_start(out=bt[:], in_=bias.partition_broadcast(128)), w=["bt"], group="gt")
    for k in range(kc):
        s = k % 2
        S.dma(lambda e, k=k, s=s: e.dma_start(out=wst[s][:], in_=w[k * 128:(k + 1) * 128, :]),
              w=[f"wst{s}"], group=f"wst{s}")
        if k % 2 == 0:
            S.act(lambda e, k=k, s=s: e.copy(wbf[:, k, :], wst[s][:]), r=[f"wst{s}"], w=[f"wbf{k}"])
        else:
            S.dve(lambda e, k=k, s=s: e.tensor_copy(wbf[:, k, :], wst[s][:]), r=[f"wst{s}"], w=[f"wbf{k}"])
    NX = 2
    xt = [g.sb([128, K], F32) for i in range(NX)]
    xn = [g.sb([128, K], F32) for i in range(NX)]
    hb = [g.sb([128, K], BF16) for i in range(NX)]
    hT = [g.sb([128, kc, 128], BF16) for i in range(NX)]
    st = [g.sb([128, 8, 6], F32) for i in range(NX)]
    mv = [g.sb([128, 4], F32) for i in range(NX)]
    zt = [g.sb([128, N], F32) for i in range(NX)]
    tp = [g.ps(i, [128, kc, 128], BF16) for i in range(2)]
    zp = [g.ps(2 + i, [128, 512], F32) for i in range(4)]
    wkeys = [f"wbf{k}" for k in range(kc)]
    zpi = 0
    for i in range(nt):
        s = i % NX
        S.dma(lambda e, i=i, s=s: e.dma_start(out=xt[s][:], in_=xsrc(i)), w=[f"xt{s}"], group=f"xt{s}")
        if mode == "ln":
            nsub = max(1, K // 512)
            fs = K // nsub
            for j in range(nsub):
                S.dve(lambda e, s=s, j=j, fs=fs: e.bn_stats(st[s][:, j, :], xt[s][:, j * fs:(j + 1) * fs]),
                      r=[f"xt{s}"], w=[f"st{s}"])
            S.dve(lambda e, s=s, nsub=nsub: e.bn_aggr(mv[s][:, 0:2], st[s][:, 0:nsub, :]), r=[f"st{s}"], w=[f"mv{s}"])
            S.dve(lambda e, s=s: e.tensor_scalar_add(mv[s][:, 3:4], mv[s][:, 1:2], eps), r=[f"mv{s}"], w=[f"mv{s}"])
            S.act(lambda e, s=s: e.sqrt(mv[s][:, 3:4], mv[s][:, 3:4]), r=[f"mv{s}"], w=[f"mv{s}"])
            S.dve(lambda e, s=s: e.reciprocal(mv[s][:, 2:3], mv[s][:, 3:4]), r=[f"mv{s}"], w=[f"mv{s}"])
            S.dve(lambda e, s=s: e.tensor_scalar(xn[s][:], xt[s][:], mv[s][:, 0:1], mv[s][:, 2:3],
                                                 ALU.subtract, ALU.mult),
                  r=[f"xt{s}", f"mv{s}"], w=[f"xn{s}"])
            a = 0 if (n_lat_tiles is None or i < n_lat_tiles) else 2
            S.pool(lambda e, s=s, a=a: e.tensor_tensor(xn[s][:], xn[s][:], modt[:, a + 1, :], ALU.mult),
                   r=[f"xn{s}", "modt"], w=[f"xn{s}"])
            S.pool(lambda e, s=s, a=a: e.tensor_tensor(hb[s][:], xn[s][:], modt[:, a, :], ALU.add),
                   r=[f"xn{s}", "modt"], w=[f"hb{s}"])
        elif mode == "silu":
            S.act(lambda e, s=s: e.activation(hb[s][:], xt[s][:], AF.Silu), r=[f"xt{s}"], w=[f"hb{s}"])
        else:
            S.act(lambda e, s=s: e.activation(xn[s][:], xt[s][:], AF.Square, accum_out=mv[s][:, 0:1]),
                  r=[f"xt{s}"], w=[f"xn{s}", f"mv{s}"])
            S.dve(lambda e, s=s: e.tensor_scalar(mv[s][:, 1:2], mv[s][:, 0:1], 1.0 / K, eps, ALU.mult, ALU.add),
                  r=[f"mv{s}"], w=[f"mv{s}"])
            S.act(lambda e, s=s: e.sqrt(mv[s][:, 3:4], mv[s][:, 1:2]), r=[f"mv{s}"], w=[f"mv{s}"])
            S.dve(lambda e, s=s: e.reciprocal(mv[s][:, 2:3], mv[s][:, 3:4]), r=[f"mv{s}"], w=[f"mv{s}"])
            S.dve(lambda e, s=s: e.scalar_tensor_tensor(hb[s][:], xt[s][:], mv[s][:, 2:3], gt[:],
                                                        ALU.mult, ALU.mult),
                  r=[f"xt{s}", f"mv{s}", "gt"], w=[f"hb{s}"])
        t = i % 2
        for k in range(kc):
            S.pe(lambda e, s=s, t=t, k=k: e.transpose(tp[t][:, k, :], hb[s][:, k * 128:(k + 1) * 128], ident[:]),
                 r=[f"hb{s}", "ident"], w=[f"tp{t}"])
        S.act(lambda e, s=s, t=t: e.copy(hT[s][:], tp[t][:]), r=[f"tp{t}"], w=[f"hT{s}"])
        for c in range(nch):
            c0, c1 = c * 512, min(N, (c + 1) * 512)
            p = zpi % 4
            zpi += 1
            for k in range(kc):
                S.pe(lambda e, s=s, p=p, k=k, c0=c0, c1=c1: e.matmul(
                    zp[p][:, 0:c1 - c0], hT[s][:, k, :], wbf[:, k, c0:c1], start=(k == 0), stop=(k == kc - 1)),
                    r=[f"hT{s}", wkeys[k]], w=[f"zp{p}"])
            if mode == "silu":
                S.dve(lambda e, s=s, p=p, c0=c0, c1=c1: e.tensor_tensor(zt[s][:, c0:c1], zp[p][:, 0:c1 - c0], bt[:, c0:c1], ALU.add),
                      r=[f"zp{p}", "bt"], w=[f"zt{s}"])
            elif c % 2 == 0:
                S.dve(lambda e, s=s, p=p, c0=c0, c1=c1: e.tensor_copy(zt[s][:, c0:c1], zp[p][:, 0:c1 - c0]),
                      r=[f"zp{p}"], w=[f"zt{s}"])
            else:
                S.act(lambda e, s=s, p=p, c0=c0, c1=c1: e.copy(zt[s][:, c0:c1], zp[p][:, 0:c1 - c0]),
                      r=[f"zp{p}"], w=[f"zt{s}"])
        S.dma(lambda e, i=i, s=s: e.dma_start(out=z(i), in_=zt[s][:]),
              r=[f"zt{s}"], w=[f"zout{s}"], group=f"zo{s}")


def emit_k3(g, NQL, NQC, NK, q, kv, kr, csq, csk, out, nheads=8):
    S = g.S
    g.phase()
    NQ = NQL + NQC
    nqt = NQ // 128
    nkt = NK // 128
    identf, ident = load_ident(g)
    KH = (nkt + 1) // 2
    kvh = g.sb([128, KH, 128], F32)
    kpad = g.sb([128, nkt, 128], BF16)
    vx = [g.sb([128, nkt, 65], BF16) for i in range(2)]
    kT = [g.sb([128, nkt * 128], BF16) for i in range(2)]
    qT = [g.sb([128, nqt * 128], BF16) for i in range(2)]
    qh = g.sb([128, nqt, 96], F32)
    qpad = g.sb([128, nqt, 128], BF16)
    krl = g.sb([128, nkt, 32], F32)
    cskt = g.sb([128, nkt, 32], F32)
    csqt = g.sb([128, nqt, 32], F32)
    tk = [g.sb([128, nkt, 16], F32) for i in range(2)]
    tq = [g.sb([128, nqt, 16], F32) for i in range(2)]
    pT = [g.sb([128, 512], BF16) for i in range(3)]
    oTs = [g.sb([65, 512], F32) for i in range(2)]
    ost = [g.sb([128, 64], F32) for i in range(4)]
    rc = [g.sb([128, 1], F32) for i in range(4)]
    sTp = [g.ps(i, [128, 512], F32) for i in range(3)]
    oTp = [g.ps(3 + i, [65, 512], F32) for i in range(2)]
    tpk = g.ps(5, [128, 8, 128], BF16)
    tpo = g.ps(6, [128, 65], F32)

    S.pool(lambda e: e.memset(kpad[:], 0.0), w=["kpad"])
    S.pool(lambda e: e.memset(qpad[:], 0.0), w=["qpad"])
    for i in range(2):
        S.pool(lambda e, i=i: e.memset(vx[i][:], 1.0), w=[f"vx{i}"])
    S.dma(lambda e: e.dma_start(out=krl[:], in_=kr.rearrange("(t p) c -> p t c", p=128)), w=["krl"], group="krl")
    S.dma(lambda e: e.dma_start(out=cskt[:], in_=csk.rearrange("(t p) c -> p t c", p=128)), w=["cskt"], group="cskt")
    S.dma(lambda e: e.dma_start(out=csqt[:], in_=csq.rearrange("(t p) c -> p t c", p=128)), w=["csqt"], group="csqt")

    def rope(eng_a, eng_b, src, cs, tmp, dst, keys_r, key_tmp, key_dst, xo):
        x1 = lambda: src[:, :, xo:xo + 16]
        x2 = lambda: src[:, :, xo + 16:xo + 32]
        c = lambda: cs[:, :, 0:16]
        sn = lambda: cs[:, :, 16:32]
        S.add(eng_a, lambda e: e.tensor_tensor(tmp[0][:], x1(), c(), ALU.mult), r=keys_r, w=[key_tmp + "0"])
        S.add(eng_b, lambda e: e.tensor_tensor(tmp[1][:], x2(), sn(), ALU.mult), r=keys_r, w=[key_tmp + "1"])
        S.add(eng_a, lambda e: e.tensor_tensor(dst[:, :, 64:80], tmp[0][:], tmp[1][:], ALU.subtract),
              r=[key_tmp + "0", key_tmp + "1"], w=[key_dst])
        S.add(eng_a, lambda e: e.tensor_tensor(tmp[0][:], x1(), sn(), ALU.mult), r=keys_r, w=[key_tmp + "0"])
        S.add(eng_b, lambda e: e.tensor_tensor(tmp[1][:], x2(), c(), ALU.mult), r=keys_r, w=[key_tmp + "1"])
        S.add(eng_a, lambda e: e.tensor_tensor(dst[:, :, 96:112], tmp[0][:], tmp[1][:], ALU.add),
              r=[key_tmp + "0", key_tmp + "1"], w=[key_dst])

    rope("dve", "pool", krl, cskt, tk, kpad, ["krl", "cskt"], "tk", "kpad", 0)

    chunks = []
    for c0 in range(0, NQL, 512):
        chunks.append((c0, min(512, NQL - c0), 0, nkt))
    if NQC:
        chunks.append((NQL, NQC, 0, 2))
    sti = 0
    oti = 0
    osti = 0
    for h in range(nheads):
        hb = h % 2
        for half in range(2):
            t0 = half * KH
            t1 = min(nkt, t0 + KH)
            if t1 <= t0:
                continue
            S.dma(lambda e, h=h, t0=t0, t1=t1: e.dma_start(
                out=kvh[:, 0:t1 - t0, :],
                in_=kv[t0 * 128:t1 * 128, h * 128:(h + 1) * 128].rearrange("(t p) c -> p t c", p=128)),
                w=["kvh"], group="kvh")
            S.dve(lambda e, t0=t0, t1=t1: e.tensor_copy(kpad[:, t0:t1, 0:64], kvh[:, 0:t1 - t0, 0:64]),
                  r=["kvh"], w=["kpad"])
            S.pool(lambda e, t0=t0, t1=t1, hb=hb: e.tensor_copy(vx[hb][:, t0:t1, 0:64], kvh[:, 0:t1 - t0, 64:128]),
                   r=["kvh"], w=[f"vx{hb}"])
        for g0 in range(0, nkt, 8):
            g1 = min(nkt, g0 + 8)
            for t in range(g0, g1):
                S.pe(lambda e, t=t, g0=g0: e.transpose(tpk[:, t - g0, :], kpad[:, t, :], ident[:]),
                     r=["kpad", "ident"], w=["tpk"])
            S.dve(lambda e, g0=g0, g1=g1, hb=hb: e.tensor_copy(
                kT[hb][:, g0 * 128:g1 * 128], tpk[:, 0:g1 - g0, :].rearrange("p a b -> p (a b)")),
                r=["tpk"], w=[f"kT{hb}"])
        S.dma(lambda e, h=h: e.dma_start(out=qh[:], in_=q[:, h * 96:(h + 1) * 96].rearrange("(t p) c -> p t c", p=128)),
              w=["qh"], group="qh")
        S.pool(lambda e: e.tensor_copy(qpad[:, :, 0:64], qh[:, :, 0:64]), r=["qh"], w=["qpad"])
        rope("pool", "dve", qh, csqt, tq, qpad, ["qh", "csqt"], "tq", "qpad", 64)
        for g0 in range(0, nqt, 8):
            g1 = min(nqt, g0 + 8)
            for t in range(g0, g1):
                S.pe(lambda e, t=t, g0=g0: e.transpose(tpk[:, t - g0, :], qpad[:, t, :], ident[:]),
                     r=["qpad", "ident"], w=["tpk"])
            S.dve(lambda e, g0=g0, g1=g1, hb=hb: e.tensor_copy(
                qT[hb][:, g0 * 128:g1 * 128], tpk[:, 0:g1 - g0, :].rearrange("p a b -> p (a b)")),
                r=["tpk"], w=[f"qT{hb}"])
        for (q0, qn, k0, k1) in chunks:
            op = oti % 2
            oti += 1
            for kt in range(k0, k1):
                sp = sti % 3
                sti += 1
                S.pe(lambda e, sp=sp, hb=hb, kt=kt, q0=q0, qn=qn: e.matmul(
                    sTp[sp][:, 0:qn], kT[hb][:, kt * 128:(kt + 1) * 128], qT[hb][:, q0:q0 + qn],
                    start=True, stop=True), r=[f"kT{hb}", f"qT{hb}"], w=[f"sTp{sp}"])
                S.act(lambda e, sp=sp, qn=qn: e.activation(pT[sp][:, 0:qn], sTp[sp][:, 0:qn], AF.Exp, scale=MLA_SCALE),
                      r=[f"sTp{sp}"], w=[f"pT{sp}"])
                S.pe(lambda e, sp=sp, hb=hb, kt=kt, qn=qn, op=op, k0=k0, k1=k1: e.matmul(
                    oTp[op][:, 0:qn], vx[hb][:, kt, :], pT[sp][:, 0:qn], start=(kt == k0), stop=(kt == k1 - 1)),
                    r=[f"vx{hb}", f"pT{sp}"], w=[f"oTp{op}"])
            S.dve(lambda e, op=op, qn=qn: e.tensor_copy(oTs[op][:, 0:qn], oTp[op][:, 0:qn]),
                  r=[f"oTp{op}"], w=[f"oTs{op}"])
            for j in range(qn // 128):
                os_ = osti % 4
                osti += 1
                S.pe(lambda e, op=op, j=j: e.transpose(tpo[:], oTs[op][:, j * 128:(j + 1) * 128], identf[0:65, 0:65]),
                     r=[f"oTs{op}", "identf"], w=["tpo"])
                S.dve(lambda e, os_=os_: e.reciprocal(rc[os_][:], tpo[:, 64:65]), r=["tpo"], w=[f"rc{os_}"])
                S.dve(lambda e, os_=os_: e.tensor_scalar(ost[os_][:], tpo[:, 0:64], rc[os_][:, 0:1], None, ALU.mult),
                      r=["tpo", f"rc{os_}"], w=[f"ost{os_}"])
                r0 = q0 + j * 128
                S.dma(lambda e, os_=os_, r0=r0, h=h: e.dma_start(out=out[r0:r0 + 128, h * 64:(h + 1) * 64], in_=ost[os_][:]),
                      r=[f"ost{os_}"], w=[f"oo{os_}"], group=f"oo{os_}")


def gla_consts2():
    s = np.arange(64)[:, None]
    t = np.arange(64)[None, :]
    out = np.zeros((2, 64, 320), np.float32)
    U = (s <= t).astype(np.float32)
    out[0, :, 0:64] = U - (s <= 31)
    out[0, :, 64:128] = U
    out[0, :, 128:192] = (s > t)
    out[0, :, 192:256] = U
    Ub = (s >= t).astype(np.float32)
    out[1, :, 0:64] = Ub - (s >= 32)
    out[1, :, 64:128] = Ub
    out[1, :, 128:192] = (s < t)
    out[1, :, 192:256] = Ub
    out[:, 0, 256:320] = 1.0
    return out


def emit_k4s(g, Z, OG, segloc, segall, w2, bb, cst_d, mask):
    S = g.S
    g.phase()
    NLC, NCC = 32, 4
    ZH = ["Zh0", "Zh1", "Zh2", "Zh3"]
    NCH = NLC + NCC
    identf, ident = load_ident(g)
    cst = g.sb([64, 2, 320], F32)
    maskb = g.sb([64, 2, 64], BF16)
    w2t = g.sb([16, 2, 256], F32)
    bbt = g.sb([1, 2, 256], F32)
    mk = g.sb([64, 16], F32)
    zh_off = g.off
    Zh = g.sb([64, NCH, 288], F32)
    vb = g.sb([64, NCH, 128], BF16)
    OLs = g.sb([64, NCH, 512], F32)
    qbP = g.sb([64, 8, NLC, 64], BF16)
    Sctx = g.sb([64, 8, 128], F32)
    SEG = g.sb([64, 8, 129], F32)
    Sib = g.sb([64, 8, 128], BF16)
    Wk = g.sb([64, 128], F32)
    cand = g.sb([64, 128], F32)
    diff = g.sb([64, 128], F32)
    NP = 4
    mk_t = lambda shape, dt: [g.sb(shape, dt) for i in range(NP)]
    qTs = mk_t([64, 64], F32); kTs = mk_t([64, 64], F32); glTs = mk_t([16, 64], F32)
    Et = mk_t([64, 64], F32); Lt = mk_t([64, 64], F32); e12 = mk_t([64, 192], F32); e4 = mk_t([64, 64], F32)
    dec = mk_t([64, 1], F32)
    qe = mk_t([64, 64], BF16); ke = mk_t([64, 64], BF16); qb = mk_t([64, 64], BF16); kd = mk_t([64, 64], BF16)
    attm = mk_t([64, 64], BF16)
    Sf = [g.sb([64, 128], F32) for i in range(2)]
    Sb = [g.sb([64, 128], BF16) for i in range(2)]
    Pt = [g.sb([64, 1], F32) for i in range(2)]
    pA = [g.ps(i, [64, 512], F32) for i in range(NP)]
    pB = [g.ps(4 + i, [64, 512], F32) for i in range(NP)]
    id64 = identf[0:64, 0:64]

    S.dma(lambda e: e.dma_start(out=cst[:], in_=cst_d.rearrange("d p f -> p d f")), w=["cst"], group="cst")
    S.dve(lambda e: e.tensor_copy(maskb[:], cst[:, :, 192:256]), r=["cst"], w=["maskb"])
    S.dma(lambda e: e.dma_start(out=w2t[:], in_=w2.rearrange("d r e -> r d e")), w=["w2t"], group="w2t")
    S.dma(lambda e: e.dma_start(out=bbt[:], in_=bb.rearrange("(o d) e -> o d e", o=1)), w=["bbt"], group="bbt")
    S.dma(lambda e: e.dma_start(out=mk[:], in_=mask[0:64, :]), w=["mk"], group="mk")
    Zv = Z.rearrange("(c p) f -> p c f", p=64)
    ci = 0
    for h in range(4):
        srcs = [(slice(h * 64, (h + 1) * 64), slice(0, 64)), (slice(256 + h * 64, 256 + (h + 1) * 64), slice(64, 128)),
                (slice(512 + h * 128, 512 + (h + 1) * 128), slice(128, 256)), (slice(1024, 1056), slice(256, 288))]
        for k, (sc, dc) in enumerate(srcs):
            S.dma(lambda e, sc=sc, dc=dc: e.dma_start(out=Zh[:, :, dc], in_=Zv[:, :, sc]), w=[f"Zh{k}"], group=f"Zh{k}")
        S.pool(lambda e: e.tensor_copy(vb[:], Zh[:, :, 128:256]), r=ZH, w=["vb"])
        for d in range(2):
            hd = h * 2 + d
            w2hd = w2t[:, d, h * 64:(h + 1) * 64]
            bbhd = bbt[0:1, d, h * 64:(h + 1) * 64]
            cD = lambda a, b, d=d: cst[:, d, a:b]
            deccol = 127 if d == 0 else 64
            for part in ("ctx", "lat"):
                lat = part == "lat"
                if lat:
                    order = list(range(NLC))
                else:
                    order = list(range(NLC, NCH))
                if d == 1:
                    order = order[::-1]
                S.dve(lambda e: e.memset(Sf[0][:], 0.0), w=["Sf0"])
                S.pool(lambda e: e.memset(Sb[0][:], 0.0), w=["Sb0"])
                if lat:
                    S.dve(lambda e: e.memset(Pt[0][:], 1.0), w=["P0"])
                si = 0
                for c in order:
                    p = ci % NP
                    ci += 1
                    A, B = f"pA{p}", f"pB{p}"
                    gcol = 256 + 16 * d
                    S.pe(lambda e, p=p, c=c: e.transpose(pA[p][:, 320:384], Zh[:, c, 0:64], id64), r=ZH + ["identf"], w=[A])
                    S.pe(lambda e, p=p, c=c: e.transpose(pA[p][:, 384:448], Zh[:, c, 64:128], id64), r=ZH + ["identf"], w=[A])
                    S.pe(lambda e, p=p, c=c, gcol=gcol: e.transpose(pA[p][0:16, 448:512], Zh[:, c, gcol:gcol + 16], id64),
                         r=ZH + ["identf"], w=[A])
                    S.act(lambda e, p=p: e.mul(qTs[p][:], pA[p][:, 320:384], 0.125), r=[A], w=[f"qTs{p}"])
                    S.act(lambda e, p=p: e.copy(kTs[p][:], pA[p][:, 384:448]), r=[A], w=[f"kTs{p}"])
                    S.act(lambda e, p=p: e.copy(glTs[p][:], pA[p][0:16, 448:512]), r=[A], w=[f"glTs{p}"])
                    S.pe(lambda e, p=p, w2hd=w2hd: e.matmul(pA[p][:, 0:64], glTs[p][:], w2hd, start=True, stop=False),
                         r=[f"glTs{p}", "w2t"], w=[A])
                    S.pe(lambda e, p=p, bbhd=bbhd: e.matmul(pA[p][:, 0:64], cst[0:1, 0, 256:320], bbhd, start=False, stop=True),
                         r=["cst", "bbt"], w=[A])
                    S.act(lambda e, p=p: e.activation(Et[p][:], pA[p][:, 0:64], AF.Exp, scale=-1.0), r=[A], w=[f"Et{p}"])
                    S.act(lambda e, p=p: e.activation(Lt[p][:], Et[p][:], AF.Ln, bias=1.0), r=[f"Et{p}"], w=[f"Lt{p}"])
                    S.pe(lambda e, p=p, cD=cD: e.matmul(pA[p][:, 64:128], Lt[p][:], cD(0, 64), start=True, stop=True),
                         r=[f"Lt{p}", "cst"], w=[A])
                    S.pe(lambda e, p=p, cD=cD: e.matmul(pA[p][:, 128:192], Lt[p][:], cD(64, 128), start=True, stop=True),
                         r=[f"Lt{p}", "cst"], w=[A])
                    S.pe(lambda e, p=p, cD=cD: e.matmul(pA[p][:, 192:256], cD(128, 192), Lt[p][:], start=True, stop=True),
                         r=[f"Lt{p}", "cst"], w=[A])
                    S.act(lambda e, p=p: e.activation(e12[p][:, 0:128], pA[p][:, 64:192], AF.Exp, scale=-1.0 / 16),
                          r=[A], w=[f"e12{p}"])
                    S.act(lambda e, p=p: e.activation(e12[p][:, 128:192], pA[p][:, 64:128], AF.Exp, scale=1.0 / 16),
                          r=[A], w=[f"e12{p}"])
                    S.act(lambda e, p=p: e.activation(e4[p][:], pA[p][:, 192:256], AF.Exp, scale=-1.0 / 16),
                          r=[A], w=[f"e4{p}"])
                    S.act(lambda e, p=p, deccol=deccol: e.copy(dec[p][:], e12[p][:, deccol:deccol + 1]),
                          r=[f"e12{p}"], w=[f"dec{p}"])
                    S.dve(lambda e, p=p: e.tensor_tensor(qe[p][:], qTs[p][:], e12[p][:, 0:64], ALU.mult),
                          r=[f"qTs{p}", f"e12{p}"], w=[f"qe{p}"])
                    S.dve(lambda e, p=p: e.tensor_tensor(ke[p][:], kTs[p][:], e12[p][:, 128:192], ALU.mult),
                          r=[f"kTs{p}", f"e12{p}"], w=[f"ke{p}"])
                    S.pool(lambda e, p=p: e.tensor_tensor(qb[p][:], qTs[p][:], e12[p][:, 64:128], ALU.mult),
                           r=[f"qTs{p}", f"e12{p}"], w=[f"qb{p}"])
                    S.pool(lambda e, p=p, c=c: e.tensor_tensor(kd[p][:], Zh[:, c, 64:128], e4[p][:], ALU.mult),
                           r=ZH + [f"e4{p}"], w=[f"kd{p}"])
                    s0 = si % 2
                    s1 = (si + 1) % 2
                    si += 1
                    if lat:
                        S.dve(lambda e, p=p, s0=s0, hd=hd, c=c: e.scalar_tensor_tensor(
                            qbP[:, hd, c, :], qTs[p][:], Pt[s0][:, 0:1], e12[p][:, 64:128], ALU.mult, ALU.mult),
                            r=[f"qTs{p}", f"P{s0}", f"e12{p}"], w=["qbP"])
                        S.dve(lambda e, p=p, s0=s0, s1=s1: e.tensor_tensor(Pt[s1][:], Pt[s0][:], dec[p][:], ALU.mult),
                              r=[f"P{s0}", f"dec{p}"], w=[f"P{s1}"])
                    S.pe(lambda e, p=p: e.matmul(pA[p][:, 256:320], ke[p][:], qe[p][:], start=True, stop=True),
                         r=[f"ke{p}", f"qe{p}"], w=[A])
                    S.dve(lambda e, p=p, d=d: e.tensor_tensor(attm[p][:], pA[p][:, 256:320], maskb[:, d, :], ALU.mult),
                          r=[A, "maskb"], w=[f"attm{p}"])
                    S.pe(lambda e, p=p, c=c: e.matmul(pB[p][:, 0:128], attm[p][:], vb[:, c, :], start=True, stop=False),
                         r=[f"attm{p}", "vb"], w=[B])
                    S.pe(lambda e, p=p, s0=s0: e.matmul(pB[p][:, 0:128], qb[p][:], Sb[s0][:], start=False, stop=True),
                         r=[f"qb{p}", f"Sb{s0}"], w=[B])
                    S.pe(lambda e, p=p, c=c: e.matmul(pB[p][:, 128:256], kd[p][:], vb[:, c, :], start=True, stop=True),
                         r=[f"kd{p}", "vb"], w=[B])
                    S.dve(lambda e, p=p, s0=s0, s1=s1: e.scalar_tensor_tensor(
                        Sf[s1][:], Sf[s0][:], dec[p][:, 0:1], pB[p][:, 128:256], ALU.mult, ALU.add),
                        r=[f"Sf{s0}", f"dec{p}", B], w=[f"Sf{s1}"])
                    S.pool(lambda e, s1=s1: e.tensor_copy(Sb[s1][:], Sf[s1][:]), r=[f"Sf{s1}"], w=[f"Sb{s1}"])
                    ocols = slice(h * 128, (h + 1) * 128)
                    if d == 0:
                        S.dve(lambda e, p=p, c=c, ocols=ocols: e.tensor_copy(OLs[:, c, ocols], pB[p][:, 0:128]),
                              r=[B], w=[f"OL{c}"])
                    else:
                        S.dve(lambda e, p=p, c=c, ocols=ocols: e.tensor_tensor(OLs[:, c, ocols], pB[p][:, 0:128],
                                                                                OLs[:, c, ocols], ALU.add),
                              r=[B, f"OL{c}"], w=[f"OL{c}"])
                sl = si % 2
                if lat:
                    S.dve(lambda e, sl=sl, hd=hd: e.tensor_copy(SEG[:, hd, 0:128], Sf[sl][:]), r=[f"Sf{sl}"], w=["SEG"])
                    S.dve(lambda e, sl=sl, hd=hd: e.tensor_copy(SEG[:, hd, 128:129], Pt[sl][:]), r=[f"P{sl}"], w=["SEG"])
                else:
                    S.dve(lambda e, sl=sl, hd=hd: e.tensor_copy(Sctx[:, hd, :], Sf[sl][:]), r=[f"Sf{sl}"], w=["Sctx"])
    S.dma(lambda e: e.dma_start(out=segloc.ap(), in_=SEG[:].rearrange("p a b -> p (a b)")), r=["SEG"], w=["segloc"],
          group="segio")
    allgather(g, segall, segloc, "segall", r=["segloc"])
    SEGa = g.arena[0:64, zh_off:zh_off + 4 * 8 * 129].rearrange("p (r a b) -> p r a b", r=4, a=8, b=129)
    S.dma(lambda e: e.dma_start(out=SEGa.rearrange("p r a b -> p r (a b)"),
                                in_=segall.ap().rearrange("(r p) f -> p r f", p=64)),
          r=["segall"], w=ZH + ["SEGa"], group="segio")
    for h in range(4):
        for d in range(2):
            hd = h * 2 + d
            S.dve(lambda e, hd=hd: e.tensor_copy(Wk[:], Sctx[:, hd, :]), r=["Sctx"], w=["Wk"])
            js = range(4) if d == 0 else range(3, -1, -1)
            for j in js:
                mcol = (4 if d == 0 else 8) + j
                S.dve(lambda e, j=j, hd=hd: e.scalar_tensor_tensor(cand[:], Wk[:], SEGa[:, j, hd, 128:129],
                                                                   SEGa[:, j, hd, 0:128], ALU.mult, ALU.add),
                      r=["Wk", "SEGa"], w=["cand"])
                S.dve(lambda e: e.tensor_tensor(diff[:], cand[:], Wk[:], ALU.subtract), r=["cand", "Wk"], w=["diff"])
                S.dve(lambda e, mcol=mcol: e.scalar_tensor_tensor(Wk[:], diff[:], mk[:, mcol:mcol + 1], Wk[:],
                                                                  ALU.mult, ALU.add), r=["diff", "mk", "Wk"], w=["Wk"])
            S.dve(lambda e, hd=hd: e.tensor_copy(Sib[:, hd, :], Wk[:]), r=["Wk"], w=["Sib"])
    for c in range(NLC):
        p = c % NP
        A = f"pA{p}"
        for h in range(4):
            for d in range(2):
                hd = h * 2 + d
                S.pe(lambda e, p=p, h=h, d=d, hd=hd, c=c: e.matmul(pA[p][:, h * 128:(h + 1) * 128], qbP[:, hd, c, :],
                                                                    Sib[:, hd, :], start=(d == 0), stop=(d == 1)),
                     r=["qbP", "Sib"], w=[A])
        S.dve(lambda e, p=p, c=c: e.tensor_tensor(OLs[:, c, :], pA[p][:], OLs[:, c, :], ALU.add), r=[A, f"OL{c}"], w=[f"OL{c}"])
    OGv = OG.rearrange("(c p) f -> p c f", p=64)
    for k in range(4):
        cs = slice(k * 9, (k + 1) * 9)
        S.dma(lambda e, cs=cs: e.dma_start(out=OGv[:, cs, :], in_=OLs[:, cs, :]),
              r=[f"OL{c}" for c in range(k * 9, (k + 1) * 9)], w=[f"ogo{k}"], group=f"ogo{k}")


def emit_k5(g, T, kind, n_lat_tiles, x, w, vecs, z, out, og=None, oml=None, gng=None, fr_lat=None, fr_ctx=None,
            swT=None, sbT=None, mask=None, eps=1e-6):
    S = g.S
    g.phase()
    D = 1024
    nt = T // 128
    identf, ident = load_ident(g)
    wbf = g.sb([128, 8, D], BF16)
    wst = [g.sb([128, D], F32) for i in range(2)]
    vt = g.sb([128, 4, D], F32)
    for a in range(4):
        S.dma(lambda e, a=a: e.dma_start(out=vt[:, a, :], in_=vecs[a].partition_broadcast(128)), w=["vt"], group="modt")
    if kind == "even":
        gn = g.sb([128, 128], F32)
        S.dma(lambda e: e.dma_start(out=gn[:], in_=gng.partition_broadcast(128)), w=["gn"], group="gt")
    else:
        swf = g.sb([128, 4, 128], F32)
        swb = g.sb([128, 4, 128], BF16)
        sbt = g.sb([128, 4], F32)
        mk = g.sb([128, 16], F32)
        S.dma(lambda e: e.dma_start(out=swf[:], in_=swT.rearrange("g s t -> s g t")), w=["swf"], group="swf")
        S.dve(lambda e: e.tensor_copy(swb[:], swf[:]), r=["swf"], w=["swb"])
        S.dma(lambda e: e.dma_start(out=sbt[:], in_=sbT), w=["sbt"], group="sbt")
        S.dma(lambda e: e.dma_start(out=mk[:], in_=mask), w=["mk"], group="mk")
    for k in range(8):
        s = k % 2
        S.dma(lambda e, k=k, s=s: e.dma_start(out=wst[s][:], in_=w[k * 128:(k + 1) * 128, :]),
              w=[f"wst{s}"], group=f"wst{s}")
        if k % 2 == 0:
            S.act(lambda e, k=k, s=s: e.copy(wbf[:, k, :], wst[s][:]), r=[f"wst{s}"], w=[f"wbf{k}"])
        else:
            S.dve(lambda e, k=k, s=s: e.tensor_copy(wbf[:, k, :], wst[s][:]), r=[f"wst{s}"], w=[f"wbf{k}"])
    NX = 2
    xt = [g.sb([128, D], F32) for i in range(NX)]
    ab = [g.sb([128, D], BF16) for i in range(NX)]
    aT = [g.sb([128, 8, 128], BF16) for i in range(NX)]
    rr = [g.sb([128, D], F32) for i in range(NX)]
    ot = [g.sb([128, D], F32) for i in range(NX)]
    st = [g.sb([128, 8, 6], F32) for i in range(NX)]
    mv = [g.sb([128, 16], F32) for i in range(NX)]
    if kind == "even":
        i1 = [g.sb([128, 512], F32) for i in range(NX)]
        i2 = [g.sb([128, 512], F32) for i in range(NX)]
        i3 = [g.sb([128, 512], F32) for i in range(NX)]
        i4 = [g.sb([128, 512], F32) for i in range(NX)]
        i5 = [g.sb([128, 512], F32) for i in range(NX)]
    else:
        i1 = [g.sb([128, 512], F32) for i in range(NX)]
        fc4 = [g.sb([128, 4, 512], F32) for i in range(NX)]
        zz = [g.sb([128, 2048], F32) for i in range(NX)]
        vgb = [g.sb([128, 512], BF16) for i in range(NX)]
        svp = [g.ps(0, [128, 512], F32)]
    tp = [g.ps(1 + i, [128, 8, 128], BF16) for i in range(2)]
    yp = [g.ps(3 + i, [128, 512], F32) for i in range(4)]
    wkeys = [f"wbf{k}" for k in range(8)]

    def rstd_chain(s, col_var, col_out, ncol=1):
        S.dve(lambda e: e.tensor_scalar_add(mv[s][:, col_var:col_var + ncol], mv[s][:, col_var:col_var + ncol], eps),
              r=[f"mv{s}"], w=[f"mv{s}"])
        S.act(lambda e: e.sqrt(mv[s][:, col_var:col_var + ncol], mv[s][:, col_var:col_var + ncol]),
              r=[f"mv{s}"], w=[f"mv{s}"])
        S.dve(lambda e: e.reciprocal(mv[s][:, col_out:col_out + ncol], mv[s][:, col_var:col_var + ncol]),
              r=[f"mv{s}"], w=[f"mv{s}"])

    for i in range(nt):
        s = i % NX
        rows = slice(i * 128, (i + 1) * 128)
        S.dma(lambda e, s=s, rows=rows: e.dma_start(out=xt[s][:], in_=x[rows, :]), w=[f"xt{s}"], group=f"xt{s}")
        if kind == "even":
            S.dma(lambda e, s=s, rows=rows: e.dma_start(out=i1[s][:], in_=og[rows, :]), w=[f"i1{s}"], group=f"i1{s}")
            S.dma(lambda e, s=s, rows=rows: e.dma_start(out=i3[s][:], in_=z[rows, 1056:1568]), w=[f"i3{s}"], group=f"i3{s}")
            S.dma(lambda e, s=s, rows=rows: e.dma_start(out=i4[s][:], in_=oml[rows, :]), w=[f"i4{s}"], group=f"i4{s}")
            S.dma(lambda e, s=s, rows=rows: e.dma_start(out=i5[s][:], in_=z[rows, 1984:2496]), w=[f"i5{s}"], group=f"i5{s}")
            S.pool(lambda e, s=s: e.tensor_tensor(i2[s][:], i1[s][:], i1[s][:], ALU.mult), r=[f"i1{s}"], w=[f"i2{s}"])
            S.dve(lambda e, s=s: e.reduce_sum(mv[s][:, 0:4], i2[s][:].rearrange("p (h d) -> p h d", h=4), AX.X),
                  r=[f"i2{s}"], w=[f"mv{s}"])
            S.dve(lambda e, s=s: e.tensor_scalar(mv[s][:, 0:4], mv[s][:, 0:4], 1.0 / 128, None, ALU.mult),
                  r=[f"mv{s}"], w=[f"mv{s}"])
            rstd_chain(s, 0, 4, 4)
            S.act(lambda e, s=s: e.activation(i3[s][:], i3[s][:], AF.Silu), r=[f"i3{s}"], w=[f"i3{s}"])
            S.act(lambda e, s=s: e.activation(i5[s][:], i5[s][:], AF.Silu), r=[f"i5{s}"], w=[f"i5{s}"])
            for h in range(4):
                hs = slice(h * 128, (h + 1) * 128)
                S.dve(lambda e, s=s, h=h, hs=hs: e.scalar_tensor_tensor(
                    i1[s][:, hs], i1[s][:, hs], mv[s][:, 4 + h:5 + h], gn[:], ALU.mult, ALU.mult),
                    r=[f"i1{s}", f"mv{s}", "gn"], w=[f"i1{s}"])
            S.pool(lambda e, s=s: e.tensor_tensor(ab[s][:, 0:512], i1[s][:], i3[s][:], ALU.mult),
                   r=[f"i1{s}", f"i3{s}"], w=[f"ab{s}"])
            S.pool(lambda e, s=s: e.tensor_tensor(ab[s][:, 512:1024], i4[s][:], i5[s][:], ALU.mult),
                   r=[f"i4{s}", f"i5{s}"], w=[f"ab{s}"])
        else:
            if i < n_lat_tiles:
                for qq in range(4):
                    S.dma(lambda e, s=s, qq=qq, i=i: e.dma_start(
                        out=fc4[s][:, qq, :].rearrange("p (g c) -> p g c", g=4), in_=fr_lat(qq, i)),
                        w=[f"fc4{s}.{qq}"], group=f"fc4{s}")
                S.dve(lambda e, s=s: e.tensor_scalar(i1[s][:], fc4[s][:, 0, :], mk[:, 0:1], None, ALU.mult),
                      r=[f"fc4{s}.0", f"fc4{s}.3", "mk"], w=[f"i1{s}"])
                for qq in range(1, 4):
                    S.dve(lambda e, s=s, qq=qq: e.scalar_tensor_tensor(
                        i1[s][:], fc4[s][:, qq, :], mk[:, qq:qq + 1], i1[s][:], ALU.mult, ALU.add),
                        r=[f"fc4{s}.{qq}", f"fc4{s}.3", "mk", f"i1{s}"], w=[f"i1{s}"])
            else:
                S.dma(lambda e, s=s, i=i: e.dma_start(out=i1[s][:].rearrange("p (g c) -> p g c", g=4), in_=fr_ctx(i)),
                      w=[f"i1{s}"], group=f"i1{s}")
            S.dma(lambda e, s=s, rows=rows: e.dma_start(out=zz[s][:], in_=z[rows, 512:2560]), w=[f"zz{s}"], group=f"zz{s}")
            S.act(lambda e, s=s: e.activation(zz[s][:, 0:512], zz[s][:, 0:512], AF.Silu), r=[f"zz{s}"], w=[f"zz{s}"])
            S.act(lambda e, s=s: e.activation(zz[s][:, 1536:2048], zz[s][:, 1536:2048], AF.Silu), r=[f"zz{s}"], w=[f"zz{s}"])
            S.act(lambda e, s=s: e.activation(zz[s][:, 512:1536], zz[s][:, 512:1536], AF.Gelu), r=[f"zz{s}"], w=[f"zz{s}"])
            S.pool(lambda e, s=s: e.tensor_tensor(ab[s][:, 0:512], i1[s][:], zz[s][:, 0:512], ALU.mult),
                   r=[f"i1{s}", f"zz{s}"], w=[f"ab{s}"])
            for g in range(4):
                S.dve(lambda e, s=s, g=g: e.bn_stats(st[s][:, g, :], zz[s][:, 1024 + g * 128:1024 + (g + 1) * 128]),
                      r=[f"zz{s}"], w=[f"st{s}"])
                S.dve(lambda e, s=s, g=g: e.bn_aggr(mv[s][:, 2 * g:2 * g + 2], st[s][:, g:g + 1, :]),
                      r=[f"st{s}"], w=[f"mv{s}"])
            for g in range(4):
                rstd_chain(s, 2 * g + 1, 8 + g, 1)
            for g in range(4):
                S.dve(lambda e, s=s, g=g: e.tensor_scalar(
                    vgb[s][:, g * 128:(g + 1) * 128], zz[s][:, 1024 + g * 128:1024 + (g + 1) * 128],
                    mv[s][:, 2 * g:2 * g + 1], mv[s][:, 8 + g:9 + g], ALU.subtract, ALU.mult),
                    r=[f"zz{s}", f"mv{s}"], w=[f"vgb{s}"])
            for g in range(4):
                S.pe(lambda e, s=s, g=g: e.matmul(svp[0][:, g * 128:(g + 1) * 128], swb[:, g, :],
                                                  vgb[s][:, g * 128:(g + 1) * 128], start=True, stop=True),
                     r=[f"vgb{s}", "swb"], w=["svp0"])
            for g in range(4):
                gs = slice(g * 128, (g + 1) * 128)
                S.dve(lambda e, s=s, g=g, gs=gs: e.scalar_tensor_tensor(
                    zz[s][:, 512 + g * 128:512 + (g + 1) * 128], svp[0][:, gs], sbt[:, g:g + 1],
                    zz[s][:, 512 + g * 128:512 + (g + 1) * 128], ALU.add, ALU.mult),
                    r=["svp0", "sbt", f"zz{s}"], w=[f"zz{s}"])
            S.pool(lambda e, s=s: e.tensor_tensor(ab[s][:, 512:1024], zz[s][:, 512:1024], zz[s][:, 1536:2048], ALU.mult),
                   r=[f"zz{s}"], w=[f"ab{s}"])
        t = i % 2
        for k in range(8):
            S.pe(lambda e, s=s, t=t, k=k: e.transpose(tp[t][:, k, :], ab[s][:, k * 128:(k + 1) * 128], ident[:]),
                 r=[f"ab{s}", "ident"], w=[f"tp{t}"])
        S.act(lambda e, s=s, t=t: e.copy(aT[s][:], tp[t][:]), r=[f"tp{t}"], w=[f"aT{s}"])
        gi = 0 if i < n_lat_tiles else 1
        for c in range(2):
            p = (2 * i + c) % 4
            cs = slice(c * 512, (c + 1) * 512)
            for k in range(8):
                S.pe(lambda e, s=s, p=p, k=k, cs=cs: e.matmul(yp[p][:], aT[s][:, k, :], wbf[:, k, cs],
                                                             start=(k == 0), stop=(k == 7)),
                     r=[f"aT{s}", wkeys[k]], w=[f"yp{p}"])
            S.dve(lambda e, s=s, p=p, cs=cs, gi=gi: e.tensor_tensor(rr[s][:, cs], yp[p][:], vt[:, gi, cs], ALU.mult),
                  r=[f"yp{p}", "vt"], w=[f"rr{s}"])
        S.dve(lambda e, s=s: e.scalar_tensor_tensor(rr[s][:], xt[s][:], ALPHA, rr[s][:], ALU.mult, ALU.add),
               r=[f"xt{s}", f"rr{s}"], w=[f"rr{s}"])
        for j in range(2):
            S.dve(lambda e, s=s, j=j: e.bn_stats(st[s][:, 4 + j, :], rr[s][:, j * 512:(j + 1) * 512]),
                  r=[f"rr{s}"], w=[f"st{s}"])
        S.dve(lambda e, s=s: e.bn_aggr(mv[s][:, 12:14], st[s][:, 4:6, :]), r=[f"st{s}"], w=[f"mv{s}"])
        rstd_chain(s, 13, 14, 1)
        S.dve(lambda e, s=s: e.tensor_scalar(rr[s][:], rr[s][:], mv[s][:, 12:13], mv[s][:, 14:15],
                                             ALU.subtract, ALU.mult), r=[f"rr{s}", f"mv{s}"], w=[f"rr{s}"])
        S.pool(lambda e, s=s: e.tensor_tensor(rr[s][:], rr[s][:], vt[:, 2, :], ALU.mult), r=[f"rr{s}", "vt"], w=[f"rr{s}"])
        S.pool(lambda e, s=s: e.tensor_tensor(ot[s][:], rr[s][:], vt[:, 3, :], ALU.add), r=[f"rr{s}", "vt"], w=[f"ot{s}"])
        S.dma(lambda e, s=s, rows=rows: e.dma_start(out=out[rows, :], in_=ot[s][:]), r=[f"ot{s}"], w=[f"oo{s}"],
              group=f"oo{s}")


def emit_k7(g, fall, fout, TW, FC, W3, TWc, mask, with_ctx=True):
    S = g.S
    g.phase()
    st = [g.sb([128, 4, 512], F32) for i in range(2)]
    tmp = [g.sb([128, 4, 128], F32) for i in range(2)]
    mk = g.sb([128, 16], F32)
    TWb = g.sb([128, 64, 256], BF16)
    fb = g.sb([128, 64, 128], BF16)
    Y = g.sb([128, 2, 64, 128], BF16)
    U = g.sb([128, 64, 256], BF16)
    FCb = g.sb([128, 512], BF16)
    W3b = g.sb([128, 256], BF16)
    frt = g.sb([128, 64, 128], F32)
    ps = [g.ps(i, [128, 512], F32) for i in range(4)]
    S.dma(lambda e: e.dma_start(out=mk[:], in_=mask), w=["mk"], group="mk")
    stf = lambda s: st[s][:].rearrange("p a b -> p (a b)")
    for i in range(8):
        s = i % 2
        S.dma(lambda e, i=i, s=s: e.dma_start(out=stf(s), in_=TW[:, i * 8:(i + 1) * 8, :].rearrange("p a b -> p (a b)")),
              w=[f"st{s}"], group=f"st{s}")
        S.add("dve" if i % 2 == 0 else "pool",
              lambda e, i=i, s=s: e.tensor_copy(TWb[:, i * 8:(i + 1) * 8, :].rearrange("p a b -> p (a b)"), stf(s)),
              r=[f"st{s}"], w=["TWb"])
    S.dma(lambda e: e.dma_start(out=stf(0)[:, 0:512], in_=FC), w=["st0"], group="st0")
    S.dve(lambda e: e.tensor_copy(FCb[:], stf(0)[:, 0:512]), r=["st0"], w=["FCb"])
    S.dma(lambda e: e.dma_start(out=stf(1)[:, 0:256], in_=W3), w=["st1"], group="st1")
    S.dve(lambda e: e.tensor_copy(W3b[:], stf(1)[:, 0:256]), r=["st1"], w=["W3b"])

    S.barrier()

    def select(s, t, dst):
        stk = [f"st{s}.{r}.{pp}" for r in range(4) for pp in range(4)]
        S.dve(lambda e: e.tensor_scalar(tmp[t][:], st[s][:, :, 0:128], mk[:, 0:1], None, ALU.mult),
              r=stk + ["mk"], w=[f"tmp{t}"])
        for gg in range(1, 3):
            S.dve(lambda e, gg=gg: e.scalar_tensor_tensor(tmp[t][:], st[s][:, :, gg * 128:(gg + 1) * 128],
                                                         mk[:, gg:gg + 1], tmp[t][:], ALU.mult, ALU.add),
                  r=stk + ["mk", f"tmp{t}"], w=[f"tmp{t}"])
        S.dve(lambda e: e.scalar_tensor_tensor(dst, st[s][:, :, 384:512], mk[:, 3:4], tmp[t][:], ALU.mult, ALU.add),
              r=stk + ["mk", f"tmp{t}"], w=["fb"])

    for j in range(16):
        s = j % 2
        for r in range(4):
            for pp in range(4):
                src = fall[pp][r * 512:(r + 1) * 512, :].rearrange("(a n) c -> a n c", n=64)[:, 4 * j:4 * j + 4, :]
                p0 = 32 * r + 8 * pp
                S.dma(lambda e, s=s, p0=p0, src=src: e.dma_start(out=st[s][p0:p0 + 8, :, :], in_=src),
                      w=[f"st{s}.{r}.{pp}"], group=f"st{s}")
        select(s, s, fb[:, 4 * j:4 * j + 4, :])
    pi = 0
    for n2 in range(0, 64, 2):
        p = pi % 4; pi += 1
        for d in range(2):
            S.pe(lambda e, p=p, n2=n2, d=d: e.matmul(ps[p][:, d * 256:(d + 1) * 256], fb[:, n2 + d, :], TWb[:, n2 + d, :],
                                                     start=True, stop=True), r=["fb", "TWb"], w=[f"ps{p}"])
        for d in range(2):
            src = lambda p=p, d=d: ps[p][:, d * 256:(d + 1) * 256].rearrange("p (r j a) -> p r j a", r=2, a=2)
            dst = lambda n2=n2, d=d: Y[:, :, :, 2 * (n2 + d):2 * (n2 + d) + 2]
            if d == 0:
                S.act(lambda e, src=src, dst=dst: e.copy(dst(), src()), r=[], w=["Y", f"ps{p}"])
            else:
                S.dve(lambda e, src=src, dst=dst: e.tensor_copy(dst(), src()), r=[], w=["Y", f"ps{p}"])
    for j in range(0, 64, 2):
        p = pi % 4; pi += 1
        for d in range(2):
            jj = j + d
            S.pe(lambda e, p=p, jj=jj, d=d: e.matmul(ps[p][:, d * 256:(d + 1) * 256], Y[:, 0, jj, :],
                                                     FCb[:, 0:256], start=True, stop=False), r=["Y", "FCb"], w=[f"ps{p}"])
            S.pe(lambda e, p=p, jj=jj, d=d: e.matmul(ps[p][:, d * 256:(d + 1) * 256], Y[:, 1, jj, :],
                                                     FCb[:, 256:512], start=False, stop=True), r=["Y", "FCb"], w=[f"ps{p}"])
        if (j // 2) % 2 == 0:
            S.act(lambda e, p=p, j=j: e.copy(U[:, j:j + 2, :].rearrange("p a b -> p (a b)"), ps[p][:]), r=[f"ps{p}"], w=["U"])
        else:
            S.dve(lambda e, p=p, j=j: e.tensor_copy(U[:, j:j + 2, :].rearrange("p a b -> p (a b)"), ps[p][:]), r=[f"ps{p}"], w=["U"])
    scale = 1.0 / 1024.0
    for j0 in range(0, 64, 4):
        p = pi % 4; pi += 1
        for d in range(4):
            jj = j0 + d
            S.pe(lambda e, p=p, jj=jj, d=d: e.matmul(ps[p][:, d * 128:(d + 1) * 128], W3b[:, 0:128], U[:, jj, 0:128],
                                                     start=True, stop=False), r=["U", "W3b"], w=[f"ps{p}"])
            S.pe(lambda e, p=p, jj=jj, d=d: e.matmul(ps[p][:, d * 128:(d + 1) * 128], W3b[:, 128:256], U[:, jj, 128:256],
                                                     start=False, stop=True), r=["U", "W3b"], w=[f"ps{p}"])
        S.dve(lambda e, p=p, j0=j0: e.tensor_scalar(frt[:, j0:j0 + 4, :].rearrange("p a b -> p (a b)"), ps[p][:],
                                                    scale, None, ALU.mult), r=[f"ps{p}"], w=["frt"])
    for q in range(4):
        fv = fout[q].rearrange("(k2 jj a) c -> a k2 jj c", jj=64, a=2)
        for a in range(2):
            S.dma(lambda e, a=a, q=q, fv=fv: e.dma_start(out=fv[a], in_=frt[a * 64 + 16 * q:a * 64 + 16 * (q + 1), :, :]),
                  r=["frt"], w=[f"fo{a}{q}"], group=f"fo{a}")
    if with_ctx:
        S.barrier()
        fcb = g.sb([128, 2, 128], BF16)
        TWcb = g.sb([128, 2, 512], BF16)
        Yc = g.sb([128, 512], BF16)
        oc = g.sb([128, 2, 128], F32)
        for t in range(2):
            S.dma(lambda e, t=t: e.dma_start(out=st[t][:, 0, :], in_=fall[4][t * 128:(t + 1) * 128, :]),
                  w=[f"st{t}"], group=f"st{t}")
            S.dve(lambda e, t=t: e.tensor_scalar(tmp[t][:, 0, :], st[t][:, 0, 0:128], mk[:, 0:1], None, ALU.mult),
                  r=[f"st{t}", "mk"], w=[f"tmp{t}"])
            for gg in range(1, 4):
                S.dve(lambda e, t=t, gg=gg: e.scalar_tensor_tensor(
                    tmp[t][:, 0, :], st[t][:, 0, gg * 128:(gg + 1) * 128], mk[:, gg:gg + 1], tmp[t][:, 0, :], ALU.mult, ALU.add),
                    r=[f"st{t}", "mk", f"tmp{t}"], w=[f"tmp{t}"])
            S.dve(lambda e, t=t: e.tensor_copy(fcb[:, t, :], tmp[t][:, 0, :]), r=[f"tmp{t}"], w=["fcb"])
        for t in range(2):
            S.dma(lambda e, t=t: e.dma_start(out=stf(t)[:, 0:512], in_=TWc[:, t, :]), w=[f"st{t}"], group=f"st{t}")
            S.dve(lambda e, t=t: e.tensor_copy(TWcb[:, t, :], stf(t)[:, 0:512]), r=[f"st{t}"], w=["TWcb"])
        p = pi % 4; pi += 1
        for t in range(2):
            S.pe(lambda e, p=p, t=t: e.matmul(ps[p][:], fcb[:, t, :], TWcb[:, t, :], start=(t == 0), stop=(t == 1)),
                 r=["fcb", "TWcb"], w=[f"ps{p}"])
        S.dve(lambda e, p=p: e.tensor_copy(Yc[:], ps[p][:]), r=[f"ps{p}"], w=["Yc"])
        p = pi % 4; pi += 1
        for kt in range(2):
            S.pe(lambda e, p=p, kt=kt: e.matmul(ps[p][:, kt * 128:(kt + 1) * 128], Yc[:, kt * 128:(kt + 1) * 128],
                                                FCb[:, 0:128], start=True, stop=False), r=["Yc", "FCb"], w=[f"ps{p}"])
            S.pe(lambda e, p=p, kt=kt: e.matmul(ps[p][:, kt * 128:(kt + 1) * 128], Yc[:, 256 + kt * 128:256 + (kt + 1) * 128],
                                                FCb[:, 256:384], start=False, stop=True), r=["Yc", "FCb"], w=[f"ps{p}"])
        S.dve(lambda e, p=p: e.tensor_scalar(oc[:].rearrange("p a b -> p (a b)"), ps[p][:, 0:256],
                                             1.0 / np.sqrt(256.0 * 128.0), None, ALU.mult), r=[f"ps{p}"], w=["oc"])
        S.dma(lambda e: e.dma_start(out=fout[4].rearrange("(t p) c -> p t c", p=128), in_=oc[:]),
              r=["oc"], w=["oco"], group="oco")


Q, L, SEQ, D = 2048, 256, 8192, 1024
T = Q + L
NCORES = 8


def build_fused(depth=4, stop_after=None):
    nc = bass.Bass(target_bir_lowering=False)
    g = G(nc)
    if stop_after is not None:
        g.max_phase = stop_after
    ext = lambda name, shape: nc.dram_tensor(name, list(shape), F32, kind="ExternalInput")
    xin = ext("xin", [T, D]); cin = ext("cin", [128, D])
    ada_w = ext("ada_w", [4, D, 3 * D]); ada_b = ext("ada_b", [4, 3 * D])
    plg = ext("post_ln_g", [4, D]); plb = ext("post_ln_b", [4, D])
    ewi = ext("even_w_in", [2, D, 2496]); ewo = ext("even_w_out", [2, D, D])
    owi = ext("odd_w_in", [2, D, 2560]); owo = ext("odd_w_out", [2, D, D])
    gw2 = ext("gla_w2", [2, 2, 16, 256]); gb = ext("gla_b", [2, 2, 256]); gng = ext("gla_norm_g", [2, 128])
    qng = ext("mla_q_norm_g", [2, 256]); wuq = ext("mla_w_uq", [2, 256, 768])
    kng = ext("mla_kv_norm_g", [2, 128]); wukv = ext("mla_w_ukv", [2, 128, 1024])
    swT = ext("sgu_wT", [2, 4, 128, 128]); sbT = ext("sgu_bT", [2, 128, 4])
    csq = ext("csq", [T, 32]); csk = ext("csk", [SEQ + L, 32])
    ident_d = ext("ident", [128, 128]); g.ident_d = ident_d.ap()
    gcst = ext("gcst", [2, 64, 320]); mask = ext("mask", [128, 16])
    TW = ext("TW", [128, 64, 256]); FC = ext("FC", [128, 512]); W3 = ext("W3", [128, 256]); TWc = ext("TWc", [128, 2, 512])
    xout = nc.dram_tensor("xout", [Q, D], F32, kind="ExternalOutput")
    M = [g.dram(f"M{l}", [128, 3 * D]) for l in range(4)]
    X = [g.dram(f"X{i}", [T, D]) for i in range(2)]
    Z = g.dram("Z", [T, 2560])
    QP = g.dram("QP", [T, 768])
    KSIN = [g.dram(f"KSIN{p}", [Q // 2, 160]) for p in range(2)]
    KSG = [g.dram(f"KSG{p}", [4 * Q // 2, 160]) for p in range(2)]
    KSA = g.dram("KSA", [SEQ + L, 160])
    KVA = g.dram("KVA", [SEQ + L, 1024])
    OML = g.dram("OML", [T, 512]); OG = g.dram("OG", [T, 512])
    SEGL = g.dram("SEGL", [64, 8 * 129]); SEGA = g.dram("SEGA", [256, 8 * 129])
    FPR = [512, 512, 512, 512, 256]
    FIN = [g.dram(f"FIN{p}", [FPR[p], 512]) for p in range(5)]
    FALL = [g.dram(f"FALL{p}", [4 * FPR[p], 512]) for p in range(5)]
    OPR = [2048, 2048, 2048, 2048, 256]
    FOUT = [g.dram(f"FOUT{p}", [OPR[p], 128]) for p in range(5)]
    FOALL = [g.dram(f"FOALL{p}", [4 * OPR[p], 128]) for p in range(5)]
    S = g.S
    rows = lambda i: slice(i * 128, (i + 1) * 128)
    HN = 3 * D // 2
    for l in range(depth):
        for hh in range(2):
            cs = slice(hh * HN, (hh + 1) * HN)
            emit_k1(g, lambda i: cin.ap()[rows(i), :], 1, D, HN, ada_w.ap()[l][:, cs],
                    lambda i, l=l, cs=cs: M[l].ap()[rows(i), cs], "silu", bias=ada_b.ap()[l][cs])
    xcur = xin
    for l in range(depth):
        li = l // 2
        even = l % 2 == 0
        last = l == depth - 1
        Ml = M[l].ap()
        mods = (Ml[0, 0:D], Ml[0, D:2 * D], Ml[1, 0:D], Ml[1, D:2 * D])
        vecs = (Ml[0, 2 * D:3 * D], Ml[1, 2 * D:3 * D], plg.ap()[l], plb.ap()[l])
        N = 2496 if even else 2560
        Zl = Z.ap()[:, 0:N]
        xap = xcur.ap()
        emit_k1(g, lambda i, xap=xap: xap[rows(i), :], T // 128, D, N, (ewi if even else owi).ap()[li],
                lambda i, Zl=Zl: Zl[rows(i), :], "ln", n_lat_tiles=Q // 128, mods=mods)
        Tk = Q if last else T
        xn = xout if last else X[l % 2]
        if even:
            emit_k1(g, lambda i, Zl=Zl: Zl[rows(i), 1568:1824], T // 128, 256, 768, wuq.ap()[li],
                    lambda i: QP.ap()[rows(i), :], "rms", gvec=qng.ap()[li])
            g.phase()
            S.dma(lambda e, Zl=Zl: e.dma_start(out=KSA.ap()[0:L, :], in_=Zl[Q:T, 1824:1984]), w=["ksa0"], group="dc1")
            HQ = Q // 2
            for p in range(2):
                S.dma(lambda e, Zl=Zl, p=p: e.dma_start(out=KSIN[p].ap(), in_=Zl[HQ * p:HQ * (p + 1), 1824:1984]),
                      w=[f"ksin{p}"], group=f"dc0{p}")
                allgather(g, KSG[p], KSIN[p], f"ksg{p}", r=[f"ksin{p}"])
                dst = KSA.ap()[L:L + SEQ, :].rearrange("(r h n) c -> h r n c", r=4, h=2)[p]
                S.dma(lambda e, p=p, dst=dst: e.dma_start(out=dst, in_=KSG[p].ap().rearrange("(r n) c -> r n c", r=4)),
                      r=[f"ksg{p}"], w=[f"ksa1{p}"], group=f"dc2{p}")
            emit_k1(g, lambda i: KSA.ap()[rows(i), 0:128], (SEQ + L) // 128, 128, 1024, wukv.ap()[li],
                    lambda i: KVA.ap()[rows(i), :], "rms", gvec=kng.ap()[li])
            emit_k3(g, Q, L, SEQ + L, QP.ap(), KVA.ap(), KSA.ap()[:, 128:160], csq.ap(), csk.ap(), OML.ap())
            emit_k4s(g, Zl, OG.ap(), SEGL, SEGA, gw2.ap()[li], gb.ap()[li], gcst.ap(), mask.ap())
            emit_k5(g, Tk, "even", Q // 128, xap, ewo.ap()[li], vecs, Zl, xn.ap(), og=OG.ap(), oml=OML.ap(),
                    gng=gng.ap()[li])
        else:
            g.phase()
            r0 = 0
            for p in range(5):
                S.dma(lambda e, Zl=Zl, p=p, r0=r0: e.dma_start(out=FIN[p].ap(), in_=Zl[r0:r0 + FPR[p], 0:512]),
                      w=[f"fin{p}"], group=f"dcf{p}")
                allgather(g, FALL[p], FIN[p], f"fall{p}", r=[f"fin{p}"])
                r0 += FPR[p]
            emit_k7(g, [a.ap() for a in FALL], [a.ap() for a in FOUT], TW.ap(), FC.ap(), W3.ap(), TWc.ap(), mask.ap())
            g.phase()
            for p in range(5):
                allgather(g, FOALL[p], FOUT[p], f"foall{p}")
            fo = [a.ap().rearrange("(g r) c -> r g c", g=4) for a in FOALL]
            emit_k5(g, Tk, "odd", Q // 128, xap, owo.ap()[li], vecs, Zl, xn.ap(),
                    fr_lat=lambda qq, i, fo=fo: fo[qq][128 * i:128 * (i + 1), :, :],
                    fr_ctx=lambda i, fo=fo: fo[4][128 * (i - Q // 128):128 * (i - Q // 128 + 1), :, :],
                    swT=swT.ap()[li], sbT=sbT.ap()[li], mask=mask.ap())
        xcur = xn
    g.finish()
    return nc


def make_inputs(x, c, ctx, c_ctx, ada_w, ada_b, post_ln_g, post_ln_b, even_w_in, gla_w2, gla_b, gla_norm_g,
                mla_q_norm_g, mla_w_uq, mla_kv_norm_g, mla_w_ukv, even_w_out, odd_w_in, sgu_w, sgu_b, odd_w_out,
                rope_tables, fnet_consts):
    f32 = np.float32
    cc = lambda a: np.ascontiguousarray(a, dtype=f32)
    shared = dict(ada_w=cc(ada_w), ada_b=cc(ada_b), post_ln_g=cc(post_ln_g), post_ln_b=cc(post_ln_b),
                  even_w_in=cc(even_w_in), even_w_out=cc(even_w_out), odd_w_in=cc(odd_w_in), odd_w_out=cc(odd_w_out),
                  gla_w2=cc(gla_w2), gla_b=cc(gla_b), gla_norm_g=cc(gla_norm_g), mla_q_norm_g=cc(mla_q_norm_g),
                  mla_w_uq=cc(mla_w_uq), mla_kv_norm_g=cc(mla_kv_norm_g), mla_w_ukv=cc(mla_w_ukv),
                  sgu_wT=cc(np.transpose(sgu_w, (0, 1, 3, 2))), sgu_bT=cc(np.transpose(sgu_b, (0, 2, 1))),
                  csk=rope_tables(np.concatenate([-np.ones(L, int), np.arange(SEQ)])),
                  ident=np.eye(128, dtype=f32), gcst=gla_consts2(), **fnet_consts())
    maps = []
    for j in range(NCORES):
        b, i = j // 4, j % 4
        m = dict(shared)
        m["xin"] = cc(np.concatenate([x[b, Q * i:Q * (i + 1)], ctx[b]], 0))
        cin = np.zeros((128, D), f32)
        cin[0] = c[b]; cin[1] = c_ctx
        m["cin"] = cin
        m["csq"] = rope_tables(np.concatenate([np.arange(Q) + Q * i, -np.ones(L, int)]))
        mk = np.zeros((128, 16), f32)
        mk[:, i] = 1.0
        for jj in range(4):
            mk[:, 4 + jj] = 1.0 if jj < i else 0.0
            mk[:, 8 + jj] = 1.0 if jj > i else 0.0
        m["mask"] = mk
        maps.append(m)
    return maps

def rope_tables(pos):
    pos = np.asarray(pos)
    row = (pos // 64).astype(np.float32)
    col = (pos % 64).astype(np.float32)
    inv = (10000.0 ** (-np.arange(8, dtype=np.float32) / 8)).astype(np.float32)
    ang = np.concatenate([row[:, None] * inv, col[:, None] * inv], -1).astype(np.float32)
    c = np.cos(ang).astype(np.float32)
    s = np.sin(ang).astype(np.float32)
    ident = pos < 0
    c[ident] = 1.0
    s[ident] = 0.0
    return np.concatenate([c, s], -1).astype(np.float32)


def fnet_consts():
    n1 = np.arange(128)[:, None, None].astype(np.float64)
    n2 = np.arange(64)[None, :, None].astype(np.float64)
    k1 = np.arange(128)[None, None, :].astype(np.float64)
    ang = 2 * np.pi * k1 * (64 * n1 + n2) / 8192.0
    TW = np.concatenate([np.cos(ang), -np.sin(ang)], -1).astype(np.float32)
    c = np.arange(128)[:, None].astype(np.float64)
    cp = np.arange(128)[None, :].astype(np.float64)
    a = 2 * np.pi * c * cp / 128.0
    Cc, Sc = np.cos(a), np.sin(a)
    FC = np.concatenate([Cc, -Sc, Sc, Cc], -1).astype(np.float32)
    W3 = np.zeros((64, 2, 2, 2, 64), np.float64)
    n2v = np.arange(64)[:, None]
    k2v = np.arange(64)[None, :]
    a3 = 2 * np.pi * n2v * k2v / 64.0
    for aa in range(2):
        W3[:, aa, 0, aa, :] = np.cos(a3)
        W3[:, aa, 1, aa, :] = np.sin(a3)
    W3 = W3.reshape(128, 256).astype(np.float32)
    n = np.arange(256)[:, None].astype(np.float64)
    k = np.arange(256)[None, :].astype(np.float64)
    ac = 2 * np.pi * n * k / 256.0
    TWc = np.concatenate([np.cos(ac), -np.sin(ac)], -1).reshape(2, 128, 512).transpose(1, 0, 2)
    TWc = np.ascontiguousarray(TWc).astype(np.float32)
    return dict(TW=TW, FC=FC, W3=W3, TWc=TWc)


_NC = {}


def kernel(x, c, ctx, c_ctx, ada_w, ada_b, post_ln_g, post_ln_b, even_w_in, gla_w2, gla_b, gla_norm_g,
           mla_q_norm_g, mla_w_uq, mla_kv_norm_g, mla_w_ukv, even_w_out, odd_w_in, sgu_w, sgu_b, odd_w_out):
    if "nc" not in _NC:
        _NC["nc"] = build_fused(4)
    maps = make_inputs(np.asarray(x), np.asarray(c), np.asarray(ctx), np.asarray(c_ctx), np.asarray(ada_w),
                       np.asarray(ada_b), np.asarray(post_ln_g), np.asarray(post_ln_b), np.asarray(even_w_in),
                       np.asarray(gla_w2), np.asarray(gla_b), np.asarray(gla_norm_g), np.asarray(mla_q_norm_g),
                       np.asarray(mla_w_uq), np.asarray(mla_kv_norm_g), np.asarray(mla_w_ukv), np.asarray(even_w_out),
                       np.asarray(odd_w_in), np.asarray(sgu_w), np.asarray(sgu_b), np.asarray(odd_w_out),
                       rope_tables, fnet_consts)
    res = run_bass_kernel_spmd(_NC["nc"], maps, core_ids=list(range(NCORES)))
    out = np.empty((2, SEQ, D), np.float32)
    for j in range(NCORES):
        out[j // 4, (j % 4) * Q:(j % 4 + 1) * Q] = res.results[j]["xout"]
    return out
```

```python
import contextlib
import numpy as np
import concourse.bass as bass
import concourse.mybir as mybir
from concourse.bass_utils import run_bass_kernel_spmd

F32 = mybir.dt.float32
BF16 = mybir.dt.bfloat16
AF = mybir.ActivationFunctionType
ALU = mybir.AluOpType
AX = mybir.AxisListType
ALPHA = 8 ** 0.25
MLA_SCALE = 96 ** -0.5
ARENA_F32 = 52900


class Sched:
    def __init__(self, nc):
        self.nc = nc
        self.ops = []
        self.last_w = {}
        self.readers = {}
        self.stack = contextlib.ExitStack()
        self.bar = set()
        self.pending = {}

    def barrier(self):
        last = {}
        for i, op in enumerate(self.ops):
            k = ("dma", op["dma"]) if op["dma"] is not None else ("eng", op["eng"])
            last[k] = i
        self.bar = set(last.values())
        self.pending = {e: True for e in ["pe", "act", "dve", "pool", "sp"]}
        self.last_w = {}
        self.readers = {}

    muted = False

    def add(self, eng, fn, r=(), w=(), dma=None, inc=16):
        if self.muted:
            return -1
        idx = len(self.ops)
        deps = set()
        for k in r:
            if k in self.last_w:
                deps.add(self.last_w[k])
        for k in w:
            if k in self.last_w:
                deps.add(self.last_w[k])
            for x in self.readers.get(k, ()):
                deps.add(x)
        if self.pending.get(eng):
            deps |= self.bar
            self.pending[eng] = False
        deps.discard(idx)
        self.ops.append(dict(eng=eng, fn=fn, deps=deps, dma=dma, inc=inc))
        for k in r:
            self.readers.setdefault(k, []).append(idx)
        for k in w:
            self.last_w[k] = idx
            self.readers[k] = []
        return idx

    def pe(self, fn, r=(), w=()):
        return self.add("pe", fn, r, w)

    def act(self, fn, r=(), w=()):
        return self.add("act", fn, r, w)

    def dve(self, fn, r=(), w=()):
        return self.add("dve", fn, r, w)

    def pool(self, fn, r=(), w=()):
        return self.add("pool", fn, r, w)

    def dma(self, fn, r=(), w=(), group=None, eng="sp", inc=16):
        assert group is not None
        return self.add(eng, fn, r, w, dma=group, inc=inc)

    def emit(self):
        nc = self.nc
        ops = self.ops
        n = len(ops)
        needs_signal = [False] * n
        for i, op in enumerate(ops):
            keep = set()
            for d in op["deps"]:
                dop = ops[d]
                if dop["dma"] is None and dop["eng"] == op["eng"] and op["eng"] == "pe":
                    continue
                keep.add(d)
                needs_signal[d] = True
            op["deps"] = keep
        engs = ["pe", "act", "dve", "pool", "sp"]
        sems = {e: self.stack.enter_context(nc.semaphore(f"s_{e}")) for e in engs}
        cnt = {e: 0 for e in engs}
        groups = {}
        gcnt = {}
        for i, op in enumerate(ops):
            if op["dma"] is not None:
                g = op["dma"]
                if g not in groups:
                    groups[g] = self.stack.enter_context(nc.semaphore(f"d_{len(groups)}"))
                    gcnt[g] = 0
                op["sem"] = groups[g]
                gcnt[g] += op["inc"] * getattr(op["fn"], "ndma", 1)
                op["val"] = gcnt[g]
            elif needs_signal[i]:
                cnt[op["eng"]] += 1
                op["sem"] = sems[op["eng"]]
                op["val"] = cnt[op["eng"]]
        print("sched: ops", n, "dma groups", len(groups), "sem counts", cnt, flush=True)
        final = dict((g, (groups[g], gcnt[g])) for g in groups)

        def stream(ename):
            def body(eng):
                known = {}
                for i, op in enumerate(ops):
                    if op["eng"] != ename:
                        continue
                    for d in sorted(op["deps"]):
                        dop = ops[d]
                        s, v = dop["sem"], dop["val"]
                        if known.get(id(s), 0) < v:
                            eng.wait_ge(s, v)
                            known[id(s)] = v
                    ins = op["fn"](eng)
                    if op["dma"] is not None:
                        if not isinstance(ins, (list, tuple)):
                            ins = [ins]
                        assert len(ins) == getattr(op["fn"], "ndma", 1)
                        for x in ins:
                            x.then_inc(op["sem"], op["inc"])
                    elif needs_signal[i]:
                        ins.then_inc(op["sem"], 1)
                if ename == "sp":
                    for g, (s, v) in final.items():
                        if known.get(id(s), 0) < v:
                            eng.wait_ge(s, v)
            return body

        with nc.Block() as block:
            block.tensor(stream("pe"))
            block.scalar(stream("act"))
            block.vector(stream("dve"))
            block.gpsimd(stream("pool"))
            block.sync(stream("sp"))

    def close(self):
        self.stack.close()


class G:
    def __init__(self, nc):
        self.nc = nc
        self.S = Sched(nc)
        self.arena = self.S.stack.enter_context(nc.sbuf_tensor("arena", [128, ARENA_F32], F32))
        self.banks = [self.S.stack.enter_context(nc.psum_tensor(f"bank{i}", [128, 512], F32)) for i in range(8)]
        self.off = 0
        self.nd = 0

    nphase = 0
    max_phase = 10 ** 9

    def phase(self):
        self.nphase += 1
        if self.nphase > self.max_phase:
            self.S.muted = True
        self.S.barrier()
        self.off = 0

    def sb(self, shape, dtype=F32, name=None):
        P = shape[0]
        n = int(np.prod(shape[1:]))
        words = n if dtype == F32 else (n + 1) // 2
        o = self.off
        self.off += words
        assert self.off <= ARENA_F32, ("SBUF arena overflow", self.off)
        ap = self.arena[0:P, o:o + words]
        if dtype != F32:
            ap = ap.bitcast(dtype)[:, 0:n]
        if len(shape) == 3:
            ap = ap.rearrange("p (a b) -> p a b", a=shape[1], b=shape[2])
        elif len(shape) == 4:
            ap = ap.rearrange("p (a b c) -> p a b c", a=shape[1], b=shape[2], c=shape[3])
        return ap

    def ps(self, bank, shape, dtype=F32):
        P = shape[0]
        n = int(np.prod(shape[1:]))
        ap = self.banks[bank][0:P, :]
        if dtype != F32:
            ap = ap.bitcast(dtype)
        ap = ap[:, 0:n]
        if len(shape) == 3:
            ap = ap.rearrange("p (a b) -> p a b", a=shape[1], b=shape[2])
        return ap

    def dram(self, name, shape, kind="Internal"):
        return self.nc.dram_tensor(name, list(shape), F32, kind=kind)

    def finish(self):
        self.S.emit()
        self.S.close()


def load_ident(g, tag="id"):
    S = g.S
    identf = g.sb([128, 128], F32)
    ident = g.sb([128, 128], BF16)
    S.dma(lambda e: e.dma_start(out=identf[:], in_=g.ident_d), w=["identf"], group="identf")
    S.dve(lambda e: e.tensor_copy(ident[:], identf[:]), r=["identf"], w=["ident"])
    return identf, ident


def dcopy(g, dst, src, name, grp="dcopy"):
    g.S.dma(lambda e: e.dma_start(out=dst, in_=src), w=[name], group=grp)


def allgather(g, dst, src, name, r=()):
    groups = [[0, 1, 2, 3], [4, 5, 6, 7]]
    g.S.dma(lambda e: e.collective_compute("AllGather", ALU.bypass, replica_groups=groups,
                                            ins=[src.ap().opt()], outs=[dst.ap().opt()]),
            r=list(r), w=[name], group="cc", eng="pool", inc=1)


def emit_k1(g, xsrc, nt, K, N, w, z, mode, n_lat_tiles=None, mods=None, gvec=None, bias=None, eps=1e-6):
    S = g.S
    g.phase()
    kc = K // 128
    nch = (N + 511) // 512
    identf, ident = load_ident(g)
    wbf = g.sb([128, kc, N], BF16)
    wst = [g.sb([128, N], F32) for i in range(2)]
    if mode == "ln":
        modt = g.sb([128, 4, K], F32)
        for a in range(4):
            S.dma(lambda e, a=a: e.dma_start(out=modt[:, a, :], in_=mods[a].partition_broadcast(128)),
                  w=["modt"], group="modt")
        for a in (1, 3):
            S.dve(lambda e, a=a: e.tensor_scalar_add(modt[:, a, :], modt[:, a, :], 1.0), r=["modt"], w=["modt"])
    elif mode == "rms":
        gt = g.sb([128, K], F32)
        S.dma(lambda e: e.dma_start(out=gt[:], in_=gvec.partition_broadcast(128)), w=["gt"], group="gt")
    else:
        bt = g.sb([128, N], F32)
        S.dma(lambda e: e.dma_start(out=bt[:], in_=bias.partition_broadcast(128)), w=["bt"], group="gt")
    for k in range(kc):
        s = k % 2
        S.dma(lambda e, k=k, s=s: e.dma_start(out=wst[s][:], in_=w[k * 128:(k + 1) * 128, :]),
              w=[f"wst{s}"], group=f"wst{s}")
        if k % 2 == 0:
            S.act(lambda e, k=k, s=s: e.copy(wbf[:, k, :], wst[s][:]), r=[f"wst{s}"], w=[f"wbf{k}"])
        else:
            S.dve(lambda e, k=k, s=s: e.tensor_copy(wbf[:, k, :], wst[s][:]), r=[f"wst{s}"], w=[f"wbf{k}"])
    NX = 2
    xt = [g.sb([128, K], F32) for i in range(NX)]
    xn = [g.sb([128, K], F32) for i in range(NX)]
    hb = [g.sb([128, K], BF16) for i in range(NX)]
    hT = [g.sb([128, kc, 128], BF16) for i in range(NX)]
    st = [g.sb([128, 8, 6], F32) for i in range(NX)]
    mv = [g.sb([128, 4], F32) for i in range(NX)]
    zt = [g.sb([128, N], F32) for i in range(NX)]
    tp = [g.ps(i, [128, kc, 128], BF16) for i in range(2)]
    zp = [g.ps(2 + i, [128, 512], F32) for i in range(4)]
    wkeys = [f"wbf{k}" for k in range(kc)]
    zpi = 0
    def emit_load(i):
        s = i % NX
        S.dma(lambda e, i=i, s=s: e.dma_start(out=xt[s][:], in_=xsrc(i)), w=[f"xt{s}"], group=f"xt{s}")

    emit_load(0)
    for i in range(nt):
        s = i % NX
        if i + 1 < nt:
            emit_load(i + 1)
        if mode == "ln":
            nsub = max(1, K // 512)
            fs = K // nsub
            for j in range(nsub):
                S.dve(lambda e, s=s, j=j, fs=fs: e.bn_stats(st[s][:, j, :], xt[s][:, j * fs:(j + 1) * fs]),
                      r=[f"xt{s}"], w=[f"st{s}"])
            S.dve(lambda e, s=s, nsub=nsub: e.bn_aggr(mv[s][:, 0:2], st[s][:, 0:nsub, :]), r=[f"st{s}"], w=[f"mv{s}"])
            S.dve(lambda e, s=s: e.tensor_scalar_add(mv[s][:, 3:4], mv[s][:, 1:2], eps), r=[f"mv{s}"], w=[f"mv{s}"])
            S.act(lambda e, s=s: e.sqrt(mv[s][:, 3:4], mv[s][:, 3:4]), r=[f"mv{s}"], w=[f"mv{s}"])
            S.dve(lambda e, s=s: e.reciprocal(mv[s][:, 2:3], mv[s][:, 3:4]), r=[f"mv{s}"], w=[f"mv{s}"])
            S.dve(lambda e, s=s: e.tensor_scalar(xn[s][:], xt[s][:], mv[s][:, 0:1], mv[s][:, 2:3],
                                                 ALU.subtract, ALU.mult),
                  r=[f"xt{s}", f"mv{s}"], w=[f"xn{s}"])
            a = 0 if (n_lat_tiles is None or i < n_lat_tiles) else 2
            S.pool(lambda e, s=s, a=a: e.tensor_tensor(xn[s][:], xn[s][:], modt[:, a + 1, :], ALU.mult),
                   r=[f"xn{s}", "modt"], w=[f"xn{s}"])
            S.pool(lambda e, s=s, a=a: e.tensor_tensor(hb[s][:], xn[s][:], modt[:, a, :], ALU.add),
                   r=[f"xn{s}", "modt"], w=[f"hb{s}"])
        elif mode == "silu":
            S.act(lambda e, s=s: e.activation(hb[s][:], xt[s][:], AF.Silu), r=[f"xt{s}"], w=[f"hb{s}"])
        else:
            S.act(lambda e, s=s: e.activation(xn[s][:], xt[s][:], AF.Square, accum_out=mv[s][:, 0:1]),
                  r=[f"xt{s}"], w=[f"xn{s}", f"mv{s}"])
            S.dve(lambda e, s=s: e.tensor_scalar(mv[s][:, 1:2], mv[s][:, 0:1], 1.0 / K, eps, ALU.mult, ALU.add),
                  r=[f"mv{s}"], w=[f"mv{s}"])
            S.act(lambda e, s=s: e.sqrt(mv[s][:, 3:4], mv[s][:, 1:2]), r=[f"mv{s}"], w=[f"mv{s}"])
            S.dve(lambda e, s=s: e.reciprocal(mv[s][:, 2:3], mv[s][:, 3:4]), r=[f"mv{s}"], w=[f"mv{s}"])
            S.dve(lambda e, s=s: e.scalar_tensor_tensor(hb[s][:], xt[s][:], mv[s][:, 2:3], gt[:],
                                                        ALU.mult, ALU.mult),
                  r=[f"xt{s}", f"mv{s}", "gt"], w=[f"hb{s}"])
        t = i % 2
        for k in range(kc):
            S.pe(lambda e, s=s, t=t, k=k: e.transpose(tp[t][:, k, :], hb[s][:, k * 128:(k + 1) * 128], ident[:]),
                 r=[f"hb{s}", "ident"], w=[f"tp{t}"])
        S.act(lambda e, s=s, t=t: e.copy(hT[s][:], tp[t][:]), r=[f"tp{t}"], w=[f"hT{s}"])
        for c in range(nch):
            c0, c1 = c * 512, min(N, (c + 1) * 512)
            p = zpi % 4
            zpi += 1
            for k in range(kc):
                S.pe(lambda e, s=s, p=p, k=k, c0=c0, c1=c1: e.matmul(
                    zp[p][:, 0:c1 - c0], hT[s][:, k, :], wbf[:, k, c0:c1], start=(k == 0), stop=(k == kc - 1)),
                    r=[f"hT{s}", wkeys[k]], w=[f"zp{p}"])
            if mode == "silu":
                S.dve(lambda e, s=s, p=p, c0=c0, c1=c1: e.tensor_tensor(zt[s][:, c0:c1], zp[p][:, 0:c1 - c0], bt[:, c0:c1], ALU.add),
                      r=[f"zp{p}", "bt"], w=[f"zt{s}"])
            elif c % 2 == 0:
                S.dve(lambda e, s=s, p=p, c0=c0, c1=c1: e.tensor_copy(zt[s][:, c0:c1], zp[p][:, 0:c1 - c0]),
                      r=[f"zp{p}"], w=[f"zt{s}"])
            else:
                S.act(lambda e, s=s, p=p, c0=c0, c1=c1: e.copy(zt[s][:, c0:c1], zp[p][:, 0:c1 - c0]),
                      r=[f"zp{p}"], w=[f"zt{s}"])
        S.dma(lambda e, i=i, s=s: e.dma_start(out=z(i), in_=zt[s][:]),
              r=[f"zt{s}"], w=[f"zout{s}"], group=f"zo{s}")


def emit_k3(g, NQL, NQC, NK, q, kv, kr, csq, csk, out, nheads=8):
    S = g.S
    g.phase()
    NQ = NQL + NQC
    nqt = NQ // 128
    nkt = NK // 128
    identf, ident = load_ident(g)
    KH = (nkt + 1) // 2
    kvh = g.sb([128, KH, 128], F32)
    kpad = g.sb([128, nkt, 128], BF16)
    vx = [g.sb([128, nkt, 65], BF16) for i in range(2)]
    kT = [g.sb([128, nkt * 128], BF16) for i in range(2)]
    qT = [g.sb([128, nqt * 128], BF16) for i in range(2)]
    qh = g.sb([128, nqt, 96], F32)
    qpad = g.sb([128, nqt, 128], BF16)
    krl = g.sb([128, nkt, 32], F32)
    cskt = g.sb([128, nkt, 32], F32)
    csqt = g.sb([128, nqt, 32], F32)
    tk = [g.sb([128, nkt, 16], F32) for i in range(2)]
    tq = [g.sb([128, nqt, 16], F32) for i in range(2)]
    pT = [g.sb([128, 512], BF16) for i in range(3)]
    oTs = [g.sb([65, 512], F32) for i in range(2)]
    ost = [g.sb([128, 64], F32) for i in range(4)]
    rc = [g.sb([128, 1], F32) for i in range(4)]
    sTp = [g.ps(i, [128, 512], F32) for i in range(3)]
    oTp = [g.ps(3 + i, [65, 512], F32) for i in range(2)]
    tpk = g.ps(5, [128, 8, 128], BF16)
    tpo = g.ps(6, [128, 65], F32)

    S.pool(lambda e: e.memset(kpad[:], 0.0), w=["kpad"])
    S.pool(lambda e: e.memset(qpad[:], 0.0), w=["qpad"])
    for i in range(2):
        S.pool(lambda e, i=i: e.memset(vx[i][:], 1.0), w=[f"vx{i}"])
    S.dma(lambda e: e.dma_start(out=krl[:], in_=kr.rearrange("(t p) c -> p t c", p=128)), w=["krl"], group="krl")
    S.dma(lambda e: e.dma_start(out=cskt[:], in_=csk.rearrange("(t p) c -> p t c", p=128)), w=["cskt"], group="cskt")
    S.dma(lambda e: e.dma_start(out=csqt[:], in_=csq.rearrange("(t p) c -> p t c", p=128)), w=["csqt"], group="csqt")

    def rope(eng_a, eng_b, src, cs, tmp, dst, keys_r, key_tmp, key_dst, xo):
        x1 = lambda: src[:, :, xo:xo + 16]
        x2 = lambda: src[:, :, xo + 16:xo + 32]
        c = lambda: cs[:, :, 0:16]
        sn = lambda: cs[:, :, 16:32]
        S.add(eng_a, lambda e: e.tensor_tensor(tmp[0][:], x1(), c(), ALU.mult), r=keys_r, w=[key_tmp + "0"])
        S.add(eng_b, lambda e: e.tensor_tensor(tmp[1][:], x2(), sn(), ALU.mult), r=keys_r, w=[key_tmp + "1"])
        S.add(eng_a, lambda e: e.tensor_tensor(dst[:, :, 64:80], tmp[0][:], tmp[1][:], ALU.subtract),
              r=[key_tmp + "0", key_tmp + "1"], w=[key_dst])
        S.add(eng_a, lambda e: e.tensor_tensor(tmp[0][:], x1(), sn(), ALU.mult), r=keys_r, w=[key_tmp + "0"])
        S.add(eng_b, lambda e: e.tensor_tensor(tmp[1][:], x2(), c(), ALU.mult), r=keys_r, w=[key_tmp + "1"])
        S.add(eng_a, lambda e: e.tensor_tensor(dst[:, :, 96:112], tmp[0][:], tmp[1][:], ALU.add),
              r=[key_tmp + "0", key_tmp + "1"], w=[key_dst])

    rope("dve", "pool", krl, cskt, tk, kpad, ["krl", "cskt"], "tk", "kpad", 0)

    chunks = []
    for c0 in range(0, NQL, 512):
        chunks.append((c0, min(512, NQL - c0), 0, nkt))
    if NQC:
        chunks.append((NQL, NQC, 0, 2))
    sti = 0
    oti = 0
    osti = 0
    def prologue(h):
        hb = h % 2
        for half in range(2):
            t0 = half * KH
            t1 = min(nkt, t0 + KH)
            if t1 <= t0:
                continue
            S.dma(lambda e, h=h, t0=t0, t1=t1: e.dma_start(
                out=kvh[:, 0:t1 - t0, :],
                in_=kv[t0 * 128:t1 * 128, h * 128:(h + 1) * 128].rearrange("(t p) c -> p t c", p=128)),
                w=["kvh"], group="kvh")
            S.dve(lambda e, t0=t0, t1=t1: e.tensor_copy(kpad[:, t0:t1, 0:64], kvh[:, 0:t1 - t0, 0:64]),
                  r=["kvh"], w=["kpad"])
            S.pool(lambda e, t0=t0, t1=t1, hb=hb: e.tensor_copy(vx[hb][:, t0:t1, 0:64], kvh[:, 0:t1 - t0, 64:128]),
                   r=["kvh"], w=[f"vx{hb}"])
        for g0 in range(0, nkt, 8):
            g1 = min(nkt, g0 + 8)
            for t in range(g0, g1):
                S.pe(lambda e, t=t, g0=g0: e.transpose(tpk[:, t - g0, :], kpad[:, t, :], ident[:]),
                     r=["kpad", "ident"], w=["tpk"])
            S.dve(lambda e, g0=g0, g1=g1, hb=hb: e.tensor_copy(
                kT[hb][:, g0 * 128:g1 * 128], tpk[:, 0:g1 - g0, :].rearrange("p a b -> p (a b)")),
                r=["tpk"], w=[f"kT{hb}"])
        S.dma(lambda e, h=h: e.dma_start(out=qh[:], in_=q[:, h * 96:(h + 1) * 96].rearrange("(t p) c -> p t c", p=128)),
              w=["qh"], group="qh")
        S.pool(lambda e: e.tensor_copy(qpad[:, :, 0:64], qh[:, :, 0:64]), r=["qh"], w=["qpad"])
        rope("pool", "dve", qh, csqt, tq, qpad, ["qh", "csqt"], "tq", "qpad", 64)
        for g0 in range(0, nqt, 8):
            g1 = min(nqt, g0 + 8)
            for t in range(g0, g1):
                S.pe(lambda e, t=t, g0=g0: e.transpose(tpk[:, t - g0, :], qpad[:, t, :], ident[:]),
                     r=["qpad", "ident"], w=["tpk"])
            S.dve(lambda e, g0=g0, g1=g1, hb=hb: e.tensor_copy(
                qT[hb][:, g0 * 128:g1 * 128], tpk[:, 0:g1 - g0, :].rearrange("p a b -> p (a b)")),
                r=["tpk"], w=[f"qT{hb}"])

    st = dict(sti=0, oti=0, osti=0)
    LAG = 2

    def mainloop(h):
        hb = h % 2
        its = []
        for (q0, qn, k0, k1) in chunks:
            op = st["oti"] % 2
            st["oti"] += 1
            for kt in range(k0, k1):
                its.append((q0, qn, k0, k1, kt, op))

        def emit_qk(n):
            q0, qn, k0, k1, kt, op = its[n]
            sp = st["sti"] % 3
            st["sti"] += 1
            its[n] = its[n] + (sp,)
            S.pe(lambda e: e.matmul(sTp[sp][:, 0:qn], kT[hb][:, kt * 128:(kt + 1) * 128], qT[hb][:, q0:q0 + qn],
                                    start=True, stop=True), r=[f"kT{hb}", f"qT{hb}"], w=[f"sTp{sp}"])
            S.act(lambda e: e.activation(pT[sp][:, 0:qn], sTp[sp][:, 0:qn], AF.Exp, scale=MLA_SCALE),
                  r=[f"sTp{sp}"], w=[f"pT{sp}"])

        def emit_pv(n):
            q0, qn, k0, k1, kt, op, sp = its[n]
            S.pe(lambda e: e.matmul(oTp[op][:, 0:qn], vx[hb][:, kt, :], pT[sp][:, 0:qn], start=(kt == k0), stop=(kt == k1 - 1)),
                 r=[f"vx{hb}", f"pT{sp}"], w=[f"oTp{op}"])
            if kt != k1 - 1:
                return
            S.dve(lambda e: e.tensor_copy(oTs[op][:, 0:qn], oTp[op][:, 0:qn]), r=[f"oTp{op}"], w=[f"oTs{op}"])
            for j in range(qn // 128):
                os_ = st["osti"] % 4
                st["osti"] += 1
                S.pe(lambda e, j=j: e.transpose(tpo[:], oTs[op][:, j * 128:(j + 1) * 128], identf[0:65, 0:65]),
                     r=[f"oTs{op}", "identf"], w=["tpo"])
                S.dve(lambda e, os_=os_: e.reciprocal(rc[os_][:], tpo[:, 64:65]), r=["tpo"], w=[f"rc{os_}"])
                S.dve(lambda e, os_=os_: e.tensor_scalar(ost[os_][:], tpo[:, 0:64], rc[os_][:, 0:1], None, ALU.mult),
                      r=["tpo", f"rc{os_}"], w=[f"ost{os_}"])
                r0 = q0 + j * 128
                S.dma(lambda e, os_=os_, r0=r0: e.dma_start(out=out[r0:r0 + 128, h * 64:(h + 1) * 64], in_=ost[os_][:]),
                      r=[f"ost{os_}"], w=[f"oo{os_}"], group=f"oo{os_}")

        nI = len(its)
        for n in range(nI):
            emit_qk(n)
            if n >= LAG:
                emit_pv(n - LAG)
            if n == nI // 2 and h + 1 < nheads:
                prologue(h + 1)
        for n in range(max(0, nI - LAG), nI):
            emit_pv(n)

    prologue(0)
    for h in range(nheads):
        mainloop(h)


def gla_consts2():
    s = np.arange(64)[:, None]
    t = np.arange(64)[None, :]
    out = np.zeros((2, 64, 320), np.float32)
    U = (s <= t).astype(np.float32)
    out[0, :, 0:64] = U - (s <= 31)
    out[0, :, 64:128] = U
    out[0, :, 128:192] = (s > t)
    out[0, :, 192:256] = U
    Ub = (s >= t).astype(np.float32)
    out[1, :, 0:64] = Ub - (s >= 32)
    out[1, :, 64:128] = Ub
    out[1, :, 128:192] = (s < t)
    out[1, :, 192:256] = Ub
    out[:, 0, 256:320] = 1.0
    return out


def emit_k4s(g, Z, OG, segloc, segall, w2, bb, cst_d, mask):
    S = g.S
    g.phase()
    NLC, NCC = 32, 4
    ZH = ["Zh0", "Zh1", "Zh2", "Zh3"]
    NCH = NLC + NCC
    identf, ident = load_ident(g)
    cst = g.sb([64, 2, 320], F32)
    maskb = g.sb([64, 2, 64], BF16)
    w2t = g.sb([16, 2, 256], F32)
    bbt = g.sb([1, 2, 256], F32)
    mk = g.sb([64, 16], F32)
    zh_off = g.off
    Zh = g.sb([64, NCH, 288], F32)
    vb = g.sb([64, NCH, 128], BF16)
    OLs = g.sb([64, NCH, 512], F32)
    qbP = g.sb([64, 8, NLC, 64], BF16)
    Sctx = g.sb([64, 8, 128], F32)
    SEG = g.sb([64, 8, 129], F32)
    Sib = g.sb([64, 8, 128], BF16)
    Wk = g.sb([64, 128], F32)
    cand = g.sb([64, 128], F32)
    diff = g.sb([64, 128], F32)
    NP = 4
    mk_t = lambda shape, dt: [g.sb(shape, dt) for i in range(NP)]
    qTs = mk_t([64, 64], F32); kTs = mk_t([64, 64], F32); glTs = mk_t([16, 64], F32)
    Et = mk_t([64, 64], F32); Lt = mk_t([64, 64], F32); e12 = mk_t([64, 192], F32); e4 = mk_t([64, 64], F32)
    dec = mk_t([64, 1], F32)
    qe = mk_t([64, 64], BF16); ke = mk_t([64, 64], BF16); qb = mk_t([64, 64], BF16); kd = mk_t([64, 64], BF16)
    attm = mk_t([64, 64], BF16)
    Sf = [g.sb([64, 128], F32) for i in range(2)]
    Sb = [g.sb([64, 128], BF16) for i in range(2)]
    Pt = [g.sb([64, 1], F32) for i in range(2)]
    pA = [g.ps(i, [64, 512], F32) for i in range(NP)]
    pB = [g.ps(4 + i, [64, 512], F32) for i in range(NP)]
    id64 = identf[0:64, 0:64]

    S.dma(lambda e: e.dma_start(out=cst[:], in_=cst_d.rearrange("d p f -> p d f")), w=["cst"], group="cst")
    S.dve(lambda e: e.tensor_copy(maskb[:], cst[:, :, 192:256]), r=["cst"], w=["maskb"])
    S.dma(lambda e: e.dma_start(out=w2t[:], in_=w2.rearrange("d r e -> r d e")), w=["w2t"], group="w2t")
    S.dma(lambda e: e.dma_start(out=bbt[:], in_=bb.rearrange("(o d) e -> o d e", o=1)), w=["bbt"], group="bbt")
    S.dma(lambda e: e.dma_start(out=mk[:], in_=mask[0:64, :]), w=["mk"], group="mk")
    Zv = Z.rearrange("(c p) f -> p c f", p=64)
    tasks = []
    ci = 0
    for h in range(4):
        first_of_head = True
        for d in range(2):
            hd = h * 2 + d
            for part in ("ctx", "lat"):
                lat = part == "lat"
                order = list(range(NLC)) if lat else list(range(NLC, NCH))
                if d == 1:
                    order = order[::-1]
                si = 0
                for idx, c in enumerate(order):
                    p = ci % NP
                    ci += 1
                    s0, s1 = si % 2, (si + 1) % 2
                    si += 1
                    tasks.append(dict(h=h, d=d, hd=hd, lat=lat, c=c, p=p, s0=s0, s1=s1, first=(idx == 0),
                                      last=(idx == len(order) - 1), head_start=first_of_head))
                    first_of_head = False

    def load_head(h):
        srcs = [(slice(h * 64, (h + 1) * 64), slice(0, 64)), (slice(256 + h * 64, 256 + (h + 1) * 64), slice(64, 128)),
                (slice(512 + h * 128, 512 + (h + 1) * 128), slice(128, 256)), (slice(1024, 1056), slice(256, 288))]
        for k, (sc, dc) in enumerate(srcs):
            S.dma(lambda e, sc=sc, dc=dc: e.dma_start(out=Zh[:, :, dc], in_=Zv[:, :, sc]), w=[f"Zh{k}"], group=f"Zh{k}")
        S.pool(lambda e: e.tensor_copy(vb[:], Zh[:, :, 128:256]), r=ZH, w=["vb"])

    def stageA(t):
        h, d, c, p = t["h"], t["d"], t["c"], t["p"]
        A = f"pA{p}"
        w2hd = w2t[:, d, h * 64:(h + 1) * 64]
        bbhd = bbt[0:1, d, h * 64:(h + 1) * 64]
        cD = lambda a, b: cst[:, d, a:b]
        deccol = 127 if d == 0 else 64
        gcol = 256 + 16 * d
        S.pe(lambda e: e.transpose(pA[p][:, 320:384], Zh[:, c, 0:64], id64), r=ZH + ["identf"], w=[A])
        S.pe(lambda e: e.transpose(pA[p][:, 384:448], Zh[:, c, 64:128], id64), r=ZH + ["identf"], w=[A])
        S.pe(lambda e: e.transpose(pA[p][0:16, 448:512], Zh[:, c, gcol:gcol + 16], id64), r=ZH + ["identf"], w=[A])
        S.act(lambda e: e.mul(qTs[p][:], pA[p][:, 320:384], 0.125), r=[A], w=[f"qTs{p}"])
        S.act(lambda e: e.copy(kTs[p][:], pA[p][:, 384:448]), r=[A], w=[f"kTs{p}"])
        S.act(lambda e: e.copy(glTs[p][:], pA[p][0:16, 448:512]), r=[A], w=[f"glTs{p}"])
        S.pe(lambda e: e.matmul(pA[p][:, 0:64], glTs[p][:], w2hd, start=True, stop=False), r=[f"glTs{p}", "w2t"], w=[A])
        S.pe(lambda e: e.matmul(pA[p][:, 0:64], cst[0:1, 0, 256:320], bbhd, start=False, stop=True), r=["cst", "bbt"], w=[A])
        S.act(lambda e: e.activation(Et[p][:], pA[p][:, 0:64], AF.Exp, scale=-1.0), r=[A], w=[f"Et{p}"])
        S.act(lambda e: e.activation(Lt[p][:], Et[p][:], AF.Ln, bias=1.0), r=[f"Et{p}"], w=[f"Lt{p}"])
        S.pe(lambda e: e.matmul(pA[p][:, 64:128], Lt[p][:], cD(0, 64), start=True, stop=True), r=[f"Lt{p}", "cst"], w=[A])
        S.pe(lambda e: e.matmul(pA[p][:, 128:192], Lt[p][:], cD(64, 128), start=True, stop=True), r=[f"Lt{p}", "cst"], w=[A])
        S.pe(lambda e: e.matmul(pA[p][:, 192:256], cD(128, 192), Lt[p][:], start=True, stop=True), r=[f"Lt{p}", "cst"], w=[A])
        S.act(lambda e: e.activation(e12[p][:, 0:128], pA[p][:, 64:192], AF.Exp, scale=-1.0 / 16), r=[A], w=[f"e12{p}"])
        S.act(lambda e: e.activation(e12[p][:, 128:192], pA[p][:, 64:128], AF.Exp, scale=1.0 / 16), r=[A], w=[f"e12{p}"])
        S.act(lambda e: e.activation(e4[p][:], pA[p][:, 192:256], AF.Exp, scale=-1.0 / 16), r=[A], w=[f"e4{p}"])
        S.act(lambda e: e.copy(dec[p][:], e12[p][:, deccol:deccol + 1]), r=[f"e12{p}"], w=[f"dec{p}"])
        S.dve(lambda e: e.tensor_tensor(qe[p][:], qTs[p][:], e12[p][:, 0:64], ALU.mult), r=[f"qTs{p}", f"e12{p}"], w=[f"qe{p}"])
        S.dve(lambda e: e.tensor_tensor(ke[p][:], kTs[p][:], e12[p][:, 128:192], ALU.mult), r=[f"kTs{p}", f"e12{p}"], w=[f"ke{p}"])
        S.pool(lambda e: e.tensor_tensor(qb[p][:], qTs[p][:], e12[p][:, 64:128], ALU.mult), r=[f"qTs{p}", f"e12{p}"], w=[f"qb{p}"])
        S.pool(lambda e: e.tensor_tensor(kd[p][:], Zh[:, c, 64:128], e4[p][:], ALU.mult), r=ZH + [f"e4{p}"], w=[f"kd{p}"])

    def stageB(t):
        h, d, hd, c, p, s0, s1, lat = t["h"], t["d"], t["hd"], t["c"], t["p"], t["s0"], t["s1"], t["lat"]
        B = f"pB{p}"
        if t["first"]:
            S.dve(lambda e: e.memset(Sf[0][:], 0.0), w=["Sf0"])
            S.pool(lambda e: e.memset(Sb[0][:], 0.0), w=["Sb0"])
            if lat:
                S.dve(lambda e: e.memset(Pt[0][:], 1.0), w=["P0"])
        if lat:
            S.dve(lambda e: e.scalar_tensor_tensor(qbP[:, hd, c, :], qTs[p][:], Pt[s0][:, 0:1], e12[p][:, 64:128],
                                                   ALU.mult, ALU.mult), r=[f"qTs{p}", f"P{s0}", f"e12{p}"], w=["qbP"])
            S.dve(lambda e: e.tensor_tensor(Pt[s1][:], Pt[s0][:], dec[p][:], ALU.mult), r=[f"P{s0}", f"dec{p}"], w=[f"P{s1}"])
        S.pe(lambda e: e.matmul(pB[p][:, 256:320], ke[p][:], qe[p][:], start=True, stop=True), r=[f"ke{p}", f"qe{p}"], w=[B])
        S.dve(lambda e: e.tensor_tensor(attm[p][:], pB[p][:, 256:320], maskb[:, d, :], ALU.mult), r=[B, "maskb"], w=[f"attm{p}"])
        S.pe(lambda e: e.matmul(pB[p][:, 0:128], attm[p][:], vb[:, c, :], start=True, stop=False), r=[f"attm{p}", "vb"], w=[B])
        S.pe(lambda e: e.matmul(pB[p][:, 0:128], qb[p][:], Sb[s0][:], start=False, stop=True), r=[f"qb{p}", f"Sb{s0}"], w=[B])
        S.pe(lambda e: e.matmul(pB[p][:, 128:256], kd[p][:], vb[:, c, :], start=True, stop=True), r=[f"kd{p}", "vb"], w=[B])
        S.dve(lambda e: e.scalar_tensor_tensor(Sf[s1][:], Sf[s0][:], dec[p][:, 0:1], pB[p][:, 128:256], ALU.mult, ALU.add),
              r=[f"Sf{s0}", f"dec{p}", B], w=[f"Sf{s1}"])
        S.pool(lambda e: e.tensor_copy(Sb[s1][:], Sf[s1][:]), r=[f"Sf{s1}"], w=[f"Sb{s1}"])
        ocols = slice(h * 128, (h + 1) * 128)
        if d == 0:
            S.dve(lambda e: e.tensor_copy(OLs[:, c, ocols], pB[p][:, 0:128]), r=[B], w=[f"OL{c}"])
        else:
            S.dve(lambda e: e.tensor_tensor(OLs[:, c, ocols], pB[p][:, 0:128], OLs[:, c, ocols], ALU.add),
                  r=[B, f"OL{c}"], w=[f"OL{c}"])
        if t["last"]:
            if lat:
                S.dve(lambda e: e.tensor_copy(SEG[:, hd, 0:128], Sf[s1][:]), r=[f"Sf{s1}"], w=["SEG"])
                S.dve(lambda e: e.tensor_copy(SEG[:, hd, 128:129], Pt[s1][:]), r=[f"P{s1}"], w=["SEG"])
            else:
                S.dve(lambda e: e.tensor_copy(Sctx[:, hd, :], Sf[s1][:]), r=[f"Sf{s1}"], w=["Sctx"])

    SK = 2
    pend = []
    for t in tasks:
        if t["head_start"]:
            for u in pend:
                stageB(u)
            pend = []
            load_head(t["h"])
        stageA(t)
        pend.append(t)
        if len(pend) > SK:
            stageB(pend.pop(0))
    for u in pend:
        stageB(u)
    S.dma(lambda e: e.dma_start(out=segloc.ap(), in_=SEG[:].rearrange("p a b -> p (a b)")), r=["SEG"], w=["segloc"],
          group="segio")
    allgather(g, segall, segloc, "segall", r=["segloc"])
    SEGa = g.arena[0:64, zh_off:zh_off + 4 * 8 * 129].rearrange("p (r a b) -> p r a b", r=4, a=8, b=129)
    S.dma(lambda e: e.dma_start(out=SEGa.rearrange("p r a b -> p r (a b)"),
                                in_=segall.ap().rearrange("(r p) f -> p r f", p=64)),
          r=["segall"], w=ZH + ["SEGa"], group="segio")
    for h in range(4):
        for d in range(2):
            hd = h * 2 + d
            S.dve(lambda e, hd=hd: e.tensor_copy(Wk[:], Sctx[:, hd, :]), r=["Sctx"], w=["Wk"])
            js = range(4) if d == 0 else range(3, -1, -1)
            for j in js:
                mcol = (4 if d == 0 else 8) + j
                S.dve(lambda e, j=j, hd=hd: e.scalar_tensor_tensor(cand[:], Wk[:], SEGa[:, j, hd, 128:129],
                                                                   SEGa[:, j, hd, 0:128], ALU.mult, ALU.add),
                      r=["Wk", "SEGa"], w=["cand"])
                S.dve(lambda e: e.tensor_tensor(diff[:], cand[:], Wk[:], ALU.subtract), r=["cand", "Wk"], w=["diff"])
                S.dve(lambda e, mcol=mcol: e.scalar_tensor_tensor(Wk[:], diff[:], mk[:, mcol:mcol + 1], Wk[:],
                                                                  ALU.mult, ALU.add), r=["diff", "mk", "Wk"], w=["Wk"])
            S.dve(lambda e, hd=hd: e.tensor_copy(Sib[:, hd, :], Wk[:]), r=["Wk"], w=["Sib"])
    for c in range(NLC):
        p = c % NP
        A = f"pA{p}"
        for h in range(4):
            for d in range(2):
                hd = h * 2 + d
                S.pe(lambda e, p=p, h=h, d=d, hd=hd, c=c: e.matmul(pA[p][:, h * 128:(h + 1) * 128], qbP[:, hd, c, :],
                                                                    Sib[:, hd, :], start=(d == 0), stop=(d == 1)),
                     r=["qbP", "Sib"], w=[A])
        S.dve(lambda e, p=p, c=c: e.tensor_tensor(OLs[:, c, :], pA[p][:], OLs[:, c, :], ALU.add), r=[A, f"OL{c}"], w=[f"OL{c}"])
    OGv = OG.rearrange("(c p) f -> p c f", p=64)
    for k in range(4):
        cs = slice(k * 9, (k + 1) * 9)
        S.dma(lambda e, cs=cs: e.dma_start(out=OGv[:, cs, :], in_=OLs[:, cs, :]),
              r=[f"OL{c}" for c in range(k * 9, (k + 1) * 9)], w=[f"ogo{k}"], group=f"ogo{k}")


def emit_k5(g, T, kind, n_lat_tiles, x, w, vecs, z, out, og=None, oml=None, gng=None, fr_lat=None, fr_ctx=None,
            swT=None, sbT=None, mask=None, eps=1e-6):
    S = g.S
    g.phase()
    D = 1024
    nt = T // 128
    identf, ident = load_ident(g)
    wbf = g.sb([128, 8, D], BF16)
    wst = [g.sb([128, D], F32) for i in range(2)]
    vt = g.sb([128, 4, D], F32)
    for a in range(4):
        S.dma(lambda e, a=a: e.dma_start(out=vt[:, a, :], in_=vecs[a].partition_broadcast(128)), w=["vt"], group="modt")
    if kind == "even":
        gn = g.sb([128, 128], F32)
        S.dma(lambda e: e.dma_start(out=gn[:], in_=gng.partition_broadcast(128)), w=["gn"], group="gt")
    else:
        swf = g.sb([128, 4, 128], F32)
        swb = g.sb([128, 4, 128], BF16)
        sbt = g.sb([128, 4], F32)
        mk = g.sb([128, 16], F32)
        S.dma(lambda e: e.dma_start(out=swf[:], in_=swT.rearrange("g s t -> s g t")), w=["swf"], group="swf")
        S.dve(lambda e: e.tensor_copy(swb[:], swf[:]), r=["swf"], w=["swb"])
        S.dma(lambda e: e.dma_start(out=sbt[:], in_=sbT), w=["sbt"], group="sbt")
        S.dma(lambda e: e.dma_start(out=mk[:], in_=mask), w=["mk"], group="mk")
    for k in range(8):
        s = k % 2
        S.dma(lambda e, k=k, s=s: e.dma_start(out=wst[s][:], in_=w[k * 128:(k + 1) * 128, :]),
              w=[f"wst{s}"], group=f"wst{s}")
        if k % 2 == 0:
            S.act(lambda e, k=k, s=s: e.copy(wbf[:, k, :], wst[s][:]), r=[f"wst{s}"], w=[f"wbf{k}"])
        else:
            S.dve(lambda e, k=k, s=s: e.tensor_copy(wbf[:, k, :], wst[s][:]), r=[f"wst{s}"], w=[f"wbf{k}"])
    NX = 2
    xt = [g.sb([128, D], F32) for i in range(NX)]
    ab = [g.sb([128, D], BF16) for i in range(NX)]
    aT = [g.sb([128, 8, 128], BF16) for i in range(NX)]
    rr = [g.sb([128, D], F32) for i in range(NX)]
    ot = [g.sb([128, D], F32) for i in range(NX)]
    st = [g.sb([128, 8, 6], F32) for i in range(NX)]
    mv = [g.sb([128, 16], F32) for i in range(NX)]
    if kind == "even":
        i1 = [g.sb([128, 512], F32) for i in range(NX)]
        i2 = [g.sb([128, 512], F32) for i in range(NX)]
        i3 = [g.sb([128, 512], F32) for i in range(NX)]
        i4 = [g.sb([128, 512], F32) for i in range(NX)]
        i5 = [g.sb([128, 512], F32) for i in range(NX)]
    else:
        i1 = [g.sb([128, 512], F32) for i in range(NX)]
        fc4 = [g.sb([128, 4, 512], F32) for i in range(NX)]
        zz = [g.sb([128, 2048], F32) for i in range(NX)]
        vgb = [g.sb([128, 512], BF16) for i in range(NX)]
        svp = [g.ps(0, [128, 512], F32)]
    tp = [g.ps(1 + i, [128, 8, 128], BF16) for i in range(2)]
    yp = [g.ps(3 + i, [128, 512], F32) for i in range(4)]
    wkeys = [f"wbf{k}" for k in range(8)]

    def rstd_chain(s, col_var, col_out, ncol=1):
        S.dve(lambda e: e.tensor_scalar_add(mv[s][:, col_var:col_var + ncol], mv[s][:, col_var:col_var + ncol], eps),
              r=[f"mv{s}"], w=[f"mv{s}"])
        S.act(lambda e: e.sqrt(mv[s][:, col_var:col_var + ncol], mv[s][:, col_var:col_var + ncol]),
              r=[f"mv{s}"], w=[f"mv{s}"])
        S.dve(lambda e: e.reciprocal(mv[s][:, col_out:col_out + ncol], mv[s][:, col_var:col_var + ncol]),
              r=[f"mv{s}"], w=[f"mv{s}"])

    def emit_loads(i):
        s = i % NX
        rows = slice(i * 128, (i + 1) * 128)
        S.dma(lambda e, s=s, rows=rows: e.dma_start(out=xt[s][:], in_=x[rows, :]), w=[f"xt{s}"], group=f"xt{s}")
        if kind == "even":
            S.dma(lambda e, s=s, rows=rows: e.dma_start(out=i1[s][:], in_=og[rows, :]), w=[f"i1{s}"], group=f"i1{s}")
            S.dma(lambda e, s=s, rows=rows: e.dma_start(out=i3[s][:], in_=z[rows, 1056:1568]), w=[f"i3{s}"], group=f"i3{s}")
            S.dma(lambda e, s=s, rows=rows: e.dma_start(out=i4[s][:], in_=oml[rows, :]), w=[f"i4{s}"], group=f"i4{s}")
            S.dma(lambda e, s=s, rows=rows: e.dma_start(out=i5[s][:], in_=z[rows, 1984:2496]), w=[f"i5{s}"], group=f"i5{s}")
        else:
            if i < n_lat_tiles:
                for qq in range(4):
                    S.dma(lambda e, s=s, qq=qq, i=i: e.dma_start(
                        out=fc4[s][:, qq, :].rearrange("p (g c) -> p g c", g=4), in_=fr_lat(qq, i)),
                        w=[f"fc4{s}.{qq}"], group=f"fc4{s}")
            else:
                S.dma(lambda e, s=s, i=i: e.dma_start(out=i1[s][:].rearrange("p (g c) -> p g c", g=4), in_=fr_ctx(i)),
                      w=[f"i1{s}"], group=f"i1{s}")
            S.dma(lambda e, s=s, rows=rows: e.dma_start(out=zz[s][:], in_=z[rows, 512:2560]), w=[f"zz{s}"], group=f"zz{s}")

    emit_loads(0)
    for i in range(nt):
        s = i % NX
        rows = slice(i * 128, (i + 1) * 128)
        if i + 1 < nt:
            emit_loads(i + 1)
        if kind == "even":
            S.pool(lambda e, s=s: e.tensor_tensor(i2[s][:], i1[s][:], i1[s][:], ALU.mult), r=[f"i1{s}"], w=[f"i2{s}"])
            S.dve(lambda e, s=s: e.reduce_sum(mv[s][:, 0:4], i2[s][:].rearrange("p (h d) -> p h d", h=4), AX.X),
                  r=[f"i2{s}"], w=[f"mv{s}"])
            S.dve(lambda e, s=s: e.tensor_scalar(mv[s][:, 0:4], mv[s][:, 0:4], 1.0 / 128, None, ALU.mult),
                  r=[f"mv{s}"], w=[f"mv{s}"])
            rstd_chain(s, 0, 4, 4)
            S.act(lambda e, s=s: e.activation(i3[s][:], i3[s][:], AF.Silu), r=[f"i3{s}"], w=[f"i3{s}"])
            S.act(lambda e, s=s: e.activation(i5[s][:], i5[s][:], AF.Silu), r=[f"i5{s}"], w=[f"i5{s}"])
            for h in range(4):
                hs = slice(h * 128, (h + 1) * 128)
                S.dve(lambda e, s=s, h=h, hs=hs: e.scalar_tensor_tensor(
                    i1[s][:, hs], i1[s][:, hs], mv[s][:, 4 + h:5 + h], gn[:], ALU.mult, ALU.mult),
                    r=[f"i1{s}", f"mv{s}", "gn"], w=[f"i1{s}"])
            S.pool(lambda e, s=s: e.tensor_tensor(ab[s][:, 0:512], i1[s][:], i3[s][:], ALU.mult),
                   r=[f"i1{s}", f"i3{s}"], w=[f"ab{s}"])
            S.pool(lambda e, s=s: e.tensor_tensor(ab[s][:, 512:1024], i4[s][:], i5[s][:], ALU.mult),
                   r=[f"i4{s}", f"i5{s}"], w=[f"ab{s}"])
        else:
            if i < n_lat_tiles:
                S.dve(lambda e, s=s: e.tensor_scalar(i1[s][:], fc4[s][:, 0, :], mk[:, 0:1], None, ALU.mult),
                      r=[f"fc4{s}.0", f"fc4{s}.3", "mk"], w=[f"i1{s}"])
                for qq in range(1, 4):
                    S.dve(lambda e, s=s, qq=qq: e.scalar_tensor_tensor(
                        i1[s][:], fc4[s][:, qq, :], mk[:, qq:qq + 1], i1[s][:], ALU.mult, ALU.add),
                        r=[f"fc4{s}.{qq}", f"fc4{s}.3", "mk", f"i1{s}"], w=[f"i1{s}"])
            S.act(lambda e, s=s: e.activation(zz[s][:, 0:512], zz[s][:, 0:512], AF.Silu), r=[f"zz{s}"], w=[f"zz{s}"])
            S.act(lambda e, s=s: e.activation(zz[s][:, 1536:2048], zz[s][:, 1536:2048], AF.Silu), r=[f"zz{s}"], w=[f"zz{s}"])
            S.act(lambda e, s=s: e.activation(zz[s][:, 512:1536], zz[s][:, 512:1536], AF.Gelu), r=[f"zz{s}"], w=[f"zz{s}"])
            S.pool(lambda e, s=s: e.tensor_tensor(ab[s][:, 0:512], i1[s][:], zz[s][:, 0:512], ALU.mult),
                   r=[f"i1{s}", f"zz{s}"], w=[f"ab{s}"])
            for g in range(4):
                S.dve(lambda e, s=s, g=g: e.bn_stats(st[s][:, g, :], zz[s][:, 1024 + g * 128:1024 + (g + 1) * 128]),
                      r=[f"zz{s}"], w=[f"st{s}"])
                S.dve(lambda e, s=s, g=g: e.bn_aggr(mv[s][:, 2 * g:2 * g + 2], st[s][:, g:g + 1, :]),
                      r=[f"st{s}"], w=[f"mv{s}"])
            for g in range(4):
                rstd_chain(s, 2 * g + 1, 8 + g, 1)
            for g in range(4):
                S.dve(lambda e, s=s, g=g: e.tensor_scalar(
                    vgb[s][:, g * 128:(g + 1) * 128], zz[s][:, 1024 + g * 128:1024 + (g + 1) * 128],
                    mv[s][:, 2 * g:2 * g + 1], mv[s][:, 8 + g:9 + g], ALU.subtract, ALU.mult),
                    r=[f"zz{s}", f"mv{s}"], w=[f"vgb{s}"])
            for g in range(4):
                S.pe(lambda e, s=s, g=g: e.matmul(svp[0][:, g * 128:(g + 1) * 128], swb[:, g, :],
                                                  vgb[s][:, g * 128:(g + 1) * 128], start=True, stop=True),
                     r=[f"vgb{s}", "swb"], w=["svp0"])
            for g in range(4):
                gs = slice(g * 128, (g + 1) * 128)
                S.dve(lambda e, s=s, g=g, gs=gs: e.scalar_tensor_tensor(
                    zz[s][:, 512 + g * 128:512 + (g + 1) * 128], svp[0][:, gs], sbt[:, g:g + 1],
                    zz[s][:, 512 + g * 128:512 + (g + 1) * 128], ALU.add, ALU.mult),
                    r=["svp0", "sbt", f"zz{s}"], w=[f"zz{s}"])
            S.pool(lambda e, s=s: e.tensor_tensor(ab[s][:, 512:1024], zz[s][:, 512:1024], zz[s][:, 1536:2048], ALU.mult),
                   r=[f"zz{s}"], w=[f"ab{s}"])
        t = i % 2
        for k in range(8):
            S.pe(lambda e, s=s, t=t, k=k: e.transpose(tp[t][:, k, :], ab[s][:, k * 128:(k + 1) * 128], ident[:]),
                 r=[f"ab{s}", "ident"], w=[f"tp{t}"])
        S.act(lambda e, s=s, t=t: e.copy(aT[s][:], tp[t][:]), r=[f"tp{t}"], w=[f"aT{s}"])
        gi = 0 if i < n_lat_tiles else 1
        for c in range(2):
            p = (2 * i + c) % 4
            cs = slice(c * 512, (c + 1) * 512)
            for k in range(8):
                S.pe(lambda e, s=s, p=p, k=k, cs=cs: e.matmul(yp[p][:], aT[s][:, k, :], wbf[:, k, cs],
                                                             start=(k == 0), stop=(k == 7)),
                     r=[f"aT{s}", wkeys[k]], w=[f"yp{p}"])
            S.dve(lambda e, s=s, p=p, cs=cs, gi=gi: e.tensor_tensor(rr[s][:, cs], yp[p][:], vt[:, gi, cs], ALU.mult),
                  r=[f"yp{p}", "vt"], w=[f"rr{s}"])
        S.dve(lambda e, s=s: e.scalar_tensor_tensor(rr[s][:], xt[s][:], ALPHA, rr[s][:], ALU.mult, ALU.add),
               r=[f"xt{s}", f"rr{s}"], w=[f"rr{s}"])
        for j in range(2):
            S.dve(lambda e, s=s, j=j: e.bn_stats(st[s][:, 4 + j, :], rr[s][:, j * 512:(j + 1) * 512]),
                  r=[f"rr{s}"], w=[f"st{s}"])
        S.dve(lambda e, s=s: e.bn_aggr(mv[s][:, 12:14], st[s][:, 4:6, :]), r=[f"st{s}"], w=[f"mv{s}"])
        rstd_chain(s, 13, 14, 1)
        S.dve(lambda e, s=s: e.tensor_scalar(rr[s][:], rr[s][:], mv[s][:, 12:13], mv[s][:, 14:15],
                                             ALU.subtract, ALU.mult), r=[f"rr{s}", f"mv{s}"], w=[f"rr{s}"])
        S.pool(lambda e, s=s: e.tensor_tensor(rr[s][:], rr[s][:], vt[:, 2, :], ALU.mult), r=[f"rr{s}", "vt"], w=[f"rr{s}"])
        S.pool(lambda e, s=s: e.tensor_tensor(ot[s][:], rr[s][:], vt[:, 3, :], ALU.add), r=[f"rr{s}", "vt"], w=[f"ot{s}"])
        S.dma(lambda e, s=s, rows=rows: e.dma_start(out=out[rows, :], in_=ot[s][:]), r=[f"ot{s}"], w=[f"oo{s}"],
              group=f"oo{s}")


def emit_k7(g, fall, fout, TW, FC, W3, TWc, mask, with_ctx=True):
    S = g.S
    g.phase()
    st = [g.sb([128, 4, 512], F32) for i in range(2)]
    tmp = [g.sb([128, 4, 128], F32) for i in range(2)]
    mk = g.sb([128, 16], F32)
    TWb = g.sb([128, 64, 256], BF16)
    fb = g.sb([128, 64, 128], BF16)
    Y = g.sb([128, 2, 64, 128], BF16)
    U = g.sb([128, 64, 256], BF16)
    FCb = g.sb([128, 512], BF16)
    W3b = g.sb([128, 256], BF16)
    frt = g.sb([128, 64, 128], F32)
    ps = [g.ps(i, [128, 512], F32) for i in range(4)]
    S.dma(lambda e: e.dma_start(out=mk[:], in_=mask), w=["mk"], group="mk")
    stf = lambda s: st[s][:].rearrange("p a b -> p (a b)")
    for i in range(8):
        s = i % 2
        S.dma(lambda e, i=i, s=s: e.dma_start(out=stf(s), in_=TW[:, i * 8:(i + 1) * 8, :].rearrange("p a b -> p (a b)")),
              w=[f"st{s}"], group=f"st{s}")
        S.add("dve" if i % 2 == 0 else "pool",
              lambda e, i=i, s=s: e.tensor_copy(TWb[:, i * 8:(i + 1) * 8, :].rearrange("p a b -> p (a b)"), stf(s)),
              r=[f"st{s}"], w=["TWb"])
    S.dma(lambda e: e.dma_start(out=stf(0)[:, 0:512], in_=FC), w=["st0"], group="st0")
    S.dve(lambda e: e.tensor_copy(FCb[:], stf(0)[:, 0:512]), r=["st0"], w=["FCb"])
    S.dma(lambda e: e.dma_start(out=stf(1)[:, 0:256], in_=W3), w=["st1"], group="st1")
    S.dve(lambda e: e.tensor_copy(W3b[:], stf(1)[:, 0:256]), r=["st1"], w=["W3b"])

    S.barrier()

    def select(s, t, dst):
        stk = [f"st{s}.{r}.{pp}" for r in range(4) for pp in range(4)]
        S.dve(lambda e: e.tensor_scalar(tmp[t][:], st[s][:, :, 0:128], mk[:, 0:1], None, ALU.mult),
              r=stk + ["mk"], w=[f"tmp{t}"])
        for gg in range(1, 3):
            S.dve(lambda e, gg=gg: e.scalar_tensor_tensor(tmp[t][:], st[s][:, :, gg * 128:(gg + 1) * 128],
                                                         mk[:, gg:gg + 1], tmp[t][:], ALU.mult, ALU.add),
                  r=stk + ["mk", f"tmp{t}"], w=[f"tmp{t}"])
        S.dve(lambda e: e.scalar_tensor_tensor(dst, st[s][:, :, 384:512], mk[:, 3:4], tmp[t][:], ALU.mult, ALU.add),
              r=stk + ["mk", f"tmp{t}"], w=["fb"])

    for j in range(16):
        s = j % 2
        for r in range(4):
            for pp in range(4):
                src = fall[pp][r * 512:(r + 1) * 512, :].rearrange("(a n) c -> a n c", n=64)[:, 4 * j:4 * j + 4, :]
                p0 = 32 * r + 8 * pp
                S.dma(lambda e, s=s, p0=p0, src=src: e.dma_start(out=st[s][p0:p0 + 8, :, :], in_=src),
                      w=[f"st{s}.{r}.{pp}"], group=f"st{s}")
        select(s, s, fb[:, 4 * j:4 * j + 4, :])
    pi = 0
    for n2 in range(0, 64, 2):
        p = pi % 4; pi += 1
        for d in range(2):
            S.pe(lambda e, p=p, n2=n2, d=d: e.matmul(ps[p][:, d * 256:(d + 1) * 256], fb[:, n2 + d, :], TWb[:, n2 + d, :],
                                                     start=True, stop=True), r=["fb", "TWb"], w=[f"ps{p}"])
        for d in range(2):
            src = lambda p=p, d=d: ps[p][:, d * 256:(d + 1) * 256].rearrange("p (r j a) -> p r j a", r=2, a=2)
            dst = lambda n2=n2, d=d: Y[:, :, :, 2 * (n2 + d):2 * (n2 + d) + 2]
            if d == 0:
                S.act(lambda e, src=src, dst=dst: e.copy(dst(), src()), r=[], w=["Y", f"ps{p}"])
            else:
                S.dve(lambda e, src=src, dst=dst: e.tensor_copy(dst(), src()), r=[], w=["Y", f"ps{p}"])
    for j in range(0, 64, 2):
        p = pi % 4; pi += 1
        for d in range(2):
            jj = j + d
            S.pe(lambda e, p=p, jj=jj, d=d: e.matmul(ps[p][:, d * 256:(d + 1) * 256], Y[:, 0, jj, :],
                                                     FCb[:, 0:256], start=True, stop=False), r=["Y", "FCb"], w=[f"ps{p}"])
            S.pe(lambda e, p=p, jj=jj, d=d: e.matmul(ps[p][:, d * 256:(d + 1) * 256], Y[:, 1, jj, :],
                                                     FCb[:, 256:512], start=False, stop=True), r=["Y", "FCb"], w=[f"ps{p}"])
        if (j // 2) % 2 == 0:
            S.act(lambda e, p=p, j=j: e.copy(U[:, j:j + 2, :].rearrange("p a b -> p (a b)"), ps[p][:]), r=[f"ps{p}"], w=["U"])
        else:
            S.dve(lambda e, p=p, j=j: e.tensor_copy(U[:, j:j + 2, :].rearrange("p a b -> p (a b)"), ps[p][:]), r=[f"ps{p}"], w=["U"])
    scale = 1.0 / 1024.0
    for j0 in range(0, 64, 4):
        p = pi % 4; pi += 1
        for d in range(4):
            jj = j0 + d
            S.pe(lambda e, p=p, jj=jj, d=d: e.matmul(ps[p][:, d * 128:(d + 1) * 128], W3b[:, 0:128], U[:, jj, 0:128],
                                                     start=True, stop=False), r=["U", "W3b"], w=[f"ps{p}"])
            S.pe(lambda e, p=p, jj=jj, d=d: e.matmul(ps[p][:, d * 128:(d + 1) * 128], W3b[:, 128:256], U[:, jj, 128:256],
                                                     start=False, stop=True), r=["U", "W3b"], w=[f"ps{p}"])
        S.dve(lambda e, p=p, j0=j0: e.tensor_scalar(frt[:, j0:j0 + 4, :].rearrange("p a b -> p (a b)"), ps[p][:],
                                                    scale, None, ALU.mult), r=[f"ps{p}"], w=["frt"])
    for q in range(4):
        fv = fout[q].rearrange("(k2 jj a) c -> a k2 jj c", jj=64, a=2)
        for a in range(2):
            S.dma(lambda e, a=a, q=q, fv=fv: e.dma_start(out=fv[a], in_=frt[a * 64 + 16 * q:a * 64 + 16 * (q + 1), :, :]),
                  r=["frt"], w=[f"fo{a}{q}"], group=f"fo{a}")
    if with_ctx:
        S.barrier()
        fcb = g.sb([128, 2, 128], BF16)
        TWcb = g.sb([128, 2, 512], BF16)
        Yc = g.sb([128, 512], BF16)
        oc = g.sb([128, 2, 128], F32)
        for t in range(2):
            S.dma(lambda e, t=t: e.dma_start(out=st[t][:, 0, :], in_=fall[4][t * 128:(t + 1) * 128, :]),
                  w=[f"st{t}"], group=f"st{t}")
            S.dve(lambda e, t=t: e.tensor_scalar(tmp[t][:, 0, :], st[t][:, 0, 0:128], mk[:, 0:1], None, ALU.mult),
                  r=[f"st{t}", "mk"], w=[f"tmp{t}"])
            for gg in range(1, 4):
                S.dve(lambda e, t=t, gg=gg: e.scalar_tensor_tensor(
                    tmp[t][:, 0, :], st[t][:, 0, gg * 128:(gg + 1) * 128], mk[:, gg:gg + 1], tmp[t][:, 0, :], ALU.mult, ALU.add),
                    r=[f"st{t}", "mk", f"tmp{t}"], w=[f"tmp{t}"])
            S.dve(lambda e, t=t: e.tensor_copy(fcb[:, t, :], tmp[t][:, 0, :]), r=[f"tmp{t}"], w=["fcb"])
        for t in range(2):
            S.dma(lambda e, t=t: e.dma_start(out=stf(t)[:, 0:512], in_=TWc[:, t, :]), w=[f"st{t}"], group=f"st{t}")
            S.dve(lambda e, t=t: e.tensor_copy(TWcb[:, t, :], stf(t)[:, 0:512]), r=[f"st{t}"], w=["TWcb"])
        p = pi % 4; pi += 1
        for t in range(2):
            S.pe(lambda e, p=p, t=t: e.matmul(ps[p][:], fcb[:, t, :], TWcb[:, t, :], start=(t == 0), stop=(t == 1)),
                 r=["fcb", "TWcb"], w=[f"ps{p}"])
        S.dve(lambda e, p=p: e.tensor_copy(Yc[:], ps[p][:]), r=[f"ps{p}"], w=["Yc"])
        p = pi % 4; pi += 1
        for kt in range(2):
            S.pe(lambda e, p=p, kt=kt: e.matmul(ps[p][:, kt * 128:(kt + 1) * 128], Yc[:, kt * 128:(kt + 1) * 128],
                                                FCb[:, 0:128], start=True, stop=False), r=["Yc", "FCb"], w=[f"ps{p}"])
            S.pe(lambda e, p=p, kt=kt: e.matmul(ps[p][:, kt * 128:(kt + 1) * 128], Yc[:, 256 + kt * 128:256 + (kt + 1) * 128],
                                                FCb[:, 256:384], start=False, stop=True), r=["Yc", "FCb"], w=[f"ps{p}"])
        S.dve(lambda e, p=p: e.tensor_scalar(oc[:].rearrange("p a b -> p (a b)"), ps[p][:, 0:256],
                                             1.0 / np.sqrt(256.0 * 128.0), None, ALU.mult), r=[f"ps{p}"], w=["oc"])
        S.dma(lambda e: e.dma_start(out=fout[4].rearrange("(t p) c -> p t c", p=128), in_=oc[:]),
              r=["oc"], w=["oco"], group="oco")


Q, L, SEQ, D = 2048, 256, 8192, 1024
T = Q + L
NCORES = 8


def build_fused(depth=4, stop_after=None):
    nc = bass.Bass(target_bir_lowering=False)
    g = G(nc)
    if stop_after is not None:
        g.max_phase = stop_after
    ext = lambda name, shape: nc.dram_tensor(name, list(shape), F32, kind="ExternalInput")
    xin = ext("xin", [T, D]); cin = ext("cin", [128, D])
    ada_w = ext("ada_w", [4, D, 3 * D]); ada_b = ext("ada_b", [4, 3 * D])
    plg = ext("post_ln_g", [4, D]); plb = ext("post_ln_b", [4, D])
    ewi = ext("even_w_in", [2, D, 2496]); ewo = ext("even_w_out", [2, D, D])
    owi = ext("odd_w_in", [2, D, 2560]); owo = ext("odd_w_out", [2, D, D])
    gw2 = ext("gla_w2", [2, 2, 16, 256]); gb = ext("gla_b", [2, 2, 256]); gng = ext("gla_norm_g", [2, 128])
    qng = ext("mla_q_norm_g", [2, 256]); wuq = ext("mla_w_uq", [2, 256, 768])
    kng = ext("mla_kv_norm_g", [2, 128]); wukv = ext("mla_w_ukv", [2, 128, 1024])
    swT = ext("sgu_wT", [2, 4, 128, 128]); sbT = ext("sgu_bT", [2, 128, 4])
    csq = ext("csq", [T, 32]); csk = ext("csk", [SEQ + L, 32])
    ident_d = ext("ident", [128, 128]); g.ident_d = ident_d.ap()
    gcst = ext("gcst", [2, 64, 320]); mask = ext("mask", [128, 16])
    TW = ext("TW", [128, 64, 256]); FC = ext("FC", [128, 512]); W3 = ext("W3", [128, 256]); TWc = ext("TWc", [128, 2, 512])
    xout = nc.dram_tensor("xout", [Q, D], F32, kind="ExternalOutput")
    M = [g.dram(f"M{l}", [128, 3 * D]) for l in range(4)]
    X = [g.dram(f"X{i}", [T, D]) for i in range(2)]
    Z = g.dram("Z", [T, 2560])
    QP = g.dram("QP", [T, 768])
    KSIN = [g.dram(f"KSIN{p}", [Q // 2, 160]) for p in range(2)]
    KSG = [g.dram(f"KSG{p}", [4 * Q // 2, 160]) for p in range(2)]
    KSA = g.dram("KSA", [SEQ + L, 160])
    KVA = g.dram("KVA", [SEQ + L, 1024])
    OML = g.dram("OML", [T, 512]); OG = g.dram("OG", [T, 512])
    SEGL = g.dram("SEGL", [64, 8 * 129]); SEGA = g.dram("SEGA", [256, 8 * 129])
    FPR = [512, 512, 512, 512, 256]
    FIN = [g.dram(f"FIN{p}", [FPR[p], 512]) for p in range(5)]
    FALL = [g.dram(f"FALL{p}", [4 * FPR[p], 512]) for p in range(5)]
    OPR = [2048, 2048, 2048, 2048, 256]
    FOUT = [g.dram(f"FOUT{p}", [OPR[p], 128]) for p in range(5)]
    FOALL = [g.dram(f"FOALL{p}", [4 * OPR[p], 128]) for p in range(5)]
    S = g.S
    rows = lambda i: slice(i * 128, (i + 1) * 128)
    HN = 3 * D // 2
    for l in range(depth):
        for hh in range(2):
            cs = slice(hh * HN, (hh + 1) * HN)
            emit_k1(g, lambda i: cin.ap()[rows(i), :], 1, D, HN, ada_w.ap()[l][:, cs],
                    lambda i, l=l, cs=cs: M[l].ap()[rows(i), cs], "silu", bias=ada_b.ap()[l][cs])
    xcur = xin
    for l in range(depth):
        li = l // 2
        even = l % 2 == 0
        last = l == depth - 1
        Ml = M[l].ap()
        mods = (Ml[0, 0:D], Ml[0, D:2 * D], Ml[1, 0:D], Ml[1, D:2 * D])
        vecs = (Ml[0, 2 * D:3 * D], Ml[1, 2 * D:3 * D], plg.ap()[l], plb.ap()[l])
        N = 2496 if even else 2560
        Zl = Z.ap()[:, 0:N]
        xap = xcur.ap()
        emit_k1(g, lambda i, xap=xap: xap[rows(i), :], T // 128, D, N, (ewi if even else owi).ap()[li],
                lambda i, Zl=Zl: Zl[rows(i), :], "ln", n_lat_tiles=Q // 128, mods=mods)
        Tk = Q if last else T
        xn = xout if last else X[l % 2]
        if even:
            emit_k1(g, lambda i, Zl=Zl: Zl[rows(i), 1568:1824], T // 128, 256, 768, wuq.ap()[li],
                    lambda i: QP.ap()[rows(i), :], "rms", gvec=qng.ap()[li])
            g.phase()
            S.dma(lambda e, Zl=Zl: e.dma_start(out=KSA.ap()[0:L, :], in_=Zl[Q:T, 1824:1984]), w=["ksa0"], group="dc1")
            HQ = Q // 2
            for p in range(2):
                S.dma(lambda e, Zl=Zl, p=p: e.dma_start(out=KSIN[p].ap(), in_=Zl[HQ * p:HQ * (p + 1), 1824:1984]),
                      w=[f"ksin{p}"], group=f"dc0{p}")
                allgather(g, KSG[p], KSIN[p], f"ksg{p}", r=[f"ksin{p}"])
                dst = KSA.ap()[L:L + SEQ, :].rearrange("(r h n) c -> h r n c", r=4, h=2)[p]
                S.dma(lambda e, p=p, dst=dst: e.dma_start(out=dst, in_=KSG[p].ap().rearrange("(r n) c -> r n c", r=4)),
                      r=[f"ksg{p}"], w=[f"ksa1{p}"], group=f"dc2{p}")
            emit_k1(g, lambda i: KSA.ap()[rows(i), 0:128], (SEQ + L) // 128, 128, 1024, wukv.ap()[li],
                    lambda i: KVA.ap()[rows(i), :], "rms", gvec=kng.ap()[li])
            emit_k3(g, Q, L, SEQ + L, QP.ap(), KVA.ap(), KSA.ap()[:, 128:160], csq.ap(), csk.ap(), OML.ap())
            emit_k4s(g, Zl, OG.ap(), SEGL, SEGA, gw2.ap()[li], gb.ap()[li], gcst.ap(), mask.ap())
            emit_k5(g, Tk, "even", Q // 128, xap, ewo.ap()[li], vecs, Zl, xn.ap(), og=OG.ap(), oml=OML.ap(),
                    gng=gng.ap()[li])
        else:
            g.phase()
            r0 = 0
            for p in range(5):
                S.dma(lambda e, Zl=Zl, p=p, r0=r0: e.dma_start(out=FIN[p].ap(), in_=Zl[r0:r0 + FPR[p], 0:512]),
                      w=[f"fin{p}"], group=f"dcf{p}")
                allgather(g, FALL[p], FIN[p], f"fall{p}", r=[f"fin{p}"])
                r0 += FPR[p]
            emit_k7(g, [a.ap() for a in FALL], [a.ap() for a in FOUT], TW.ap(), FC.ap(), W3.ap(), TWc.ap(), mask.ap())
            g.phase()
            for p in range(5):
                allgather(g, FOALL[p], FOUT[p], f"foall{p}")
            fo = [a.ap().rearrange("(g r) c -> r g c", g=4) for a in FOALL]
            emit_k5(g, Tk, "odd", Q // 128, xap, owo.ap()[li], vecs, Zl, xn.ap(),
                    fr_lat=lambda qq, i, fo=fo: fo[qq][128 * i:128 * (i + 1), :, :],
                    fr_ctx=lambda i, fo=fo: fo[4][128 * (i - Q // 128):128 * (i - Q // 128 + 1), :, :],
                    swT=swT.ap()[li], sbT=sbT.ap()[li], mask=mask.ap())
        xcur = xn
    g.finish()
    return nc


def make_inputs(x, c, ctx, c_ctx, ada_w, ada_b, post_ln_g, post_ln_b, even_w_in, gla_w2, gla_b, gla_norm_g,
                mla_q_norm_g, mla_w_uq, mla_kv_norm_g, mla_w_ukv, even_w_out, odd_w_in, sgu_w, sgu_b, odd_w_out,
                rope_tables, fnet_consts):
    f32 = np.float32
    cc = lambda a: np.ascontiguousarray(a, dtype=f32)
    shared = dict(ada_w=cc(ada_w), ada_b=cc(ada_b), post_ln_g=cc(post_ln_g), post_ln_b=cc(post_ln_b),
                  even_w_in=cc(even_w_in), even_w_out=cc(even_w_out), odd_w_in=cc(odd_w_in), odd_w_out=cc(odd_w_out),
                  gla_w2=cc(gla_w2), gla_b=cc(gla_b), gla_norm_g=cc(gla_norm_g), mla_q_norm_g=cc(mla_q_norm_g),
                  mla_w_uq=cc(mla_w_uq), mla_kv_norm_g=cc(mla_kv_norm_g), mla_w_ukv=cc(mla_w_ukv),
                  sgu_wT=cc(np.transpose(sgu_w, (0, 1, 3, 2))), sgu_bT=cc(np.transpose(sgu_b, (0, 2, 1))),
                  csk=rope_tables(np.concatenate([-np.ones(L, int), np.arange(SEQ)])),
                  ident=np.eye(128, dtype=f32), gcst=gla_consts2(), **fnet_consts())
    maps = []
    for j in range(NCORES):
        b, i = j // 4, j % 4
        m = dict(shared)
        m["xin"] = cc(np.concatenate([x[b, Q * i:Q * (i + 1)], ctx[b]], 0))
        cin = np.zeros((128, D), f32)
        cin[0] = c[b]; cin[1] = c_ctx
        m["cin"] = cin
        m["csq"] = rope_tables(np.concatenate([np.arange(Q) + Q * i, -np.ones(L, int)]))
        mk = np.zeros((128, 16), f32)
        mk[:, i] = 1.0
        for jj in range(4):
            mk[:, 4 + jj] = 1.0 if jj < i else 0.0
            mk[:, 8 + jj] = 1.0 if jj > i else 0.0
        m["mask"] = mk
        maps.append(m)
    return maps

def rope_tables(pos):
    pos = np.asarray(pos)
    row = (pos // 64).astype(np.float32)
    col = (pos % 64).astype(np.float32)
    inv = (10000.0 ** (-np.arange(8, dtype=np.float32) / 8)).astype(np.float32)
    ang = np.concatenate([row[:, None] * inv, col[:, None] * inv], -1).astype(np.float32)
    c = np.cos(ang).astype(np.float32)
    s = np.sin(ang).astype(np.float32)
    ident = pos < 0
    c[ident] = 1.0
    s[ident] = 0.0
    return np.concatenate([c, s], -1).astype(np.float32)


def fnet_consts():
    n1 = np.arange(128)[:, None, None].astype(np.float64)
    n2 = np.arange(64)[None, :, None].astype(np.float64)
    k1 = np.arange(128)[None, None, :].astype(np.float64)
    ang = 2 * np.pi * k1 * (64 * n1 + n2) / 8192.0
    TW = np.concatenate([np.cos(ang), -np.sin(ang)], -1).astype(np.float32)
    c = np.arange(128)[:, None].astype(np.float64)
    cp = np.arange(128)[None, :].astype(np.float64)
    a = 2 * np.pi * c * cp / 128.0
    Cc, Sc = np.cos(a), np.sin(a)
    FC = np.concatenate([Cc, -Sc, Sc, Cc], -1).astype(np.float32)
    W3 = np.zeros((64, 2, 2, 2, 64), np.float64)
    n2v = np.arange(64)[:, None]
    k2v = np.arange(64)[None, :]
    a3 = 2 * np.pi * n2v * k2v / 64.0
    for aa in range(2):
        W3[:, aa, 0, aa, :] = np.cos(a3)
        W3[:, aa, 1, aa, :] = np.sin(a3)
    W3 = W3.reshape(128, 256).astype(np.float32)
    n = np.arange(256)[:, None].astype(np.float64)
    k = np.arange(256)[None, :].astype(np.float64)
    ac = 2 * np.pi * n * k / 256.0
    TWc = np.concatenate([np.cos(ac), -np.sin(ac)], -1).reshape(2, 128, 512).transpose(1, 0, 2)
    TWc = np.ascontiguousarray(TWc).astype(np.float32)
    return dict(TW=TW, FC=FC, W3=W3, TWc=TWc)


_NC = {}


def kernel(x, c, ctx, c_ctx, ada_w, ada_b, post_ln_g, post_ln_b, even_w_in, gla_w2, gla_b, gla_norm_g,
           mla_q_norm_g, mla_w_uq, mla_kv_norm_g, mla_w_ukv, even_w_out, odd_w_in, sgu_w, sgu_b, odd_w_out):
    if "nc" not in _NC:
        _NC["nc"] = build_fused(4)
    maps = make_inputs(np.asarray(x), np.asarray(c), np.asarray(ctx), np.asarray(c_ctx), np.asarray(ada_w),
                       np.asarray(ada_b), np.asarray(post_ln_g), np.asarray(post_ln_b), np.asarray(even_w_in),
                       np.asarray(gla_w2), np.asarray(gla_b), np.asarray(gla_norm_g), np.asarray(mla_q_norm_g),
                       np.asarray(mla_w_uq), np.asarray(mla_kv_norm_g), np.asarray(mla_w_ukv), np.asarray(even_w_out),
                       np.asarray(odd_w_in), np.asarray(sgu_w), np.asarray(sgu_b), np.asarray(odd_w_out),
                       rope_tables, fnet_consts)
    res = run_bass_kernel_spmd(_NC["nc"], maps, core_ids=list(range(NCORES)))
    out = np.empty((2, SEQ, D), np.float32)
    for j in range(NCORES):
        out[j // 4, (j % 4) * Q:(j % 4 + 1) * Q] = res.results[j]["xout"]
    return out
```

```python
import contextlib
import numpy as np
import concourse.bass as bass
import concourse.mybir as mybir
from concourse.bass_utils import run_bass_kernel_spmd

F32 = mybir.dt.float32
BF16 = mybir.dt.bfloat16
AF = mybir.ActivationFunctionType
ALU = mybir.AluOpType
AX = mybir.AxisListType
ALPHA = 8 ** 0.25
MLA_SCALE = 96 ** -0.5
ARENA_F32 = 52900


class Sched:
    def __init__(self, nc):
        self.nc = nc
        self.ops = []
        self.last_w = {}
        self.readers = {}
        self.stack = contextlib.ExitStack()
        self.bar = set()
        self.pending = {}

    def barrier(self):
        last = {}
        for i, op in enumerate(self.ops):
            k = ("dma", op["dma"]) if op["dma"] is not None else ("eng", op["eng"])
            last[k] = i
        self.bar = set(last.values())
        self.pending = {e: True for e in ["pe", "act", "dve", "pool", "sp"]}
        self.last_w = {}
        self.readers = {}

    muted = False

    def add(self, eng, fn, r=(), w=(), dma=None, inc=16):
        if self.muted:
            return -1
        idx = len(self.ops)
        deps = set()
        for k in r:
            if k in self.last_w:
                deps.add(self.last_w[k])
        for k in w:
            if k in self.last_w:
                deps.add(self.last_w[k])
            for x in self.readers.get(k, ()):
                deps.add(x)
        if self.pending.get(eng):
            deps |= self.bar
            self.pending[eng] = False
        deps.discard(idx)
        self.ops.append(dict(eng=eng, fn=fn, deps=deps, dma=dma, inc=inc))
        for k in r:
            self.readers.setdefault(k, []).append(idx)
        for k in w:
            self.last_w[k] = idx
            self.readers[k] = []
        return idx

    def pe(self, fn, r=(), w=()):
        return self.add("pe", fn, r, w)

    def act(self, fn, r=(), w=()):
        return self.add("act", fn, r, w)

    def dve(self, fn, r=(), w=()):
        return self.add("dve", fn, r, w)

    def pool(self, fn, r=(), w=()):
        return self.add("pool", fn, r, w)

    def dma(self, fn, r=(), w=(), group=None, eng="sp", inc=16):
        assert group is not None
        return self.add(eng, fn, r, w, dma=group, inc=inc)

    def emit(self):
        nc = self.nc
        ops = self.ops
        n = len(ops)
        needs_signal = [False] * n
        for i, op in enumerate(ops):
            keep = set()
            for d in op["deps"]:
                dop = ops[d]
                if dop["dma"] is None and dop["eng"] == op["eng"] and op["eng"] == "pe":
                    continue
                keep.add(d)
                needs_signal[d] = True
            op["deps"] = keep
        engs = ["pe", "act", "dve", "pool", "sp"]
        sems = {e: self.stack.enter_context(nc.semaphore(f"s_{e}")) for e in engs}
        cnt = {e: 0 for e in engs}
        groups = {}
        gcnt = {}
        for i, op in enumerate(ops):
            if op["dma"] is not None:
                g = op["dma"]
                if g not in groups:
                    groups[g] = self.stack.enter_context(nc.semaphore(f"d_{len(groups)}"))
                    gcnt[g] = 0
                op["sem"] = groups[g]
                gcnt[g] += op["inc"] * getattr(op["fn"], "ndma", 1)
                op["val"] = gcnt[g]
            elif needs_signal[i]:
                cnt[op["eng"]] += 1
                op["sem"] = sems[op["eng"]]
                op["val"] = cnt[op["eng"]]
        print("sched: ops", n, "dma groups", len(groups), "sem counts", cnt, flush=True)
        final = dict((g, (groups[g], gcnt[g])) for g in groups)

        def stream(ename):
            def body(eng):
                known = {}
                for i, op in enumerate(ops):
                    if op["eng"] != ename:
                        continue
                    for d in sorted(op["deps"]):
                        dop = ops[d]
                        s, v = dop["sem"], dop["val"]
                        if known.get(id(s), 0) < v:
                            eng.wait_ge(s, v)
                            known[id(s)] = v
                    ins = op["fn"](eng)
                    if op["dma"] is not None:
                        if not isinstance(ins, (list, tuple)):
                            ins = [ins]
                        assert len(ins) == getattr(op["fn"], "ndma", 1)
                        for x in ins:
                            x.then_inc(op["sem"], op["inc"])
                    elif needs_signal[i]:
                        ins.then_inc(op["sem"], 1)
                if ename == "sp":
                    for g, (s, v) in final.items():
                        if known.get(id(s), 0) < v:
                            eng.wait_ge(s, v)
            return body

        with nc.Block() as block:
            block.tensor(stream("pe"))
            block.scalar(stream("act"))
            block.vector(stream("dve"))
            block.gpsimd(stream("pool"))
            block.sync(stream("sp"))

    def close(self):
        self.stack.close()


class G:
    def __init__(self, nc):
        self.nc = nc
        self.S = Sched(nc)
        self.arena = self.S.stack.enter_context(nc.sbuf_tensor("arena", [128, ARENA_F32], F32))
        self.banks = [self.S.stack.enter_context(nc.psum_tensor(f"bank{i}", [128, 512], F32)) for i in range(8)]
        self.off = 0
        self.nd = 0

    nphase = 0
    max_phase = 10 ** 9

    def phase(self):
        self.nphase += 1
        if self.nphase > self.max_phase:
            self.S.muted = True
        self.S.barrier()
        self.off = 0

    def sb(self, shape, dtype=F32, name=None):
        P = shape[0]
        n = int(np.prod(shape[1:]))
        words = n if dtype == F32 else (n + 1) // 2
        o = self.off
        self.off += words
        assert self.off <= ARENA_F32, ("SBUF arena overflow", self.off)
        ap = self.arena[0:P, o:o + words]
        if dtype != F32:
            ap = ap.bitcast(dtype)[:, 0:n]
        if len(shape) == 3:
            ap = ap.rearrange("p (a b) -> p a b", a=shape[1], b=shape[2])
        elif len(shape) == 4:
            ap = ap.rearrange("p (a b c) -> p a b c", a=shape[1], b=shape[2], c=shape[3])
        return ap

    def ps(self, bank, shape, dtype=F32):
        P = shape[0]
        n = int(np.prod(shape[1:]))
        ap = self.banks[bank][0:P, :]
        if dtype != F32:
            ap = ap.bitcast(dtype)
        ap = ap[:, 0:n]
        if len(shape) == 3:
            ap = ap.rearrange("p (a b) -> p a b", a=shape[1], b=shape[2])
        return ap

    def dram(self, name, shape, kind="Internal"):
        return self.nc.dram_tensor(name, list(shape), F32, kind=kind)

    def finish(self):
        self.S.emit()
        self.S.close()


def load_ident(g, tag="id"):
    S = g.S
    identf = g.sb([128, 128], F32)
    ident = g.sb([128, 128], BF16)
    S.dma(lambda e: e.dma_start(out=identf[:], in_=g.ident_d), w=["identf"], group="identf")
    S.dve(lambda e: e.tensor_copy(ident[:], identf[:]), r=["identf"], w=["ident"])
    return identf, ident


def dcopy(g, dst, src, name, grp="dcopy"):
    g.S.dma(lambda e: e.dma_start(out=dst, in_=src), w=[name], group=grp)


def allgather(g, dst, src, name, r=()):
    groups = [[0, 1, 2, 3], [4, 5, 6, 7]]
    g.S.dma(lambda e: e.collective_compute("AllGather", ALU.bypass, replica_groups=groups,
                                            ins=[src.ap().opt()], outs=[dst.ap().opt()]),
            r=list(r), w=[name], group="cc", eng="pool", inc=1)


def emit_k1(g, xsrc, nt, K, N, w, z, mode, n_lat_tiles=None, mods=None, gvec=None, bias=None, eps=1e-6):
    S = g.S
    g.phase()
    kc = K // 128
    nch = (N + 511) // 512
    identf, ident = load_ident(g)
    wbf = g.sb([128, kc, N], BF16)
    wst = [g.sb([128, N], F32) for i in range(2)]
    if mode == "ln":
        modt = g.sb([128, 4, K], F32)
        for a in range(4):
            S.dma(lambda e, a=a: e.dma_start(out=modt[:, a, :], in_=mods[a].partition_broadcast(128)),
                  w=["modt"], group="modt")
        for a in (1, 3):
            S.dve(lambda e, a=a: e.tensor_scalar_add(modt[:, a, :], modt[:, a, :], 1.0), r=["modt"], w=["modt"])
    elif mode == "rms":
        gt = g.sb([128, K], F32)
        S.dma(lambda e: e.dma_start(out=gt[:], in_=gvec.partition_broadcast(128)), w=["gt"], group="gt")
    else:
        bt = g.sb([128, N], F32)
        S.dma(lambda e: e.dma_start(out=bt[:], in_=bias.partition_broadcast(128)), w=["bt"], group="gt")
    for k in range(kc):
        s = k % 2
        S.dma(lambda e, k=k, s=s: e.dma_start(out=wst[s][:], in_=w[k * 128:(k + 1) * 128, :]),
              w=[f"wst{s}"], group=f"wst{s}")
        if k % 2 == 0:
            S.act(lambda e, k=k, s=s: e.copy(wbf[:, k, :], wst[s][:]), r=[f"wst{s}"], w=[f"wbf{k}"])
        else:
            S.dve(lambda e, k=k, s=s: e.tensor_copy(wbf[:, k, :], wst[s][:]), r=[f"wst{s}"], w=[f"wbf{k}"])
    NX = 2
    xt = [g.sb([128, K], F32) for i in range(NX)]
    xn = [g.sb([128, K], F32) for i in range(NX)]
    hb = [g.sb([128, K], BF16) for i in range(NX)]
    hT = [g.sb([128, kc, 128], BF16) for i in range(NX)]
    st = [g.sb([128, 8, 6], F32) for i in range(NX)]
    mv = [g.sb([128, 4], F32) for i in range(NX)]
    zt = [g.sb([128, N], F32) for i in range(NX)]
    tp = [g.ps(i, [128, kc, 128], BF16) for i in range(2)]
    zp = [g.ps(2 + i, [128, 512], F32) for i in range(4)]
    wkeys = [f"wbf{k}" for k in range(kc)]
    zpi = 0
    def emit_load(i):
        s = i % NX
        S.dma(lambda e, i=i, s=s: e.dma_start(out=xt[s][:], in_=xsrc(i)), w=[f"xt{s}"], group=f"xt{s}")

    emit_load(0)
    for i in range(nt):
        s = i % NX
        if i + 1 < nt:
            emit_load(i + 1)
        if mode == "ln":
            nsub = max(1, K // 512)
            fs = K // nsub
            for j in range(nsub):
                S.dve(lambda e, s=s, j=j, fs=fs: e.bn_stats(st[s][:, j, :], xt[s][:, j * fs:(j + 1) * fs]),
                      r=[f"xt{s}"], w=[f"st{s}"])
            S.dve(lambda e, s=s, nsub=nsub: e.bn_aggr(mv[s][:, 0:2], st[s][:, 0:nsub, :]), r=[f"st{s}"], w=[f"mv{s}"])
            S.dve(lambda e, s=s: e.tensor_scalar_add(mv[s][:, 3:4], mv[s][:, 1:2], eps), r=[f"mv{s}"], w=[f"mv{s}"])
            S.act(lambda e, s=s: e.sqrt(mv[s][:, 3:4], mv[s][:, 3:4]), r=[f"mv{s}"], w=[f"mv{s}"])
            S.dve(lambda e, s=s: e.reciprocal(mv[s][:, 2:3], mv[s][:, 3:4]), r=[f"mv{s}"], w=[f"mv{s}"])
            S.dve(lambda e, s=s: e.tensor_scalar(xn[s][:], xt[s][:], mv[s][:, 0:1], mv[s][:, 2:3],
                                                 ALU.subtract, ALU.mult),
                  r=[f"xt{s}", f"mv{s}"], w=[f"xn{s}"])
            a = 0 if (n_lat_tiles is None or i < n_lat_tiles) else 2
            S.pool(lambda e, s=s, a=a: e.tensor_tensor(xn[s][:], xn[s][:], modt[:, a + 1, :], ALU.mult),
                   r=[f"xn{s}", "modt"], w=[f"xn{s}"])
            S.pool(lambda e, s=s, a=a: e.tensor_tensor(hb[s][:], xn[s][:], modt[:, a, :], ALU.add),
                   r=[f"xn{s}", "modt"], w=[f"hb{s}"])
        elif mode == "silu":
            S.act(lambda e, s=s: e.activation(hb[s][:], xt[s][:], AF.Silu), r=[f"xt{s}"], w=[f"hb{s}"])
        else:
            S.act(lambda e, s=s: e.activation(xn[s][:], xt[s][:], AF.Square, accum_out=mv[s][:, 0:1]),
                  r=[f"xt{s}"], w=[f"xn{s}", f"mv{s}"])
            S.dve(lambda e, s=s: e.tensor_scalar(mv[s][:, 1:2], mv[s][:, 0:1], 1.0 / K, eps, ALU.mult, ALU.add),
                  r=[f"mv{s}"], w=[f"mv{s}"])
            S.act(lambda e, s=s: e.sqrt(mv[s][:, 3:4], mv[s][:, 1:2]), r=[f"mv{s}"], w=[f"mv{s}"])
            S.dve(lambda e, s=s: e.reciprocal(mv[s][:, 2:3], mv[s][:, 3:4]), r=[f"mv{s}"], w=[f"mv{s}"])
            S.dve(lambda e, s=s: e.scalar_tensor_tensor(hb[s][:], xt[s][:], mv[s][:, 2:3], gt[:],
                                                        ALU.mult, ALU.mult),
                  r=[f"xt{s}", f"mv{s}", "gt"], w=[f"hb{s}"])
        t = i % 2
        for k in range(kc):
            S.pe(lambda e, s=s, t=t, k=k: e.transpose(tp[t][:, k, :], hb[s][:, k * 128:(k + 1) * 128], ident[:]),
                 r=[f"hb{s}", "ident"], w=[f"tp{t}"])
        S.act(lambda e, s=s, t=t: e.copy(hT[s][:], tp[t][:]), r=[f"tp{t}"], w=[f"hT{s}"])
        for c in range(nch):
            c0, c1 = c * 512, min(N, (c + 1) * 512)
            p = zpi % 4
            zpi += 1
            for k in range(kc):
                S.pe(lambda e, s=s, p=p, k=k, c0=c0, c1=c1: e.matmul(
                    zp[p][:, 0:c1 - c0], hT[s][:, k, :], wbf[:, k, c0:c1], start=(k == 0), stop=(k == kc - 1)),
                    r=[f"hT{s}", wkeys[k]], w=[f"zp{p}"])
            if mode == "silu":
                S.dve(lambda e, s=s, p=p, c0=c0, c1=c1: e.tensor_tensor(zt[s][:, c0:c1], zp[p][:, 0:c1 - c0], bt[:, c0:c1], ALU.add),
                      r=[f"zp{p}", "bt"], w=[f"zt{s}"])
            elif c % 2 == 0:
                S.dve(lambda e, s=s, p=p, c0=c0, c1=c1: e.tensor_copy(zt[s][:, c0:c1], zp[p][:, 0:c1 - c0]),
                      r=[f"zp{p}"], w=[f"zt{s}"])
            else:
                S.act(lambda e, s=s, p=p, c0=c0, c1=c1: e.copy(zt[s][:, c0:c1], zp[p][:, 0:c1 - c0]),
                      r=[f"zp{p}"], w=[f"zt{s}"])
        S.dma(lambda e, i=i, s=s: e.dma_start(out=z(i), in_=zt[s][:]),
              r=[f"zt{s}"], w=[f"zout{s}"], group=f"zo{s}", eng="act")


def emit_k3(g, NQL, NQC, NK, q, kv, kr, csq, csk, out, nheads=8):
    S = g.S
    g.phase()
    NQ = NQL + NQC
    nqt = NQ // 128
    nkt = NK // 128
    identf, ident = load_ident(g)
    KH = (nkt + 1) // 2
    kvh = g.sb([128, KH, 128], F32)
    kpad = g.sb([128, nkt, 128], BF16)
    vx = [g.sb([128, nkt, 65], BF16) for i in range(2)]
    kT = [g.sb([128, nkt * 128], BF16) for i in range(2)]
    qT = [g.sb([128, nqt * 128], BF16) for i in range(2)]
    qh = g.sb([128, nqt, 96], F32)
    qpad = g.sb([128, nqt, 128], BF16)
    krl = g.sb([128, nkt, 32], F32)
    cskt = g.sb([128, nkt, 32], F32)
    csqt = g.sb([128, nqt, 32], F32)
    tk = [g.sb([128, nkt, 16], F32) for i in range(2)]
    tq = [g.sb([128, nqt, 16], F32) for i in range(2)]
    pT = [g.sb([128, 512], BF16) for i in range(3)]
    oTs = [g.sb([65, 512], F32) for i in range(2)]
    ost = [g.sb([128, 64], F32) for i in range(4)]
    rc = [g.sb([128, 1], F32) for i in range(4)]
    sTp = [g.ps(i, [128, 512], F32) for i in range(3)]
    oTp = [g.ps(3 + i, [65, 512], F32) for i in range(2)]
    tpk = g.ps(5, [128, 8, 128], BF16)
    tpo = g.ps(6, [128, 65], F32)

    S.pool(lambda e: e.memset(kpad[:], 0.0), w=["kpad"])
    S.pool(lambda e: e.memset(qpad[:], 0.0), w=["qpad"])
    for i in range(2):
        S.pool(lambda e, i=i: e.memset(vx[i][:], 1.0), w=[f"vx{i}"])
    S.dma(lambda e: e.dma_start(out=krl[:], in_=kr.rearrange("(t p) c -> p t c", p=128)), w=["krl"], group="krl")
    S.dma(lambda e: e.dma_start(out=cskt[:], in_=csk.rearrange("(t p) c -> p t c", p=128)), w=["cskt"], group="cskt")
    S.dma(lambda e: e.dma_start(out=csqt[:], in_=csq.rearrange("(t p) c -> p t c", p=128)), w=["csqt"], group="csqt")

    def rope(eng_a, eng_b, src, cs, tmp, dst, keys_r, key_tmp, key_dst, xo):
        x1 = lambda: src[:, :, xo:xo + 16]
        x2 = lambda: src[:, :, xo + 16:xo + 32]
        c = lambda: cs[:, :, 0:16]
        sn = lambda: cs[:, :, 16:32]
        S.add(eng_a, lambda e: e.tensor_tensor(tmp[0][:], x1(), c(), ALU.mult), r=keys_r, w=[key_tmp + "0"])
        S.add(eng_b, lambda e: e.tensor_tensor(tmp[1][:], x2(), sn(), ALU.mult), r=keys_r, w=[key_tmp + "1"])
        S.add(eng_a, lambda e: e.tensor_tensor(dst[:, :, 64:80], tmp[0][:], tmp[1][:], ALU.subtract),
              r=[key_tmp + "0", key_tmp + "1"], w=[key_dst])
        S.add(eng_a, lambda e: e.tensor_tensor(tmp[0][:], x1(), sn(), ALU.mult), r=keys_r, w=[key_tmp + "0"])
        S.add(eng_b, lambda e: e.tensor_tensor(tmp[1][:], x2(), c(), ALU.mult), r=keys_r, w=[key_tmp + "1"])
        S.add(eng_a, lambda e: e.tensor_tensor(dst[:, :, 96:112], tmp[0][:], tmp[1][:], ALU.add),
              r=[key_tmp + "0", key_tmp + "1"], w=[key_dst])

    rope("dve", "pool", krl, cskt, tk, kpad, ["krl", "cskt"], "tk", "kpad", 0)

    chunks = []
    for c0 in range(0, NQL, 512):
        chunks.append((c0, min(512, NQL - c0), 0, nkt))
    if NQC:
        chunks.append((NQL, NQC, 0, 2))
    sti = 0
    oti = 0
    osti = 0
    def prologue(h):
        hb = h % 2
        for half in range(2):
            t0 = half * KH
            t1 = min(nkt, t0 + KH)
            if t1 <= t0:
                continue
            S.dma(lambda e, h=h, t0=t0, t1=t1: e.dma_start(
                out=kvh[:, 0:t1 - t0, :],
                in_=kv[t0 * 128:t1 * 128, h * 128:(h + 1) * 128].rearrange("(t p) c -> p t c", p=128)),
                w=["kvh"], group="kvh")
            S.dve(lambda e, t0=t0, t1=t1: e.tensor_copy(kpad[:, t0:t1, 0:64], kvh[:, 0:t1 - t0, 0:64]),
                  r=["kvh"], w=["kpad"])
            S.pool(lambda e, t0=t0, t1=t1, hb=hb: e.tensor_copy(vx[hb][:, t0:t1, 0:64], kvh[:, 0:t1 - t0, 64:128]),
                   r=["kvh"], w=[f"vx{hb}"])
        for g0 in range(0, nkt, 8):
            g1 = min(nkt, g0 + 8)
            for t in range(g0, g1):
                S.pe(lambda e, t=t, g0=g0: e.transpose(tpk[:, t - g0, :], kpad[:, t, :], ident[:]),
                     r=["kpad", "ident"], w=["tpk"])
            S.dve(lambda e, g0=g0, g1=g1, hb=hb: e.tensor_copy(
                kT[hb][:, g0 * 128:g1 * 128], tpk[:, 0:g1 - g0, :].rearrange("p a b -> p (a b)")),
                r=["tpk"], w=[f"kT{hb}"])
        S.dma(lambda e, h=h: e.dma_start(out=qh[:], in_=q[:, h * 96:(h + 1) * 96].rearrange("(t p) c -> p t c", p=128)),
              w=["qh"], group="qh")
        S.pool(lambda e: e.tensor_copy(qpad[:, :, 0:64], qh[:, :, 0:64]), r=["qh"], w=["qpad"])
        rope("pool", "dve", qh, csqt, tq, qpad, ["qh", "csqt"], "tq", "qpad", 64)
        for g0 in range(0, nqt, 8):
            g1 = min(nqt, g0 + 8)
            for t in range(g0, g1):
                S.pe(lambda e, t=t, g0=g0: e.transpose(tpk[:, t - g0, :], qpad[:, t, :], ident[:]),
                     r=["qpad", "ident"], w=["tpk"])
            S.dve(lambda e, g0=g0, g1=g1, hb=hb: e.tensor_copy(
                qT[hb][:, g0 * 128:g1 * 128], tpk[:, 0:g1 - g0, :].rearrange("p a b -> p (a b)")),
                r=["tpk"], w=[f"qT{hb}"])

    st = dict(sti=0, oti=0, osti=0)
    LAG = 2

    def mainloop(h):
        hb = h % 2
        its = []
        for (q0, qn, k0, k1) in chunks:
            op = st["oti"] % 2
            st["oti"] += 1
            for kt in range(k0, k1):
                its.append((q0, qn, k0, k1, kt, op))

        def emit_qk(n):
            q0, qn, k0, k1, kt, op = its[n]
            sp = st["sti"] % 3
            st["sti"] += 1
            its[n] = its[n] + (sp,)
            S.pe(lambda e: e.matmul(sTp[sp][:, 0:qn], kT[hb][:, kt * 128:(kt + 1) * 128], qT[hb][:, q0:q0 + qn],
                                    start=True, stop=True), r=[f"kT{hb}", f"qT{hb}"], w=[f"sTp{sp}"])
            S.act(lambda e: e.activation(pT[sp][:, 0:qn], sTp[sp][:, 0:qn], AF.Exp, scale=MLA_SCALE),
                  r=[f"sTp{sp}"], w=[f"pT{sp}"])

        def emit_pv(n):
            q0, qn, k0, k1, kt, op, sp = its[n]
            S.pe(lambda e: e.matmul(oTp[op][:, 0:qn], vx[hb][:, kt, :], pT[sp][:, 0:qn], start=(kt == k0), stop=(kt == k1 - 1)),
                 r=[f"vx{hb}", f"pT{sp}"], w=[f"oTp{op}"])
            if kt != k1 - 1:
                return
            S.dve(lambda e: e.tensor_copy(oTs[op][:, 0:qn], oTp[op][:, 0:qn]), r=[f"oTp{op}"], w=[f"oTs{op}"])
            for j in range(qn // 128):
                os_ = st["osti"] % 4
                st["osti"] += 1
                S.pe(lambda e, j=j: e.transpose(tpo[:], oTs[op][:, j * 128:(j + 1) * 128], identf[0:65, 0:65]),
                     r=[f"oTs{op}", "identf"], w=["tpo"])
                S.dve(lambda e, os_=os_: e.reciprocal(rc[os_][:], tpo[:, 64:65]), r=["tpo"], w=[f"rc{os_}"])
                S.dve(lambda e, os_=os_: e.tensor_scalar(ost[os_][:], tpo[:, 0:64], rc[os_][:, 0:1], None, ALU.mult),
                      r=["tpo", f"rc{os_}"], w=[f"ost{os_}"])
                r0 = q0 + j * 128
                S.dma(lambda e, os_=os_, r0=r0: e.dma_start(out=out[r0:r0 + 128, h * 64:(h + 1) * 64], in_=ost[os_][:]),
                      r=[f"ost{os_}"], w=[f"oo{os_}"], group=f"oo{os_}")

        nI = len(its)
        for n in range(nI):
            emit_qk(n)
            if n >= LAG:
                emit_pv(n - LAG)
            if n == nI // 2 and h + 1 < nheads:
                prologue(h + 1)
        for n in range(max(0, nI - LAG), nI):
            emit_pv(n)

    prologue(0)
    for h in range(nheads):
        mainloop(h)


def gla_consts2():
    s = np.arange(64)[:, None]
    t = np.arange(64)[None, :]
    out = np.zeros((2, 64, 320), np.float32)
    U = (s <= t).astype(np.float32)
    out[0, :, 0:64] = U - (s <= 31)
    out[0, :, 64:128] = U
    out[0, :, 128:192] = (s > t)
    out[0, :, 192:256] = U
    Ub = (s >= t).astype(np.float32)
    out[1, :, 0:64] = Ub - (s >= 32)
    out[1, :, 64:128] = Ub
    out[1, :, 128:192] = (s < t)
    out[1, :, 192:256] = Ub
    out[:, 0, 256:320] = 1.0
    return out


def emit_k4s(g, Z, OG, segloc, segall, w2, bb, cst_d, mask):
    S = g.S
    g.phase()
    NLC, NCC = 32, 4
    ZH = ["Zh0", "Zh1", "Zh2", "Zh3"]
    NCH = NLC + NCC
    identf, ident = load_ident(g)
    cst = g.sb([64, 2, 320], F32)
    maskb = g.sb([64, 2, 64], BF16)
    w2t = g.sb([16, 2, 256], F32)
    bbt = g.sb([1, 2, 256], F32)
    mk = g.sb([64, 16], F32)
    zh_off = g.off
    Zh = g.sb([64, NCH, 288], F32)
    vb = g.sb([64, NCH, 128], BF16)
    OLs = g.sb([64, NCH, 512], F32)
    qbP = g.sb([64, 8, NLC, 64], BF16)
    Sctx = g.sb([64, 8, 128], F32)
    SEG = g.sb([64, 8, 129], F32)
    Sib = g.sb([64, 8, 128], BF16)
    Wk = g.sb([64, 128], F32)
    cand = g.sb([64, 128], F32)
    diff = g.sb([64, 128], F32)
    NP = 6
    NPA, NPB = 5, 3
    mk_t = lambda shape, dt: [g.sb(shape, dt) for i in range(NP)]
    qTs = mk_t([64, 64], F32); kTs = mk_t([64, 64], F32); glTs = mk_t([16, 64], F32)
    Et = mk_t([64, 64], F32); Lt = mk_t([64, 64], F32); e12 = mk_t([64, 192], F32); e4 = mk_t([64, 64], F32)
    dec = mk_t([64, 1], F32)
    qe = mk_t([64, 64], BF16); ke = mk_t([64, 64], BF16); qb = mk_t([64, 64], BF16); kd = mk_t([64, 64], BF16)
    attm = mk_t([64, 64], BF16)
    Sf = [g.sb([64, 128], F32) for i in range(2)]
    Sb = [g.sb([64, 128], BF16) for i in range(2)]
    Pt = [g.sb([64, 1], F32) for i in range(2)]
    pA = [g.ps(i, [64, 512], F32) for i in range(NPA)]
    pB = [g.ps(NPA + i, [64, 512], F32) for i in range(NPB)]
    id64 = identf[0:64, 0:64]

    S.dma(lambda e: e.dma_start(out=cst[:], in_=cst_d.rearrange("d p f -> p d f")), w=["cst"], group="cst")
    S.dve(lambda e: e.tensor_copy(maskb[:], cst[:, :, 192:256]), r=["cst"], w=["maskb"])
    S.dma(lambda e: e.dma_start(out=w2t[:], in_=w2.rearrange("d r e -> r d e")), w=["w2t"], group="w2t")
    S.dma(lambda e: e.dma_start(out=bbt[:], in_=bb.rearrange("(o d) e -> o d e", o=1)), w=["bbt"], group="bbt")
    S.dma(lambda e: e.dma_start(out=mk[:], in_=mask[0:64, :]), w=["mk"], group="mk")
    Zv = Z.rearrange("(c p) f -> p c f", p=64)
    tasks = []
    ci = 0
    for h in range(4):
        first_of_head = True
        for d in range(2):
            hd = h * 2 + d
            for part in ("ctx", "lat"):
                lat = part == "lat"
                order = list(range(NLC)) if lat else list(range(NLC, NCH))
                if d == 1:
                    order = order[::-1]
                si = 0
                for idx, c in enumerate(order):
                    p = ci % NP
                    pa, pb = ci % NPA, ci % NPB
                    ci += 1
                    s0, s1 = si % 2, (si + 1) % 2
                    si += 1
                    tasks.append(dict(h=h, d=d, hd=hd, lat=lat, c=c, p=p, pa=pa, pb=pb, s0=s0, s1=s1, first=(idx == 0),
                                      last=(idx == len(order) - 1), head_start=first_of_head))
                    first_of_head = False

    def load_head(h):
        srcs = [(slice(h * 64, (h + 1) * 64), slice(0, 64)), (slice(256 + h * 64, 256 + (h + 1) * 64), slice(64, 128)),
                (slice(512 + h * 128, 512 + (h + 1) * 128), slice(128, 256)), (slice(1024, 1056), slice(256, 288))]
        for k, (sc, dc) in enumerate(srcs):
            S.dma(lambda e, sc=sc, dc=dc: e.dma_start(out=Zh[:, :, dc], in_=Zv[:, :, sc]), w=[f"Zh{k}"], group=f"Zh{k}")
        S.pool(lambda e: e.tensor_copy(vb[:], Zh[:, :, 128:256]), r=ZH, w=["vb"])

    def hdr(t):
        h, d, c, p, pa = t["h"], t["d"], t["c"], t["p"], t["pa"]
        return h, d, c, p, pa, f"pA{pa}"

    def stageA1(t):
        h, d, c, p, pa, A = hdr(t)
        gcol = 256 + 16 * d
        S.pe(lambda e: e.transpose(pA[pa][:, 320:384], Zh[:, c, 0:64], id64), r=ZH + ["identf"], w=[A])
        S.pe(lambda e: e.transpose(pA[pa][:, 384:448], Zh[:, c, 64:128], id64), r=ZH + ["identf"], w=[A])
        S.pe(lambda e: e.transpose(pA[pa][0:16, 448:512], Zh[:, c, gcol:gcol + 16], id64), r=ZH + ["identf"], w=[A])
        S.act(lambda e: e.mul(qTs[p][:], pA[pa][:, 320:384], 0.125), r=[A], w=[f"qTs{p}"])
        S.act(lambda e: e.copy(kTs[p][:], pA[pa][:, 384:448]), r=[A], w=[f"kTs{p}"])
        S.act(lambda e: e.copy(glTs[p][:], pA[pa][0:16, 448:512]), r=[A], w=[f"glTs{p}"])

    def stageA2(t):
        h, d, c, p, pa, A = hdr(t)
        w2hd = w2t[:, d, h * 64:(h + 1) * 64]
        bbhd = bbt[0:1, d, h * 64:(h + 1) * 64]
        S.pe(lambda e: e.matmul(pA[pa][:, 0:64], glTs[p][:], w2hd, start=True, stop=False), r=[f"glTs{p}", "w2t"], w=[A])
        S.pe(lambda e: e.matmul(pA[pa][:, 0:64], cst[0:1, 0, 256:320], bbhd, start=False, stop=True), r=["cst", "bbt"], w=[A])
        S.act(lambda e: e.activation(Et[p][:], pA[pa][:, 0:64], AF.Exp, scale=-1.0), r=[A], w=[f"Et{p}"])
        S.act(lambda e: e.activation(Lt[p][:], Et[p][:], AF.Ln, bias=1.0), r=[f"Et{p}"], w=[f"Lt{p}"])

    def stageA3(t):
        h, d, c, p, pa, A = hdr(t)
        cD = lambda a, b: cst[:, d, a:b]
        deccol = 127 if d == 0 else 64
        S.pe(lambda e: e.matmul(pA[pa][:, 64:128], Lt[p][:], cD(0, 64), start=True, stop=True), r=[f"Lt{p}", "cst"], w=[A])
        S.pe(lambda e: e.matmul(pA[pa][:, 128:192], Lt[p][:], cD(64, 128), start=True, stop=True), r=[f"Lt{p}", "cst"], w=[A])
        S.pe(lambda e: e.matmul(pA[pa][:, 192:256], cD(128, 192), Lt[p][:], start=True, stop=True), r=[f"Lt{p}", "cst"], w=[A])
        S.act(lambda e: e.activation(e12[p][:, 0:128], pA[pa][:, 64:192], AF.Exp, scale=-1.0 / 16), r=[A], w=[f"e12{p}"])
        S.act(lambda e: e.activation(e12[p][:, 128:192], pA[pa][:, 64:128], AF.Exp, scale=1.0 / 16), r=[A], w=[f"e12{p}"])
        S.act(lambda e: e.activation(e4[p][:], pA[pa][:, 192:256], AF.Exp, scale=-1.0 / 16), r=[A], w=[f"e4{p}"])
        S.act(lambda e: e.copy(dec[p][:], e12[p][:, deccol:deccol + 1]), r=[f"e12{p}"], w=[f"dec{p}"])
        S.dve(lambda e: e.tensor_tensor(qe[p][:], qTs[p][:], e12[p][:, 0:64], ALU.mult), r=[f"qTs{p}", f"e12{p}"], w=[f"qe{p}"])
        S.dve(lambda e: e.tensor_tensor(ke[p][:], kTs[p][:], e12[p][:, 128:192], ALU.mult), r=[f"kTs{p}", f"e12{p}"], w=[f"ke{p}"])
        S.pool(lambda e: e.tensor_tensor(qb[p][:], qTs[p][:], e12[p][:, 64:128], ALU.mult), r=[f"qTs{p}", f"e12{p}"], w=[f"qb{p}"])
        S.pool(lambda e: e.tensor_tensor(kd[p][:], Zh[:, c, 64:128], e4[p][:], ALU.mult), r=ZH + [f"e4{p}"], w=[f"kd{p}"])

    def stageB(t):
        h, d, hd, c, p, s0, s1, lat = t["h"], t["d"], t["hd"], t["c"], t["p"], t["s0"], t["s1"], t["lat"]
        pb = t["pb"]
        B = f"pB{pb}"
        if t["first"]:
            S.dve(lambda e: e.memset(Sf[0][:], 0.0), w=["Sf0"])
            S.pool(lambda e: e.memset(Sb[0][:], 0.0), w=["Sb0"])
            if lat:
                S.dve(lambda e: e.memset(Pt[0][:], 1.0), w=["P0"])
        if lat:
            S.dve(lambda e: e.scalar_tensor_tensor(qbP[:, hd, c, :], qTs[p][:], Pt[s0][:, 0:1], e12[p][:, 64:128],
                                                   ALU.mult, ALU.mult), r=[f"qTs{p}", f"P{s0}", f"e12{p}"], w=["qbP"])
            S.dve(lambda e: e.tensor_tensor(Pt[s1][:], Pt[s0][:], dec[p][:], ALU.mult), r=[f"P{s0}", f"dec{p}"], w=[f"P{s1}"])
        S.pe(lambda e: e.matmul(pB[pb][:, 256:320], ke[p][:], qe[p][:], start=True, stop=True), r=[f"ke{p}", f"qe{p}"], w=[B])
        S.dve(lambda e: e.tensor_tensor(attm[p][:], pB[pb][:, 256:320], maskb[:, d, :], ALU.mult), r=[B, "maskb"], w=[f"attm{p}"])
        S.pe(lambda e: e.matmul(pB[pb][:, 0:128], attm[p][:], vb[:, c, :], start=True, stop=False), r=[f"attm{p}", "vb"], w=[B])
        S.pe(lambda e: e.matmul(pB[pb][:, 0:128], qb[p][:], Sb[s0][:], start=False, stop=True), r=[f"qb{p}", f"Sb{s0}"], w=[B])
        S.pe(lambda e: e.matmul(pB[pb][:, 128:256], kd[p][:], vb[:, c, :], start=True, stop=True), r=[f"kd{p}", "vb"], w=[B])
        S.dve(lambda e: e.scalar_tensor_tensor(Sf[s1][:], Sf[s0][:], dec[p][:, 0:1], pB[pb][:, 128:256], ALU.mult, ALU.add),
              r=[f"Sf{s0}", f"dec{p}", B], w=[f"Sf{s1}"])
        S.pool(lambda e: e.tensor_copy(Sb[s1][:], Sf[s1][:]), r=[f"Sf{s1}"], w=[f"Sb{s1}"])
        ocols = slice(h * 128, (h + 1) * 128)
        if d == 0:
            S.dve(lambda e: e.tensor_copy(OLs[:, c, ocols], pB[pb][:, 0:128]), r=[B], w=[f"OL{c}"])
        else:
            S.dve(lambda e: e.tensor_tensor(OLs[:, c, ocols], pB[pb][:, 0:128], OLs[:, c, ocols], ALU.add),
                  r=[B, f"OL{c}"], w=[f"OL{c}"])
        if t["last"]:
            if lat:
                S.dve(lambda e: e.tensor_copy(SEG[:, hd, 0:128], Sf[s1][:]), r=[f"Sf{s1}"], w=["SEG"])
                S.dve(lambda e: e.tensor_copy(SEG[:, hd, 128:129], Pt[s1][:]), r=[f"P{s1}"], w=["SEG"])
            else:
                S.dve(lambda e: e.tensor_copy(Sctx[:, hd, :], Sf[s1][:]), r=[f"Sf{s1}"], w=["Sctx"])

    stages = [stageB, stageA3, stageA2, stageA1]
    heads = {}
    for t in tasks:
        heads.setdefault(t["h"], []).append(t)
    for h in range(4):
        tl = heads[h]
        load_head(h)
        n = len(tl)
        for step in range(n + 3):
            for k, stg in enumerate(stages):
                idx = step - (3 - k)
                if 0 <= idx < n:
                    stg(tl[idx])
    S.dma(lambda e: e.dma_start(out=segloc.ap(), in_=SEG[:].rearrange("p a b -> p (a b)")), r=["SEG"], w=["segloc"],
          group="segio")
    allgather(g, segall, segloc, "segall", r=["segloc"])
    SEGa = g.arena[0:64, zh_off:zh_off + 4 * 8 * 129].rearrange("p (r a b) -> p r a b", r=4, a=8, b=129)
    S.dma(lambda e: e.dma_start(out=SEGa.rearrange("p r a b -> p r (a b)"),
                                in_=segall.ap().rearrange("(r p) f -> p r f", p=64)),
          r=["segall"], w=ZH + ["SEGa"], group="segio")
    for h in range(4):
        for d in range(2):
            hd = h * 2 + d
            S.dve(lambda e, hd=hd: e.tensor_copy(Wk[:], Sctx[:, hd, :]), r=["Sctx"], w=["Wk"])
            js = range(4) if d == 0 else range(3, -1, -1)
            for j in js:
                mcol = (4 if d == 0 else 8) + j
                S.dve(lambda e, j=j, hd=hd: e.scalar_tensor_tensor(cand[:], Wk[:], SEGa[:, j, hd, 128:129],
                                                                   SEGa[:, j, hd, 0:128], ALU.mult, ALU.add),
                      r=["Wk", "SEGa"], w=["cand"])
                S.dve(lambda e: e.tensor_tensor(diff[:], cand[:], Wk[:], ALU.subtract), r=["cand", "Wk"], w=["diff"])
                S.dve(lambda e, mcol=mcol: e.scalar_tensor_tensor(Wk[:], diff[:], mk[:, mcol:mcol + 1], Wk[:],
                                                                  ALU.mult, ALU.add), r=["diff", "mk", "Wk"], w=["Wk"])
            S.dve(lambda e, hd=hd: e.tensor_copy(Sib[:, hd, :], Wk[:]), r=["Wk"], w=["Sib"])
    for c in range(NLC):
        p = c % NPA
        A = f"pA{p}"
        for h in range(4):
            for d in range(2):
                hd = h * 2 + d
                S.pe(lambda e, p=p, h=h, d=d, hd=hd, c=c: e.matmul(pA[p][:, h * 128:(h + 1) * 128], qbP[:, hd, c, :],
                                                                    Sib[:, hd, :], start=(d == 0), stop=(d == 1)),
                     r=["qbP", "Sib"], w=[A])
        S.dve(lambda e, p=p, c=c: e.tensor_tensor(OLs[:, c, :], pA[p][:], OLs[:, c, :], ALU.add), r=[A, f"OL{c}"], w=[f"OL{c}"])
    OGv = OG.rearrange("(c p) f -> p c f", p=64)
    for k in range(4):
        cs = slice(k * 9, (k + 1) * 9)
        S.dma(lambda e, cs=cs: e.dma_start(out=OGv[:, cs, :], in_=OLs[:, cs, :]),
              r=[f"OL{c}" for c in range(k * 9, (k + 1) * 9)], w=[f"ogo{k}"], group=f"ogo{k}")


def emit_k5(g, T, kind, n_lat_tiles, x, w, vecs, z, out, og=None, oml=None, gng=None, fr_lat=None, fr_ctx=None,
            swT=None, sbT=None, mask=None, eps=1e-6):
    S = g.S
    g.phase()
    D = 1024
    nt = T // 128
    identf, ident = load_ident(g)
    wbf = g.sb([128, 8, D], BF16)
    wst = [g.sb([128, D], F32) for i in range(2)]
    vt = g.sb([128, 4, D], F32)
    for a in range(4):
        S.dma(lambda e, a=a: e.dma_start(out=vt[:, a, :], in_=vecs[a].partition_broadcast(128)), w=["vt"], group="modt")
    if kind == "even":
        gn = g.sb([128, 128], F32)
        S.dma(lambda e: e.dma_start(out=gn[:], in_=gng.partition_broadcast(128)), w=["gn"], group="gt")
    else:
        swf = g.sb([128, 4, 128], F32)
        swb = g.sb([128, 4, 128], BF16)
        sbt = g.sb([128, 4], F32)
        mk = g.sb([128, 16], F32)
        S.dma(lambda e: e.dma_start(out=swf[:], in_=swT.rearrange("g s t -> s g t")), w=["swf"], group="swf")
        S.dve(lambda e: e.tensor_copy(swb[:], swf[:]), r=["swf"], w=["swb"])
        S.dma(lambda e: e.dma_start(out=sbt[:], in_=sbT), w=["sbt"], group="sbt")
        S.dma(lambda e: e.dma_start(out=mk[:], in_=mask), w=["mk"], group="mk")
    for k in range(8):
        s = k % 2
        S.dma(lambda e, k=k, s=s: e.dma_start(out=wst[s][:], in_=w[k * 128:(k + 1) * 128, :]),
              w=[f"wst{s}"], group=f"wst{s}")
        if k % 2 == 0:
            S.act(lambda e, k=k, s=s: e.copy(wbf[:, k, :], wst[s][:]), r=[f"wst{s}"], w=[f"wbf{k}"])
        else:
            S.dve(lambda e, k=k, s=s: e.tensor_copy(wbf[:, k, :], wst[s][:]), r=[f"wst{s}"], w=[f"wbf{k}"])
    NX = 2
    xt = [g.sb([128, D], F32) for i in range(NX)]
    ab = [g.sb([128, D], BF16) for i in range(NX)]
    aT = [g.sb([128, 8, 128], BF16) for i in range(NX)]
    rr = [g.sb([128, D], F32) for i in range(NX)]
    ot = [g.sb([128, D], F32) for i in range(NX)]
    st = [g.sb([128, 8, 6], F32) for i in range(NX)]
    mv = [g.sb([128, 16], F32) for i in range(NX)]
    if kind == "even":
        i1 = [g.sb([128, 512], F32) for i in range(NX)]
        i2 = [g.sb([128, 512], F32) for i in range(NX)]
        i3 = [g.sb([128, 512], F32) for i in range(NX)]
        i4 = [g.sb([128, 512], F32) for i in range(NX)]
        i5 = [g.sb([128, 512], F32) for i in range(NX)]
    else:
        i1 = [g.sb([128, 512], F32) for i in range(NX)]
        fc4 = [g.sb([128, 4, 512], F32) for i in range(NX)]
        zz = [g.sb([128, 2048], F32) for i in range(NX)]
        vgb = [g.sb([128, 512], BF16) for i in range(NX)]
        svp = [g.ps(0, [128, 512], F32)]
    tp = [g.ps(1 + i, [128, 8, 128], BF16) for i in range(2)]
    yp = [g.ps(3 + i, [128, 512], F32) for i in range(4)]
    wkeys = [f"wbf{k}" for k in range(8)]

    def rstd_chain(s, col_var, col_out, ncol=1):
        S.dve(lambda e: e.tensor_scalar_add(mv[s][:, col_var:col_var + ncol], mv[s][:, col_var:col_var + ncol], eps),
              r=[f"mv{s}"], w=[f"mv{s}"])
        S.act(lambda e: e.sqrt(mv[s][:, col_var:col_var + ncol], mv[s][:, col_var:col_var + ncol]),
              r=[f"mv{s}"], w=[f"mv{s}"])
        S.dve(lambda e: e.reciprocal(mv[s][:, col_out:col_out + ncol], mv[s][:, col_var:col_var + ncol]),
              r=[f"mv{s}"], w=[f"mv{s}"])

    def emit_loads(i):
        s = i % NX
        rows = slice(i * 128, (i + 1) * 128)
        S.dma(lambda e, s=s, rows=rows: e.dma_start(out=xt[s][:], in_=x[rows, :]), w=[f"xt{s}"], group=f"xt{s}")
        if kind == "even":
            S.dma(lambda e, s=s, rows=rows: e.dma_start(out=i1[s][:], in_=og[rows, :]), w=[f"i1{s}"], group=f"i1{s}")
            S.dma(lambda e, s=s, rows=rows: e.dma_start(out=i3[s][:], in_=z[rows, 1056:1568]), w=[f"i3{s}"], group=f"i3{s}")
            S.dma(lambda e, s=s, rows=rows: e.dma_start(out=i4[s][:], in_=oml[rows, :]), w=[f"i4{s}"], group=f"i4{s}")
            S.dma(lambda e, s=s, rows=rows: e.dma_start(out=i5[s][:], in_=z[rows, 1984:2496]), w=[f"i5{s}"], group=f"i5{s}")
        else:
            if i < n_lat_tiles:
                for qq in range(4):
                    S.dma(lambda e, s=s, qq=qq, i=i: e.dma_start(
                        out=fc4[s][:, qq, :].rearrange("p (g c) -> p g c", g=4), in_=fr_lat(qq, i)),
                        w=[f"fc4{s}.{qq}"], group=f"fc4{s}")
            else:
                S.dma(lambda e, s=s, i=i: e.dma_start(out=i1[s][:].rearrange("p (g c) -> p g c", g=4), in_=fr_ctx(i)),
                      w=[f"i1{s}"], group=f"i1{s}")
            S.dma(lambda e, s=s, rows=rows: e.dma_start(out=zz[s][:], in_=z[rows, 512:2560]), w=[f"zz{s}"], group=f"zz{s}")

    emit_loads(0)
    for i in range(nt):
        s = i % NX
        rows = slice(i * 128, (i + 1) * 128)
        if i + 1 < nt:
            emit_loads(i + 1)
        if kind == "even":
            S.pool(lambda e, s=s: e.tensor_tensor(i2[s][:], i1[s][:], i1[s][:], ALU.mult), r=[f"i1{s}"], w=[f"i2{s}"])
            S.dve(lambda e, s=s: e.reduce_sum(mv[s][:, 0:4], i2[s][:].rearrange("p (h d) -> p h d", h=4), AX.X),
                  r=[f"i2{s}"], w=[f"mv{s}"])
            S.dve(lambda e, s=s: e.tensor_scalar(mv[s][:, 0:4], mv[s][:, 0:4], 1.0 / 128, None, ALU.mult),
                  r=[f"mv{s}"], w=[f"mv{s}"])
            rstd_chain(s, 0, 4, 4)
            S.act(lambda e, s=s: e.activation(i3[s][:], i3[s][:], AF.Silu), r=[f"i3{s}"], w=[f"i3{s}"])
            S.act(lambda e, s=s: e.activation(i5[s][:], i5[s][:], AF.Silu), r=[f"i5{s}"], w=[f"i5{s}"])
            for h in range(4):
                hs = slice(h * 128, (h + 1) * 128)
                S.dve(lambda e, s=s, h=h, hs=hs: e.scalar_tensor_tensor(
                    i1[s][:, hs], i1[s][:, hs], mv[s][:, 4 + h:5 + h], gn[:], ALU.mult, ALU.mult),
                    r=[f"i1{s}", f"mv{s}", "gn"], w=[f"i1{s}"])
            S.pool(lambda e, s=s: e.tensor_tensor(ab[s][:, 0:512], i1[s][:], i3[s][:], ALU.mult),
                   r=[f"i1{s}", f"i3{s}"], w=[f"ab{s}"])
            S.pool(lambda e, s=s: e.tensor_tensor(ab[s][:, 512:1024], i4[s][:], i5[s][:], ALU.mult),
                   r=[f"i4{s}", f"i5{s}"], w=[f"ab{s}"])
        else:
            if i < n_lat_tiles:
                S.dve(lambda e, s=s: e.tensor_scalar(i1[s][:], fc4[s][:, 0, :], mk[:, 0:1], None, ALU.mult),
                      r=[f"fc4{s}.0", f"fc4{s}.3", "mk"], w=[f"i1{s}"])
                for qq in range(1, 4):
                    S.dve(lambda e, s=s, qq=qq: e.scalar_tensor_tensor(
                        i1[s][:], fc4[s][:, qq, :], mk[:, qq:qq + 1], i1[s][:], ALU.mult, ALU.add),
                        r=[f"fc4{s}.{qq}", f"fc4{s}.3", "mk", f"i1{s}"], w=[f"i1{s}"])
            S.act(lambda e, s=s: e.activation(zz[s][:, 0:512], zz[s][:, 0:512], AF.Silu), r=[f"zz{s}"], w=[f"zz{s}"])
            S.act(lambda e, s=s: e.activation(zz[s][:, 1536:2048], zz[s][:, 1536:2048], AF.Silu), r=[f"zz{s}"], w=[f"zz{s}"])
            S.act(lambda e, s=s: e.activation(zz[s][:, 512:1536], zz[s][:, 512:1536], AF.Gelu), r=[f"zz{s}"], w=[f"zz{s}"])
            S.pool(lambda e, s=s: e.tensor_tensor(ab[s][:, 0:512], i1[s][:], zz[s][:, 0:512], ALU.mult),
                   r=[f"i1{s}", f"zz{s}"], w=[f"ab{s}"])
            for g in range(4):
                S.dve(lambda e, s=s, g=g: e.bn_stats(st[s][:, g, :], zz[s][:, 1024 + g * 128:1024 + (g + 1) * 128]),
                      r=[f"zz{s}"], w=[f"st{s}"])
                S.dve(lambda e, s=s, g=g: e.bn_aggr(mv[s][:, 2 * g:2 * g + 2], st[s][:, g:g + 1, :]),
                      r=[f"st{s}"], w=[f"mv{s}"])
            for g in range(4):
                rstd_chain(s, 2 * g + 1, 8 + g, 1)
            for g in range(4):
                S.dve(lambda e, s=s, g=g: e.tensor_scalar(
                    vgb[s][:, g * 128:(g + 1) * 128], zz[s][:, 1024 + g * 128:1024 + (g + 1) * 128],
                    mv[s][:, 2 * g:2 * g + 1], mv[s][:, 8 + g:9 + g], ALU.subtract, ALU.mult),
                    r=[f"zz{s}", f"mv{s}"], w=[f"vgb{s}"])
            for g in range(4):
                S.pe(lambda e, s=s, g=g: e.matmul(svp[0][:, g * 128:(g + 1) * 128], swb[:, g, :],
                                                  vgb[s][:, g * 128:(g + 1) * 128], start=True, stop=True),
                     r=[f"vgb{s}", "swb"], w=["svp0"])
            for g in range(4):
                gs = slice(g * 128, (g + 1) * 128)
                S.dve(lambda e, s=s, g=g, gs=gs: e.scalar_tensor_tensor(
                    zz[s][:, 512 + g * 128:512 + (g + 1) * 128], svp[0][:, gs], sbt[:, g:g + 1],
                    zz[s][:, 512 + g * 128:512 + (g + 1) * 128], ALU.add, ALU.mult),
                    r=["svp0", "sbt", f"zz{s}"], w=[f"zz{s}"])
            S.pool(lambda e, s=s: e.tensor_tensor(ab[s][:, 512:1024], zz[s][:, 512:1024], zz[s][:, 1536:2048], ALU.mult),
                   r=[f"zz{s}"], w=[f"ab{s}"])
        t = i % 2
        for k in range(8):
            S.pe(lambda e, s=s, t=t, k=k: e.transpose(tp[t][:, k, :], ab[s][:, k * 128:(k + 1) * 128], ident[:]),
                 r=[f"ab{s}", "ident"], w=[f"tp{t}"])
        S.act(lambda e, s=s, t=t: e.copy(aT[s][:], tp[t][:]), r=[f"tp{t}"], w=[f"aT{s}"])
        gi = 0 if i < n_lat_tiles else 1
        for c in range(2):
            p = (2 * i + c) % 4
            cs = slice(c * 512, (c + 1) * 512)
            for k in range(8):
                S.pe(lambda e, s=s, p=p, k=k, cs=cs: e.matmul(yp[p][:], aT[s][:, k, :], wbf[:, k, cs],
                                                             start=(k == 0), stop=(k == 7)),
                     r=[f"aT{s}", wkeys[k]], w=[f"yp{p}"])
            S.dve(lambda e, s=s, p=p, cs=cs, gi=gi: e.tensor_tensor(rr[s][:, cs], yp[p][:], vt[:, gi, cs], ALU.mult),
                  r=[f"yp{p}", "vt"], w=[f"rr{s}"])
        S.dve(lambda e, s=s: e.scalar_tensor_tensor(rr[s][:], xt[s][:], ALPHA, rr[s][:], ALU.mult, ALU.add),
               r=[f"xt{s}", f"rr{s}"], w=[f"rr{s}"])
        for j in range(2):
            S.dve(lambda e, s=s, j=j: e.bn_stats(st[s][:, 4 + j, :], rr[s][:, j * 512:(j + 1) * 512]),
                  r=[f"rr{s}"], w=[f"st{s}"])
        S.dve(lambda e, s=s: e.bn_aggr(mv[s][:, 12:14], st[s][:, 4:6, :]), r=[f"st{s}"], w=[f"mv{s}"])
        rstd_chain(s, 13, 14, 1)
        S.dve(lambda e, s=s: e.tensor_scalar(rr[s][:], rr[s][:], mv[s][:, 12:13], mv[s][:, 14:15],
                                             ALU.subtract, ALU.mult), r=[f"rr{s}", f"mv{s}"], w=[f"rr{s}"])
        S.pool(lambda e, s=s: e.tensor_tensor(rr[s][:], rr[s][:], vt[:, 2, :], ALU.mult), r=[f"rr{s}", "vt"], w=[f"rr{s}"])
        S.pool(lambda e, s=s: e.tensor_tensor(ot[s][:], rr[s][:], vt[:, 3, :], ALU.add), r=[f"rr{s}", "vt"], w=[f"ot{s}"])
        S.dma(lambda e, s=s, rows=rows: e.dma_start(out=out[rows, :], in_=ot[s][:]), r=[f"ot{s}"], w=[f"oo{s}"],
              group=f"oo{s}", eng="act")


def emit_k7(g, fall, fout, TW, FC, W3, TWc, mask, with_ctx=True):
    S = g.S
    g.phase()
    st = [g.sb([128, 4, 512], F32) for i in range(2)]
    tmp = [g.sb([128, 4, 128], F32) for i in range(2)]
    mk = g.sb([128, 16], F32)
    TWb = g.sb([128, 64, 256], BF16)
    fb = g.sb([128, 64, 128], BF16)
    Y = g.sb([128, 2, 64, 128], BF16)
    U = g.sb([128, 64, 256], BF16)
    FCb = g.sb([128, 512], BF16)
    W3b = g.sb([128, 256], BF16)
    frt = g.sb([128, 64, 128], F32)
    ps = [g.ps(i, [128, 512], F32) for i in range(4)]
    S.dma(lambda e: e.dma_start(out=mk[:], in_=mask), w=["mk"], group="mk")
    stf = lambda s: st[s][:].rearrange("p a b -> p (a b)")
    for i in range(8):
        s = i % 2
        S.dma(lambda e, i=i, s=s: e.dma_start(out=stf(s), in_=TW[:, i * 8:(i + 1) * 8, :].rearrange("p a b -> p (a b)")),
              w=[f"st{s}"], group=f"st{s}")
        S.add("dve" if i % 2 == 0 else "pool",
              lambda e, i=i, s=s: e.tensor_copy(TWb[:, i * 8:(i + 1) * 8, :].rearrange("p a b -> p (a b)"), stf(s)),
              r=[f"st{s}"], w=["TWb"])
    S.dma(lambda e: e.dma_start(out=stf(0)[:, 0:512], in_=FC), w=["st0"], group="st0")
    S.dve(lambda e: e.tensor_copy(FCb[:], stf(0)[:, 0:512]), r=["st0"], w=["FCb"])
    S.dma(lambda e: e.dma_start(out=stf(1)[:, 0:256], in_=W3), w=["st1"], group="st1")
    S.dve(lambda e: e.tensor_copy(W3b[:], stf(1)[:, 0:256]), r=["st1"], w=["W3b"])

    S.barrier()

    def select(s, t, dst):
        stk = [f"st{s}.{r}.{pp}" for r in range(4) for pp in range(4)]
        S.dve(lambda e: e.tensor_scalar(tmp[t][:], st[s][:, :, 0:128], mk[:, 0:1], None, ALU.mult),
              r=stk + ["mk"], w=[f"tmp{t}"])
        for gg in range(1, 3):
            S.dve(lambda e, gg=gg: e.scalar_tensor_tensor(tmp[t][:], st[s][:, :, gg * 128:(gg + 1) * 128],
                                                         mk[:, gg:gg + 1], tmp[t][:], ALU.mult, ALU.add),
                  r=stk + ["mk", f"tmp{t}"], w=[f"tmp{t}"])
        S.dve(lambda e: e.scalar_tensor_tensor(dst, st[s][:, :, 384:512], mk[:, 3:4], tmp[t][:], ALU.mult, ALU.add),
              r=stk + ["mk", f"tmp{t}"], w=["fb"])

    for j in range(16):
        s = j % 2
        for r in range(4):
            for pp in range(4):
                src = fall[pp][r * 512:(r + 1) * 512, :].rearrange("(a n) c -> a n c", n=64)[:, 4 * j:4 * j + 4, :]
                p0 = 32 * r + 8 * pp
                S.dma(lambda e, s=s, p0=p0, src=src: e.dma_start(out=st[s][p0:p0 + 8, :, :], in_=src),
                      w=[f"st{s}.{r}.{pp}"], group=f"st{s}")
        select(s, s, fb[:, 4 * j:4 * j + 4, :])
    pi = 0
    for n2 in range(0, 64, 2):
        p = pi % 4; pi += 1
        for d in range(2):
            S.pe(lambda e, p=p, n2=n2, d=d: e.matmul(ps[p][:, d * 256:(d + 1) * 256], fb[:, n2 + d, :], TWb[:, n2 + d, :],
                                                     start=True, stop=True), r=["fb", "TWb"], w=[f"ps{p}"])
        for d in range(2):
            src = lambda p=p, d=d: ps[p][:, d * 256:(d + 1) * 256].rearrange("p (r j a) -> p r j a", r=2, a=2)
            dst = lambda n2=n2, d=d: Y[:, :, :, 2 * (n2 + d):2 * (n2 + d) + 2]
            if d == 0:
                S.act(lambda e, src=src, dst=dst: e.copy(dst(), src()), r=[], w=["Y", f"ps{p}"])
            else:
                S.dve(lambda e, src=src, dst=dst: e.tensor_copy(dst(), src()), r=[], w=["Y", f"ps{p}"])
    for j in range(0, 64, 2):
        p = pi % 4; pi += 1
        for d in range(2):
            jj = j + d
            S.pe(lambda e, p=p, jj=jj, d=d: e.matmul(ps[p][:, d * 256:(d + 1) * 256], Y[:, 0, jj, :],
                                                     FCb[:, 0:256], start=True, stop=False), r=["Y", "FCb"], w=[f"ps{p}"])
            S.pe(lambda e, p=p, jj=jj, d=d: e.matmul(ps[p][:, d * 256:(d + 1) * 256], Y[:, 1, jj, :],
                                                     FCb[:, 256:512], start=False, stop=True), r=["Y", "FCb"], w=[f"ps{p}"])
        if (j // 2) % 2 == 0:
            S.act(lambda e, p=p, j=j: e.copy(U[:, j:j + 2, :].rearrange("p a b -> p (a b)"), ps[p][:]), r=[f"ps{p}"], w=["U"])
        else:
            S.dve(lambda e, p=p, j=j: e.tensor_copy(U[:, j:j + 2, :].rearrange("p a b -> p (a b)"), ps[p][:]), r=[f"ps{p}"], w=["U"])
    scale = 1.0 / 1024.0
    for j0 in range(0, 64, 4):
        p = pi % 4; pi += 1
        for d in range(4):
            jj = j0 + d
            S.pe(lambda e, p=p, jj=jj, d=d: e.matmul(ps[p][:, d * 128:(d + 1) * 128], W3b[:, 0:128], U[:, jj, 0:128],
                                                     start=True, stop=False), r=["U", "W3b"], w=[f"ps{p}"])
            S.pe(lambda e, p=p, jj=jj, d=d: e.matmul(ps[p][:, d * 128:(d + 1) * 128], W3b[:, 128:256], U[:, jj, 128:256],
                                                     start=False, stop=True), r=["U", "W3b"], w=[f"ps{p}"])
        S.dve(lambda e, p=p, j0=j0: e.tensor_scalar(frt[:, j0:j0 + 4, :].rearrange("p a b -> p (a b)"), ps[p][:],
                                                    scale, None, ALU.mult), r=[f"ps{p}"], w=["frt"])
    for q in range(4):
        fv = fout[q].rearrange("(k2 jj a) c -> a k2 jj c", jj=64, a=2)
        for a in range(2):
            S.dma(lambda e, a=a, q=q, fv=fv: e.dma_start(out=fv[a], in_=frt[a * 64 + 16 * q:a * 64 + 16 * (q + 1), :, :]),
                  r=["frt"], w=[f"fo{a}{q}"], group=f"fo{a}")
    if with_ctx:
        S.barrier()
        fcb = g.sb([128, 2, 128], BF16)
        TWcb = g.sb([128, 2, 512], BF16)
        Yc = g.sb([128, 512], BF16)
        oc = g.sb([128, 2, 128], F32)
        for t in range(2):
            S.dma(lambda e, t=t: e.dma_start(out=st[t][:, 0, :], in_=fall[4][t * 128:(t + 1) * 128, :]),
                  w=[f"st{t}"], group=f"st{t}")
            S.dve(lambda e, t=t: e.tensor_scalar(tmp[t][:, 0, :], st[t][:, 0, 0:128], mk[:, 0:1], None, ALU.mult),
                  r=[f"st{t}", "mk"], w=[f"tmp{t}"])
            for gg in range(1, 4):
                S.dve(lambda e, t=t, gg=gg: e.scalar_tensor_tensor(
                    tmp[t][:, 0, :], st[t][:, 0, gg * 128:(gg + 1) * 128], mk[:, gg:gg + 1], tmp[t][:, 0, :], ALU.mult, ALU.add),
                    r=[f"st{t}", "mk", f"tmp{t}"], w=[f"tmp{t}"])
            S.dve(lambda e, t=t: e.tensor_copy(fcb[:, t, :], tmp[t][:, 0, :]), r=[f"tmp{t}"], w=["fcb"])
        for t in range(2):
            S.dma(lambda e, t=t: e.dma_start(out=stf(t)[:, 0:512], in_=TWc[:, t, :]), w=[f"st{t}"], group=f"st{t}")
            S.dve(lambda e, t=t: e.tensor_copy(TWcb[:, t, :], stf(t)[:, 0:512]), r=[f"st{t}"], w=["TWcb"])
        p = pi % 4; pi += 1
        for t in range(2):
            S.pe(lambda e, p=p, t=t: e.matmul(ps[p][:], fcb[:, t, :], TWcb[:, t, :], start=(t == 0), stop=(t == 1)),
                 r=["fcb", "TWcb"], w=[f"ps{p}"])
        S.dve(lambda e, p=p: e.tensor_copy(Yc[:], ps[p][:]), r=[f"ps{p}"], w=["Yc"])
        p = pi % 4; pi += 1
        for kt in range(2):
            S.pe(lambda e, p=p, kt=kt: e.matmul(ps[p][:, kt * 128:(kt + 1) * 128], Yc[:, kt * 128:(kt + 1) * 128],
                                                FCb[:, 0:128], start=True, stop=False), r=["Yc", "FCb"], w=[f"ps{p}"])
            S.pe(lambda e, p=p, kt=kt: e.matmul(ps[p][:, kt * 128:(kt + 1) * 128], Yc[:, 256 + kt * 128:256 + (kt + 1) * 128],
                                                FCb[:, 256:384], start=False, stop=True), r=["Yc", "FCb"], w=[f"ps{p}"])
        S.dve(lambda e, p=p: e.tensor_scalar(oc[:].rearrange("p a b -> p (a b)"), ps[p][:, 0:256],
                                             1.0 / np.sqrt(256.0 * 128.0), None, ALU.mult), r=[f"ps{p}"], w=["oc"])
        S.dma(lambda e: e.dma_start(out=fout[4].rearrange("(t p) c -> p t c", p=128), in_=oc[:]),
              r=["oc"], w=["oco"], group="oco")


Q, L, SEQ, D = 2048, 256, 8192, 1024
T = Q + L
NCORES = 8


def build_fused(depth=4, stop_after=None):
    nc = bass.Bass(target_bir_lowering=False)
    g = G(nc)
    if stop_after is not None:
        g.max_phase = stop_after
    ext = lambda name, shape: nc.dram_tensor(name, list(shape), F32, kind="ExternalInput")
    xin = ext("xin", [T, D]); cin = ext("cin", [128, D])
    ada_w = ext("ada_w", [D, 3 * D]); ada_b = ext("ada_b", [3 * D])
    plg = ext("post_ln_g", [4, D]); plb = ext("post_ln_b", [4, D])
    ewi = ext("even_w_in", [2, D, 2496]); ewo = ext("even_w_out", [2, D, D])
    owi = ext("odd_w_in", [2, D, 2560]); owo = ext("odd_w_out", [2, D, D])
    gw2 = ext("gla_w2", [2, 2, 16, 256]); gb = ext("gla_b", [2, 2, 256]); gng = ext("gla_norm_g", [2, 128])
    qng = ext("mla_q_norm_g", [2, 256]); wuq = ext("mla_w_uq", [2, 256, 768])
    kng = ext("mla_kv_norm_g", [2, 128]); wukv = ext("mla_w_ukv", [2, 128, 1024])
    swT = ext("sgu_wT", [2, 4, 128, 128]); sbT = ext("sgu_bT", [2, 128, 4])
    csq = ext("csq", [T, 32]); csk = ext("csk", [SEQ + L, 32])
    ident_d = ext("ident", [128, 128]); g.ident_d = ident_d.ap()
    gcst = ext("gcst", [2, 64, 320]); mask = ext("mask", [128, 16])
    TW = ext("TW", [128, 64, 256]); FC = ext("FC", [128, 512]); W3 = ext("W3", [128, 256]); TWc = ext("TWc", [128, 2, 512])
    xout = nc.dram_tensor("xout", [Q, D], F32, kind="ExternalOutput")
    ML = g.dram("ML", [128, 3 * D]); MS = g.dram("MS", [2, 3 * D]); MALL = g.dram("MALL", [8, 3 * D])
    X = [g.dram(f"X{i}", [T, D]) for i in range(2)]
    Z = g.dram("Z", [T, 2560])
    QP = g.dram("QP", [T, 768])
    KSIN = [g.dram(f"KSIN{p}", [Q // 2, 160]) for p in range(2)]
    KSG = [g.dram(f"KSG{p}", [4 * Q // 2, 160]) for p in range(2)]
    KSA = g.dram("KSA", [SEQ + L, 160])
    KVA = g.dram("KVA", [SEQ + L, 1024])
    OML = g.dram("OML", [T, 512]); OG = g.dram("OG", [T, 512])
    SEGL = g.dram("SEGL", [64, 8 * 129]); SEGA = g.dram("SEGA", [256, 8 * 129])
    FPR = [512, 512, 512, 512, 256]
    FIN = [g.dram(f"FIN{p}", [FPR[p], 512]) for p in range(5)]
    FALL = [g.dram(f"FALL{p}", [4 * FPR[p], 512]) for p in range(5)]
    OPR = [2048, 2048, 2048, 2048, 256]
    FOUT = [g.dram(f"FOUT{p}", [OPR[p], 128]) for p in range(5)]
    FOALL = [g.dram(f"FOALL{p}", [4 * OPR[p], 128]) for p in range(5)]
    S = g.S
    rows = lambda i: slice(i * 128, (i + 1) * 128)
    HN = 3 * D // 2
    for hh in range(2):
        cs = slice(hh * HN, (hh + 1) * HN)
        emit_k1(g, lambda i: cin.ap()[rows(i), :], 1, D, HN, ada_w.ap()[:, cs],
                lambda i, cs=cs: ML.ap()[rows(i), cs], "silu", bias=ada_b.ap()[cs])
    g.phase()
    S.dma(lambda e: e.dma_start(out=MS.ap(), in_=ML.ap()[0:2, :]), w=["ms"], group="dc0")
    allgather(g, MALL, MS, "mall", r=["ms"])
    xcur = xin
    for l in range(depth):
        li = l // 2
        even = l % 2 == 0
        last = l == depth - 1
        Ml = MALL.ap()
        r0, r1 = 2 * l, 2 * l + 1
        mods = (Ml[r0, 0:D], Ml[r0, D:2 * D], Ml[r1, 0:D], Ml[r1, D:2 * D])
        vecs = (Ml[r0, 2 * D:3 * D], Ml[r1, 2 * D:3 * D], plg.ap()[l], plb.ap()[l])
        N = 2496 if even else 2560
        Zl = Z.ap()[:, 0:N]
        xap = xcur.ap()
        emit_k1(g, lambda i, xap=xap: xap[rows(i), :], T // 128, D, N, (ewi if even else owi).ap()[li],
                lambda i, Zl=Zl: Zl[rows(i), :], "ln", n_lat_tiles=Q // 128, mods=mods)
        Tk = Q if last else T
        xn = xout if last else X[l % 2]
        if even:
            emit_k1(g, lambda i, Zl=Zl: Zl[rows(i), 1568:1824], T // 128, 256, 768, wuq.ap()[li],
                    lambda i: QP.ap()[rows(i), :], "rms", gvec=qng.ap()[li])
            g.phase()
            S.dma(lambda e, Zl=Zl: e.dma_start(out=KSA.ap()[0:L, :], in_=Zl[Q:T, 1824:1984]), w=["ksa0"], group="dc1")
            HQ = Q // 2
            for p in range(2):
                S.dma(lambda e, Zl=Zl, p=p: e.dma_start(out=KSIN[p].ap(), in_=Zl[HQ * p:HQ * (p + 1), 1824:1984]),
                      w=[f"ksin{p}"], group=f"dc0{p}")
                allgather(g, KSG[p], KSIN[p], f"ksg{p}", r=[f"ksin{p}"])
                dst = KSA.ap()[L:L + SEQ, :].rearrange("(r h n) c -> h r n c", r=4, h=2)[p]
                S.dma(lambda e, p=p, dst=dst: e.dma_start(out=dst, in_=KSG[p].ap().rearrange("(r n) c -> r n c", r=4)),
                      r=[f"ksg{p}"], w=[f"ksa1{p}"], group=f"dc2{p}")
            emit_k1(g, lambda i: KSA.ap()[rows(i), 0:128], (SEQ + L) // 128, 128, 1024, wukv.ap()[li],
                    lambda i: KVA.ap()[rows(i), :], "rms", gvec=kng.ap()[li])
            emit_k3(g, Q, L, SEQ + L, QP.ap(), KVA.ap(), KSA.ap()[:, 128:160], csq.ap(), csk.ap(), OML.ap())
            emit_k4s(g, Zl, OG.ap(), SEGL, SEGA, gw2.ap()[li], gb.ap()[li], gcst.ap(), mask.ap())
            emit_k5(g, Tk, "even", Q // 128, xap, ewo.ap()[li], vecs, Zl, xn.ap(), og=OG.ap(), oml=OML.ap(),
                    gng=gng.ap()[li])
        else:
            g.phase()
            r0 = 0
            for p in range(5):
                S.dma(lambda e, Zl=Zl, p=p, r0=r0: e.dma_start(out=FIN[p].ap(), in_=Zl[r0:r0 + FPR[p], 0:512]),
                      w=[f"fin{p}"], group=f"dcf{p}")
                allgather(g, FALL[p], FIN[p], f"fall{p}", r=[f"fin{p}"])
                r0 += FPR[p]
            emit_k7(g, [a.ap() for a in FALL], [a.ap() for a in FOUT], TW.ap(), FC.ap(), W3.ap(), TWc.ap(), mask.ap())
            g.phase()
            for p in range(5):
                allgather(g, FOALL[p], FOUT[p], f"foall{p}")
            fo = [a.ap().rearrange("(g r) c -> r g c", g=4) for a in FOALL]
            emit_k5(g, Tk, "odd", Q // 128, xap, owo.ap()[li], vecs, Zl, xn.ap(),
                    fr_lat=lambda qq, i, fo=fo: fo[qq][128 * i:128 * (i + 1), :, :],
                    fr_ctx=lambda i, fo=fo: fo[4][128 * (i - Q // 128):128 * (i - Q // 128 + 1), :, :],
                    swT=swT.ap()[li], sbT=sbT.ap()[li], mask=mask.ap())
        xcur = xn
    g.finish()
    return nc


def make_inputs(x, c, ctx, c_ctx, ada_w, ada_b, post_ln_g, post_ln_b, even_w_in, gla_w2, gla_b, gla_norm_g,
                mla_q_norm_g, mla_w_uq, mla_kv_norm_g, mla_w_ukv, even_w_out, odd_w_in, sgu_w, sgu_b, odd_w_out,
                rope_tables, fnet_consts):
    f32 = np.float32
    cc = lambda a: np.ascontiguousarray(a, dtype=f32)
    shared = dict(post_ln_g=cc(post_ln_g), post_ln_b=cc(post_ln_b),
                  even_w_in=cc(even_w_in), even_w_out=cc(even_w_out), odd_w_in=cc(odd_w_in), odd_w_out=cc(odd_w_out),
                  gla_w2=cc(gla_w2), gla_b=cc(gla_b), gla_norm_g=cc(gla_norm_g), mla_q_norm_g=cc(mla_q_norm_g),
                  mla_w_uq=cc(mla_w_uq), mla_kv_norm_g=cc(mla_kv_norm_g), mla_w_ukv=cc(mla_w_ukv),
                  sgu_wT=cc(np.transpose(sgu_w, (0, 1, 3, 2))), sgu_bT=cc(np.transpose(sgu_b, (0, 2, 1))),
                  csk=rope_tables(np.concatenate([-np.ones(L, int), np.arange(SEQ)])),
                  ident=np.eye(128, dtype=f32), gcst=gla_consts2(), **fnet_consts())
    maps = []
    for j in range(NCORES):
        b, i = j // 4, j % 4
        m = dict(shared)
        m["xin"] = cc(np.concatenate([x[b, Q * i:Q * (i + 1)], ctx[b]], 0))
        cin = np.zeros((128, D), f32)
        cin[0] = c[b]; cin[1] = c_ctx
        m["cin"] = cin
        m["ada_w"] = cc(ada_w[i])
        m["ada_b"] = cc(ada_b[i])
        m["csq"] = rope_tables(np.concatenate([np.arange(Q) + Q * i, -np.ones(L, int)]))
        mk = np.zeros((128, 16), f32)
        mk[:, i] = 1.0
        for jj in range(4):
            mk[:, 4 + jj] = 1.0 if jj < i else 0.0
            mk[:, 8 + jj] = 1.0 if jj > i else 0.0
        m["mask"] = mk
        maps.append(m)
    return maps

def rope_tables(pos):
    pos = np.asarray(pos)
    row = (pos // 64).astype(np.float32)
    col = (pos % 64).astype(np.float32)
    inv = (10000.0 ** (-np.arange(8, dtype=np.float32) / 8)).astype(np.float32)
    ang = np.concatenate([row[:, None] * inv, col[:, None] * inv], -1).astype(np.float32)
    c = np.cos(ang).astype(np.float32)
    s = np.sin(ang).astype(np.float32)
    ident = pos < 0
    c[ident] = 1.0
    s[ident] = 0.0
    return np.concatenate([c, s], -1).astype(np.float32)


def fnet_consts():
    n1 = np.arange(128)[:, None, None].astype(np.float64)
    n2 = np.arange(64)[None, :, None].astype(np.float64)
    k1 = np.arange(128)[None, None, :].astype(np.float64)
    ang = 2 * np.pi * k1 * (64 * n1 + n2) / 8192.0
    TW = np.concatenate([np.cos(ang), -np.sin(ang)], -1).astype(np.float32)
    c = np.arange(128)[:, None].astype(np.float64)
    cp = np.arange(128)[None, :].astype(np.float64)
    a = 2 * np.pi * c * cp / 128.0
    Cc, Sc = np.cos(a), np.sin(a)
    FC = np.concatenate([Cc, -Sc, Sc, Cc], -1).astype(np.float32)
    W3 = np.zeros((64, 2, 2, 2, 64), np.float64)
    n2v = np.arange(64)[:, None]
    k2v = np.arange(64)[None, :]
    a3 = 2 * np.pi * n2v * k2v / 64.0
    for aa in range(2):
        W3[:, aa, 0, aa, :] = np.cos(a3)
        W3[:, aa, 1, aa, :] = np.sin(a3)
    W3 = W3.reshape(128, 256).astype(np.float32)
    n = np.arange(256)[:, None].astype(np.float64)
    k = np.arange(256)[None, :].astype(np.float64)
    ac = 2 * np.pi * n * k / 256.0
    TWc = np.concatenate([np.cos(ac), -np.sin(ac)], -1).reshape(2, 128, 512).transpose(1, 0, 2)
    TWc = np.ascontiguousarray(TWc).astype(np.float32)
    return dict(TW=TW, FC=FC, W3=W3, TWc=TWc)


_NC = {}


def kernel(x, c, ctx, c_ctx, ada_w, ada_b, post_ln_g, post_ln_b, even_w_in, gla_w2, gla_b, gla_norm_g,
           mla_q_norm_g, mla_w_uq, mla_kv_norm_g, mla_w_ukv, even_w_out, odd_w_in, sgu_w, sgu_b, odd_w_out):
    if "nc" not in _NC:
        _NC["nc"] = build_fused(4)
    maps = make_inputs(np.asarray(x), np.asarray(c), np.asarray(ctx), np.asarray(c_ctx), np.asarray(ada_w),
                       np.asarray(ada_b), np.asarray(post_ln_g), np.asarray(post_ln_b), np.asarray(even_w_in),
                       np.asarray(gla_w2), np.asarray(gla_b), np.asarray(gla_norm_g), np.asarray(mla_q_norm_g),
                       np.asarray(mla_w_uq), np.asarray(mla_kv_norm_g), np.asarray(mla_w_ukv), np.asarray(even_w_out),
                       np.asarray(odd_w_in), np.asarray(sgu_w), np.asarray(sgu_b), np.asarray(odd_w_out),
                       rope_tables, fnet_consts)
    res = run_bass_kernel_spmd(_NC["nc"], maps, core_ids=list(range(NCORES)))
    out = np.empty((2, SEQ, D), np.float32)
    for j in range(NCORES):
        out[j // 4, (j % 4) * Q:(j % 4 + 1) * Q] = res.results[j]["xout"]
    return out
```

```python
import contextlib
import numpy as np
import concourse.bass as bass
import concourse.mybir as mybir
from concourse.bass_utils import run_bass_kernel_spmd

F32 = mybir.dt.float32
BF16 = mybir.dt.bfloat16
AF = mybir.ActivationFunctionType
ALU = mybir.AluOpType
AX = mybir.AxisListType
ALPHA = 8 ** 0.25
MLA_SCALE = 96 ** -0.5
ARENA_F32 = 52900


class Sched:
    def __init__(self, nc):
        self.nc = nc
        self.ops = []
        self.last_w = {}
        self.readers = {}
        self.stack = contextlib.ExitStack()
        self.bar = set()
        self.pending = {}

    def barrier(self):
        last = {}
        for i, op in enumerate(self.ops):
            k = ("dma", op["dma"]) if op["dma"] is not None else ("eng", op["eng"])
            last[k] = i
        self.bar = set(last.values())
        self.pending = {e: True for e in ["pe", "act", "dve", "pool", "sp"]}
        self.last_w = {}
        self.readers = {}

    muted = False

    def add(self, eng, fn, r=(), w=(), dma=None, inc=16):
        if self.muted:
            return -1
        idx = len(self.ops)
        deps = set()
        for k in r:
            if k in self.last_w:
                deps.add(self.last_w[k])
        for k in w:
            if k in self.last_w:
                deps.add(self.last_w[k])
            for x in self.readers.get(k, ()):
                deps.add(x)
        if self.pending.get(eng):
            deps |= self.bar
            self.pending[eng] = False
        deps.discard(idx)
        self.ops.append(dict(eng=eng, fn=fn, deps=deps, dma=dma, inc=inc))
        for k in r:
            self.readers.setdefault(k, []).append(idx)
        for k in w:
            self.last_w[k] = idx
            self.readers[k] = []
        return idx

    def pe(self, fn, r=(), w=()):
        return self.add("pe", fn, r, w)

    def act(self, fn, r=(), w=()):
        return self.add("act", fn, r, w)

    def dve(self, fn, r=(), w=()):
        return self.add("dve", fn, r, w)

    def pool(self, fn, r=(), w=()):
        return self.add("pool", fn, r, w)

    def dma(self, fn, r=(), w=(), group=None, eng="sp", inc=16):
        assert group is not None
        return self.add(eng, fn, r, w, dma=group, inc=inc)

    def emit(self):
        nc = self.nc
        ops = self.ops
        n = len(ops)
        needs_signal = [False] * n
        for i, op in enumerate(ops):
            keep = set()
            for d in op["deps"]:
                dop = ops[d]
                if dop["dma"] is None and dop["eng"] == op["eng"] and op["eng"] == "pe":
                    continue
                keep.add(d)
                needs_signal[d] = True
            op["deps"] = keep
        engs = ["pe", "act", "dve", "pool", "sp"]
        sems = {e: self.stack.enter_context(nc.semaphore(f"s_{e}")) for e in engs}
        cnt = {e: 0 for e in engs}
        groups = {}
        gcnt = {}
        for i, op in enumerate(ops):
            if op["dma"] is not None:
                g = op["dma"]
                if g not in groups:
                    groups[g] = self.stack.enter_context(nc.semaphore(f"d_{len(groups)}"))
                    gcnt[g] = 0
                op["sem"] = groups[g]
                gcnt[g] += op["inc"] * getattr(op["fn"], "ndma", 1)
                op["val"] = gcnt[g]
            elif needs_signal[i]:
                cnt[op["eng"]] += 1
                op["sem"] = sems[op["eng"]]
                op["val"] = cnt[op["eng"]]
        print("sched: ops", n, "dma groups", len(groups), "sem counts", cnt, flush=True)
        final = dict((g, (groups[g], gcnt[g])) for g in groups)

        def stream(ename):
            def body(eng):
                known = {}
                for i, op in enumerate(ops):
                    if op["eng"] != ename:
                        continue
                    for d in sorted(op["deps"]):
                        dop = ops[d]
                        s, v = dop["sem"], dop["val"]
                        if known.get(id(s), 0) < v:
                            eng.wait_ge(s, v)
                            known[id(s)] = v
                    ins = op["fn"](eng)
                    if op["dma"] is not None:
                        if not isinstance(ins, (list, tuple)):
                            ins = [ins]
                        assert len(ins) == getattr(op["fn"], "ndma", 1)
                        for x in ins:
                            x.then_inc(op["sem"], op["inc"])
                    elif needs_signal[i]:
                        ins.then_inc(op["sem"], 1)
                if ename == "sp":
                    for g, (s, v) in final.items():
                        if known.get(id(s), 0) < v:
                            eng.wait_ge(s, v)
            return body

        with nc.Block() as block:
            block.tensor(stream("pe"))
            block.scalar(stream("act"))
            block.vector(stream("dve"))
            block.gpsimd(stream("pool"))
            block.sync(stream("sp"))

    def close(self):
        self.stack.close()


class G:
    def __init__(self, nc):
        self.nc = nc
        self.S = Sched(nc)
        self.arena = self.S.stack.enter_context(nc.sbuf_tensor("arena", [128, ARENA_F32], F32))
        self.banks = [self.S.stack.enter_context(nc.psum_tensor(f"bank{i}", [128, 512], F32)) for i in range(8)]
        self.off = 0
        self.nd = 0

    nphase = 0
    max_phase = 10 ** 9

    def phase(self):
        self.nphase += 1
        if self.nphase > self.max_phase:
            self.S.muted = True
        self.S.barrier()
        self.off = 0

    def sb(self, shape, dtype=F32, name=None):
        P = shape[0]
        n = int(np.prod(shape[1:]))
        words = n if dtype == F32 else (n + 1) // 2
        o = self.off
        self.off += words
        assert self.off <= ARENA_F32, ("SBUF arena overflow", self.off)
        ap = self.arena[0:P, o:o + words]
        if dtype != F32:
            ap = ap.bitcast(dtype)[:, 0:n]
        if len(shape) == 3:
            ap = ap.rearrange("p (a b) -> p a b", a=shape[1], b=shape[2])
        elif len(shape) == 4:
            ap = ap.rearrange("p (a b c) -> p a b c", a=shape[1], b=shape[2], c=shape[3])
        return ap

    def ps(self, bank, shape, dtype=F32):
        P = shape[0]
        n = int(np.prod(shape[1:]))
        ap = self.banks[bank][0:P, :]
        if dtype != F32:
            ap = ap.bitcast(dtype)
        ap = ap[:, 0:n]
        if len(shape) == 3:
            ap = ap.rearrange("p (a b) -> p a b", a=shape[1], b=shape[2])
        return ap

    def dram(self, name, shape, kind="Internal"):
        return self.nc.dram_tensor(name, list(shape), F32, kind=kind)

    def finish(self):
        self.S.emit()
        self.S.close()


def load_ident(g, tag="id"):
    S = g.S
    identf = g.sb([128, 128], F32)
    ident = g.sb([128, 128], BF16)
    S.dma(lambda e: e.dma_start(out=identf[:], in_=g.ident_d), w=["identf"], group="identf")
    S.dve(lambda e: e.tensor_copy(ident[:], identf[:]), r=["identf"], w=["ident"])
    return identf, ident


def dcopy(g, dst, src, name, grp="dcopy"):
    g.S.dma(lambda e: e.dma_start(out=dst, in_=src), w=[name], group=grp)


def allgather(g, dst, src, name, r=()):
    groups = [[0, 1, 2, 3], [4, 5, 6, 7]]
    g.S.dma(lambda e: e.collective_compute("AllGather", ALU.bypass, replica_groups=groups,
                                            ins=[src.ap().opt()], outs=[dst.ap().opt()]),
            r=list(r), w=[name], group="cc", eng="pool", inc=1)


def emit_k1(g, xsrc, nt, K, N, w, z, mode, n_lat_tiles=None, mods=None, gvec=None, bias=None, eps=1e-6):
    S = g.S
    g.phase()
    kc = K // 128
    nch = (N + 511) // 512
    identf, ident = load_ident(g)
    wbf = g.sb([128, kc, N], BF16)
    wst = [g.sb([128, N], F32) for i in range(2)]
    if mode == "ln":
        modt = g.sb([128, 4, K], F32)
        for a in range(4):
            S.dma(lambda e, a=a: e.dma_start(out=modt[:, a, :], in_=mods[a].partition_broadcast(128)),
                  w=["modt"], group="modt")
        for a in (1, 3):
            S.dve(lambda e, a=a: e.tensor_scalar_add(modt[:, a, :], modt[:, a, :], 1.0), r=["modt"], w=["modt"])
    elif mode == "rms":
        gt = g.sb([128, K], F32)
        S.dma(lambda e: e.dma_start(out=gt[:], in_=gvec.partition_broadcast(128)), w=["gt"], group="gt")
    else:
        bt = g.sb([128, N], F32)
        S.dma(lambda e: e.dma_start(out=bt[:], in_=bias.partition_broadcast(128)), w=["bt"], group="gt")
    for k in range(kc):
        s = k % 2
        S.dma(lambda e, k=k, s=s: e.dma_start(out=wst[s][:], in_=w[k * 128:(k + 1) * 128, :]),
              w=[f"wst{s}"], group=f"wst{s}")
        if k % 2 == 0:
            S.act(lambda e, k=k, s=s: e.copy(wbf[:, k, :], wst[s][:]), r=[f"wst{s}"], w=[f"wbf{k}"])
        else:
            S.dve(lambda e, k=k, s=s: e.tensor_copy(wbf[:, k, :], wst[s][:]), r=[f"wst{s}"], w=[f"wbf{k}"])
    NX = 2
    xt = [g.sb([128, K], F32) for i in range(NX)]
    xn = [g.sb([128, K], F32) for i in range(NX)]
    hb = [g.sb([128, K], BF16) for i in range(NX)]
    hT = [g.sb([128, kc, 128], BF16) for i in range(NX)]
    st = [g.sb([128, 8, 6], F32) for i in range(NX)]
    mv = [g.sb([128, 4], F32) for i in range(NX)]
    zt = [g.sb([128, N], F32) for i in range(NX)]
    tp = [g.ps(i, [128, kc, 128], BF16) for i in range(2)]
    zp = [g.ps(2 + i, [128, 512], F32) for i in range(4)]
    wkeys = [f"wbf{k}" for k in range(kc)]
    def emit_load(i):
        s = i % NX
        S.dma(lambda e, i=i, s=s: e.dma_start(out=xt[s][:], in_=xsrc(i)), w=[f"xt{s}"], group=f"xt{s}")

    zst = dict(zpi=0)

    def stageP(i):
        s = i % NX
        if i + 1 < nt:
            emit_load(i + 1)
        if mode == "ln":
            nsub = max(1, K // 512)
            fs = K // nsub
            for j in range(nsub):
                S.dve(lambda e, s=s, j=j, fs=fs: e.bn_stats(st[s][:, j, :], xt[s][:, j * fs:(j + 1) * fs]),
                      r=[f"xt{s}"], w=[f"st{s}"])
            S.dve(lambda e, s=s, nsub=nsub: e.bn_aggr(mv[s][:, 0:2], st[s][:, 0:nsub, :]), r=[f"st{s}"], w=[f"mv{s}"])
            S.dve(lambda e, s=s: e.tensor_scalar_add(mv[s][:, 3:4], mv[s][:, 1:2], eps), r=[f"mv{s}"], w=[f"mv{s}"])
            S.act(lambda e, s=s: e.sqrt(mv[s][:, 3:4], mv[s][:, 3:4]), r=[f"mv{s}"], w=[f"mv{s}"])
            S.dve(lambda e, s=s: e.reciprocal(mv[s][:, 2:3], mv[s][:, 3:4]), r=[f"mv{s}"], w=[f"mv{s}"])
            S.dve(lambda e, s=s: e.tensor_scalar(xn[s][:], xt[s][:], mv[s][:, 0:1], mv[s][:, 2:3],
                                                 ALU.subtract, ALU.mult),
                  r=[f"xt{s}", f"mv{s}"], w=[f"xn{s}"])
            a = 0 if (n_lat_tiles is None or i < n_lat_tiles) else 2
            S.pool(lambda e, s=s, a=a: e.tensor_tensor(xn[s][:], xn[s][:], modt[:, a + 1, :], ALU.mult),
                   r=[f"xn{s}", "modt"], w=[f"xn{s}"])
            S.pool(lambda e, s=s, a=a: e.tensor_tensor(hb[s][:], xn[s][:], modt[:, a, :], ALU.add),
                   r=[f"xn{s}", "modt"], w=[f"hb{s}"])
        elif mode == "silu":
            S.act(lambda e, s=s: e.activation(hb[s][:], xt[s][:], AF.Silu), r=[f"xt{s}"], w=[f"hb{s}"])
        else:
            S.act(lambda e, s=s: e.activation(xn[s][:], xt[s][:], AF.Square, accum_out=mv[s][:, 0:1]),
                  r=[f"xt{s}"], w=[f"xn{s}", f"mv{s}"])
            S.dve(lambda e, s=s: e.tensor_scalar(mv[s][:, 1:2], mv[s][:, 0:1], 1.0 / K, eps, ALU.mult, ALU.add),
                  r=[f"mv{s}"], w=[f"mv{s}"])
            S.act(lambda e, s=s: e.sqrt(mv[s][:, 3:4], mv[s][:, 1:2]), r=[f"mv{s}"], w=[f"mv{s}"])
            S.dve(lambda e, s=s: e.reciprocal(mv[s][:, 2:3], mv[s][:, 3:4]), r=[f"mv{s}"], w=[f"mv{s}"])
            S.dve(lambda e, s=s: e.scalar_tensor_tensor(hb[s][:], xt[s][:], mv[s][:, 2:3], gt[:],
                                                        ALU.mult, ALU.mult),
                  r=[f"xt{s}", f"mv{s}", "gt"], w=[f"hb{s}"])
        t = i % 2
        for k in range(kc):
            S.pe(lambda e, s=s, t=t, k=k: e.transpose(tp[t][:, k, :], hb[s][:, k * 128:(k + 1) * 128], ident[:]),
                 r=[f"hb{s}", "ident"], w=[f"tp{t}"])
        S.act(lambda e, s=s, t=t: e.copy(hT[s][:], tp[t][:]), r=[f"tp{t}"], w=[f"hT{s}"])

    def stageM(i):
        s = i % NX
        for c in range(nch):
            c0, c1 = c * 512, min(N, (c + 1) * 512)
            p = zst["zpi"] % 4
            zst["zpi"] += 1
            for k in range(kc):
                S.pe(lambda e, s=s, p=p, k=k, c0=c0, c1=c1: e.matmul(
                    zp[p][:, 0:c1 - c0], hT[s][:, k, :], wbf[:, k, c0:c1], start=(k == 0), stop=(k == kc - 1)),
                    r=[f"hT{s}", wkeys[k]], w=[f"zp{p}"])
            if mode == "silu":
                S.dve(lambda e, s=s, p=p, c0=c0, c1=c1: e.tensor_tensor(zt[s][:, c0:c1], zp[p][:, 0:c1 - c0], bt[:, c0:c1], ALU.add),
                      r=[f"zp{p}", "bt"], w=[f"zt{s}"])
            elif c % 2 == 0:
                S.dve(lambda e, s=s, p=p, c0=c0, c1=c1: e.tensor_copy(zt[s][:, c0:c1], zp[p][:, 0:c1 - c0]),
                      r=[f"zp{p}"], w=[f"zt{s}"])
            else:
                S.act(lambda e, s=s, p=p, c0=c0, c1=c1: e.copy(zt[s][:, c0:c1], zp[p][:, 0:c1 - c0]),
                      r=[f"zp{p}"], w=[f"zt{s}"])
        S.dma(lambda e, i=i, s=s: e.dma_start(out=z(i), in_=zt[s][:]),
              r=[f"zt{s}"], w=[f"zout{s}"], group=f"zo{s}", eng="act")

    emit_load(0)
    stageP(0)
    for i in range(nt):
        if i + 1 < nt:
            stageP(i + 1)
        stageM(i)


def emit_k3(g, NQL, NQC, NK, q, ksa, kvg, wukv, csq, csk, out, nheads=8, eps=1e-6):
    kr = ksa[:, 128:160]
    S = g.S
    g.phase()
    NQ = NQL + NQC
    nqt = NQ // 128
    nkt = NK // 128
    identf, ident = load_ident(g)
    KH = (nkt + 1) // 2
    ckst = g.sb([128, KH, 128], F32)
    cknT = g.sb([128, nkt * 128], BF16)
    wst = g.sb([128, 1024], F32)
    wb = g.sb([128, 1024], BF16)
    gt = g.sb([128, 128], F32)
    ss = g.sb([128, nkt], F32)
    rstd = g.sb([128, nkt], F32)
    junk = g.sb([128, 128], F32)
    pk = g.ps(7, [128, 512], F32)
    kpad = g.sb([128, nkt, 128], BF16)
    vx = [g.sb([128, nkt, 65], BF16) for i in range(2)]
    kT = [g.sb([128, nkt * 128], BF16) for i in range(2)]
    qT = [g.sb([128, nqt * 128], BF16) for i in range(2)]
    qh = g.sb([128, nqt, 96], F32)
    qpad = g.sb([128, nqt, 128], BF16)
    krl = g.sb([128, nkt, 32], F32)
    cskt = g.sb([128, nkt, 32], F32)
    csqt = g.sb([128, nqt, 32], F32)
    tk = [g.sb([128, nkt, 16], F32) for i in range(2)]
    tq = [g.sb([128, nqt, 16], F32) for i in range(2)]
    pT = [g.sb([128, 512], BF16) for i in range(3)]
    oTs = [g.sb([65, 512], F32) for i in range(2)]
    ost = [g.sb([128, 64], F32) for i in range(4)]
    rc = [g.sb([128, 1], F32) for i in range(4)]
    sTp = [g.ps(i, [128, 512], F32) for i in range(3)]
    oTp = [g.ps(3 + i, [65, 512], F32) for i in range(2)]
    tpk = g.ps(5, [128, 8, 128], BF16)
    tpo = g.ps(6, [128, 65], F32)

    S.pool(lambda e: e.memset(kpad[:], 0.0), w=["kpad"])
    S.pool(lambda e: e.memset(qpad[:], 0.0), w=["qpad"])
    for i in range(2):
        S.pool(lambda e, i=i: e.memset(vx[i][:], 1.0), w=[f"vx{i}"])
    S.dma(lambda e: e.dma_start(out=krl[:], in_=kr.rearrange("(t p) c -> p t c", p=128)), w=["krl"], group="krl")
    S.dma(lambda e: e.dma_start(out=cskt[:], in_=csk.rearrange("(t p) c -> p t c", p=128)), w=["cskt"], group="cskt")
    S.dma(lambda e: e.dma_start(out=csqt[:], in_=csq.rearrange("(t p) c -> p t c", p=128)), w=["csqt"], group="csqt")

    def rope(eng_a, eng_b, src, cs, tmp, dst, keys_r, key_tmp, key_dst, xo):
        x1 = lambda: src[:, :, xo:xo + 16]
        x2 = lambda: src[:, :, xo + 16:xo + 32]
        c = lambda: cs[:, :, 0:16]
        sn = lambda: cs[:, :, 16:32]
        S.add(eng_a, lambda e: e.tensor_tensor(tmp[0][:], x1(), c(), ALU.mult), r=keys_r, w=[key_tmp + "0"])
        S.add(eng_b, lambda e: e.tensor_tensor(tmp[1][:], x2(), sn(), ALU.mult), r=keys_r, w=[key_tmp + "1"])
        S.add(eng_a, lambda e: e.tensor_tensor(dst[:, :, 64:80], tmp[0][:], tmp[1][:], ALU.subtract),
              r=[key_tmp + "0", key_tmp + "1"], w=[key_dst])
        S.add(eng_a, lambda e: e.tensor_tensor(tmp[0][:], x1(), sn(), ALU.mult), r=keys_r, w=[key_tmp + "0"])
        S.add(eng_b, lambda e: e.tensor_tensor(tmp[1][:], x2(), c(), ALU.mult), r=keys_r, w=[key_tmp + "1"])
        S.add(eng_a, lambda e: e.tensor_tensor(dst[:, :, 96:112], tmp[0][:], tmp[1][:], ALU.add),
              r=[key_tmp + "0", key_tmp + "1"], w=[key_dst])

    rope("dve", "pool", krl, cskt, tk, kpad, ["krl", "cskt"], "tk", "kpad", 0)

    for g0 in range(0, nkt, 8):
        g1 = min(nkt, g0 + 8)
        for t in range(g0, g1):
            S.pe(lambda e, t=t, g0=g0: e.transpose(tpk[:, t - g0, :], kpad[:, t, :], ident[:]), r=["kpad", "ident"], w=["tpk"])
        for i in range(2):
            S.dve(lambda e, g0=g0, g1=g1, i=i: e.tensor_copy(
                kT[i][64:128, g0 * 128:g1 * 128], tpk[64:128, 0:g1 - g0, :].rearrange("p a b -> p (a b)")),
                r=["tpk"], w=[f"kT{i}"])
    S.dma(lambda e: e.dma_start(out=gt[:], in_=kvg.partition_broadcast(128)), w=["gt"], group="gt")
    S.dma(lambda e: e.dma_start(out=wst[:], in_=wukv), w=["wst"], group="wst0")
    S.dve(lambda e: e.tensor_copy(wb[:], wst[:]), r=["wst"], w=["wb"])
    for half in range(2):
        t0 = half * KH
        t1 = min(nkt, t0 + KH)
        S.dma(lambda e, t0=t0, t1=t1: e.dma_start(
            out=ckst[:, 0:t1 - t0, :], in_=ksa[t0 * 128:t1 * 128, 0:128].rearrange("(t p) c -> p t c", p=128)),
            w=["ckst"], group="kvh")
        for t in range(t0, t1):
            S.act(lambda e, t=t, t0=t0: e.activation(junk[:], ckst[:, t - t0, :], AF.Square, accum_out=ss[:, t:t + 1]),
                  r=["ckst"], w=["junk", "ss"])
        S.dve(lambda e, t0=t0, t1=t1: e.tensor_scalar(ss[:, t0:t1], ss[:, t0:t1], 1.0 / 128, eps, ALU.mult, ALU.add),
              r=["ss"], w=["ss"])
        S.act(lambda e, t0=t0, t1=t1: e.sqrt(ss[:, t0:t1], ss[:, t0:t1]), r=["ss"], w=["ss"])
        S.dve(lambda e, t0=t0, t1=t1: e.reciprocal(rstd[:, t0:t1], ss[:, t0:t1]), r=["ss"], w=["rstd"])
        for t in range(t0, t1):
            S.dve(lambda e, t=t, t0=t0: e.scalar_tensor_tensor(kpad[:, t, :], ckst[:, t - t0, :], rstd[:, t:t + 1], gt[:],
                                                               ALU.mult, ALU.mult),
                  r=["ckst", "rstd", "gt"], w=["kpad"])
    for g0 in range(0, nkt, 8):
        g1 = min(nkt, g0 + 8)
        for t in range(g0, g1):
            S.pe(lambda e, t=t, g0=g0: e.transpose(tpk[:, t - g0, :], kpad[:, t, :], ident[:]), r=["kpad", "ident"], w=["tpk"])
        S.dve(lambda e, g0=g0, g1=g1: e.tensor_copy(cknT[:, g0 * 128:g1 * 128],
                                                    tpk[:, 0:g1 - g0, :].rearrange("p a b -> p (a b)")),
              r=["tpk"], w=["cknT"])

    chunks = []
    for c0 in range(0, NQL, 512):
        chunks.append((c0, min(512, NQL - c0), 0, nkt))
    if NQC:
        chunks.append((NQL, NQC, 0, 2))
    sti = 0
    oti = 0
    osti = 0
    def prologue(h):
        hb = h % 2
        for c0 in range(0, NK, 512):
            n = min(512, NK - c0)
            S.pe(lambda e, c0=c0, n=n: e.matmul(pk[0:64, 0:n], wb[:, h * 128:h * 128 + 64], cknT[:, c0:c0 + n],
                                                start=True, stop=True), r=["wb", "cknT"], w=["pk"])
            S.dve(lambda e, c0=c0, n=n: e.tensor_copy(kT[hb][0:64, c0:c0 + n], pk[0:64, 0:n]), r=["pk"], w=[f"kT{hb}"])
        for g0 in range(0, nkt, 8):
            g1 = min(nkt, g0 + 8)
            for t in range(g0, g1):
                S.pe(lambda e, t=t, g0=g0: e.matmul(pk[:, (t - g0) * 64:(t - g0 + 1) * 64], cknT[:, t * 128:(t + 1) * 128],
                                                    wb[:, h * 128 + 64:h * 128 + 128], start=True, stop=True),
                     r=["wb", "cknT"], w=["pk"])
            S.dve(lambda e, g0=g0, g1=g1: e.tensor_copy(
                vx[hb][:, g0:g1, 0:64], pk[:, 0:(g1 - g0) * 64].rearrange("p (a b) -> p a b", b=64)),
                r=["pk"], w=[f"vx{hb}"])
        S.dma(lambda e, h=h: e.dma_start(out=qh[:], in_=q[:, h * 96:(h + 1) * 96].rearrange("(t p) c -> p t c", p=128)),
              w=["qh"], group="qh")
        S.pool(lambda e: e.tensor_copy(qpad[:, :, 0:64], qh[:, :, 0:64]), r=["qh"], w=["qpad"])
        rope("pool", "dve", qh, csqt, tq, qpad, ["qh", "csqt"], "tq", "qpad", 64)
        for g0 in range(0, nqt, 8):
            g1 = min(nqt, g0 + 8)
            for t in range(g0, g1):
                S.pe(lambda e, t=t, g0=g0: e.transpose(tpk[:, t - g0, :], qpad[:, t, :], ident[:]),
                     r=["qpad", "ident"], w=["tpk"])
            S.dve(lambda e, g0=g0, g1=g1, hb=hb: e.tensor_copy(
                qT[hb][:, g0 * 128:g1 * 128], tpk[:, 0:g1 - g0, :].rearrange("p a b -> p (a b)")),
                r=["tpk"], w=[f"qT{hb}"])

    st = dict(sti=0, oti=0, osti=0)
    LAG = 2

    def mainloop(h):
        hb = h % 2
        its = []
        for (q0, qn, k0, k1) in chunks:
            op = st["oti"] % 2
            st["oti"] += 1
            for kt in range(k0, k1):
                its.append((q0, qn, k0, k1, kt, op))

        def emit_qk(n):
            q0, qn, k0, k1, kt, op = its[n]
            sp = st["sti"] % 3
            st["sti"] += 1
            its[n] = its[n] + (sp,)
            S.pe(lambda e: e.matmul(sTp[sp][:, 0:qn], kT[hb][:, kt * 128:(kt + 1) * 128], qT[hb][:, q0:q0 + qn],
                                    start=True, stop=True), r=[f"kT{hb}", f"qT{hb}"], w=[f"sTp{sp}"])
            S.act(lambda e: e.activation(pT[sp][:, 0:qn], sTp[sp][:, 0:qn], AF.Exp, scale=MLA_SCALE),
                  r=[f"sTp{sp}"], w=[f"pT{sp}"])

        def emit_pv(n):
            q0, qn, k0, k1, kt, op, sp = its[n]
            S.pe(lambda e: e.matmul(oTp[op][:, 0:qn], vx[hb][:, kt, :], pT[sp][:, 0:qn], start=(kt == k0), stop=(kt == k1 - 1)),
                 r=[f"vx{hb}", f"pT{sp}"], w=[f"oTp{op}"])
            if kt != k1 - 1:
                return
            S.dve(lambda e: e.tensor_copy(oTs[op][:, 0:qn], oTp[op][:, 0:qn]), r=[f"oTp{op}"], w=[f"oTs{op}"])
            for j in range(qn // 128):
                os_ = st["osti"] % 4
                st["osti"] += 1
                S.pe(lambda e, j=j: e.transpose(tpo[:], oTs[op][:, j * 128:(j + 1) * 128], identf[0:65, 0:65]),
                     r=[f"oTs{op}", "identf"], w=["tpo"])
                S.dve(lambda e, os_=os_: e.reciprocal(rc[os_][:], tpo[:, 64:65]), r=["tpo"], w=[f"rc{os_}"])
                S.dve(lambda e, os_=os_: e.tensor_scalar(ost[os_][:], tpo[:, 0:64], rc[os_][:, 0:1], None, ALU.mult),
                      r=["tpo", f"rc{os_}"], w=[f"ost{os_}"])
                r0 = q0 + j * 128
                S.dma(lambda e, os_=os_, r0=r0: e.dma_start(out=out[r0:r0 + 128, h * 64:(h + 1) * 64], in_=ost[os_][:]),
                      r=[f"ost{os_}"], w=[f"oo{os_}"], group=f"oo{os_}")

        nI = len(its)
        for n in range(nI):
            emit_qk(n)
            if n >= LAG:
                emit_pv(n - LAG)
            if n == nI // 2 and h + 1 < nheads:
                prologue(h + 1)
        for n in range(max(0, nI - LAG), nI):
            emit_pv(n)

    prologue(0)
    for h in range(nheads):
        mainloop(h)


def gla_consts2():
    s = np.arange(64)[:, None]
    t = np.arange(64)[None, :]
    out = np.zeros((2, 64, 320), np.float32)
    U = (s <= t).astype(np.float32)
    out[0, :, 0:64] = U - (s <= 31)
    out[0, :, 64:128] = U
    out[0, :, 128:192] = (s > t)
    out[0, :, 192:256] = U
    Ub = (s >= t).astype(np.float32)
    out[1, :, 0:64] = Ub - (s >= 32)
    out[1, :, 64:128] = Ub
    out[1, :, 128:192] = (s < t)
    out[1, :, 192:256] = Ub
    out[:, 0, 256:320] = 1.0
    return out


def emit_k4s(g, Z, OG, segloc, segall, w2, bb, cst_d, mask):
    S = g.S
    g.phase()
    NLC, NCC = 32, 4
    ZH = ["Zh0", "Zh1", "Zh2", "Zh3"]
    NCH = NLC + NCC
    identf, ident = load_ident(g)
    cst = g.sb([64, 2, 320], F32)
    maskb = g.sb([64, 2, 64], BF16)
    w2t = g.sb([16, 2, 256], F32)
    bbt = g.sb([1, 2, 256], F32)
    mk = g.sb([64, 16], F32)
    zh_off = g.off
    Zh = g.sb([64, NCH, 288], F32)
    vb = g.sb([64, NCH, 128], BF16)
    OLs = g.sb([64, NCH, 512], F32)
    qbP = g.sb([64, 8, NLC, 64], BF16)
    Sctx = g.sb([64, 8, 128], F32)
    SEG = g.sb([64, 8, 129], F32)
    Sib = g.sb([64, 8, 128], BF16)
    Wk = g.sb([64, 128], F32)
    cand = g.sb([64, 128], F32)
    diff = g.sb([64, 128], F32)
    NP = 6
    NPA, NPB = 5, 3
    mk_t = lambda shape, dt: [g.sb(shape, dt) for i in range(NP)]
    qTs = mk_t([64, 64], F32); kTs = mk_t([64, 64], F32); glTs = mk_t([16, 64], F32)
    Et = mk_t([64, 64], F32); Lt = mk_t([64, 64], F32); e12 = mk_t([64, 192], F32); e4 = mk_t([64, 64], F32)
    dec = mk_t([64, 1], F32)
    qe = mk_t([64, 64], BF16); ke = mk_t([64, 64], BF16); qb = mk_t([64, 64], BF16); kd = mk_t([64, 64], BF16)
    attm = mk_t([64, 64], BF16)
    Sf = [g.sb([64, 128], F32) for i in range(2)]
    Sb = [g.sb([64, 128], BF16) for i in range(2)]
    Pt = [g.sb([64, 1], F32) for i in range(2)]
    pA = [g.ps(i, [64, 512], F32) for i in range(NPA)]
    pB = [g.ps(NPA + i, [64, 512], F32) for i in range(NPB)]
    id64 = identf[0:64, 0:64]

    S.dma(lambda e: e.dma_start(out=cst[:], in_=cst_d.rearrange("d p f -> p d f")), w=["cst"], group="cst")
    S.dve(lambda e: e.tensor_copy(maskb[:], cst[:, :, 192:256]), r=["cst"], w=["maskb"])
    S.dma(lambda e: e.dma_start(out=w2t[:], in_=w2.rearrange("d r e -> r d e")), w=["w2t"], group="w2t")
    S.dma(lambda e: e.dma_start(out=bbt[:], in_=bb.rearrange("(o d) e -> o d e", o=1)), w=["bbt"], group="bbt")
    S.dma(lambda e: e.dma_start(out=mk[:], in_=mask[0:64, :]), w=["mk"], group="mk")
    Zv = Z.rearrange("(c p) f -> p c f", p=64)
    tasks = []
    ci = 0
    for h in range(4):
        first_of_head = True
        for d in range(2):
            hd = h * 2 + d
            for part in ("ctx", "lat"):
                lat = part == "lat"
                order = list(range(NLC)) if lat else list(range(NLC, NCH))
                if d == 1:
                    order = order[::-1]
                si = 0
                for idx, c in enumerate(order):
                    p = ci % NP
                    pa, pb = ci % NPA, ci % NPB
                    ci += 1
                    s0, s1 = si % 2, (si + 1) % 2
                    si += 1
                    tasks.append(dict(h=h, d=d, hd=hd, lat=lat, c=c, p=p, pa=pa, pb=pb, s0=s0, s1=s1, first=(idx == 0),
                                      last=(idx == len(order) - 1), head_start=first_of_head))
                    first_of_head = False

    def load_head(h):
        srcs = [(slice(h * 64, (h + 1) * 64), slice(0, 64)), (slice(256 + h * 64, 256 + (h + 1) * 64), slice(64, 128)),
                (slice(512 + h * 128, 512 + (h + 1) * 128), slice(128, 256)), (slice(1024, 1056), slice(256, 288))]
        for k, (sc, dc) in enumerate(srcs):
            S.dma(lambda e, sc=sc, dc=dc: e.dma_start(out=Zh[:, :, dc], in_=Zv[:, :, sc]), w=[f"Zh{k}"], group=f"Zh{k}")
        S.pool(lambda e: e.tensor_copy(vb[:], Zh[:, :, 128:256]), r=ZH, w=["vb"])

    def hdr(t):
        h, d, c, p, pa = t["h"], t["d"], t["c"], t["p"], t["pa"]
        return h, d, c, p, pa, f"pA{pa}"

    def stageA1(t):
        h, d, c, p, pa, A = hdr(t)
        gcol = 256 + 16 * d
        S.pe(lambda e: e.transpose(pA[pa][:, 320:384], Zh[:, c, 0:64], id64), r=ZH + ["identf"], w=[A])
        S.pe(lambda e: e.transpose(pA[pa][:, 384:448], Zh[:, c, 64:128], id64), r=ZH + ["identf"], w=[A])
        S.pe(lambda e: e.transpose(pA[pa][0:16, 448:512], Zh[:, c, gcol:gcol + 16], id64), r=ZH + ["identf"], w=[A])
        S.dve(lambda e: e.tensor_copy(glTs[p][:], pA[pa][0:16, 448:512]), r=[A], w=[f"glTs{p}"])
        S.dve(lambda e: e.tensor_scalar(qTs[p][:], pA[pa][:, 320:384], 0.125, None, ALU.mult), r=[A], w=[f"qTs{p}"])
        S.dve(lambda e: e.tensor_copy(kTs[p][:], pA[pa][:, 384:448]), r=[A], w=[f"kTs{p}"])

    def stageA2(t):
        h, d, c, p, pa, A = hdr(t)
        w2hd = w2t[:, d, h * 64:(h + 1) * 64]
        bbhd = bbt[0:1, d, h * 64:(h + 1) * 64]
        S.pe(lambda e: e.matmul(pA[pa][:, 0:64], glTs[p][:], w2hd, start=True, stop=False), r=[f"glTs{p}", "w2t"], w=[A])
        S.pe(lambda e: e.matmul(pA[pa][:, 0:64], cst[0:1, 0, 256:320], bbhd, start=False, stop=True), r=["cst", "bbt"], w=[A])
        S.act(lambda e: e.activation(Et[p][:], pA[pa][:, 0:64], AF.Exp, scale=-1.0), r=[A], w=[f"Et{p}"])
        S.act(lambda e: e.activation(Lt[p][:], Et[p][:], AF.Ln, bias=1.0), r=[f"Et{p}"], w=[f"Lt{p}"])

    def stageA3(t):
        h, d, c, p, pa, A = hdr(t)
        cD = lambda a, b: cst[:, d, a:b]
        deccol = 127 if d == 0 else 64
        S.pe(lambda e: e.matmul(pA[pa][:, 64:128], Lt[p][:], cD(0, 64), start=True, stop=True), r=[f"Lt{p}", "cst"], w=[A])
        S.pe(lambda e: e.matmul(pA[pa][:, 128:192], Lt[p][:], cD(64, 128), start=True, stop=True), r=[f"Lt{p}", "cst"], w=[A])
        S.pe(lambda e: e.matmul(pA[pa][:, 192:256], cD(128, 192), Lt[p][:], start=True, stop=True), r=[f"Lt{p}", "cst"], w=[A])
        S.act(lambda e: e.activation(e12[p][:, 0:128], pA[pa][:, 64:192], AF.Exp, scale=-1.0 / 16), r=[A], w=[f"e12{p}"])
        S.act(lambda e: e.activation(e12[p][:, 128:192], pA[pa][:, 64:128], AF.Exp, scale=1.0 / 16), r=[A], w=[f"e12{p}"])
        S.act(lambda e: e.activation(e4[p][:], pA[pa][:, 192:256], AF.Exp, scale=-1.0 / 16), r=[A], w=[f"e4{p}"])
        S.pool(lambda e: e.tensor_tensor(qe[p][:], qTs[p][:], e12[p][:, 0:64], ALU.mult), r=[f"qTs{p}", f"e12{p}"], w=[f"qe{p}"])
        S.pool(lambda e: e.tensor_tensor(ke[p][:], kTs[p][:], e12[p][:, 128:192], ALU.mult), r=[f"kTs{p}", f"e12{p}"], w=[f"ke{p}"])
        S.pool(lambda e: e.tensor_tensor(qb[p][:], qTs[p][:], e12[p][:, 64:128], ALU.mult), r=[f"qTs{p}", f"e12{p}"], w=[f"qb{p}"])
        S.pool(lambda e: e.tensor_tensor(kd[p][:], Zh[:, c, 64:128], e4[p][:], ALU.mult), r=ZH + [f"e4{p}"], w=[f"kd{p}"])

    def stageB(t):
        h, d, hd, c, p, s0, s1, lat = t["h"], t["d"], t["hd"], t["c"], t["p"], t["s0"], t["s1"], t["lat"]
        pb = t["pb"]
        B = f"pB{pb}"
        dc = 127 if d == 0 else 64
        decp = e12[p][:, dc:dc + 1]
        if t["first"]:
            S.dve(lambda e: e.memset(Sf[0][:], 0.0), w=["Sf0"])
            S.pool(lambda e: e.memset(Sb[0][:], 0.0), w=["Sb0"])
            if lat:
                S.dve(lambda e: e.memset(Pt[0][:], 1.0), w=["P0"])
        if lat:
            S.dve(lambda e: e.scalar_tensor_tensor(qbP[:, hd, c, :], qTs[p][:], Pt[s0][:, 0:1], e12[p][:, 64:128],
                                                   ALU.mult, ALU.mult), r=[f"qTs{p}", f"P{s0}", f"e12{p}"], w=["qbP"])
            S.dve(lambda e: e.tensor_tensor(Pt[s1][:], Pt[s0][:], decp, ALU.mult), r=[f"P{s0}", f"e12{p}"], w=[f"P{s1}"])
        S.pe(lambda e: e.matmul(pB[pb][:, 256:320], ke[p][:], qe[p][:], start=True, stop=True), r=[f"ke{p}", f"qe{p}"], w=[B])
        S.dve(lambda e: e.tensor_tensor(attm[p][:], pB[pb][:, 256:320], maskb[:, d, :], ALU.mult), r=[B, "maskb"], w=[f"attm{p}"])
        S.pe(lambda e: e.matmul(pB[pb][:, 0:128], attm[p][:], vb[:, c, :], start=True, stop=False), r=[f"attm{p}", "vb"], w=[B])
        S.pe(lambda e: e.matmul(pB[pb][:, 0:128], qb[p][:], Sb[s0][:], start=False, stop=True), r=[f"qb{p}", f"Sb{s0}"], w=[B])
        S.pe(lambda e: e.matmul(pB[pb][:, 128:256], kd[p][:], vb[:, c, :], start=True, stop=True), r=[f"kd{p}", "vb"], w=[B])
        S.dve(lambda e: e.scalar_tensor_tensor(Sf[s1][:], Sf[s0][:], decp, pB[pb][:, 128:256], ALU.mult, ALU.add),
              r=[f"Sf{s0}", f"e12{p}", B], w=[f"Sf{s1}"])
        S.pool(lambda e: e.tensor_copy(Sb[s1][:], Sf[s1][:]), r=[f"Sf{s1}"], w=[f"Sb{s1}"])
        ocols = slice(h * 128, (h + 1) * 128)
        if d == 0:
            S.dve(lambda e: e.tensor_copy(OLs[:, c, ocols], pB[pb][:, 0:128]), r=[B], w=[f"OL{c}"])
        else:
            S.dve(lambda e: e.tensor_tensor(OLs[:, c, ocols], pB[pb][:, 0:128], OLs[:, c, ocols], ALU.add),
                  r=[B, f"OL{c}"], w=[f"OL{c}"])
        if t["last"]:
            if lat:
                S.dve(lambda e: e.tensor_copy(SEG[:, hd, 0:128], Sf[s1][:]), r=[f"Sf{s1}"], w=["SEG"])
                S.dve(lambda e: e.tensor_copy(SEG[:, hd, 128:129], Pt[s1][:]), r=[f"P{s1}"], w=["SEG"])
            else:
                S.dve(lambda e: e.tensor_copy(Sctx[:, hd, :], Sf[s1][:]), r=[f"Sf{s1}"], w=["Sctx"])

    stages = [stageB, stageA3, stageA2, stageA1]
    heads = {}
    for t in tasks:
        heads.setdefault(t["h"], []).append(t)
    for h in range(4):
        tl = heads[h]
        load_head(h)
        n = len(tl)
        for step in range(n + 3):
            for k, stg in enumerate(stages):
                idx = step - (3 - k)
                if 0 <= idx < n:
                    stg(tl[idx])
    S.dma(lambda e: e.dma_start(out=segloc.ap(), in_=SEG[:].rearrange("p a b -> p (a b)")), r=["SEG"], w=["segloc"],
          group="segio")
    allgather(g, segall, segloc, "segall", r=["segloc"])
    SEGa = g.arena[0:64, zh_off:zh_off + 4 * 8 * 129].rearrange("p (r a b) -> p r a b", r=4, a=8, b=129)
    S.dma(lambda e: e.dma_start(out=SEGa.rearrange("p r a b -> p r (a b)"),
                                in_=segall.ap().rearrange("(r p) f -> p r f", p=64)),
          r=["segall"], w=ZH + ["SEGa"], group="segio")
    for h in range(4):
        for d in range(2):
            hd = h * 2 + d
            S.dve(lambda e, hd=hd: e.tensor_copy(Wk[:], Sctx[:, hd, :]), r=["Sctx"], w=["Wk"])
            js = range(4) if d == 0 else range(3, -1, -1)
            for j in js:
                mcol = (4 if d == 0 else 8) + j
                S.dve(lambda e, j=j, hd=hd: e.scalar_tensor_tensor(cand[:], Wk[:], SEGa[:, j, hd, 128:129],
                                                                   SEGa[:, j, hd, 0:128], ALU.mult, ALU.add),
                      r=["Wk", "SEGa"], w=["cand"])
                S.dve(lambda e: e.tensor_tensor(diff[:], cand[:], Wk[:], ALU.subtract), r=["cand", "Wk"], w=["diff"])
                S.dve(lambda e, mcol=mcol: e.scalar_tensor_tensor(Wk[:], diff[:], mk[:, mcol:mcol + 1], Wk[:],
                                                                  ALU.mult, ALU.add), r=["diff", "mk", "Wk"], w=["Wk"])
            S.dve(lambda e, hd=hd: e.tensor_copy(Sib[:, hd, :], Wk[:]), r=["Wk"], w=["Sib"])
    for c in range(NLC):
        p = c % NPA
        A = f"pA{p}"
        for h in range(4):
            for d in range(2):
                hd = h * 2 + d
                S.pe(lambda e, p=p, h=h, d=d, hd=hd, c=c: e.matmul(pA[p][:, h * 128:(h + 1) * 128], qbP[:, hd, c, :],
                                                                    Sib[:, hd, :], start=(d == 0), stop=(d == 1)),
                     r=["qbP", "Sib"], w=[A])
        S.dve(lambda e, p=p, c=c: e.tensor_tensor(OLs[:, c, :], pA[p][:], OLs[:, c, :], ALU.add), r=[A, f"OL{c}"], w=[f"OL{c}"])
    OGv = OG.rearrange("(c p) f -> p c f", p=64)
    for k in range(4):
        cs = slice(k * 9, (k + 1) * 9)
        S.dma(lambda e, cs=cs: e.dma_start(out=OGv[:, cs, :], in_=OLs[:, cs, :]),
              r=[f"OL{c}" for c in range(k * 9, (k + 1) * 9)], w=[f"ogo{k}"], group=f"ogo{k}")


def emit_k5(g, T, kind, n_lat_tiles, x, w, vecs, z, out, og=None, oml=None, gng=None, fr_lat=None, fr_ctx=None,
            swT=None, sbT=None, mask=None, eps=1e-6):
    S = g.S
    g.phase()
    D = 1024
    nt = T // 128
    identf, ident = load_ident(g)
    wbf = g.sb([128, 8, D], BF16)
    wst = [g.sb([128, D], F32) for i in range(2)]
    vt = g.sb([128, 4, D], F32)
    for a in range(4):
        S.dma(lambda e, a=a: e.dma_start(out=vt[:, a, :], in_=vecs[a].partition_broadcast(128)), w=["vt"], group="modt")
    if kind == "even":
        gn = g.sb([128, 128], F32)
        S.dma(lambda e: e.dma_start(out=gn[:], in_=gng.partition_broadcast(128)), w=["gn"], group="gt")
    else:
        swf = g.sb([128, 4, 128], F32)
        swb = g.sb([128, 4, 128], BF16)
        sbt = g.sb([128, 4], F32)
        mk = g.sb([128, 16], F32)
        S.dma(lambda e: e.dma_start(out=swf[:], in_=swT.rearrange("g s t -> s g t")), w=["swf"], group="swf")
        S.dve(lambda e: e.tensor_copy(swb[:], swf[:]), r=["swf"], w=["swb"])
        S.dma(lambda e: e.dma_start(out=sbt[:], in_=sbT), w=["sbt"], group="sbt")
        S.dma(lambda e: e.dma_start(out=mk[:], in_=mask), w=["mk"], group="mk")
    for k in range(8):
        s = k % 2
        S.dma(lambda e, k=k, s=s: e.dma_start(out=wst[s][:], in_=w[k * 128:(k + 1) * 128, :]),
              w=[f"wst{s}"], group=f"wst{s}")
        if k % 2 == 0:
            S.act(lambda e, k=k, s=s: e.copy(wbf[:, k, :], wst[s][:]), r=[f"wst{s}"], w=[f"wbf{k}"])
        else:
            S.dve(lambda e, k=k, s=s: e.tensor_copy(wbf[:, k, :], wst[s][:]), r=[f"wst{s}"], w=[f"wbf{k}"])
    NX = 2
    xt = [g.sb([128, D], F32) for i in range(3)]
    ab = [g.sb([128, D], BF16) for i in range(NX)]
    aT = [g.sb([128, 8, 128], BF16) for i in range(NX)]
    rr = [g.sb([128, D], F32) for i in range(NX)]
    ot = [g.sb([128, D], F32) for i in range(NX)]
    st = [g.sb([128, 8, 6], F32) for i in range(NX)]
    mv = [g.sb([128, 16], F32) for i in range(NX)]
    if kind == "even":
        i1 = [g.sb([128, 512], F32) for i in range(NX)]
        i2 = [g.sb([128, 512], F32) for i in range(NX)]
        i3 = [g.sb([128, 512], F32) for i in range(NX)]
        i4 = [g.sb([128, 512], F32) for i in range(NX)]
        i5 = [g.sb([128, 512], F32) for i in range(NX)]
    else:
        i1 = [g.sb([128, 512], F32) for i in range(NX)]
        fc4 = [g.sb([128, 4, 512], F32) for i in range(NX)]
        zz = [g.sb([128, 2048], F32) for i in range(NX)]
        vgb = [g.sb([128, 512], BF16) for i in range(NX)]
        svp = [g.ps(0, [128, 512], F32)]
    tp = [g.ps(1 + i, [128, 8, 128], BF16) for i in range(2)]
    yp = [g.ps(3 + i, [128, 512], F32) for i in range(4)]
    wkeys = [f"wbf{k}" for k in range(8)]

    def rstd_chain(s, col_var, col_out, ncol=1):
        S.dve(lambda e: e.tensor_scalar_add(mv[s][:, col_var:col_var + ncol], mv[s][:, col_var:col_var + ncol], eps),
              r=[f"mv{s}"], w=[f"mv{s}"])
        S.act(lambda e: e.sqrt(mv[s][:, col_var:col_var + ncol], mv[s][:, col_var:col_var + ncol]),
              r=[f"mv{s}"], w=[f"mv{s}"])
        S.dve(lambda e: e.reciprocal(mv[s][:, col_out:col_out + ncol], mv[s][:, col_var:col_var + ncol]),
              r=[f"mv{s}"], w=[f"mv{s}"])

    def emit_loads(i):
        s = i % NX
        rows = slice(i * 128, (i + 1) * 128)
        xs = i % 3
        S.dma(lambda e, xs=xs, rows=rows: e.dma_start(out=xt[xs][:], in_=x[rows, :]), w=[f"xt{xs}"], group=f"xt{xs}")
        if kind == "even":
            S.dma(lambda e, s=s, rows=rows: e.dma_start(out=i1[s][:], in_=og[rows, :]), w=[f"i1{s}"], group=f"i1{s}")
            S.dma(lambda e, s=s, rows=rows: e.dma_start(out=i3[s][:], in_=z[rows, 1056:1568]), w=[f"i3{s}"], group=f"i3{s}")
            S.dma(lambda e, s=s, rows=rows: e.dma_start(out=i4[s][:], in_=oml[rows, :]), w=[f"i4{s}"], group=f"i4{s}")
            S.dma(lambda e, s=s, rows=rows: e.dma_start(out=i5[s][:], in_=z[rows, 1984:2496]), w=[f"i5{s}"], group=f"i5{s}")
        else:
            if i < n_lat_tiles:
                for qq in range(4):
                    S.dma(lambda e, s=s, qq=qq, i=i: e.dma_start(
                        out=fc4[s][:, qq, :].rearrange("p (g c) -> p g c", g=4), in_=fr_lat(qq, i)),
                        w=[f"fc4{s}.{qq}"], group=f"fc4{s}")
            else:
                S.dma(lambda e, s=s, i=i: e.dma_start(out=i1[s][:].rearrange("p (g c) -> p g c", g=4), in_=fr_ctx(i)),
                      w=[f"i1{s}"], group=f"i1{s}")
            S.dma(lambda e, s=s, rows=rows: e.dma_start(out=zz[s][:], in_=z[rows, 512:2560]), w=[f"zz{s}"], group=f"zz{s}")

    def stageP(i):
        s = i % NX
        rows = slice(i * 128, (i + 1) * 128)
        if i + 1 < nt:
            emit_loads(i + 1)
        if kind == "even":
            S.pool(lambda e, s=s: e.tensor_tensor(i2[s][:], i1[s][:], i1[s][:], ALU.mult), r=[f"i1{s}"], w=[f"i2{s}"])
            S.dve(lambda e, s=s: e.reduce_sum(mv[s][:, 0:4], i2[s][:].rearrange("p (h d) -> p h d", h=4), AX.X),
                  r=[f"i2{s}"], w=[f"mv{s}"])
            S.dve(lambda e, s=s: e.tensor_scalar(mv[s][:, 0:4], mv[s][:, 0:4], 1.0 / 128, None, ALU.mult),
                  r=[f"mv{s}"], w=[f"mv{s}"])
            rstd_chain(s, 0, 4, 4)
            S.act(lambda e, s=s: e.activation(i3[s][:], i3[s][:], AF.Silu), r=[f"i3{s}"], w=[f"i3{s}"])
            S.act(lambda e, s=s: e.activation(i5[s][:], i5[s][:], AF.Silu), r=[f"i5{s}"], w=[f"i5{s}"])
            for h in range(4):
                hs = slice(h * 128, (h + 1) * 128)
                S.dve(lambda e, s=s, h=h, hs=hs: e.scalar_tensor_tensor(
                    i1[s][:, hs], i1[s][:, hs], mv[s][:, 4 + h:5 + h], gn[:], ALU.mult, ALU.mult),
                    r=[f"i1{s}", f"mv{s}", "gn"], w=[f"i1{s}"])
            S.pool(lambda e, s=s: e.tensor_tensor(ab[s][:, 0:512], i1[s][:], i3[s][:], ALU.mult),
                   r=[f"i1{s}", f"i3{s}"], w=[f"ab{s}"])
            S.pool(lambda e, s=s: e.tensor_tensor(ab[s][:, 512:1024], i4[s][:], i5[s][:], ALU.mult),
                   r=[f"i4{s}", f"i5{s}"], w=[f"ab{s}"])
        else:
            if i < n_lat_tiles:
                S.dve(lambda e, s=s: e.tensor_scalar(i1[s][:], fc4[s][:, 0, :], mk[:, 0:1], None, ALU.mult),
                      r=[f"fc4{s}.0", f"fc4{s}.3", "mk"], w=[f"i1{s}"])
                for qq in range(1, 4):
                    S.dve(lambda e, s=s, qq=qq: e.scalar_tensor_tensor(
                        i1[s][:], fc4[s][:, qq, :], mk[:, qq:qq + 1], i1[s][:], ALU.mult, ALU.add),
                        r=[f"fc4{s}.{qq}", f"fc4{s}.3", "mk", f"i1{s}"], w=[f"i1{s}"])
            S.act(lambda e, s=s: e.activation(zz[s][:, 0:512], zz[s][:, 0:512], AF.Silu), r=[f"zz{s}"], w=[f"zz{s}"])
            S.act(lambda e, s=s: e.activation(zz[s][:, 1536:2048], zz[s][:, 1536:2048], AF.Silu), r=[f"zz{s}"], w=[f"zz{s}"])
            S.act(lambda e, s=s: e.activation(zz[s][:, 512:1536], zz[s][:, 512:1536], AF.Gelu), r=[f"zz{s}"], w=[f"zz{s}"])
            S.pool(lambda e, s=s: e.tensor_tensor(ab[s][:, 0:512], i1[s][:], zz[s][:, 0:512], ALU.mult),
                   r=[f"i1{s}", f"zz{s}"], w=[f"ab{s}"])
            for g in range(4):
                S.dve(lambda e, s=s, g=g: e.bn_stats(st[s][:, g, :], zz[s][:, 1024 + g * 128:1024 + (g + 1) * 128]),
                      r=[f"zz{s}"], w=[f"st{s}"])
                S.dve(lambda e, s=s, g=g: e.bn_aggr(mv[s][:, 2 * g:2 * g + 2], st[s][:, g:g + 1, :]),
                      r=[f"st{s}"], w=[f"mv{s}"])
            for g in range(4):
                rstd_chain(s, 2 * g + 1, 8 + g, 1)
            for g in range(4):
                S.dve(lambda e, s=s, g=g: e.tensor_scalar(
                    vgb[s][:, g * 128:(g + 1) * 128], zz[s][:, 1024 + g * 128:1024 + (g + 1) * 128],
                    mv[s][:, 2 * g:2 * g + 1], mv[s][:, 8 + g:9 + g], ALU.subtract, ALU.mult),
                    r=[f"zz{s}", f"mv{s}"], w=[f"vgb{s}"])
            for g in range(4):
                S.pe(lambda e, s=s, g=g: e.matmul(svp[0][:, g * 128:(g + 1) * 128], swb[:, g, :],
                                                  vgb[s][:, g * 128:(g + 1) * 128], start=True, stop=True),
                     r=[f"vgb{s}", "swb"], w=["svp0"])
            for g in range(4):
                gs = slice(g * 128, (g + 1) * 128)
                S.dve(lambda e, s=s, g=g, gs=gs: e.scalar_tensor_tensor(
                    zz[s][:, 512 + g * 128:512 + (g + 1) * 128], svp[0][:, gs], sbt[:, g:g + 1],
                    zz[s][:, 512 + g * 128:512 + (g + 1) * 128], ALU.add, ALU.mult),
                    r=["svp0", "sbt", f"zz{s}"], w=[f"zz{s}"])
            S.pool(lambda e, s=s: e.tensor_tensor(ab[s][:, 512:1024], zz[s][:, 512:1024], zz[s][:, 1536:2048], ALU.mult),
                   r=[f"zz{s}"], w=[f"ab{s}"])
        t = i % 2
        for k in range(8):
            S.pe(lambda e, s=s, t=t, k=k: e.transpose(tp[t][:, k, :], ab[s][:, k * 128:(k + 1) * 128], ident[:]),
                 r=[f"ab{s}", "ident"], w=[f"tp{t}"])
        S.act(lambda e, s=s, t=t: e.copy(aT[s][:], tp[t][:]), r=[f"tp{t}"], w=[f"aT{s}"])

    def stageM(i):
        s = i % NX
        rows = slice(i * 128, (i + 1) * 128)
        gi = 0 if i < n_lat_tiles else 1
        for c in range(2):
            p = (2 * i + c) % 4
            cs = slice(c * 512, (c + 1) * 512)
            for k in range(8):
                S.pe(lambda e, s=s, p=p, k=k, cs=cs: e.matmul(yp[p][:], aT[s][:, k, :], wbf[:, k, cs],
                                                             start=(k == 0), stop=(k == 7)),
                     r=[f"aT{s}", wkeys[k]], w=[f"yp{p}"])
            S.dve(lambda e, s=s, p=p, cs=cs, gi=gi: e.tensor_tensor(rr[s][:, cs], yp[p][:], vt[:, gi, cs], ALU.mult),
                  r=[f"yp{p}", "vt"], w=[f"rr{s}"])
        xs = i % 3
        S.dve(lambda e, s=s, xs=xs: e.scalar_tensor_tensor(rr[s][:], xt[xs][:], ALPHA, rr[s][:], ALU.mult, ALU.add),
               r=[f"xt{xs}", f"rr{s}"], w=[f"rr{s}"])
        for j in range(2):
            S.dve(lambda e, s=s, j=j: e.bn_stats(st[s][:, 4 + j, :], rr[s][:, j * 512:(j + 1) * 512]),
                  r=[f"rr{s}"], w=[f"st{s}"])
        S.dve(lambda e, s=s: e.bn_aggr(mv[s][:, 12:14], st[s][:, 4:6, :]), r=[f"st{s}"], w=[f"mv{s}"])
        rstd_chain(s, 13, 14, 1)
        S.dve(lambda e, s=s: e.tensor_scalar(rr[s][:], rr[s][:], mv[s][:, 12:13], mv[s][:, 14:15],
                                             ALU.subtract, ALU.mult), r=[f"rr{s}", f"mv{s}"], w=[f"rr{s}"])
        S.pool(lambda e, s=s: e.tensor_tensor(rr[s][:], rr[s][:], vt[:, 2, :], ALU.mult), r=[f"rr{s}", "vt"], w=[f"rr{s}"])
        S.pool(lambda e, s=s: e.tensor_tensor(ot[s][:], rr[s][:], vt[:, 3, :], ALU.add), r=[f"rr{s}", "vt"], w=[f"ot{s}"])
        S.dma(lambda e, s=s, rows=rows: e.dma_start(out=out[rows, :], in_=ot[s][:]), r=[f"ot{s}"], w=[f"oo{s}"],
              group=f"oo{s}", eng="act")


    emit_loads(0)
    stageP(0)
    for i in range(nt):
        if i + 1 < nt:
            stageP(i + 1)
        stageM(i)


def emit_k7(g, fall, fout, TW, FC, W3, TWc, mask, with_ctx=True):
    S = g.S
    g.phase()
    st = [g.sb([128, 4, 512], F32) for i in range(2)]
    tmp = [g.sb([128, 4, 128], F32) for i in range(2)]
    mk = g.sb([128, 16], F32)
    TWb = g.sb([128, 64, 256], BF16)
    fb = g.sb([128, 64, 128], BF16)
    Y = g.sb([128, 2, 64, 128], BF16)
    U = g.sb([128, 64, 256], BF16)
    FCb = g.sb([128, 512], BF16)
    W3b = g.sb([128, 256], BF16)
    frt = g.sb([128, 64, 128], F32)
    ps = [g.ps(i, [128, 512], F32) for i in range(4)]
    S.dma(lambda e: e.dma_start(out=mk[:], in_=mask), w=["mk"], group="mk")
    stf = lambda s: st[s][:].rearrange("p a b -> p (a b)")
    for i in range(8):
        s = i % 2
        S.dma(lambda e, i=i, s=s: e.dma_start(out=stf(s), in_=TW[:, i * 8:(i + 1) * 8, :].rearrange("p a b -> p (a b)")),
              w=[f"st{s}"], group=f"st{s}")
        S.add("dve" if i % 2 == 0 else "pool",
              lambda e, i=i, s=s: e.tensor_copy(TWb[:, i * 8:(i + 1) * 8, :].rearrange("p a b -> p (a b)"), stf(s)),
              r=[f"st{s}"], w=["TWb"])
    S.dma(lambda e: e.dma_start(out=stf(0)[:, 0:512], in_=FC), w=["st0"], group="st0")
    S.dve(lambda e: e.tensor_copy(FCb[:], stf(0)[:, 0:512]), r=["st0"], w=["FCb"])
    S.dma(lambda e: e.dma_start(out=stf(1)[:, 0:256], in_=W3), w=["st1"], group="st1")
    S.dve(lambda e: e.tensor_copy(W3b[:], stf(1)[:, 0:256]), r=["st1"], w=["W3b"])

    S.barrier()

    def select(s, t, dst):
        stk = [f"st{s}.{r}.{pp}" for r in range(4) for pp in range(4)]
        S.dve(lambda e: e.tensor_scalar(tmp[t][:], st[s][:, :, 0:128], mk[:, 0:1], None, ALU.mult),
              r=stk + ["mk"], w=[f"tmp{t}"])
        for gg in range(1, 3):
            S.dve(lambda e, gg=gg: e.scalar_tensor_tensor(tmp[t][:], st[s][:, :, gg * 128:(gg + 1) * 128],
                                                         mk[:, gg:gg + 1], tmp[t][:], ALU.mult, ALU.add),
                  r=stk + ["mk", f"tmp{t}"], w=[f"tmp{t}"])
        S.dve(lambda e: e.scalar_tensor_tensor(dst, st[s][:, :, 384:512], mk[:, 3:4], tmp[t][:], ALU.mult, ALU.add),
              r=stk + ["mk", f"tmp{t}"], w=["fb"])

    for j in range(16):
        s = j % 2
        for r in range(4):
            for pp in range(4):
                src = fall[pp][r * 512:(r + 1) * 512, :].rearrange("(a n) c -> a n c", n=64)[:, 4 * j:4 * j + 4, :]
                p0 = 32 * r + 8 * pp
                S.dma(lambda e, s=s, p0=p0, src=src: e.dma_start(out=st[s][p0:p0 + 8, :, :], in_=src),
                      w=[f"st{s}.{r}.{pp}"], group=f"st{s}")
        select(s, s, fb[:, 4 * j:4 * j + 4, :])
    pi = 0
    for n2 in range(0, 64, 2):
        p = pi % 4; pi += 1
        for d in range(2):
            S.pe(lambda e, p=p, n2=n2, d=d: e.matmul(ps[p][:, d * 256:(d + 1) * 256], fb[:, n2 + d, :], TWb[:, n2 + d, :],
                                                     start=True, stop=True), r=["fb", "TWb"], w=[f"ps{p}"])
        for d in range(2):
            src = lambda p=p, d=d: ps[p][:, d * 256:(d + 1) * 256].rearrange("p (r j a) -> p r j a", r=2, a=2)
            dst = lambda n2=n2, d=d: Y[:, :, :, 2 * (n2 + d):2 * (n2 + d) + 2]
            if d == 0:
                S.act(lambda e, src=src, dst=dst: e.copy(dst(), src()), r=[], w=["Y", f"ps{p}"])
            else:
                S.dve(lambda e, src=src, dst=dst: e.tensor_copy(dst(), src()), r=[], w=["Y", f"ps{p}"])
    for j in range(0, 64, 2):
        p = pi % 4; pi += 1
        for d in range(2):
            jj = j + d
            S.pe(lambda e, p=p, jj=jj, d=d: e.matmul(ps[p][:, d * 256:(d + 1) * 256], Y[:, 0, jj, :],
                                                     FCb[:, 0:256], start=True, stop=False), r=["Y", "FCb"], w=[f"ps{p}"])
            S.pe(lambda e, p=p, jj=jj, d=d: e.matmul(ps[p][:, d * 256:(d + 1) * 256], Y[:, 1, jj, :],
                                                     FCb[:, 256:512], start=False, stop=True), r=["Y", "FCb"], w=[f"ps{p}"])
        if (j // 2) % 2 == 0:
            S.act(lambda e, p=p, j=j: e.copy(U[:, j:j + 2, :].rearrange("p a b -> p (a b)"), ps[p][:]), r=[f"ps{p}"], w=["U"])
        else:
            S.dve(lambda e, p=p, j=j: e.tensor_copy(U[:, j:j + 2, :].rearrange("p a b -> p (a b)"), ps[p][:]), r=[f"ps{p}"], w=["U"])
    scale = 1.0 / 1024.0
    for j0 in range(0, 64, 4):
        p = pi % 4; pi += 1
        for d in range(4):
            jj = j0 + d
            S.pe(lambda e, p=p, jj=jj, d=d: e.matmul(ps[p][:, d * 128:(d + 1) * 128], W3b[:, 0:128], U[:, jj, 0:128],
                                                     start=True, stop=False), r=["U", "W3b"], w=[f"ps{p}"])
            S.pe(lambda e, p=p, jj=jj, d=d: e.matmul(ps[p][:, d * 128:(d + 1) * 128], W3b[:, 128:256], U[:, jj, 128:256],
                                                     start=False, stop=True), r=["U", "W3b"], w=[f"ps{p}"])
        S.dve(lambda e, p=p, j0=j0: e.tensor_scalar(frt[:, j0:j0 + 4, :].rearrange("p a b -> p (a b)"), ps[p][:],
                                                    scale, None, ALU.mult), r=[f"ps{p}"], w=["frt"])
    for q in range(4):
        fv = fout[q].rearrange("(k2 jj a) c -> a k2 jj c", jj=64, a=2)
        for a in range(2):
            S.dma(lambda e, a=a, q=q, fv=fv: e.dma_start(out=fv[a], in_=frt[a * 64 + 16 * q:a * 64 + 16 * (q + 1), :, :]),
                  r=["frt"], w=[f"fo{a}{q}"], group=f"fo{a}")
    if with_ctx:
        S.barrier()
        fcb = g.sb([128, 2, 128], BF16)
        TWcb = g.sb([128, 2, 512], BF16)
        Yc = g.sb([128, 512], BF16)
        oc = g.sb([128, 2, 128], F32)
        for t in range(2):
            S.dma(lambda e, t=t: e.dma_start(out=st[t][:, 0, :], in_=fall[4][t * 128:(t + 1) * 128, :]),
                  w=[f"st{t}"], group=f"st{t}")
            S.dve(lambda e, t=t: e.tensor_scalar(tmp[t][:, 0, :], st[t][:, 0, 0:128], mk[:, 0:1], None, ALU.mult),
                  r=[f"st{t}", "mk"], w=[f"tmp{t}"])
            for gg in range(1, 4):
                S.dve(lambda e, t=t, gg=gg: e.scalar_tensor_tensor(
                    tmp[t][:, 0, :], st[t][:, 0, gg * 128:(gg + 1) * 128], mk[:, gg:gg + 1], tmp[t][:, 0, :], ALU.mult, ALU.add),
                    r=[f"st{t}", "mk", f"tmp{t}"], w=[f"tmp{t}"])
            S.dve(lambda e, t=t: e.tensor_copy(fcb[:, t, :], tmp[t][:, 0, :]), r=[f"tmp{t}"], w=["fcb"])
        for t in range(2):
            S.dma(lambda e, t=t: e.dma_start(out=stf(t)[:, 0:512], in_=TWc[:, t, :]), w=[f"st{t}"], group=f"st{t}")
            S.dve(lambda e, t=t: e.tensor_copy(TWcb[:, t, :], stf(t)[:, 0:512]), r=[f"st{t}"], w=["TWcb"])
        p = pi % 4; pi += 1
        for t in range(2):
            S.pe(lambda e, p=p, t=t: e.matmul(ps[p][:], fcb[:, t, :], TWcb[:, t, :], start=(t == 0), stop=(t == 1)),
                 r=["fcb", "TWcb"], w=[f"ps{p}"])
        S.dve(lambda e, p=p: e.tensor_copy(Yc[:], ps[p][:]), r=[f"ps{p}"], w=["Yc"])
        p = pi % 4; pi += 1
        for kt in range(2):
            S.pe(lambda e, p=p, kt=kt: e.matmul(ps[p][:, kt * 128:(kt + 1) * 128], Yc[:, kt * 128:(kt + 1) * 128],
                                                FCb[:, 0:128], start=True, stop=False), r=["Yc", "FCb"], w=[f"ps{p}"])
            S.pe(lambda e, p=p, kt=kt: e.matmul(ps[p][:, kt * 128:(kt + 1) * 128], Yc[:, 256 + kt * 128:256 + (kt + 1) * 128],
                                                FCb[:, 256:384], start=False, stop=True), r=["Yc", "FCb"], w=[f"ps{p}"])
        S.dve(lambda e, p=p: e.tensor_scalar(oc[:].rearrange("p a b -> p (a b)"), ps[p][:, 0:256],
                                             1.0 / np.sqrt(256.0 * 128.0), None, ALU.mult), r=[f"ps{p}"], w=["oc"])
        S.dma(lambda e: e.dma_start(out=fout[4].rearrange("(t p) c -> p t c", p=128), in_=oc[:]),
              r=["oc"], w=["oco"], group="oco")


Q, L, SEQ, D = 2048, 256, 8192, 1024
T = Q + L
NCORES = 8


def build_fused(depth=4, stop_after=None):
    nc = bass.Bass(target_bir_lowering=False)
    g = G(nc)
    if stop_after is not None:
        g.max_phase = stop_after
    ext = lambda name, shape: nc.dram_tensor(name, list(shape), F32, kind="ExternalInput")
    xin = ext("xin", [T, D]); cin = ext("cin", [128, D])
    ada_w = ext("ada_w", [D, 3 * D]); ada_b = ext("ada_b", [3 * D])
    plg = ext("post_ln_g", [4, D]); plb = ext("post_ln_b", [4, D])
    ewi = ext("even_w_in", [2, D, 2496]); ewo = ext("even_w_out", [2, D, D])
    owi = ext("odd_w_in", [2, D, 2560]); owo = ext("odd_w_out", [2, D, D])
    gw2 = ext("gla_w2", [2, 2, 16, 256]); gb = ext("gla_b", [2, 2, 256]); gng = ext("gla_norm_g", [2, 128])
    qng = ext("mla_q_norm_g", [2, 256]); wuq = ext("mla_w_uq", [2, 256, 768])
    kng = ext("mla_kv_norm_g", [2, 128]); wukv = ext("mla_w_ukv", [2, 128, 1024])
    swT = ext("sgu_wT", [2, 4, 128, 128]); sbT = ext("sgu_bT", [2, 128, 4])
    csq = ext("csq", [T, 32]); csk = ext("csk", [SEQ + L, 32])
    ident_d = ext("ident", [128, 128]); g.ident_d = ident_d.ap()
    gcst = ext("gcst", [2, 64, 320]); mask = ext("mask", [128, 16])
    TW = ext("TW", [128, 64, 256]); FC = ext("FC", [128, 512]); W3 = ext("W3", [128, 256]); TWc = ext("TWc", [128, 2, 512])
    xout = nc.dram_tensor("xout", [Q, D], F32, kind="ExternalOutput")
    ML = g.dram("ML", [128, 3 * D]); MS = g.dram("MS", [2, 3 * D]); MALL = g.dram("MALL", [8, 3 * D])
    X = [g.dram(f"X{i}", [T, D]) for i in range(2)]
    Z = g.dram("Z", [T, 2560])
    QP = g.dram("QP", [T, 768])
    KSIN = [g.dram(f"KSIN{p}", [Q // 2, 160]) for p in range(2)]
    KSG = [g.dram(f"KSG{p}", [4 * Q // 2, 160]) for p in range(2)]
    KSA = g.dram("KSA", [SEQ + L, 160])
    OML = g.dram("OML", [T, 512]); OG = g.dram("OG", [T, 512])
    SEGL = g.dram("SEGL", [64, 8 * 129]); SEGA = g.dram("SEGA", [256, 8 * 129])
    FPR = [512, 512, 512, 512, 256]
    FIN = [g.dram(f"FIN{p}", [FPR[p], 512]) for p in range(5)]
    FALL = [g.dram(f"FALL{p}", [4 * FPR[p], 512]) for p in range(5)]
    OPR = [2048, 2048, 2048, 2048, 256]
    FOUT = [g.dram(f"FOUT{p}", [OPR[p], 128]) for p in range(5)]
    FOALL = [g.dram(f"FOALL{p}", [4 * OPR[p], 128]) for p in range(5)]
    S = g.S
    rows = lambda i: slice(i * 128, (i + 1) * 128)
    HN = 3 * D // 2
    for hh in range(2):
        cs = slice(hh * HN, (hh + 1) * HN)
        emit_k1(g, lambda i: cin.ap()[rows(i), :], 1, D, HN, ada_w.ap()[:, cs],
                lambda i, cs=cs: ML.ap()[rows(i), cs], "silu", bias=ada_b.ap()[cs])
    g.phase()
    S.dma(lambda e: e.dma_start(out=MS.ap(), in_=ML.ap()[0:2, :]), w=["ms"], group="dc0")
    allgather(g, MALL, MS, "mall", r=["ms"])
    xcur = xin
    for l in range(depth):
        li = l // 2
        even = l % 2 == 0
        last = l == depth - 1
        Ml = MALL.ap()
        r0, r1 = 2 * l, 2 * l + 1
        mods = (Ml[r0, 0:D], Ml[r0, D:2 * D], Ml[r1, 0:D], Ml[r1, D:2 * D])
        vecs = (Ml[r0, 2 * D:3 * D], Ml[r1, 2 * D:3 * D], plg.ap()[l], plb.ap()[l])
        N = 2496 if even else 2560
        Zl = Z.ap()[:, 0:N]
        xap = xcur.ap()
        emit_k1(g, lambda i, xap=xap: xap[rows(i), :], T // 128, D, N, (ewi if even else owi).ap()[li],
                lambda i, Zl=Zl: Zl[rows(i), :], "ln", n_lat_tiles=Q // 128, mods=mods)
        Tk = Q if last else T
        xn = xout if last else X[l % 2]
        if even:
            emit_k1(g, lambda i, Zl=Zl: Zl[rows(i), 1568:1824], T // 128, 256, 768, wuq.ap()[li],
                    lambda i: QP.ap()[rows(i), :], "rms", gvec=qng.ap()[li])
            g.phase()
            S.dma(lambda e, Zl=Zl: e.dma_start(out=KSA.ap()[0:L, :], in_=Zl[Q:T, 1824:1984]), w=["ksa0"], group="dc1")
            HQ = Q // 2
            for p in range(2):
                S.dma(lambda e, Zl=Zl, p=p: e.dma_start(out=KSIN[p].ap(), in_=Zl[HQ * p:HQ * (p + 1), 1824:1984]),
                      w=[f"ksin{p}"], group=f"dc0{p}")
                allgather(g, KSG[p], KSIN[p], f"ksg{p}", r=[f"ksin{p}"])
                dst = KSA.ap()[L:L + SEQ, :].rearrange("(r h n) c -> h r n c", r=4, h=2)[p]
                S.dma(lambda e, p=p, dst=dst: e.dma_start(out=dst, in_=KSG[p].ap().rearrange("(r n) c -> r n c", r=4)),
                      r=[f"ksg{p}"], w=[f"ksa1{p}"], group=f"dc2{p}")
            emit_k3(g, Q, L, SEQ + L, QP.ap(), KSA.ap(), kng.ap()[li], wukv.ap()[li], csq.ap(), csk.ap(), OML.ap())
            emit_k4s(g, Zl, OG.ap(), SEGL, SEGA, gw2.ap()[li], gb.ap()[li], gcst.ap(), mask.ap())
            emit_k5(g, Tk, "even", Q // 128, xap, ewo.ap()[li], vecs, Zl, xn.ap(), og=OG.ap(), oml=OML.ap(),
                    gng=gng.ap()[li])
        else:
            g.phase()
            r0 = 0
            for p in range(5):
                S.dma(lambda e, Zl=Zl, p=p, r0=r0: e.dma_start(out=FIN[p].ap(), in_=Zl[r0:r0 + FPR[p], 0:512]),
                      w=[f"fin{p}"], group=f"dcf{p}")
                allgather(g, FALL[p], FIN[p], f"fall{p}", r=[f"fin{p}"])
                r0 += FPR[p]
            emit_k7(g, [a.ap() for a in FALL], [a.ap() for a in FOUT], TW.ap(), FC.ap(), W3.ap(), TWc.ap(), mask.ap())
            g.phase()
            for p in range(5):
                allgather(g, FOALL[p], FOUT[p], f"foall{p}")
            fo = [a.ap().rearrange("(g r) c -> r g c", g=4) for a in FOALL]
            emit_k5(g, Tk, "odd", Q // 128, xap, owo.ap()[li], vecs, Zl, xn.ap(),
                    fr_lat=lambda qq, i, fo=fo: fo[qq][128 * i:128 * (i + 1), :, :],
                    fr_ctx=lambda i, fo=fo: fo[4][128 * (i - Q // 128):128 * (i - Q // 128 + 1), :, :],
                    swT=swT.ap()[li], sbT=sbT.ap()[li], mask=mask.ap())
        xcur = xn
    g.finish()
    return nc


def make_inputs(x, c, ctx, c_ctx, ada_w, ada_b, post_ln_g, post_ln_b, even_w_in, gla_w2, gla_b, gla_norm_g,
                mla_q_norm_g, mla_w_uq, mla_kv_norm_g, mla_w_ukv, even_w_out, odd_w_in, sgu_w, sgu_b, odd_w_out,
                rope_tables, fnet_consts):
    f32 = np.float32
    cc = lambda a: np.ascontiguousarray(a, dtype=f32)
    shared = dict(post_ln_g=cc(post_ln_g), post_ln_b=cc(post_ln_b),
                  even_w_in=cc(even_w_in), even_w_out=cc(even_w_out), odd_w_in=cc(odd_w_in), odd_w_out=cc(odd_w_out),
                  gla_w2=cc(gla_w2), gla_b=cc(gla_b), gla_norm_g=cc(gla_norm_g), mla_q_norm_g=cc(mla_q_norm_g),
                  mla_w_uq=cc(mla_w_uq), mla_kv_norm_g=cc(mla_kv_norm_g), mla_w_ukv=cc(mla_w_ukv),
                  sgu_wT=cc(np.transpose(sgu_w, (0, 1, 3, 2))), sgu_bT=cc(np.transpose(sgu_b, (0, 2, 1))),
                  csk=rope_tables(np.concatenate([-np.ones(L, int), np.arange(SEQ)])),
                  ident=np.eye(128, dtype=f32), gcst=gla_consts2(), **fnet_consts())
    maps = []
    for j in range(NCORES):
        b, i = j // 4, j % 4
        m = dict(shared)
        m["xin"] = cc(np.concatenate([x[b, Q * i:Q * (i + 1)], ctx[b]], 0))
        cin = np.zeros((128, D), f32)
        cin[0] = c[b]; cin[1] = c_ctx
        m["cin"] = cin
        m["ada_w"] = cc(ada_w[i])
        m["ada_b"] = cc(ada_b[i])
        m["csq"] = rope_tables(np.concatenate([np.arange(Q) + Q * i, -np.ones(L, int)]))
        mk = np.zeros((128, 16), f32)
        mk[:, i] = 1.0
        for jj in range(4):
            mk[:, 4 + jj] = 1.0 if jj < i else 0.0
            mk[:, 8 + jj] = 1.0 if jj > i else 0.0
        m["mask"] = mk
        maps.append(m)
    return maps

def rope_tables(pos):
    pos = np.asarray(pos)
    row = (pos // 64).astype(np.float32)
    col = (pos % 64).astype(np.float32)
    inv = (10000.0 ** (-np.arange(8, dtype=np.float32) / 8)).astype(np.float32)
    ang = np.concatenate([row[:, None] * inv, col[:, None] * inv], -1).astype(np.float32)
    c = np.cos(ang).astype(np.float32)
    s = np.sin(ang).astype(np.float32)
    ident = pos < 0
    c[ident] = 1.0
    s[ident] = 0.0
    return np.concatenate([c, s], -1).astype(np.float32)


def fnet_consts():
    n1 = np.arange(128)[:, None, None].astype(np.float64)
    n2 = np.arange(64)[None, :, None].astype(np.float64)
    k1 = np.arange(128)[None, None, :].astype(np.float64)
    ang = 2 * np.pi * k1 * (64 * n1 + n2) / 8192.0
    TW = np.concatenate([np.cos(ang), -np.sin(ang)], -1).astype(np.float32)
    c = np.arange(128)[:, None].astype(np.float64)
    cp = np.arange(128)[None, :].astype(np.float64)
    a = 2 * np.pi * c * cp / 128.0
    Cc, Sc = np.cos(a), np.sin(a)
    FC = np.concatenate([Cc, -Sc, Sc, Cc], -1).astype(np.float32)
    W3 = np.zeros((64, 2, 2, 2, 64), np.float64)
    n2v = np.arange(64)[:, None]
    k2v = np.arange(64)[None, :]
    a3 = 2 * np.pi * n2v * k2v / 64.0
    for aa in range(2):
        W3[:, aa, 0, aa, :] = np.cos(a3)
        W3[:, aa, 1, aa, :] = np.sin(a3)
    W3 = W3.reshape(128, 256).astype(np.float32)
    n = np.arange(256)[:, None].astype(np.float64)
    k = np.arange(256)[None, :].astype(np.float64)
    ac = 2 * np.pi * n * k / 256.0
    TWc = np.concatenate([np.cos(ac), -np.sin(ac)], -1).reshape(2, 128, 512).transpose(1, 0, 2)
    TWc = np.ascontiguousarray(TWc).astype(np.float32)
    return dict(TW=TW, FC=FC, W3=W3, TWc=TWc)


_NC = {}


def kernel(x, c, ctx, c_ctx, ada_w, ada_b, post_ln_g, post_ln_b, even_w_in, gla_w2, gla_b, gla_norm_g,
           mla_q_norm_g, mla_w_uq, mla_kv_norm_g, mla_w_ukv, even_w_out, odd_w_in, sgu_w, sgu_b, odd_w_out):
    if "nc" not in _NC:
        _NC["nc"] = build_fused(4)
    maps = make_inputs(np.asarray(x), np.asarray(c), np.asarray(ctx), np.asarray(c_ctx), np.asarray(ada_w),
                       np.asarray(ada_b), np.asarray(post_ln_g), np.asarray(post_ln_b), np.asarray(even_w_in),
                       np.asarray(gla_w2), np.asarray(gla_b), np.asarray(gla_norm_g), np.asarray(mla_q_norm_g),
                       np.asarray(mla_w_uq), np.asarray(mla_kv_norm_g), np.asarray(mla_w_ukv), np.asarray(even_w_out),
                       np.asarray(odd_w_in), np.asarray(sgu_w), np.asarray(sgu_b), np.asarray(odd_w_out),
                       rope_tables, fnet_consts)
    res = run_bass_kernel_spmd(_NC["nc"], maps, core_ids=list(range(NCORES)))
    out = np.empty((2, SEQ, D), np.float32)
    for j in range(NCORES):
        out[j // 4, (j % 4) * Q:(j % 4 + 1) * Q] = res.results[j]["xout"]
    return out
```

```python
import contextlib
import numpy as np
import concourse.bass as bass
import concourse.mybir as mybir
from concourse.bass_utils import run_bass_kernel_spmd

F32 = mybir.dt.float32
BF16 = mybir.dt.bfloat16
AF = mybir.ActivationFunctionType
ALU = mybir.AluOpType
AX = mybir.AxisListType
ALPHA = 8 ** 0.25
MLA_SCALE = 96 ** -0.5
ARENA_F32 = 52900


class Sched:
    def __init__(self, nc):
        self.nc = nc
        self.ops = []
        self.last_w = {}
        self.readers = {}
        self.stack = contextlib.ExitStack()
        self.bar = set()
        self.pending = {}

    def barrier(self):
        last = {}
        for i, op in enumerate(self.ops):
            k = ("dma", op["dma"]) if op["dma"] is not None else ("eng", op["eng"])
            last[k] = i
        self.bar = set(last.values())
        self.pending = {e: True for e in ["pe", "act", "dve", "pool", "sp"]}
        self.last_w = {}
        self.readers = {}

    muted = False

    def add(self, eng, fn, r=(), w=(), dma=None, inc=16):
        if self.muted:
            return -1
        idx = len(self.ops)
        deps = set()
        for k in r:
            if k in self.last_w:
                deps.add(self.last_w[k])
        for k in w:
            if k in self.last_w:
                deps.add(self.last_w[k])
            for x in self.readers.get(k, ()):
                deps.add(x)
        if self.pending.get(eng):
            deps |= self.bar
            self.pending[eng] = False
        deps.discard(idx)
        self.ops.append(dict(eng=eng, fn=fn, deps=deps, dma=dma, inc=inc))
        for k in r:
            self.readers.setdefault(k, []).append(idx)
        for k in w:
            self.last_w[k] = idx
            self.readers[k] = []
        return idx

    def pe(self, fn, r=(), w=()):
        return self.add("pe", fn, r, w)

    def act(self, fn, r=(), w=()):
        return self.add("act", fn, r, w)

    def dve(self, fn, r=(), w=()):
        return self.add("dve", fn, r, w)

    def pool(self, fn, r=(), w=()):
        return self.add("pool", fn, r, w)

    def dma(self, fn, r=(), w=(), group=None, eng="sp", inc=16):
        assert group is not None
        return self.add(eng, fn, r, w, dma=group, inc=inc)

    def emit(self):
        nc = self.nc
        ops = self.ops
        n = len(ops)
        needs_signal = [False] * n
        for i, op in enumerate(ops):
            keep = set()
            for d in op["deps"]:
                dop = ops[d]
                if dop["dma"] is None and dop["eng"] == op["eng"] and op["eng"] == "pe":
                    continue
                keep.add(d)
                needs_signal[d] = True
            op["deps"] = keep
        engs = ["pe", "act", "dve", "pool", "sp"]
        sems = {e: self.stack.enter_context(nc.semaphore(f"s_{e}")) for e in engs}
        cnt = {e: 0 for e in engs}
        groups = {}
        gcnt = {}
        for i, op in enumerate(ops):
            if op["dma"] is not None:
                g = op["dma"]
                if g not in groups:
                    groups[g] = self.stack.enter_context(nc.semaphore(f"d_{len(groups)}"))
                    gcnt[g] = 0
                op["sem"] = groups[g]
                gcnt[g] += op["inc"] * getattr(op["fn"], "ndma", 1)
                op["val"] = gcnt[g]
            elif needs_signal[i]:
                cnt[op["eng"]] += 1
                op["sem"] = sems[op["eng"]]
                op["val"] = cnt[op["eng"]]
        print("sched: ops", n, "dma groups", len(groups), "sem counts", cnt, flush=True)
        final = dict((g, (groups[g], gcnt[g])) for g in groups)

        def stream(ename):
            def body(eng):
                known = {}
                for i, op in enumerate(ops):
                    if op["eng"] != ename:
                        continue
                    for d in sorted(op["deps"]):
                        dop = ops[d]
                        s, v = dop["sem"], dop["val"]
                        if known.get(id(s), 0) < v:
                            eng.wait_ge(s, v)
                            known[id(s)] = v
                    ins = op["fn"](eng)
                    if op["dma"] is not None:
                        if not isinstance(ins, (list, tuple)):
                            ins = [ins]
                        assert len(ins) == getattr(op["fn"], "ndma", 1)
                        for x in ins:
                            x.then_inc(op["sem"], op["inc"])
                    elif needs_signal[i]:
                        ins.then_inc(op["sem"], 1)
                if ename == "sp":
                    for g, (s, v) in final.items():
                        if known.get(id(s), 0) < v:
                            eng.wait_ge(s, v)
            return body

        with nc.Block() as block:
            block.tensor(stream("pe"))
            block.scalar(stream("act"))
            block.vector(stream("dve"))
            block.gpsimd(stream("pool"))
            block.sync(stream("sp"))

    def close(self):
        self.stack.close()


class G:
    def __init__(self, nc):
        self.nc = nc
        self.S = Sched(nc)
        self.arena = self.S.stack.enter_context(nc.sbuf_tensor("arena", [128, ARENA_F32], F32))
        self.banks = [self.S.stack.enter_context(nc.psum_tensor(f"bank{i}", [128, 512], F32)) for i in range(8)]
        self.off = 0
        self.nd = 0

    nphase = 0
    max_phase = 10 ** 9

    def phase(self):
        self.nphase += 1
        if self.nphase > self.max_phase:
            self.S.muted = True
        self.S.barrier()
        self.off = 0

    def sb(self, shape, dtype=F32, name=None):
        P = shape[0]
        n = int(np.prod(shape[1:]))
        words = n if dtype == F32 else (n + 1) // 2
        o = self.off
        self.off += words
        assert self.off <= ARENA_F32, ("SBUF arena overflow", self.off)
        ap = self.arena[0:P, o:o + words]
        if dtype != F32:
            ap = ap.bitcast(dtype)[:, 0:n]
        if len(shape) == 3:
            ap = ap.rearrange("p (a b) -> p a b", a=shape[1], b=shape[2])
        elif len(shape) == 4:
            ap = ap.rearrange("p (a b c) -> p a b c", a=shape[1], b=shape[2], c=shape[3])
        return ap

    def ps(self, bank, shape, dtype=F32):
        P = shape[0]
        n = int(np.prod(shape[1:]))
        ap = self.banks[bank][0:P, :]
        if dtype != F32:
            ap = ap.bitcast(dtype)
        ap = ap[:, 0:n]
        if len(shape) == 3:
            ap = ap.rearrange("p (a b) -> p a b", a=shape[1], b=shape[2])
        return ap

    def dram(self, name, shape, kind="Internal"):
        return self.nc.dram_tensor(name, list(shape), F32, kind=kind)

    def finish(self):
        self.S.emit()
        self.S.close()


def load_ident(g, tag="id"):
    S = g.S
    identf = g.sb([128, 128], F32)
    ident = g.sb([128, 128], BF16)
    S.dma(lambda e: e.dma_start(out=identf[:], in_=g.ident_d), w=["identf"], group="identf")
    S.dve(lambda e: e.tensor_copy(ident[:], identf[:]), r=["identf"], w=["ident"])
    return identf, ident


def dcopy(g, dst, src, name, grp="dcopy"):
    g.S.dma(lambda e: e.dma_start(out=dst, in_=src), w=[name], group=grp)


def allgather(g, dst, src, name, r=()):
    groups = [[0, 1, 2, 3], [4, 5, 6, 7]]
    g.S.dma(lambda e: e.collective_compute("AllGather", ALU.bypass, replica_groups=groups,
                                            ins=[src.ap().opt()], outs=[dst.ap().opt()]),
            r=list(r), w=[name], group="cc", eng="pool", inc=1)


def emit_k1(g, xsrc, nt, K, N, w, z, mode, n_lat_tiles=None, mods=None, gvec=None, bias=None, eps=1e-6):
    S = g.S
    g.phase()
    kc = K // 128
    nch = (N + 511) // 512
    identf, ident = load_ident(g)
    wbf = g.sb([128, kc, N], BF16)
    wst = [g.sb([128, N], F32) for i in range(2)]
    if mode == "ln":
        modt = g.sb([128, 4, K], F32)
        for a in range(4):
            S.dma(lambda e, a=a: e.dma_start(out=modt[:, a, :], in_=mods[a].partition_broadcast(128)),
                  w=["modt"], group="modt")
        for a in (1, 3):
            S.dve(lambda e, a=a: e.tensor_scalar_add(modt[:, a, :], modt[:, a, :], 1.0), r=["modt"], w=["modt"])
    elif mode == "rms":
        gt = g.sb([128, K], F32)
        S.dma(lambda e: e.dma_start(out=gt[:], in_=gvec.partition_broadcast(128)), w=["gt"], group="gt")
    else:
        bt = g.sb([128, N], F32)
        S.dma(lambda e: e.dma_start(out=bt[:], in_=bias.partition_broadcast(128)), w=["bt"], group="gt")
    for k in range(kc):
        s = k % 2
        S.dma(lambda e, k=k, s=s: e.dma_start(out=wst[s][:], in_=w[k * 128:(k + 1) * 128, :]),
              w=[f"wst{s}"], group=f"wst{s}")
        if k % 2 == 0:
            S.act(lambda e, k=k, s=s: e.copy(wbf[:, k, :], wst[s][:]), r=[f"wst{s}"], w=[f"wbf{k}"])
        else:
            S.dve(lambda e, k=k, s=s: e.tensor_copy(wbf[:, k, :], wst[s][:]), r=[f"wst{s}"], w=[f"wbf{k}"])
    NX = 2
    xt = [g.sb([128, K], F32) for i in range(NX)]
    xn = [g.sb([128, K], F32) for i in range(NX)]
    hb = [g.sb([128, K], BF16) for i in range(NX)]
    hT = [g.sb([128, kc, 128], BF16) for i in range(NX)]
    st = [g.sb([128, 8, 6], F32) for i in range(NX)]
    mv = [g.sb([128, 4], F32) for i in range(NX)]
    zt = [g.sb([128, N], F32) for i in range(NX)]
    tp = [g.ps(i, [128, kc, 128], BF16) for i in range(2)]
    zp = [g.ps(2 + i, [128, 512], F32) for i in range(4)]
    wkeys = [f"wbf{k}" for k in range(kc)]
    def emit_load(i):
        s = i % NX
        S.dma(lambda e, i=i, s=s: e.dma_start(out=xt[s][:], in_=xsrc(i)), w=[f"xt{s}"], group=f"xt{s}")

    zst = dict(zpi=0)

    def stageE(i):
        s = i % NX
        if i + 1 < nt:
            emit_load(i + 1)
        if mode == "ln":
            nsub = max(1, K // 512)
            fs = K // nsub
            for j in range(nsub):
                S.dve(lambda e, s=s, j=j, fs=fs: e.bn_stats(st[s][:, j, :], xt[s][:, j * fs:(j + 1) * fs]),
                      r=[f"xt{s}"], w=[f"st{s}"])
            S.dve(lambda e, s=s, nsub=nsub: e.bn_aggr(mv[s][:, 0:2], st[s][:, 0:nsub, :]), r=[f"st{s}"], w=[f"mv{s}"])
            S.dve(lambda e, s=s: e.tensor_scalar_add(mv[s][:, 3:4], mv[s][:, 1:2], eps), r=[f"mv{s}"], w=[f"mv{s}"])
            S.act(lambda e, s=s: e.sqrt(mv[s][:, 3:4], mv[s][:, 3:4]), r=[f"mv{s}"], w=[f"mv{s}"])
            S.dve(lambda e, s=s: e.reciprocal(mv[s][:, 2:3], mv[s][:, 3:4]), r=[f"mv{s}"], w=[f"mv{s}"])
            S.dve(lambda e, s=s: e.tensor_scalar(xn[s][:], xt[s][:], mv[s][:, 0:1], mv[s][:, 2:3],
                                                 ALU.subtract, ALU.mult),
                  r=[f"xt{s}", f"mv{s}"], w=[f"xn{s}"])
            a = 0 if (n_lat_tiles is None or i < n_lat_tiles) else 2
            S.pool(lambda e, s=s, a=a: e.tensor_tensor(xn[s][:], xn[s][:], modt[:, a + 1, :], ALU.mult),
                   r=[f"xn{s}", "modt"], w=[f"xn{s}"])
            S.pool(lambda e, s=s, a=a: e.tensor_tensor(hb[s][:], xn[s][:], modt[:, a, :], ALU.add),
                   r=[f"xn{s}", "modt"], w=[f"hb{s}"])
        elif mode == "silu":
            S.act(lambda e, s=s: e.activation(hb[s][:], xt[s][:], AF.Silu), r=[f"xt{s}"], w=[f"hb{s}"])
        else:
            S.act(lambda e, s=s: e.activation(xn[s][:], xt[s][:], AF.Square, accum_out=mv[s][:, 0:1]),
                  r=[f"xt{s}"], w=[f"xn{s}", f"mv{s}"])
            S.dve(lambda e, s=s: e.tensor_scalar(mv[s][:, 1:2], mv[s][:, 0:1], 1.0 / K, eps, ALU.mult, ALU.add),
                  r=[f"mv{s}"], w=[f"mv{s}"])
            S.act(lambda e, s=s: e.sqrt(mv[s][:, 3:4], mv[s][:, 1:2]), r=[f"mv{s}"], w=[f"mv{s}"])
            S.dve(lambda e, s=s: e.reciprocal(mv[s][:, 2:3], mv[s][:, 3:4]), r=[f"mv{s}"], w=[f"mv{s}"])
            S.dve(lambda e, s=s: e.scalar_tensor_tensor(hb[s][:], xt[s][:], mv[s][:, 2:3], gt[:],
                                                        ALU.mult, ALU.mult),
                  r=[f"xt{s}", f"mv{s}", "gt"], w=[f"hb{s}"])

    def stageT(i):
        s = i % NX
        t = i % 2
        for k in range(kc):
            S.pe(lambda e, s=s, t=t, k=k: e.transpose(tp[t][:, k, :], hb[s][:, k * 128:(k + 1) * 128], ident[:]),
                 r=[f"hb{s}", "ident"], w=[f"tp{t}"])
        S.act(lambda e, s=s, t=t: e.copy(hT[s][:], tp[t][:]), r=[f"tp{t}"], w=[f"hT{s}"])

    def stageM(i):
        s = i % NX
        for c in range(nch):
            c0, c1 = c * 512, min(N, (c + 1) * 512)
            p = zst["zpi"] % 4
            zst["zpi"] += 1
            for k in range(kc):
                S.pe(lambda e, s=s, p=p, k=k, c0=c0, c1=c1: e.matmul(
                    zp[p][:, 0:c1 - c0], hT[s][:, k, :], wbf[:, k, c0:c1], start=(k == 0), stop=(k == kc - 1)),
                    r=[f"hT{s}", wkeys[k]], w=[f"zp{p}"])
            if mode == "silu":
                S.dve(lambda e, s=s, p=p, c0=c0, c1=c1: e.tensor_tensor(zt[s][:, c0:c1], zp[p][:, 0:c1 - c0], bt[:, c0:c1], ALU.add),
                      r=[f"zp{p}", "bt"], w=[f"zt{s}.{c}"])
            elif c % 2 == 0:
                S.dve(lambda e, s=s, p=p, c0=c0, c1=c1: e.tensor_copy(zt[s][:, c0:c1], zp[p][:, 0:c1 - c0]),
                      r=[f"zp{p}"], w=[f"zt{s}.{c}"])
            else:
                S.act(lambda e, s=s, p=p, c0=c0, c1=c1: e.copy(zt[s][:, c0:c1], zp[p][:, 0:c1 - c0]),
                      r=[f"zp{p}"], w=[f"zt{s}.{c}"])
            S.dma(lambda e, i=i, s=s, c0=c0, c1=c1: e.dma_start(out=z(i)[:, c0:c1], in_=zt[s][:, c0:c1]),
                  r=[f"zt{s}.{c}"], w=[f"zout{s}.{c}"], group=f"zo{s}.{c}", eng=("act" if c % 2 == 0 else "sp"))


    emit_load(0)
    stageE(0)
    if nt > 1:
        stageE(1)
    stageT(0)
    for i in range(nt):
        if i + 2 < nt:
            stageE(i + 2)
        if i + 1 < nt:
            stageT(i + 1)
        stageM(i)


def emit_k3(g, NQL, NQC, NK, q, ksa, kvg, wukv, csq, csk, out, nheads=8, eps=1e-6):
    kr = ksa[:, 128:160]
    S = g.S
    g.phase()
    NQ = NQL + NQC
    nqt = NQ // 128
    nkt = NK // 128
    identf, ident = load_ident(g)
    KH = (nkt + 1) // 2
    ckst = g.sb([128, KH, 128], F32)
    cknT = g.sb([128, nkt * 128], BF16)
    wst = g.sb([128, 1024], F32)
    wb = g.sb([128, 1024], BF16)
    gt = g.sb([128, 128], F32)
    ss = g.sb([128, nkt], F32)
    rstd = g.sb([128, nkt], F32)
    junk = g.sb([128, 128], F32)
    pk = g.ps(7, [128, 512], F32)
    kpad = g.sb([128, nkt, 128], BF16)
    vx = [g.sb([128, nkt, 65], BF16) for i in range(2)]
    kT = [g.sb([128, nkt * 128], BF16) for i in range(2)]
    qT = [g.sb([128, nqt * 128], BF16) for i in range(2)]
    qh = g.sb([128, nqt, 96], F32)
    qpad = g.sb([128, nqt, 128], BF16)
    krl = g.sb([128, nkt, 32], F32)
    cskt = g.sb([128, nkt, 32], F32)
    csqt = g.sb([128, nqt, 32], F32)
    tk = [g.sb([128, nkt, 16], F32) for i in range(2)]
    tq = [g.sb([128, nqt, 16], F32) for i in range(2)]
    pT = [g.sb([128, 512], BF16) for i in range(3)]
    oTs = [g.sb([65, 512], F32) for i in range(2)]
    ost = [g.sb([128, 64], F32) for i in range(4)]
    rc = [g.sb([128, 1], F32) for i in range(4)]
    sTp = [g.ps(i, [128, 512], F32) for i in range(3)]
    oTp = [g.ps(3 + i, [65, 512], F32) for i in range(2)]
    tpk = g.ps(5, [128, 8, 128], BF16)
    tpo = g.ps(6, [128, 65], F32)

    S.pool(lambda e: e.memset(kpad[:], 0.0), w=["kpad"])
    S.pool(lambda e: e.memset(qpad[:], 0.0), w=["qpad"])
    for i in range(2):
        S.pool(lambda e, i=i: e.memset(vx[i][:], 1.0), w=[f"vx{i}"])
    S.dma(lambda e: e.dma_start(out=krl[:], in_=kr.rearrange("(t p) c -> p t c", p=128)), w=["krl"], group="krl")
    S.dma(lambda e: e.dma_start(out=cskt[:], in_=csk.rearrange("(t p) c -> p t c", p=128)), w=["cskt"], group="cskt")
    S.dma(lambda e: e.dma_start(out=csqt[:], in_=csq.rearrange("(t p) c -> p t c", p=128)), w=["csqt"], group="csqt")

    def rope(eng_a, eng_b, src, cs, tmp, dst, keys_r, key_tmp, key_dst, xo):
        x1 = lambda: src[:, :, xo:xo + 16]
        x2 = lambda: src[:, :, xo + 16:xo + 32]
        c = lambda: cs[:, :, 0:16]
        sn = lambda: cs[:, :, 16:32]
        S.add(eng_a, lambda e: e.tensor_tensor(tmp[0][:], x1(), c(), ALU.mult), r=keys_r, w=[key_tmp + "0"])
        S.add(eng_b, lambda e: e.tensor_tensor(tmp[1][:], x2(), sn(), ALU.mult), r=keys_r, w=[key_tmp + "1"])
        S.add(eng_a, lambda e: e.tensor_tensor(dst[:, :, 64:80], tmp[0][:], tmp[1][:], ALU.subtract),
              r=[key_tmp + "0", key_tmp + "1"], w=[key_dst])
        S.add(eng_a, lambda e: e.tensor_tensor(tmp[0][:], x1(), sn(), ALU.mult), r=keys_r, w=[key_tmp + "0"])
        S.add(eng_b, lambda e: e.tensor_tensor(tmp[1][:], x2(), c(), ALU.mult), r=keys_r, w=[key_tmp + "1"])
        S.add(eng_a, lambda e: e.tensor_tensor(dst[:, :, 96:112], tmp[0][:], tmp[1][:], ALU.add),
              r=[key_tmp + "0", key_tmp + "1"], w=[key_dst])

    rope("dve", "pool", krl, cskt, tk, kpad, ["krl", "cskt"], "tk", "kpad", 0)

    for g0 in range(0, nkt, 8):
        g1 = min(nkt, g0 + 8)
        for t in range(g0, g1):
            S.pe(lambda e, t=t, g0=g0: e.transpose(tpk[:, t - g0, :], kpad[:, t, :], ident[:]), r=["kpad", "ident"], w=["tpk"])
        for i in range(2):
            S.dve(lambda e, g0=g0, g1=g1, i=i: e.tensor_copy(
                kT[i][64:128, g0 * 128:g1 * 128], tpk[64:128, 0:g1 - g0, :].rearrange("p a b -> p (a b)")),
                r=["tpk"], w=[f"kT{i}"])
    S.dma(lambda e: e.dma_start(out=gt[:], in_=kvg.partition_broadcast(128)), w=["gt"], group="gt")
    S.dma(lambda e: e.dma_start(out=wst[:], in_=wukv), w=["wst"], group="wst0")
    S.dve(lambda e: e.tensor_copy(wb[:], wst[:]), r=["wst"], w=["wb"])
    for half in range(2):
        t0 = half * KH
        t1 = min(nkt, t0 + KH)
        S.dma(lambda e, t0=t0, t1=t1: e.dma_start(
            out=ckst[:, 0:t1 - t0, :], in_=ksa[t0 * 128:t1 * 128, 0:128].rearrange("(t p) c -> p t c", p=128)),
            w=["ckst"], group="kvh")
        for t in range(t0, t1):
            S.act(lambda e, t=t, t0=t0: e.activation(junk[:], ckst[:, t - t0, :], AF.Square, accum_out=ss[:, t:t + 1]),
                  r=["ckst"], w=["junk", "ss"])
        S.dve(lambda e, t0=t0, t1=t1: e.tensor_scalar(ss[:, t0:t1], ss[:, t0:t1], 1.0 / 128, eps, ALU.mult, ALU.add),
              r=["ss"], w=["ss"])
        S.act(lambda e, t0=t0, t1=t1: e.sqrt(ss[:, t0:t1], ss[:, t0:t1]), r=["ss"], w=["ss"])
        S.dve(lambda e, t0=t0, t1=t1: e.reciprocal(rstd[:, t0:t1], ss[:, t0:t1]), r=["ss"], w=["rstd"])
        for t in range(t0, t1):
            S.dve(lambda e, t=t, t0=t0: e.scalar_tensor_tensor(kpad[:, t, :], ckst[:, t - t0, :], rstd[:, t:t + 1], gt[:],
                                                               ALU.mult, ALU.mult),
                  r=["ckst", "rstd", "gt"], w=["kpad"])
    for g0 in range(0, nkt, 8):
        g1 = min(nkt, g0 + 8)
        for t in range(g0, g1):
            S.pe(lambda e, t=t, g0=g0: e.transpose(tpk[:, t - g0, :], kpad[:, t, :], ident[:]), r=["kpad", "ident"], w=["tpk"])
        S.dve(lambda e, g0=g0, g1=g1: e.tensor_copy(cknT[:, g0 * 128:g1 * 128],
                                                    tpk[:, 0:g1 - g0, :].rearrange("p a b -> p (a b)")),
              r=["tpk"], w=["cknT"])

    chunks = []
    for c0 in range(0, NQL, 512):
        chunks.append((c0, min(512, NQL - c0), 0, nkt))
    if NQC:
        chunks.append((NQL, NQC, 0, 2))
    sti = 0
    oti = 0
    osti = 0
    def prologue(h):
        hb = h % 2
        for c0 in range(0, NK, 512):
            n = min(512, NK - c0)
            S.pe(lambda e, c0=c0, n=n: e.matmul(pk[0:64, 0:n], wb[:, h * 128:h * 128 + 64], cknT[:, c0:c0 + n],
                                                start=True, stop=True), r=["wb", "cknT"], w=["pk"])
            S.dve(lambda e, c0=c0, n=n: e.tensor_copy(kT[hb][0:64, c0:c0 + n], pk[0:64, 0:n]), r=["pk"], w=[f"kT{hb}"])
        for g0 in range(0, nkt, 8):
            g1 = min(nkt, g0 + 8)
            for t in range(g0, g1):
                S.pe(lambda e, t=t, g0=g0: e.matmul(pk[:, (t - g0) * 64:(t - g0 + 1) * 64], cknT[:, t * 128:(t + 1) * 128],
                                                    wb[:, h * 128 + 64:h * 128 + 128], start=True, stop=True),
                     r=["wb", "cknT"], w=["pk"])
            S.dve(lambda e, g0=g0, g1=g1: e.tensor_copy(
                vx[hb][:, g0:g1, 0:64], pk[:, 0:(g1 - g0) * 64].rearrange("p (a b) -> p a b", b=64)),
                r=["pk"], w=[f"vx{hb}"])
        S.dma(lambda e, h=h: e.dma_start(out=qh[:], in_=q[:, h * 96:(h + 1) * 96].rearrange("(t p) c -> p t c", p=128)),
              w=["qh"], group="qh")
        S.pool(lambda e: e.tensor_copy(qpad[:, :, 0:64], qh[:, :, 0:64]), r=["qh"], w=["qpad"])
        rope("pool", "dve", qh, csqt, tq, qpad, ["qh", "csqt"], "tq", "qpad", 64)
        for g0 in range(0, nqt, 8):
            g1 = min(nqt, g0 + 8)
            for t in range(g0, g1):
                S.pe(lambda e, t=t, g0=g0: e.transpose(tpk[:, t - g0, :], qpad[:, t, :], ident[:]),
                     r=["qpad", "ident"], w=["tpk"])
            S.dve(lambda e, g0=g0, g1=g1, hb=hb: e.tensor_copy(
                qT[hb][:, g0 * 128:g1 * 128], tpk[:, 0:g1 - g0, :].rearrange("p a b -> p (a b)")),
                r=["tpk"], w=[f"qT{hb}"])

    st = dict(sti=0, oti=0, osti=0)
    LAG = 2

    def mainloop(h):
        hb = h % 2
        its = []
        for (q0, qn, k0, k1) in chunks:
            op = st["oti"] % 2
            st["oti"] += 1
            for kt in range(k0, k1):
                its.append((q0, qn, k0, k1, kt, op))

        def emit_qk(n):
            q0, qn, k0, k1, kt, op = its[n]
            sp = st["sti"] % 3
            st["sti"] += 1
            its[n] = its[n] + (sp,)
            S.pe(lambda e: e.matmul(sTp[sp][:, 0:qn], kT[hb][:, kt * 128:(kt + 1) * 128], qT[hb][:, q0:q0 + qn],
                                    start=True, stop=True), r=[f"kT{hb}", f"qT{hb}"], w=[f"sTp{sp}"])
            S.act(lambda e: e.activation(pT[sp][:, 0:qn], sTp[sp][:, 0:qn], AF.Exp, scale=MLA_SCALE),
                  r=[f"sTp{sp}"], w=[f"pT{sp}"])

        def emit_pv(n):
            q0, qn, k0, k1, kt, op, sp = its[n]
            S.pe(lambda e: e.matmul(oTp[op][:, 0:qn], vx[hb][:, kt, :], pT[sp][:, 0:qn], start=(kt == k0), stop=(kt == k1 - 1)),
                 r=[f"vx{hb}", f"pT{sp}"], w=[f"oTp{op}"])
            if kt != k1 - 1:
                return
            S.dve(lambda e: e.tensor_copy(oTs[op][:, 0:qn], oTp[op][:, 0:qn]), r=[f"oTp{op}"], w=[f"oTs{op}"])
            for j in range(qn // 128):
                os_ = st["osti"] % 4
                st["osti"] += 1
                S.pe(lambda e, j=j: e.transpose(tpo[:], oTs[op][:, j * 128:(j + 1) * 128], identf[0:65, 0:65]),
                     r=[f"oTs{op}", "identf"], w=["tpo"])
                S.dve(lambda e, os_=os_: e.reciprocal(rc[os_][:], tpo[:, 64:65]), r=["tpo"], w=[f"rc{os_}"])
                S.dve(lambda e, os_=os_: e.tensor_scalar(ost[os_][:], tpo[:, 0:64], rc[os_][:, 0:1], None, ALU.mult),
                      r=["tpo", f"rc{os_}"], w=[f"ost{os_}"])
                r0 = q0 + j * 128
                S.dma(lambda e, os_=os_, r0=r0: e.dma_start(out=out[r0:r0 + 128, h * 64:(h + 1) * 64], in_=ost[os_][:]),
                      r=[f"ost{os_}"], w=[f"oo{os_}"], group=f"oo{os_}")

        nI = len(its)
        for n in range(nI):
            emit_qk(n)
            if n >= LAG:
                emit_pv(n - LAG)
            if n == nI // 2 and h + 1 < nheads:
                prologue(h + 1)
        for n in range(max(0, nI - LAG), nI):
            emit_pv(n)

    prologue(0)
    for h in range(nheads):
        mainloop(h)


def gla_consts2():
    s = np.arange(64)[:, None]
    t = np.arange(64)[None, :]
    out = np.zeros((2, 64, 320), np.float32)
    U = (s <= t).astype(np.float32)
    out[0, :, 0:64] = U - (s <= 31)
    out[0, :, 64:128] = U
    out[0, :, 128:192] = (s > t)
    out[0, :, 192:256] = U
    Ub = (s >= t).astype(np.float32)
    out[1, :, 0:64] = Ub - (s >= 32)
    out[1, :, 64:128] = Ub
    out[1, :, 128:192] = (s < t)
    out[1, :, 192:256] = Ub
    out[:, 0, 256:320] = 1.0
    return out


def emit_k4s(g, Z, OG, segloc, segall, w2, bb, cst_d, mask):
    S = g.S
    g.phase()
    NLC, NCC = 32, 4
    ZH = ["Zh0", "Zh1", "Zh2", "Zh3"]
    NCH = NLC + NCC
    identf, ident = load_ident(g)
    cst = g.sb([64, 2, 320], F32)
    maskb = g.sb([64, 2, 64], BF16)
    w2t = g.sb([16, 2, 256], F32)
    bbt = g.sb([1, 2, 256], F32)
    mk = g.sb([64, 16], F32)
    zh_off = g.off
    Zh = g.sb([64, NCH, 288], F32)
    vb = g.sb([64, NCH, 128], BF16)
    OLs = g.sb([64, NCH, 512], F32)
    qbP = g.sb([64, 8, NLC, 64], BF16)
    Sctx = g.sb([64, 8, 128], F32)
    SEG = g.sb([64, 8, 129], F32)
    Sib = g.sb([64, 8, 128], BF16)
    Wk = g.sb([64, 128], F32)
    cand = g.sb([64, 128], F32)
    diff = g.sb([64, 128], F32)
    NP = 7
    NPA, NPB = 5, 3
    mk_t = lambda shape, dt: [g.sb(shape, dt) for i in range(NP)]
    qTs = mk_t([64, 64], F32); kTs = mk_t([64, 64], F32); glTs = mk_t([16, 64], F32)
    Et = mk_t([64, 64], F32); Lt = mk_t([64, 64], F32); e12 = mk_t([64, 192], F32); e4 = mk_t([64, 64], F32)
    dec = mk_t([64, 1], F32)
    qe = mk_t([64, 64], BF16); ke = mk_t([64, 64], BF16); qb = mk_t([64, 64], BF16); kd = mk_t([64, 64], BF16)
    attm = mk_t([64, 64], BF16)
    Sf = [g.sb([64, 128], F32) for i in range(2)]
    Sb = [g.sb([64, 128], BF16) for i in range(2)]
    Pt = [g.sb([64, 1], F32) for i in range(2)]
    pA = [g.ps(i, [64, 512], F32) for i in range(NPA)]
    pB = [g.ps(NPA + i, [64, 512], F32) for i in range(NPB)]
    id64 = identf[0:64, 0:64]

    S.dma(lambda e: e.dma_start(out=cst[:], in_=cst_d.rearrange("d p f -> p d f")), w=["cst"], group="cst")
    S.dve(lambda e: e.tensor_copy(maskb[:], cst[:, :, 192:256]), r=["cst"], w=["maskb"])
    S.dma(lambda e: e.dma_start(out=w2t[:], in_=w2.rearrange("d r e -> r d e")), w=["w2t"], group="w2t")
    S.dma(lambda e: e.dma_start(out=bbt[:], in_=bb.rearrange("(o d) e -> o d e", o=1)), w=["bbt"], group="bbt")
    S.dma(lambda e: e.dma_start(out=mk[:], in_=mask[0:64, :]), w=["mk"], group="mk")
    Zv = Z.rearrange("(c p) f -> p c f", p=64)
    tasks = []
    ci = 0
    for h in range(4):
        first_of_head = True
        for d in range(2):
            hd = h * 2 + d
            for part in ("ctx", "lat"):
                lat = part == "lat"
                order = list(range(NLC)) if lat else list(range(NLC, NCH))
                if d == 1:
                    order = order[::-1]
                si = 0
                for idx, c in enumerate(order):
                    p = ci % NP
                    pa, pb = ci % NPA, ci % NPB
                    ci += 1
                    s0, s1 = si % 2, (si + 1) % 2
                    si += 1
                    tasks.append(dict(h=h, d=d, hd=hd, lat=lat, c=c, p=p, pa=pa, pb=pb, s0=s0, s1=s1, first=(idx == 0),
                                      last=(idx == len(order) - 1), head_start=first_of_head))
                    first_of_head = False

    def load_head(h):
        srcs = [(slice(h * 64, (h + 1) * 64), slice(0, 64)), (slice(256 + h * 64, 256 + (h + 1) * 64), slice(64, 128)),
                (slice(512 + h * 128, 512 + (h + 1) * 128), slice(128, 256)), (slice(1024, 1056), slice(256, 288))]
        for k, (sc, dc) in enumerate(srcs):
            S.dma(lambda e, sc=sc, dc=dc: e.dma_start(out=Zh[:, :, dc], in_=Zv[:, :, sc]), w=[f"Zh{k}"], group=f"Zh{k}")
        S.pool(lambda e: e.tensor_copy(vb[:], Zh[:, :, 128:256]), r=ZH, w=["vb"])

    def hdr(t):
        h, d, c, p, pa = t["h"], t["d"], t["c"], t["p"], t["pa"]
        return h, d, c, p, pa, f"pA{pa}"

    def stageA1(t):
        h, d, c, p, pa, A = hdr(t)
        gcol = 256 + 16 * d
        S.pe(lambda e: e.transpose(pA[pa][:, 320:384], Zh[:, c, 0:64], id64), r=ZH + ["identf"], w=[A])
        S.pe(lambda e: e.transpose(pA[pa][:, 384:448], Zh[:, c, 64:128], id64), r=ZH + ["identf"], w=[A])
        S.pe(lambda e: e.transpose(pA[pa][0:16, 448:512], Zh[:, c, gcol:gcol + 16], id64), r=ZH + ["identf"], w=[A])
        S.dve(lambda e: e.tensor_copy(glTs[p][:], pA[pa][0:16, 448:512]), r=[A], w=[f"glTs{p}"])
        S.dve(lambda e: e.tensor_scalar(qTs[p][:], pA[pa][:, 320:384], 0.125, None, ALU.mult), r=[A], w=[f"qTs{p}"])
        S.dve(lambda e: e.tensor_copy(kTs[p][:], pA[pa][:, 384:448]), r=[A], w=[f"kTs{p}"])

    def stageA2(t):
        h, d, c, p, pa, A = hdr(t)
        w2hd = w2t[:, d, h * 64:(h + 1) * 64]
        bbhd = bbt[0:1, d, h * 64:(h + 1) * 64]
        S.pe(lambda e: e.matmul(pA[pa][:, 0:64], glTs[p][:], w2hd, start=True, stop=False), r=[f"glTs{p}", "w2t"], w=[A])
        S.pe(lambda e: e.matmul(pA[pa][:, 0:64], cst[0:1, 0, 256:320], bbhd, start=False, stop=True), r=["cst", "bbt"], w=[A])
        S.act(lambda e: e.activation(Et[p][:], pA[pa][:, 0:64], AF.Exp, scale=-1.0), r=[A], w=[f"Et{p}"])
        S.act(lambda e: e.activation(Lt[p][:], Et[p][:], AF.Ln, bias=1.0), r=[f"Et{p}"], w=[f"Lt{p}"])

    def stageA3(t):
        h, d, c, p, pa, A = hdr(t)
        cD = lambda a, b: cst[:, d, a:b]
        deccol = 127 if d == 0 else 64
        S.pe(lambda e: e.matmul(pA[pa][:, 64:128], Lt[p][:], cD(0, 64), start=True, stop=True), r=[f"Lt{p}", "cst"], w=[A])
        S.pe(lambda e: e.matmul(pA[pa][:, 128:192], Lt[p][:], cD(64, 128), start=True, stop=True), r=[f"Lt{p}", "cst"], w=[A])
        S.pe(lambda e: e.matmul(pA[pa][:, 192:256], cD(128, 192), Lt[p][:], start=True, stop=True), r=[f"Lt{p}", "cst"], w=[A])
        S.act(lambda e: e.activation(e12[p][:, 0:128], pA[pa][:, 64:192], AF.Exp, scale=-1.0 / 16), r=[A], w=[f"e12{p}"])
        S.act(lambda e: e.activation(e12[p][:, 128:192], pA[pa][:, 64:128], AF.Exp, scale=1.0 / 16), r=[A], w=[f"e12{p}"])
        S.act(lambda e: e.activation(e4[p][:], pA[pa][:, 192:256], AF.Exp, scale=-1.0 / 16), r=[A], w=[f"e4{p}"])
        S.pool(lambda e: e.tensor_tensor(qe[p][:], qTs[p][:], e12[p][:, 0:64], ALU.mult), r=[f"qTs{p}", f"e12{p}"], w=[f"qe{p}"])
        S.pool(lambda e: e.tensor_tensor(ke[p][:], kTs[p][:], e12[p][:, 128:192], ALU.mult), r=[f"kTs{p}", f"e12{p}"], w=[f"ke{p}"])
        S.pool(lambda e: e.tensor_tensor(qb[p][:], qTs[p][:], e12[p][:, 64:128], ALU.mult), r=[f"qTs{p}", f"e12{p}"], w=[f"qb{p}"])
        S.pool(lambda e: e.tensor_tensor(kd[p][:], Zh[:, c, 64:128], e4[p][:], ALU.mult), r=ZH + [f"e4{p}"], w=[f"kd{p}"])

    def stageB1(t):
        d, p, pb = t["d"], t["p"], t["pb"]
        B = f"pB{pb}"
        S.pe(lambda e: e.matmul(pB[pb][:, 256:320], ke[p][:], qe[p][:], start=True, stop=True), r=[f"ke{p}", f"qe{p}"], w=[B])
        S.dve(lambda e: e.tensor_tensor(attm[p][:], pB[pb][:, 256:320], maskb[:, d, :], ALU.mult), r=[B, "maskb"], w=[f"attm{p}"])

    def stageB(t):
        h, d, hd, c, p, s0, s1, lat = t["h"], t["d"], t["hd"], t["c"], t["p"], t["s0"], t["s1"], t["lat"]
        pb = t["pb"]
        B = f"pB{pb}"
        dc = 127 if d == 0 else 64
        decp = e12[p][:, dc:dc + 1]
        if t["first"]:
            S.dve(lambda e: e.memset(Sf[0][:], 0.0), w=["Sf0"])
            S.pool(lambda e: e.memset(Sb[0][:], 0.0), w=["Sb0"])
            if lat:
                S.dve(lambda e: e.memset(Pt[0][:], 1.0), w=["P0"])
        if lat:
            S.dve(lambda e: e.scalar_tensor_tensor(qbP[:, hd, c, :], qTs[p][:], Pt[s0][:, 0:1], e12[p][:, 64:128],
                                                   ALU.mult, ALU.mult), r=[f"qTs{p}", f"P{s0}", f"e12{p}"], w=["qbP"])
            S.dve(lambda e: e.tensor_tensor(Pt[s1][:], Pt[s0][:], decp, ALU.mult), r=[f"P{s0}", f"e12{p}"], w=[f"P{s1}"])
        S.pe(lambda e: e.matmul(pB[pb][:, 0:128], attm[p][:], vb[:, c, :], start=True, stop=False), r=[f"attm{p}", "vb"], w=[B])
        S.pe(lambda e: e.matmul(pB[pb][:, 0:128], qb[p][:], Sb[s0][:], start=False, stop=True), r=[f"qb{p}", f"Sb{s0}"], w=[B])
        S.pe(lambda e: e.matmul(pB[pb][:, 128:256], kd[p][:], vb[:, c, :], start=True, stop=True), r=[f"kd{p}", "vb"], w=[B])
        S.dve(lambda e: e.scalar_tensor_tensor(Sf[s1][:], Sf[s0][:], decp, pB[pb][:, 128:256], ALU.mult, ALU.add),
              r=[f"Sf{s0}", f"e12{p}", B], w=[f"Sf{s1}"])
        S.pool(lambda e: e.tensor_copy(Sb[s1][:], Sf[s1][:]), r=[f"Sf{s1}"], w=[f"Sb{s1}"])
        ocols = slice(h * 128, (h + 1) * 128)
        if d == 0:
            S.dve(lambda e: e.tensor_copy(OLs[:, c, ocols], pB[pb][:, 0:128]), r=[B], w=[f"OL{c}"])
        else:
            S.dve(lambda e: e.tensor_tensor(OLs[:, c, ocols], pB[pb][:, 0:128], OLs[:, c, ocols], ALU.add),
                  r=[B, f"OL{c}"], w=[f"OL{c}"])
        if t["last"]:
            if lat:
                S.dve(lambda e: e.tensor_copy(SEG[:, hd, 0:128], Sf[s1][:]), r=[f"Sf{s1}"], w=["SEG"])
                S.dve(lambda e: e.tensor_copy(SEG[:, hd, 128:129], Pt[s1][:]), r=[f"P{s1}"], w=["SEG"])
            else:
                S.dve(lambda e: e.tensor_copy(Sctx[:, hd, :], Sf[s1][:]), r=[f"Sf{s1}"], w=["Sctx"])

    stages = [stageB, stageB1, stageA3, stageA2, stageA1]
    heads = {}
    for t in tasks:
        heads.setdefault(t["h"], []).append(t)
    for h in range(4):
        tl = heads[h]
        load_head(h)
        n = len(tl)
        for step in range(n + 4):
            for k, stg in enumerate(stages):
                idx = step - (4 - k)
                if 0 <= idx < n:
                    stg(tl[idx])
    S.dma(lambda e: e.dma_start(out=segloc.ap(), in_=SEG[:].rearrange("p a b -> p (a b)")), r=["SEG"], w=["segloc"],
          group="segio")
    allgather(g, segall, segloc, "segall", r=["segloc"])
    SEGa = g.arena[0:64, zh_off:zh_off + 4 * 8 * 129].rearrange("p (r a b) -> p r a b", r=4, a=8, b=129)
    S.dma(lambda e: e.dma_start(out=SEGa.rearrange("p r a b -> p r (a b)"),
                                in_=segall.ap().rearrange("(r p) f -> p r f", p=64)),
          r=["segall"], w=ZH + ["SEGa"], group="segio")
    for h in range(4):
        for d in range(2):
            hd = h * 2 + d
            S.dve(lambda e, hd=hd: e.tensor_copy(Wk[:], Sctx[:, hd, :]), r=["Sctx"], w=["Wk"])
            js = range(4) if d == 0 else range(3, -1, -1)
            for j in js:
                mcol = (4 if d == 0 else 8) + j
                S.dve(lambda e, j=j, hd=hd: e.scalar_tensor_tensor(cand[:], Wk[:], SEGa[:, j, hd, 128:129],
                                                                   SEGa[:, j, hd, 0:128], ALU.mult, ALU.add),
                      r=["Wk", "SEGa"], w=["cand"])
                S.dve(lambda e: e.tensor_tensor(diff[:], cand[:], Wk[:], ALU.subtract), r=["cand", "Wk"], w=["diff"])
                S.dve(lambda e, mcol=mcol: e.scalar_tensor_tensor(Wk[:], diff[:], mk[:, mcol:mcol + 1], Wk[:],
                                                                  ALU.mult, ALU.add), r=["diff", "mk", "Wk"], w=["Wk"])
            S.dve(lambda e, hd=hd: e.tensor_copy(Sib[:, hd, :], Wk[:]), r=["Wk"], w=["Sib"])
    for c in range(NLC):
        p = c % NPA
        A = f"pA{p}"
        for h in range(4):
            for d in range(2):
                hd = h * 2 + d
                S.pe(lambda e, p=p, h=h, d=d, hd=hd, c=c: e.matmul(pA[p][:, h * 128:(h + 1) * 128], qbP[:, hd, c, :],
                                                                    Sib[:, hd, :], start=(d == 0), stop=(d == 1)),
                     r=["qbP", "Sib"], w=[A])
        S.dve(lambda e, p=p, c=c: e.tensor_tensor(OLs[:, c, :], pA[p][:], OLs[:, c, :], ALU.add), r=[A, f"OL{c}"], w=[f"OL{c}"])
    OGv = OG.rearrange("(c p) f -> p c f", p=64)
    for k in range(4):
        cs = slice(k * 9, (k + 1) * 9)
        S.dma(lambda e, cs=cs: e.dma_start(out=OGv[:, cs, :], in_=OLs[:, cs, :]),
              r=[f"OL{c}" for c in range(k * 9, (k + 1) * 9)], w=[f"ogo{k}"], group=f"ogo{k}")


def emit_k5(g, T, kind, n_lat_tiles, x, w, vecs, z, out, og=None, oml=None, gng=None, fr_lat=None, fr_ctx=None,
            swT=None, sbT=None, mask=None, eps=1e-6):
    S = g.S
    g.phase()
    D = 1024
    nt = T // 128
    identf, ident = load_ident(g)
    wbf = g.sb([128, 8, D], BF16)
    wst = [g.sb([128, D], F32) for i in range(2)]
    vt = g.sb([128, 4, D], F32)
    for a in range(4):
        S.dma(lambda e, a=a: e.dma_start(out=vt[:, a, :], in_=vecs[a].partition_broadcast(128)), w=["vt"], group="modt")
    if kind == "even":
        gn = g.sb([128, 128], F32)
        S.dma(lambda e: e.dma_start(out=gn[:], in_=gng.partition_broadcast(128)), w=["gn"], group="gt")
    else:
        swf = g.sb([128, 4, 128], F32)
        swb = g.sb([128, 4, 128], BF16)
        sbt = g.sb([128, 4], F32)
        mk = g.sb([128, 16], F32)
        S.dma(lambda e: e.dma_start(out=swf[:], in_=swT.rearrange("g s t -> s g t")), w=["swf"], group="swf")
        S.dve(lambda e: e.tensor_copy(swb[:], swf[:]), r=["swf"], w=["swb"])
        S.dma(lambda e: e.dma_start(out=sbt[:], in_=sbT), w=["sbt"], group="sbt")
        S.dma(lambda e: e.dma_start(out=mk[:], in_=mask), w=["mk"], group="mk")
    for k in range(8):
        s = k % 2
        S.dma(lambda e, k=k, s=s: e.dma_start(out=wst[s][:], in_=w[k * 128:(k + 1) * 128, :]),
              w=[f"wst{s}"], group=f"wst{s}")
        if k % 2 == 0:
            S.act(lambda e, k=k, s=s: e.copy(wbf[:, k, :], wst[s][:]), r=[f"wst{s}"], w=[f"wbf{k}"])
        else:
            S.dve(lambda e, k=k, s=s: e.tensor_copy(wbf[:, k, :], wst[s][:]), r=[f"wst{s}"], w=[f"wbf{k}"])
    NX = 2
    xt = [g.sb([128, D], F32) for i in range(4)]
    ab = [g.sb([128, D], BF16) for i in range(NX)]
    aT = [g.sb([128, 8, 128], BF16) for i in range(NX)]
    rr = [g.sb([128, D], F32) for i in range(NX)]
    ot = [g.sb([128, D], F32) for i in range(NX)]
    st = [g.sb([128, 8, 6], F32) for i in range(NX)]
    mv = [g.sb([128, 16], F32) for i in range(NX)]
    if kind == "even":
        i1 = [g.sb([128, 512], F32) for i in range(NX)]
        i2 = [g.sb([128, 512], F32) for i in range(NX)]
        i3 = [g.sb([128, 512], F32) for i in range(NX)]
        i4 = [g.sb([128, 512], F32) for i in range(NX)]
        i5 = [g.sb([128, 512], F32) for i in range(NX)]
    else:
        i1 = [g.sb([128, 512], F32) for i in range(NX)]
        fc4 = [g.sb([128, 4, 512], F32) for i in range(NX)]
        zz = [g.sb([128, 2048], F32) for i in range(NX)]
        vgb = [g.sb([128, 512], BF16) for i in range(NX)]
        svp = [g.ps(0, [128, 512], F32)]
    tp = [g.ps(1 + i, [128, 8, 128], BF16) for i in range(2)]
    yp = [g.ps(3 + i, [128, 512], F32) for i in range(4)]
    wkeys = [f"wbf{k}" for k in range(8)]

    def rstd_chain(s, col_var, col_out, ncol=1):
        S.dve(lambda e: e.tensor_scalar_add(mv[s][:, col_var:col_var + ncol], mv[s][:, col_var:col_var + ncol], eps),
              r=[f"mv{s}"], w=[f"mv{s}"])
        S.act(lambda e: e.sqrt(mv[s][:, col_var:col_var + ncol], mv[s][:, col_var:col_var + ncol]),
              r=[f"mv{s}"], w=[f"mv{s}"])
        S.dve(lambda e: e.reciprocal(mv[s][:, col_out:col_out + ncol], mv[s][:, col_var:col_var + ncol]),
              r=[f"mv{s}"], w=[f"mv{s}"])

    def emit_loads(i):
        s = i % NX
        rows = slice(i * 128, (i + 1) * 128)
        xs = i % 4
        S.dma(lambda e, xs=xs, rows=rows: e.dma_start(out=xt[xs][:], in_=x[rows, :]), w=[f"xt{xs}"], group=f"xt{xs}")
        if kind == "even":
            S.dma(lambda e, s=s, rows=rows: e.dma_start(out=i1[s][:], in_=og[rows, :]), w=[f"i1{s}"], group=f"i1{s}")
            S.dma(lambda e, s=s, rows=rows: e.dma_start(out=i3[s][:], in_=z[rows, 1056:1568]), w=[f"i3{s}"], group=f"i3{s}")
            S.dma(lambda e, s=s, rows=rows: e.dma_start(out=i4[s][:], in_=oml[rows, :]), w=[f"i4{s}"], group=f"i4{s}")
            S.dma(lambda e, s=s, rows=rows: e.dma_start(out=i5[s][:], in_=z[rows, 1984:2496]), w=[f"i5{s}"], group=f"i5{s}")
        else:
            if i < n_lat_tiles:
                for qq in range(4):
                    S.dma(lambda e, s=s, qq=qq, i=i: e.dma_start(
                        out=fc4[s][:, qq, :].rearrange("p (g c) -> p g c", g=4), in_=fr_lat(qq, i)),
                        w=[f"fc4{s}.{qq}"], group=f"fc4{s}")
            else:
                S.dma(lambda e, s=s, i=i: e.dma_start(out=i1[s][:].rearrange("p (g c) -> p g c", g=4), in_=fr_ctx(i)),
                      w=[f"i1{s}"], group=f"i1{s}")
            S.dma(lambda e, s=s, rows=rows: e.dma_start(out=zz[s][:], in_=z[rows, 512:2560]), w=[f"zz{s}"], group=f"zz{s}")

    def stageE(i):
        s = i % NX
        rows = slice(i * 128, (i + 1) * 128)
        if i + 1 < nt:
            emit_loads(i + 1)
        if kind == "even":
            S.pool(lambda e, s=s: e.tensor_tensor(i2[s][:], i1[s][:], i1[s][:], ALU.mult), r=[f"i1{s}"], w=[f"i2{s}"])
            S.dve(lambda e, s=s: e.reduce_sum(mv[s][:, 0:4], i2[s][:].rearrange("p (h d) -> p h d", h=4), AX.X),
                  r=[f"i2{s}"], w=[f"mv{s}"])
            S.dve(lambda e, s=s: e.tensor_scalar(mv[s][:, 0:4], mv[s][:, 0:4], 1.0 / 128, None, ALU.mult),
                  r=[f"mv{s}"], w=[f"mv{s}"])
            rstd_chain(s, 0, 4, 4)
            S.act(lambda e, s=s: e.activation(i3[s][:], i3[s][:], AF.Silu), r=[f"i3{s}"], w=[f"i3{s}"])
            S.act(lambda e, s=s: e.activation(i5[s][:], i5[s][:], AF.Silu), r=[f"i5{s}"], w=[f"i5{s}"])
            for h in range(4):
                hs = slice(h * 128, (h + 1) * 128)
                S.dve(lambda e, s=s, h=h, hs=hs: e.scalar_tensor_tensor(
                    i1[s][:, hs], i1[s][:, hs], mv[s][:, 4 + h:5 + h], gn[:], ALU.mult, ALU.mult),
                    r=[f"i1{s}", f"mv{s}", "gn"], w=[f"i1{s}"])
            S.pool(lambda e, s=s: e.tensor_tensor(ab[s][:, 0:512], i1[s][:], i3[s][:], ALU.mult),
                   r=[f"i1{s}", f"i3{s}"], w=[f"ab{s}"])
            S.pool(lambda e, s=s: e.tensor_tensor(ab[s][:, 512:1024], i4[s][:], i5[s][:], ALU.mult),
                   r=[f"i4{s}", f"i5{s}"], w=[f"ab{s}"])
        else:
            if i < n_lat_tiles:
                S.dve(lambda e, s=s: e.tensor_scalar(i1[s][:], fc4[s][:, 0, :], mk[:, 0:1], None, ALU.mult),
                      r=[f"fc4{s}.0", f"fc4{s}.3", "mk"], w=[f"i1{s}"])
                for qq in range(1, 4):
                    S.dve(lambda e, s=s, qq=qq: e.scalar_tensor_tensor(
                        i1[s][:], fc4[s][:, qq, :], mk[:, qq:qq + 1], i1[s][:], ALU.mult, ALU.add),
                        r=[f"fc4{s}.{qq}", f"fc4{s}.3", "mk", f"i1{s}"], w=[f"i1{s}"])
            S.act(lambda e, s=s: e.activation(zz[s][:, 0:512], zz[s][:, 0:512], AF.Silu), r=[f"zz{s}"], w=[f"zz{s}"])
            S.act(lambda e, s=s: e.activation(zz[s][:, 1536:2048], zz[s][:, 1536:2048], AF.Silu), r=[f"zz{s}"], w=[f"zz{s}"])
            S.act(lambda e, s=s: e.activation(zz[s][:, 512:1536], zz[s][:, 512:1536], AF.Gelu), r=[f"zz{s}"], w=[f"zz{s}"])
            S.pool(lambda e, s=s: e.tensor_tensor(ab[s][:, 0:512], i1[s][:], zz[s][:, 0:512], ALU.mult),
                   r=[f"i1{s}", f"zz{s}"], w=[f"ab{s}"])
            for g in range(4):
                S.dve(lambda e, s=s, g=g: e.bn_stats(st[s][:, g, :], zz[s][:, 1024 + g * 128:1024 + (g + 1) * 128]),
                      r=[f"zz{s}"], w=[f"st{s}"])
                S.dve(lambda e, s=s, g=g: e.bn_aggr(mv[s][:, 2 * g:2 * g + 2], st[s][:, g:g + 1, :]),
                      r=[f"st{s}"], w=[f"mv{s}"])
            for g in range(4):
                rstd_chain(s, 2 * g + 1, 8 + g, 1)
            for g in range(4):
                S.dve(lambda e, s=s, g=g: e.tensor_scalar(
                    vgb[s][:, g * 128:(g + 1) * 128], zz[s][:, 1024 + g * 128:1024 + (g + 1) * 128],
                    mv[s][:, 2 * g:2 * g + 1], mv[s][:, 8 + g:9 + g], ALU.subtract, ALU.mult),
                    r=[f"zz{s}", f"mv{s}"], w=[f"vgb{s}"])
            for g in range(4):
                S.pe(lambda e, s=s, g=g: e.matmul(svp[0][:, g * 128:(g + 1) * 128], swb[:, g, :],
                                                  vgb[s][:, g * 128:(g + 1) * 128], start=True, stop=True),
                     r=[f"vgb{s}", "swb"], w=["svp0"])
            for g in range(4):
                gs = slice(g * 128, (g + 1) * 128)
                S.dve(lambda e, s=s, g=g, gs=gs: e.scalar_tensor_tensor(
                    zz[s][:, 512 + g * 128:512 + (g + 1) * 128], svp[0][:, gs], sbt[:, g:g + 1],
                    zz[s][:, 512 + g * 128:512 + (g + 1) * 128], ALU.add, ALU.mult),
                    r=["svp0", "sbt", f"zz{s}"], w=[f"zz{s}"])
            S.pool(lambda e, s=s: e.tensor_tensor(ab[s][:, 512:1024], zz[s][:, 512:1024], zz[s][:, 1536:2048], ALU.mult),
                   r=[f"zz{s}"], w=[f"ab{s}"])

    def stageT(i):
        s = i % NX
        t = i % 2
        for k in range(8):
            S.pe(lambda e, s=s, t=t, k=k: e.transpose(tp[t][:, k, :], ab[s][:, k * 128:(k + 1) * 128], ident[:]),
                 r=[f"ab{s}", "ident"], w=[f"tp{t}"])
        S.act(lambda e, s=s, t=t: e.copy(aT[s][:], tp[t][:]), r=[f"tp{t}"], w=[f"aT{s}"])

    def stageM(i):
        s = i % NX
        rows = slice(i * 128, (i + 1) * 128)
        gi = 0 if i < n_lat_tiles else 1
        for c in range(2):
            p = (2 * i + c) % 4
            cs = slice(c * 512, (c + 1) * 512)
            for k in range(8):
                S.pe(lambda e, s=s, p=p, k=k, cs=cs: e.matmul(yp[p][:], aT[s][:, k, :], wbf[:, k, cs],
                                                             start=(k == 0), stop=(k == 7)),
                     r=[f"aT{s}", wkeys[k]], w=[f"yp{p}"])
            S.dve(lambda e, s=s, p=p, cs=cs, gi=gi: e.tensor_tensor(rr[s][:, cs], yp[p][:], vt[:, gi, cs], ALU.mult),
                  r=[f"yp{p}", "vt"], w=[f"rr{s}"])
        xs = i % 4
        S.dve(lambda e, s=s, xs=xs: e.scalar_tensor_tensor(rr[s][:], xt[xs][:], ALPHA, rr[s][:], ALU.mult, ALU.add),
               r=[f"xt{xs}", f"rr{s}"], w=[f"rr{s}"])
        for j in range(2):
            S.dve(lambda e, s=s, j=j: e.bn_stats(st[s][:, 4 + j, :], rr[s][:, j * 512:(j + 1) * 512]),
                  r=[f"rr{s}"], w=[f"st{s}"])
        S.dve(lambda e, s=s: e.bn_aggr(mv[s][:, 12:14], st[s][:, 4:6, :]), r=[f"st{s}"], w=[f"mv{s}"])
        rstd_chain(s, 13, 14, 1)
        S.dve(lambda e, s=s: e.tensor_scalar(rr[s][:], rr[s][:], mv[s][:, 12:13], mv[s][:, 14:15],
                                             ALU.subtract, ALU.mult), r=[f"rr{s}", f"mv{s}"], w=[f"rr{s}"])
        S.pool(lambda e, s=s: e.tensor_tensor(rr[s][:], rr[s][:], vt[:, 2, :], ALU.mult), r=[f"rr{s}", "vt"], w=[f"rr{s}"])
        S.pool(lambda e, s=s: e.tensor_tensor(ot[s][:], rr[s][:], vt[:, 3, :], ALU.add), r=[f"rr{s}", "vt"], w=[f"ot{s}"])
        S.dma(lambda e, s=s, rows=rows: e.dma_start(out=out[rows, :], in_=ot[s][:]), r=[f"ot{s}"], w=[f"oo{s}"],
              group=f"oo{s}", eng="act")


    emit_loads(0)
    stageE(0)
    if nt > 1:
        stageE(1)
    stageT(0)
    for i in range(nt):
        if i + 2 < nt:
            stageE(i + 2)
        if i + 1 < nt:
            stageT(i + 1)
        stageM(i)


def emit_k7(g, fall, fout, TW, FC, W3, TWc, mask, with_ctx=True):
    S = g.S
    g.phase()
    st = [g.sb([128, 4, 512], F32) for i in range(2)]
    tmp = [g.sb([128, 4, 128], F32) for i in range(2)]
    mk = g.sb([128, 16], F32)
    TWb = g.sb([128, 64, 256], BF16)
    fb = g.sb([128, 64, 128], BF16)
    Y = g.sb([128, 2, 64, 128], BF16)
    U = g.sb([128, 64, 256], BF16)
    FCb = g.sb([128, 512], BF16)
    W3b = g.sb([128, 256], BF16)
    frt = g.sb([128, 64, 128], F32)
    ps = [g.ps(i, [128, 512], F32) for i in range(4)]
    S.dma(lambda e: e.dma_start(out=mk[:], in_=mask), w=["mk"], group="mk")
    stf = lambda s: st[s][:].rearrange("p a b -> p (a b)")
    for i in range(8):
        s = i % 2
        S.dma(lambda e, i=i, s=s: e.dma_start(out=stf(s), in_=TW[:, i * 8:(i + 1) * 8, :].rearrange("p a b -> p (a b)")),
              w=[f"st{s}"], group=f"st{s}")
        S.add("dve" if i % 2 == 0 else "pool",
              lambda e, i=i, s=s: e.tensor_copy(TWb[:, i * 8:(i + 1) * 8, :].rearrange("p a b -> p (a b)"), stf(s)),
              r=[f"st{s}"], w=["TWb"])
    S.dma(lambda e: e.dma_start(out=stf(0)[:, 0:512], in_=FC), w=["st0"], group="st0")
    S.dve(lambda e: e.tensor_copy(FCb[:], stf(0)[:, 0:512]), r=["st0"], w=["FCb"])
    S.dma(lambda e: e.dma_start(out=stf(1)[:, 0:256], in_=W3), w=["st1"], group="st1")
    S.dve(lambda e: e.tensor_copy(W3b[:], stf(1)[:, 0:256]), r=["st1"], w=["W3b"])

    S.barrier()

    def select(s, t, dst):
        stk = [f"st{s}.{r}.{pp}" for r in range(4) for pp in range(4)]
        S.dve(lambda e: e.tensor_scalar(tmp[t][:], st[s][:, :, 0:128], mk[:, 0:1], None, ALU.mult),
              r=stk + ["mk"], w=[f"tmp{t}"])
        for gg in range(1, 3):
            S.dve(lambda e, gg=gg: e.scalar_tensor_tensor(tmp[t][:], st[s][:, :, gg * 128:(gg + 1) * 128],
                                                         mk[:, gg:gg + 1], tmp[t][:], ALU.mult, ALU.add),
                  r=stk + ["mk", f"tmp{t}"], w=[f"tmp{t}"])
        S.dve(lambda e: e.scalar_tensor_tensor(dst, st[s][:, :, 384:512], mk[:, 3:4], tmp[t][:], ALU.mult, ALU.add),
              r=stk + ["mk", f"tmp{t}"], w=["fb"])

    for j in range(16):
        s = j % 2
        for r in range(4):
            for pp in range(4):
                src = fall[pp][r * 512:(r + 1) * 512, :].rearrange("(a n) c -> a n c", n=64)[:, 4 * j:4 * j + 4, :]
                p0 = 32 * r + 8 * pp
                S.dma(lambda e, s=s, p0=p0, src=src: e.dma_start(out=st[s][p0:p0 + 8, :, :], in_=src),
                      w=[f"st{s}.{r}.{pp}"], group=f"st{s}")
        select(s, s, fb[:, 4 * j:4 * j + 4, :])
    pi = 0
    for n2 in range(0, 64, 2):
        p = pi % 4; pi += 1
        for d in range(2):
            S.pe(lambda e, p=p, n2=n2, d=d: e.matmul(ps[p][:, d * 256:(d + 1) * 256], fb[:, n2 + d, :], TWb[:, n2 + d, :],
                                                     start=True, stop=True), r=["fb", "TWb"], w=[f"ps{p}"])
        for d in range(2):
            src = lambda p=p, d=d: ps[p][:, d * 256:(d + 1) * 256].rearrange("p (r j a) -> p r j a", r=2, a=2)
            dst = lambda n2=n2, d=d: Y[:, :, :, 2 * (n2 + d):2 * (n2 + d) + 2]
            if d == 0:
                S.act(lambda e, src=src, dst=dst: e.copy(dst(), src()), r=[], w=["Y", f"ps{p}"])
            else:
                S.dve(lambda e, src=src, dst=dst: e.tensor_copy(dst(), src()), r=[], w=["Y", f"ps{p}"])
    for j in range(0, 64, 2):
        p = pi % 4; pi += 1
        for d in range(2):
            jj = j + d
            S.pe(lambda e, p=p, jj=jj, d=d: e.matmul(ps[p][:, d * 256:(d + 1) * 256], Y[:, 0, jj, :],
                                                     FCb[:, 0:256], start=True, stop=False), r=["Y", "FCb"], w=[f"ps{p}"])
            S.pe(lambda e, p=p, jj=jj, d=d: e.matmul(ps[p][:, d * 256:(d + 1) * 256], Y[:, 1, jj, :],
                                                     FCb[:, 256:512], start=False, stop=True), r=["Y", "FCb"], w=[f"ps{p}"])
        if (j // 2) % 2 == 0:
            S.act(lambda e, p=p, j=j: e.copy(U[:, j:j + 2, :].rearrange("p a b -> p (a b)"), ps[p][:]), r=[f"ps{p}"], w=["U"])
        else:
            S.dve(lambda e, p=p, j=j: e.tensor_copy(U[:, j:j + 2, :].rearrange("p a b -> p (a b)"), ps[p][:]), r=[f"ps{p}"], w=["U"])
    scale = 1.0 / 1024.0
    for j0 in range(0, 64, 4):
        p = pi % 4; pi += 1
        for d in range(4):
            jj = j0 + d
            S.pe(lambda e, p=p, jj=jj, d=d: e.matmul(ps[p][:, d * 128:(d + 1) * 128], W3b[:, 0:128], U[:, jj, 0:128],
                                                     start=True, stop=False), r=["U", "W3b"], w=[f"ps{p}"])
            S.pe(lambda e, p=p, jj=jj, d=d: e.matmul(ps[p][:, d * 128:(d + 1) * 128], W3b[:, 128:256], U[:, jj, 128:256],
                                                     start=False, stop=True), r=["U", "W3b"], w=[f"ps{p}"])
        S.dve(lambda e, p=p, j0=j0: e.tensor_scalar(frt[:, j0:j0 + 4, :].rearrange("p a b -> p (a b)"), ps[p][:],
                                                    scale, None, ALU.mult), r=[f"ps{p}"], w=["frt"])
    for q in range(4):
        fv = fout[q].rearrange("(k2 jj a) c -> a k2 jj c", jj=64, a=2)
        for a in range(2):
            S.dma(lambda e, a=a, q=q, fv=fv: e.dma_start(out=fv[a], in_=frt[a * 64 + 16 * q:a * 64 + 16 * (q + 1), :, :]),
                  r=["frt"], w=[f"fo{a}{q}"], group=f"fo{a}")
    if with_ctx:
        S.barrier()
        fcb = g.sb([128, 2, 128], BF16)
        TWcb = g.sb([128, 2, 512], BF16)
        Yc = g.sb([128, 512], BF16)
        oc = g.sb([128, 2, 128], F32)
        for t in range(2):
            S.dma(lambda e, t=t: e.dma_start(out=st[t][:, 0, :], in_=fall[4][t * 128:(t + 1) * 128, :]),
                  w=[f"st{t}"], group=f"st{t}")
            S.dve(lambda e, t=t: e.tensor_scalar(tmp[t][:, 0, :], st[t][:, 0, 0:128], mk[:, 0:1], None, ALU.mult),
                  r=[f"st{t}", "mk"], w=[f"tmp{t}"])
            for gg in range(1, 4):
                S.dve(lambda e, t=t, gg=gg: e.scalar_tensor_tensor(
                    tmp[t][:, 0, :], st[t][:, 0, gg * 128:(gg + 1) * 128], mk[:, gg:gg + 1], tmp[t][:, 0, :], ALU.mult, ALU.add),
                    r=[f"st{t}", "mk", f"tmp{t}"], w=[f"tmp{t}"])
            S.dve(lambda e, t=t: e.tensor_copy(fcb[:, t, :], tmp[t][:, 0, :]), r=[f"tmp{t}"], w=["fcb"])
        for t in range(2):
            S.dma(lambda e, t=t: e.dma_start(out=stf(t)[:, 0:512], in_=TWc[:, t, :]), w=[f"st{t}"], group=f"st{t}")
            S.dve(lambda e, t=t: e.tensor_copy(TWcb[:, t, :], stf(t)[:, 0:512]), r=[f"st{t}"], w=["TWcb"])
        p = pi % 4; pi += 1
        for t in range(2):
            S.pe(lambda e, p=p, t=t: e.matmul(ps[p][:], fcb[:, t, :], TWcb[:, t, :], start=(t == 0), stop=(t == 1)),
                 r=["fcb", "TWcb"], w=[f"ps{p}"])
        S.dve(lambda e, p=p: e.tensor_copy(Yc[:], ps[p][:]), r=[f"ps{p}"], w=["Yc"])
        p = pi % 4; pi += 1
        for kt in range(2):
            S.pe(lambda e, p=p, kt=kt: e.matmul(ps[p][:, kt * 128:(kt + 1) * 128], Yc[:, kt * 128:(kt + 1) * 128],
                                                FCb[:, 0:128], start=True, stop=False), r=["Yc", "FCb"], w=[f"ps{p}"])
            S.pe(lambda e, p=p, kt=kt: e.matmul(ps[p][:, kt * 128:(kt + 1) * 128], Yc[:, 256 + kt * 128:256 + (kt + 1) * 128],
                                                FCb[:, 256:384], start=False, stop=True), r=["Yc", "FCb"], w=[f"ps{p}"])
        S.dve(lambda e, p=p: e.tensor_scalar(oc[:].rearrange("p a b -> p (a b)"), ps[p][:, 0:256],
                                             1.0 / np.sqrt(256.0 * 128.0), None, ALU.mult), r=[f"ps{p}"], w=["oc"])
        S.dma(lambda e: e.dma_start(out=fout[4].rearrange("(t p) c -> p t c", p=128), in_=oc[:]),
              r=["oc"], w=["oco"], group="oco")


Q, L, SEQ, D = 2048, 256, 8192, 1024
T = Q + L
NCORES = 8


def build_fused(depth=4, stop_after=None):
    nc = bass.Bass(target_bir_lowering=False)
    g = G(nc)
    if stop_after is not None:
        g.max_phase = stop_after
    ext = lambda name, shape: nc.dram_tensor(name, list(shape), F32, kind="ExternalInput")
    xin = ext("xin", [T, D]); cin = ext("cin", [128, D])
    ada_w = ext("ada_w", [D, 3 * D]); ada_b = ext("ada_b", [3 * D])
    plg = ext("post_ln_g", [4, D]); plb = ext("post_ln_b", [4, D])
    ewi = ext("even_w_in", [2, D, 2496]); ewo = ext("even_w_out", [2, D, D])
    owi = ext("odd_w_in", [2, D, 2560]); owo = ext("odd_w_out", [2, D, D])
    gw2 = ext("gla_w2", [2, 2, 16, 256]); gb = ext("gla_b", [2, 2, 256]); gng = ext("gla_norm_g", [2, 128])
    qng = ext("mla_q_norm_g", [2, 256]); wuq = ext("mla_w_uq", [2, 256, 768])
    kng = ext("mla_kv_norm_g", [2, 128]); wukv = ext("mla_w_ukv", [2, 128, 1024])
    swT = ext("sgu_wT", [2, 4, 128, 128]); sbT = ext("sgu_bT", [2, 128, 4])
    csq = ext("csq", [T, 32]); csk = ext("csk", [SEQ + L, 32])
    ident_d = ext("ident", [128, 128]); g.ident_d = ident_d.ap()
    gcst = ext("gcst", [2, 64, 320]); mask = ext("mask", [128, 16])
    TW = ext("TW", [128, 64, 256]); FC = ext("FC", [128, 512]); W3 = ext("W3", [128, 256]); TWc = ext("TWc", [128, 2, 512])
    xout = nc.dram_tensor("xout", [Q, D], F32, kind="ExternalOutput")
    ML = g.dram("ML", [128, 3 * D]); MS = g.dram("MS", [2, 3 * D]); MALL = g.dram("MALL", [8, 3 * D])
    X = [g.dram(f"X{i}", [T, D]) for i in range(2)]
    Z = g.dram("Z", [T, 2560])
    QP = g.dram("QP", [T, 768])
    KSIN = [g.dram(f"KSIN{p}", [Q // 2, 160]) for p in range(2)]
    KSG = [g.dram(f"KSG{p}", [4 * Q // 2, 160]) for p in range(2)]
    KSA = g.dram("KSA", [SEQ + L, 160])
    OML = g.dram("OML", [T, 512]); OG = g.dram("OG", [T, 512])
    SEGL = g.dram("SEGL", [64, 8 * 129]); SEGA = g.dram("SEGA", [256, 8 * 129])
    FPR = [512, 512, 512, 512, 256]
    FIN = [g.dram(f"FIN{p}", [FPR[p], 512]) for p in range(5)]
    FALL = [g.dram(f"FALL{p}", [4 * FPR[p], 512]) for p in range(5)]
    OPR = [2048, 2048, 2048, 2048, 256]
    FOUT = [g.dram(f"FOUT{p}", [OPR[p], 128]) for p in range(5)]
    FOALL = [g.dram(f"FOALL{p}", [4 * OPR[p], 128]) for p in range(5)]
    S = g.S
    rows = lambda i: slice(i * 128, (i + 1) * 128)
    HN = 3 * D // 2
    for hh in range(2):
        cs = slice(hh * HN, (hh + 1) * HN)
        emit_k1(g, lambda i: cin.ap()[rows(i), :], 1, D, HN, ada_w.ap()[:, cs],
                lambda i, cs=cs: ML.ap()[rows(i), cs], "silu", bias=ada_b.ap()[cs])
    g.phase()
    S.dma(lambda e: e.dma_start(out=MS.ap(), in_=ML.ap()[0:2, :]), w=["ms"], group="dc0")
    allgather(g, MALL, MS, "mall", r=["ms"])
    xcur = xin
    for l in range(depth):
        li = l // 2
        even = l % 2 == 0
        last = l == depth - 1
        Ml = MALL.ap()
        r0, r1 = 2 * l, 2 * l + 1
        mods = (Ml[r0, 0:D], Ml[r0, D:2 * D], Ml[r1, 0:D], Ml[r1, D:2 * D])
        vecs = (Ml[r0, 2 * D:3 * D], Ml[r1, 2 * D:3 * D], plg.ap()[l], plb.ap()[l])
        N = 2496 if even else 2560
        Zl = Z.ap()[:, 0:N]
        xap = xcur.ap()
        emit_k1(g, lambda i, xap=xap: xap[rows(i), :], T // 128, D, N, (ewi if even else owi).ap()[li],
                lambda i, Zl=Zl: Zl[rows(i), :], "ln", n_lat_tiles=Q // 128, mods=mods)
        Tk = Q if last else T
        xn = xout if last else X[l % 2]
        if even:
            emit_k1(g, lambda i, Zl=Zl: Zl[rows(i), 1568:1824], T // 128, 256, 768, wuq.ap()[li],
                    lambda i: QP.ap()[rows(i), :], "rms", gvec=qng.ap()[li])
            g.phase()
            S.dma(lambda e, Zl=Zl: e.dma_start(out=KSA.ap()[0:L, :], in_=Zl[Q:T, 1824:1984]), w=["ksa0"], group="dc1")
            HQ = Q // 2
            for p in range(2):
                S.dma(lambda e, Zl=Zl, p=p: e.dma_start(out=KSIN[p].ap(), in_=Zl[HQ * p:HQ * (p + 1), 1824:1984]),
                      w=[f"ksin{p}"], group=f"dc0{p}")
                allgather(g, KSG[p], KSIN[p], f"ksg{p}", r=[f"ksin{p}"])
                dst = KSA.ap()[L:L + SEQ, :].rearrange("(r h n) c -> h r n c", r=4, h=2)[p]
                S.dma(lambda e, p=p, dst=dst: e.dma_start(out=dst, in_=KSG[p].ap().rearrange("(r n) c -> r n c", r=4)),
                      r=[f"ksg{p}"], w=[f"ksa1{p}"], group=f"dc2{p}")
            emit_k3(g, Q, L, SEQ + L, QP.ap(), KSA.ap(), kng.ap()[li], wukv.ap()[li], csq.ap(), csk.ap(), OML.ap())
            emit_k4s(g, Zl, OG.ap(), SEGL, SEGA, gw2.ap()[li], gb.ap()[li], gcst.ap(), mask.ap())
            emit_k5(g, Tk, "even", Q // 128, xap, ewo.ap()[li], vecs, Zl, xn.ap(), og=OG.ap(), oml=OML.ap(),
                    gng=gng.ap()[li])
        else:
            g.phase()
            r0 = 0
            for p in range(5):
                S.dma(lambda e, Zl=Zl, p=p, r0=r0: e.dma_start(out=FIN[p].ap(), in_=Zl[r0:r0 + FPR[p], 0:512]),
                      w=[f"fin{p}"], group=f"dcf{p}")
                allgather(g, FALL[p], FIN[p], f"fall{p}", r=[f"fin{p}"])
                r0 += FPR[p]
            emit_k7(g, [a.ap() for a in FALL], [a.ap() for a in FOUT], TW.ap(), FC.ap(), W3.ap(), TWc.ap(), mask.ap())
            g.phase()
            for p in range(5):
                allgather(g, FOALL[p], FOUT[p], f"foall{p}")
            fo = [a.ap().rearrange("(g r) c -> r g c", g=4) for a in FOALL]
            emit_k5(g, Tk, "odd", Q // 128, xap, owo.ap()[li], vecs, Zl, xn.ap(),
                    fr_lat=lambda qq, i, fo=fo: fo[qq][128 * i:128 * (i + 1), :, :],
                    fr_ctx=lambda i, fo=fo: fo[4][128 * (i - Q // 128):128 * (i - Q // 128 + 1), :, :],
                    swT=swT.ap()[li], sbT=sbT.ap()[li], mask=mask.ap())
        xcur = xn
    g.finish()
    return nc


def make_inputs(x, c, ctx, c_ctx, ada_w, ada_b, post_ln_g, post_ln_b, even_w_in, gla_w2, gla_b, gla_norm_g,
                mla_q_norm_g, mla_w_uq, mla_kv_norm_g, mla_w_ukv, even_w_out, odd_w_in, sgu_w, sgu_b, odd_w_out,
                rope_tables, fnet_consts):
    f32 = np.float32
    cc = lambda a: np.ascontiguousarray(a, dtype=f32)
    shared = dict(post_ln_g=cc(post_ln_g), post_ln_b=cc(post_ln_b),
                  even_w_in=cc(even_w_in), even_w_out=cc(even_w_out), odd_w_in=cc(odd_w_in), odd_w_out=cc(odd_w_out),
                  gla_w2=cc(gla_w2), gla_b=cc(gla_b), gla_norm_g=cc(gla_norm_g), mla_q_norm_g=cc(mla_q_norm_g),
                  mla_w_uq=cc(mla_w_uq), mla_kv_norm_g=cc(mla_kv_norm_g), mla_w_ukv=cc(mla_w_ukv),
                  sgu_wT=cc(np.transpose(sgu_w, (0, 1, 3, 2))), sgu_bT=cc(np.transpose(sgu_b, (0, 2, 1))),
                  csk=rope_tables(np.concatenate([-np.ones(L, int), np.arange(SEQ)])),
                  ident=np.eye(128, dtype=f32), gcst=gla_consts2(), **fnet_consts())
    maps = []
    for j in range(NCORES):
        b, i = j // 4, j % 4
        m = dict(shared)
        m["xin"] = cc(np.concatenate([x[b, Q * i:Q * (i + 1)], ctx[b]], 0))
        cin = np.zeros((128, D), f32)
        cin[0] = c[b]; cin[1] = c_ctx
        m["cin"] = cin
        m["ada_w"] = cc(ada_w[i])
        m["ada_b"] = cc(ada_b[i])
        m["csq"] = rope_tables(np.concatenate([np.arange(Q) + Q * i, -np.ones(L, int)]))
        mk = np.zeros((128, 16), f32)
        mk[:, i] = 1.0
        for jj in range(4):
            mk[:, 4 + jj] = 1.0 if jj < i else 0.0
            mk[:, 8 + jj] = 1.0 if jj > i else 0.0
        m["mask"] = mk
        maps.append(m)
    return maps

def rope_tables(pos):
    pos = np.asarray(pos)
    row = (pos // 64).astype(np.float32)
    col = (pos % 64).astype(np.float32)
    inv = (10000.0 ** (-np.arange(8, dtype=np.float32) / 8)).astype(np.float32)
    ang = np.concatenate([row[:, None] * inv, col[:, None] * inv], -1).astype(np.float32)
    c = np.cos(ang).astype(np.float32)
    s = np.sin(ang).astype(np.float32)
    ident = pos < 0
    c[ident] = 1.0
    s[ident] = 0.0
    return np.concatenate([c, s], -1).astype(np.float32)


def fnet_consts():
    n1 = np.arange(128)[:, None, None].astype(np.float64)
    n2 = np.arange(64)[None, :, None].astype(np.float64)
    k1 = np.arange(128)[None, None, :].astype(np.float64)
    ang = 2 * np.pi * k1 * (64 * n1 + n2) / 8192.0
    TW = np.concatenate([np.cos(ang), -np.sin(ang)], -1).astype(np.float32)
    c = np.arange(128)[:, None].astype(np.float64)
    cp = np.arange(128)[None, :].astype(np.float64)
    a = 2 * np.pi * c * cp / 128.0
    Cc, Sc = np.cos(a), np.sin(a)
    FC = np.concatenate([Cc, -Sc, Sc, Cc], -1).astype(np.float32)
    W3 = np.zeros((64, 2, 2, 2, 64), np.float64)
    n2v = np.arange(64)[:, None]
    k2v = np.arange(64)[None, :]
    a3 = 2 * np.pi * n2v * k2v / 64.0
    for aa in range(2):
        W3[:, aa, 0, aa, :] = np.cos(a3)
        W3[:, aa, 1, aa, :] = np.sin(a3)
    W3 = W3.reshape(128, 256).astype(np.float32)
    n = np.arange(256)[:, None].astype(np.float64)
    k = np.arange(256)[None, :].astype(np.float64)
    ac = 2 * np.pi * n * k / 256.0
    TWc = np.concatenate([np.cos(ac), -np.sin(ac)], -1).reshape(2, 128, 512).transpose(1, 0, 2)
    TWc = np.ascontiguousarray(TWc).astype(np.float32)
    return dict(TW=TW, FC=FC, W3=W3, TWc=TWc)


_NC = {}


def kernel(x, c, ctx, c_ctx, ada_w, ada_b, post_ln_g, post_ln_b, even_w_in, gla_w2, gla_b, gla_norm_g,
           mla_q_norm_g, mla_w_uq, mla_kv_norm_g, mla_w_ukv, even_w_out, odd_w_in, sgu_w, sgu_b, odd_w_out):
    if "nc" not in _NC:
        _NC["nc"] = build_fused(4)
    maps = make_inputs(np.asarray(x), np.asarray(c), np.asarray(ctx), np.asarray(c_ctx), np.asarray(ada_w),
                       np.asarray(ada_b), np.asarray(post_ln_g), np.asarray(post_ln_b), np.asarray(even_w_in),
                       np.asarray(gla_w2), np.asarray(gla_b), np.asarray(gla_norm_g), np.asarray(mla_q_norm_g),
                       np.asarray(mla_w_uq), np.asarray(mla_kv_norm_g), np.asarray(mla_w_ukv), np.asarray(even_w_out),
                       np.asarray(odd_w_in), np.asarray(sgu_w), np.asarray(sgu_b), np.asarray(odd_w_out),
                       rope_tables, fnet_consts)
    res = run_bass_kernel_spmd(_NC["nc"], maps, core_ids=list(range(NCORES)))
    out = np.empty((2, SEQ, D), np.float32)
    for j in range(NCORES):
        out[j // 4, (j % 4) * Q:(j % 4 + 1) * Q] = res.results[j]["xout"]
    return out
```

```python
import contextlib
import numpy as np
import concourse.bass as bass
import concourse.mybir as mybir
from concourse.bass_utils import run_bass_kernel_spmd

F32 = mybir.dt.float32
BF16 = mybir.dt.bfloat16
AF = mybir.ActivationFunctionType
ALU = mybir.AluOpType
AX = mybir.AxisListType
ALPHA = 8 ** 0.25
MLA_SCALE = 96 ** -0.5
ARENA_F32 = 52900


class Sched:
    def __init__(self, nc):
        self.nc = nc
        self.ops = []
        self.last_w = {}
        self.readers = {}
        self.stack = contextlib.ExitStack()
        self.bar = set()
        self.pending = {}

    def barrier(self):
        last = {}
        for i, op in enumerate(self.ops):
            k = ("dma", op["dma"]) if op["dma"] is not None else ("eng", op["eng"])
            last[k] = i
        self.bar = set(last.values())
        self.pending = {e: True for e in ["pe", "act", "dve", "pool", "sp"]}
        self.last_w = {}
        self.readers = {}

    muted = False

    def add(self, eng, fn, r=(), w=(), dma=None, inc=16):
        if self.muted:
            return -1
        idx = len(self.ops)
        deps = set()
        for k in r:
            if k in self.last_w:
                deps.add(self.last_w[k])
        for k in w:
            if k in self.last_w:
                deps.add(self.last_w[k])
            for x in self.readers.get(k, ()):
                deps.add(x)
        if self.pending.get(eng):
            deps |= self.bar
            self.pending[eng] = False
        deps.discard(idx)
        self.ops.append(dict(eng=eng, fn=fn, deps=deps, dma=dma, inc=inc))
        for k in r:
            self.readers.setdefault(k, []).append(idx)
        for k in w:
            self.last_w[k] = idx
            self.readers[k] = []
        return idx

    def pe(self, fn, r=(), w=()):
        return self.add("pe", fn, r, w)

    def act(self, fn, r=(), w=()):
        return self.add("act", fn, r, w)

    def dve(self, fn, r=(), w=()):
        return self.add("dve", fn, r, w)

    def pool(self, fn, r=(), w=()):
        return self.add("pool", fn, r, w)

    def dma(self, fn, r=(), w=(), group=None, eng="sp", inc=16):
        assert group is not None
        return self.add(eng, fn, r, w, dma=group, inc=inc)

    def emit(self):
        nc = self.nc
        ops = self.ops
        n = len(ops)
        needs_signal = [False] * n
        for i, op in enumerate(ops):
            keep = set()
            for d in op["deps"]:
                dop = ops[d]
                if dop["dma"] is None and dop["eng"] == op["eng"] and op["eng"] == "pe":
                    continue
                keep.add(d)
                needs_signal[d] = True
            op["deps"] = keep
        engs = ["pe", "act", "dve", "pool", "sp"]
        sems = {e: self.stack.enter_context(nc.semaphore(f"s_{e}")) for e in engs}
        cnt = {e: 0 for e in engs}
        groups = {}
        gcnt = {}
        for i, op in enumerate(ops):
            if op["dma"] is not None:
                g = op["dma"]
                if g not in groups:
                    groups[g] = self.stack.enter_context(nc.semaphore(f"d_{len(groups)}"))
                    gcnt[g] = 0
                op["sem"] = groups[g]
                gcnt[g] += op["inc"] * getattr(op["fn"], "ndma", 1)
                op["val"] = gcnt[g]
            elif needs_signal[i]:
                cnt[op["eng"]] += 1
                op["sem"] = sems[op["eng"]]
                op["val"] = cnt[op["eng"]]
        print("sched: ops", n, "dma groups", len(groups), "sem counts", cnt, flush=True)
        final = dict((g, (groups[g], gcnt[g])) for g in groups)

        def stream(ename):
            def body(eng):
                known = {}
                for i, op in enumerate(ops):
                    if op["eng"] != ename:
                        continue
                    for d in sorted(op["deps"]):
                        dop = ops[d]
                        s, v = dop["sem"], dop["val"]
                        if known.get(id(s), 0) < v:
                            eng.wait_ge(s, v)
                            known[id(s)] = v
                    ins = op["fn"](eng)
                    if op["dma"] is not None:
                        if not isinstance(ins, (list, tuple)):
                            ins = [ins]
                        assert len(ins) == getattr(op["fn"], "ndma", 1)
                        for x in ins:
                            x.then_inc(op["sem"], op["inc"])
                    elif needs_signal[i]:
                        ins.then_inc(op["sem"], 1)
                if ename == "sp":
                    for g, (s, v) in final.items():
                        if known.get(id(s), 0) < v:
                            eng.wait_ge(s, v)
            return body

        with nc.Block() as block:
            block.tensor(stream("pe"))
            block.scalar(stream("act"))
            block.vector(stream("dve"))
            block.gpsimd(stream("pool"))
            block.sync(stream("sp"))

    def close(self):
        self.stack.close()


class G:
    def __init__(self, nc):
        self.nc = nc
        self.S = Sched(nc)
        self.arena = self.S.stack.enter_context(nc.sbuf_tensor("arena", [128, ARENA_F32], F32))
        self.banks = [self.S.stack.enter_context(nc.psum_tensor(f"bank{i}", [128, 512], F32)) for i in range(8)]
        self.off = 0
        self.nd = 0

    nphase = 0
    max_phase = 10 ** 9

    def phase(self):
        self.nphase += 1
        if self.nphase > self.max_phase:
            self.S.muted = True
        self.S.barrier()
        self.off = 0

    def sb(self, shape, dtype=F32, name=None):
        P = shape[0]
        n = int(np.prod(shape[1:]))
        words = n if dtype == F32 else (n + 1) // 2
        o = self.off
        self.off += words
        assert self.off <= ARENA_F32, ("SBUF arena overflow", self.off)
        ap = self.arena[0:P, o:o + words]
        if dtype != F32:
            ap = ap.bitcast(dtype)[:, 0:n]
        if len(shape) == 3:
            ap = ap.rearrange("p (a b) -> p a b", a=shape[1], b=shape[2])
        elif len(shape) == 4:
            ap = ap.rearrange("p (a b c) -> p a b c", a=shape[1], b=shape[2], c=shape[3])
        return ap

    def ps(self, bank, shape, dtype=F32):
        P = shape[0]
        n = int(np.prod(shape[1:]))
        ap = self.banks[bank][0:P, :]
        if dtype != F32:
            ap = ap.bitcast(dtype)
        ap = ap[:, 0:n]
        if len(shape) == 3:
            ap = ap.rearrange("p (a b) -> p a b", a=shape[1], b=shape[2])
        return ap

    def dram(self, name, shape, kind="Internal"):
        return self.nc.dram_tensor(name, list(shape), F32, kind=kind)

    def finish(self):
        self.S.emit()
        self.S.close()


def load_ident(g, tag="id"):
    S = g.S
    identf = g.sb([128, 128], F32)
    ident = g.sb([128, 128], BF16)
    S.dma(lambda e: e.dma_start(out=identf[:], in_=g.ident_d), w=["identf"], group="identf")
    S.dve(lambda e: e.tensor_copy(ident[:], identf[:]), r=["identf"], w=["ident"])
    return identf, ident


def dcopy(g, dst, src, name, grp="dcopy"):
    g.S.dma(lambda e: e.dma_start(out=dst, in_=src), w=[name], group=grp)


def allgather(g, dst, src, name, r=()):
    groups = [[0, 1, 2, 3], [4, 5, 6, 7]]
    g.S.dma(lambda e: e.collective_compute("AllGather", ALU.bypass, replica_groups=groups,
                                            ins=[src.ap().opt()], outs=[dst.ap().opt()]),
            r=list(r), w=[name], group="cc", eng="pool", inc=1)


def emit_k1(g, xsrc, nt, K, N, w, z, mode, n_lat_tiles=None, mods=None, gvec=None, bias=None, eps=1e-6):
    S = g.S
    g.phase()
    kc = K // 128
    nch = (N + 511) // 512
    identf, ident = load_ident(g)
    wbf = g.sb([128, kc, N], BF16)
    NW = 4 if kc > 2 else 2
    wst = [g.sb([128, N], F32) for i in range(NW)]
    if mode == "ln":
        modt = g.sb([128, 4, K], F32)
        for a in range(4):
            S.dma(lambda e, a=a: e.dma_start(out=modt[:, a, :], in_=mods[a].partition_broadcast(128)),
                  w=["modt"], group="modt")
        for a in (1, 3):
            S.dve(lambda e, a=a: e.tensor_scalar_add(modt[:, a, :], modt[:, a, :], 1.0), r=["modt"], w=["modt"])
    elif mode == "rms":
        gt = g.sb([128, K], F32)
        S.dma(lambda e: e.dma_start(out=gt[:], in_=gvec.partition_broadcast(128)), w=["gt"], group="gt")
    else:
        bt = g.sb([128, N], F32)
        S.dma(lambda e: e.dma_start(out=bt[:], in_=bias.partition_broadcast(128)), w=["bt"], group="gt")
    for k in range(kc):
        s = k % NW
        S.dma(lambda e, k=k, s=s: e.dma_start(out=wst[s][:], in_=w[k * 128:(k + 1) * 128, :]),
              w=[f"wst{s}"], group=f"wst{s}", eng=("sp", "act", "pool", "sp")[s])
        if k % 2 == 0:
            S.act(lambda e, k=k, s=s: e.copy(wbf[:, k, :], wst[s][:]), r=[f"wst{s}"], w=[f"wbf{k}"])
        else:
            S.dve(lambda e, k=k, s=s: e.tensor_copy(wbf[:, k, :], wst[s][:]), r=[f"wst{s}"], w=[f"wbf{k}"])
    NX = 2
    xt = [g.sb([128, K], F32) for i in range(NX)]
    xn = [g.sb([128, K], F32) for i in range(NX)]
    hb = [g.sb([128, K], BF16) for i in range(NX)]
    hT = [g.sb([128, kc, 128], BF16) for i in range(NX)]
    st = [g.sb([128, 8, 6], F32) for i in range(NX)]
    mv = [g.sb([128, 4], F32) for i in range(NX)]
    zt = [g.sb([128, N], F32) for i in range(NX)]
    tp = [g.ps(i, [128, kc, 128], BF16) for i in range(2)]
    zp = [g.ps(2 + i, [128, 512], F32) for i in range(4)]
    wkeys = [f"wbf{k}" for k in range(kc)]
    def emit_load(i):
        s = i % NX
        S.dma(lambda e, i=i, s=s: e.dma_start(out=xt[s][:], in_=xsrc(i)), w=[f"xt{s}"], group=f"xt{s}")

    zst = dict(zpi=0)

    def stageE(i):
        s = i % NX
        if i + 1 < nt:
            emit_load(i + 1)
        if mode == "ln":
            nsub = max(1, K // 512)
            fs = K // nsub
            for j in range(nsub):
                S.dve(lambda e, s=s, j=j, fs=fs: e.bn_stats(st[s][:, j, :], xt[s][:, j * fs:(j + 1) * fs]),
                      r=[f"xt{s}"], w=[f"st{s}"])
            S.dve(lambda e, s=s, nsub=nsub: e.bn_aggr(mv[s][:, 0:2], st[s][:, 0:nsub, :]), r=[f"st{s}"], w=[f"mv{s}"])
            S.dve(lambda e, s=s: e.tensor_scalar_add(mv[s][:, 3:4], mv[s][:, 1:2], eps), r=[f"mv{s}"], w=[f"mv{s}"])
            S.act(lambda e, s=s: e.sqrt(mv[s][:, 3:4], mv[s][:, 3:4]), r=[f"mv{s}"], w=[f"mv{s}"])
            S.dve(lambda e, s=s: e.reciprocal(mv[s][:, 2:3], mv[s][:, 3:4]), r=[f"mv{s}"], w=[f"mv{s}"])
            S.dve(lambda e, s=s: e.tensor_scalar(xn[s][:], xt[s][:], mv[s][:, 0:1], mv[s][:, 2:3],
                                                 ALU.subtract, ALU.mult),
                  r=[f"xt{s}", f"mv{s}"], w=[f"xn{s}"])
            a = 0 if (n_lat_tiles is None or i < n_lat_tiles) else 2
            S.pool(lambda e, s=s, a=a: e.tensor_tensor(xn[s][:], xn[s][:], modt[:, a + 1, :], ALU.mult),
                   r=[f"xn{s}", "modt"], w=[f"xn{s}"])
            S.pool(lambda e, s=s, a=a: e.tensor_tensor(hb[s][:], xn[s][:], modt[:, a, :], ALU.add),
                   r=[f"xn{s}", "modt"], w=[f"hb{s}"])
        elif mode == "silu":
            S.act(lambda e, s=s: e.activation(hb[s][:], xt[s][:], AF.Silu), r=[f"xt{s}"], w=[f"hb{s}"])
        else:
            S.act(lambda e, s=s: e.activation(xn[s][:], xt[s][:], AF.Square, accum_out=mv[s][:, 0:1]),
                  r=[f"xt{s}"], w=[f"xn{s}", f"mv{s}"])
            S.dve(lambda e, s=s: e.tensor_scalar(mv[s][:, 1:2], mv[s][:, 0:1], 1.0 / K, eps, ALU.mult, ALU.add),
                  r=[f"mv{s}"], w=[f"mv{s}"])
            S.act(lambda e, s=s: e.sqrt(mv[s][:, 3:4], mv[s][:, 1:2]), r=[f"mv{s}"], w=[f"mv{s}"])
            S.dve(lambda e, s=s: e.reciprocal(mv[s][:, 2:3], mv[s][:, 3:4]), r=[f"mv{s}"], w=[f"mv{s}"])
            S.dve(lambda e, s=s: e.scalar_tensor_tensor(hb[s][:], xt[s][:], mv[s][:, 2:3], gt[:],
                                                        ALU.mult, ALU.mult),
                  r=[f"xt{s}", f"mv{s}", "gt"], w=[f"hb{s}"])

    def stageT(i):
        s = i % NX
        t = i % 2
        for k in range(kc):
            S.pe(lambda e, s=s, t=t, k=k: e.transpose(tp[t][:, k, :], hb[s][:, k * 128:(k + 1) * 128], ident[:]),
                 r=[f"hb{s}", "ident"], w=[f"tp{t}"])
        S.act(lambda e, s=s, t=t: e.copy(hT[s][:], tp[t][:]), r=[f"tp{t}"], w=[f"hT{s}"])

    def stageM(i):
        s = i % NX
        for c in range(nch):
            c0, c1 = c * 512, min(N, (c + 1) * 512)
            p = zst["zpi"] % 4
            zst["zpi"] += 1
            for k in range(kc):
                S.pe(lambda e, s=s, p=p, k=k, c0=c0, c1=c1: e.matmul(
                    zp[p][:, 0:c1 - c0], hT[s][:, k, :], wbf[:, k, c0:c1], start=(k == 0), stop=(k == kc - 1)),
                    r=[f"hT{s}", wkeys[k]], w=[f"zp{p}"])
            if mode == "silu":
                S.dve(lambda e, s=s, p=p, c0=c0, c1=c1: e.tensor_tensor(zt[s][:, c0:c1], zp[p][:, 0:c1 - c0], bt[:, c0:c1], ALU.add),
                      r=[f"zp{p}", "bt"], w=[f"zt{s}.{c}"])
            elif c % 2 == 0:
                S.dve(lambda e, s=s, p=p, c0=c0, c1=c1: e.tensor_copy(zt[s][:, c0:c1], zp[p][:, 0:c1 - c0]),
                      r=[f"zp{p}"], w=[f"zt{s}.{c}"])
            else:
                S.act(lambda e, s=s, p=p, c0=c0, c1=c1: e.copy(zt[s][:, c0:c1], zp[p][:, 0:c1 - c0]),
                      r=[f"zp{p}"], w=[f"zt{s}.{c}"])
            S.dma(lambda e, i=i, s=s, c0=c0, c1=c1: e.dma_start(out=z(i)[:, c0:c1], in_=zt[s][:, c0:c1]),
                  r=[f"zt{s}.{c}"], w=[f"zout{s}.{c}"], group=f"zo{s}.{c}", eng=("act" if c % 2 == 0 else "sp"))


    emit_load(0)
    stageE(0)
    if nt > 1:
        stageE(1)
    stageT(0)
    for i in range(nt):
        if i + 2 < nt:
            stageE(i + 2)
        if i + 1 < nt:
            stageT(i + 1)
        stageM(i)


def emit_k3(g, NQL, NQC, NK, q, ksa, kvg, wukv, csq, csk, out, nheads=8, eps=1e-6):
    kr = ksa[:, 128:160]
    S = g.S
    g.phase()
    NQ = NQL + NQC
    nqt = NQ // 128
    nkt = NK // 128
    identf, ident = load_ident(g)
    KH = (nkt + 1) // 2
    ckst = g.sb([128, KH, 128], F32)
    cknT = g.sb([128, nkt * 128], BF16)
    wst = g.sb([128, 1024], F32)
    wb = g.sb([128, 1024], BF16)
    gt = g.sb([128, 128], F32)
    ss = g.sb([128, nkt], F32)
    rstd = g.sb([128, nkt], F32)
    junk = g.sb([128, 128], F32)
    pk = g.ps(7, [128, 512], F32)
    kpad = g.sb([128, nkt, 128], BF16)
    vx = [g.sb([128, nkt, 65], BF16) for i in range(2)]
    kT = [g.sb([128, nkt * 128], BF16) for i in range(2)]
    qT = [g.sb([128, nqt * 128], BF16) for i in range(2)]
    qh = g.sb([128, nqt, 96], F32)
    qpad = g.sb([128, nqt, 128], BF16)
    krl = g.sb([128, nkt, 32], F32)
    cskt = g.sb([128, nkt, 32], F32)
    csqt = g.sb([128, nqt, 32], F32)
    tk = [g.sb([128, nkt, 16], F32) for i in range(2)]
    tq = [g.sb([128, nqt, 16], F32) for i in range(2)]
    pT = [g.sb([128, 512], BF16) for i in range(3)]
    oTs = [g.sb([65, 512], F32) for i in range(2)]
    ost = [g.sb([128, 64], F32) for i in range(4)]
    rc = [g.sb([128, 1], F32) for i in range(4)]
    sTp = [g.ps(i, [128, 512], F32) for i in range(3)]
    oTp = [g.ps(3 + i, [65, 512], F32) for i in range(2)]
    tpk = g.ps(5, [128, 8, 128], BF16)
    tpo = g.ps(6, [128, 65], F32)

    S.pool(lambda e: e.memset(kpad[:], 0.0), w=["kpad"])
    S.pool(lambda e: e.memset(qpad[:], 0.0), w=["qpad"])
    for i in range(2):
        S.pool(lambda e, i=i: e.memset(vx[i][:], 1.0), w=[f"vx{i}"])
    S.dma(lambda e: e.dma_start(out=krl[:], in_=kr.rearrange("(t p) c -> p t c", p=128)), w=["krl"], group="krl")
    S.dma(lambda e: e.dma_start(out=cskt[:], in_=csk.rearrange("(t p) c -> p t c", p=128)), w=["cskt"], group="cskt")
    S.dma(lambda e: e.dma_start(out=csqt[:], in_=csq.rearrange("(t p) c -> p t c", p=128)), w=["csqt"], group="csqt")

    def rope(eng_a, eng_b, src, cs, tmp, dst, keys_r, key_tmp, key_dst, xo):
        x1 = lambda: src[:, :, xo:xo + 16]
        x2 = lambda: src[:, :, xo + 16:xo + 32]
        c = lambda: cs[:, :, 0:16]
        sn = lambda: cs[:, :, 16:32]
        S.add(eng_a, lambda e: e.tensor_tensor(tmp[0][:], x1(), c(), ALU.mult), r=keys_r, w=[key_tmp + "0"])
        S.add(eng_b, lambda e: e.tensor_tensor(tmp[1][:], x2(), sn(), ALU.mult), r=keys_r, w=[key_tmp + "1"])
        S.add(eng_a, lambda e: e.tensor_tensor(dst[:, :, 64:80], tmp[0][:], tmp[1][:], ALU.subtract),
              r=[key_tmp + "0", key_tmp + "1"], w=[key_dst])
        S.add(eng_a, lambda e: e.tensor_tensor(tmp[0][:], x1(), sn(), ALU.mult), r=keys_r, w=[key_tmp + "0"])
        S.add(eng_b, lambda e: e.tensor_tensor(tmp[1][:], x2(), c(), ALU.mult), r=keys_r, w=[key_tmp + "1"])
        S.add(eng_a, lambda e: e.tensor_tensor(dst[:, :, 96:112], tmp[0][:], tmp[1][:], ALU.add),
              r=[key_tmp + "0", key_tmp + "1"], w=[key_dst])

    rope("dve", "pool", krl, cskt, tk, kpad, ["krl", "cskt"], "tk", "kpad", 0)

    for g0 in range(0, nkt, 8):
        g1 = min(nkt, g0 + 8)
        for t in range(g0, g1):
            S.pe(lambda e, t=t, g0=g0: e.transpose(tpk[:, t - g0, :], kpad[:, t, :], ident[:]), r=["kpad", "ident"], w=["tpk"])
        for i in range(2):
            S.dve(lambda e, g0=g0, g1=g1, i=i: e.tensor_copy(
                kT[i][64:128, g0 * 128:g1 * 128], tpk[64:128, 0:g1 - g0, :].rearrange("p a b -> p (a b)")),
                r=["tpk"], w=[f"kT{i}"])
    S.dma(lambda e: e.dma_start(out=gt[:], in_=kvg.partition_broadcast(128)), w=["gt"], group="gt")
    S.dma(lambda e: e.dma_start(out=wst[:], in_=wukv), w=["wst"], group="wst0")
    S.dve(lambda e: e.tensor_copy(wb[:], wst[:]), r=["wst"], w=["wb"])
    for half in range(2):
        t0 = half * KH
        t1 = min(nkt, t0 + KH)
        S.dma(lambda e, t0=t0, t1=t1: e.dma_start(
            out=ckst[:, 0:t1 - t0, :], in_=ksa[t0 * 128:t1 * 128, 0:128].rearrange("(t p) c -> p t c", p=128)),
            w=["ckst"], group="kvh")
        for t in range(t0, t1):
            S.act(lambda e, t=t, t0=t0: e.activation(junk[:], ckst[:, t - t0, :], AF.Square, accum_out=ss[:, t:t + 1]),
                  r=["ckst"], w=["junk", "ss"])
        S.dve(lambda e, t0=t0, t1=t1: e.tensor_scalar(ss[:, t0:t1], ss[:, t0:t1], 1.0 / 128, eps, ALU.mult, ALU.add),
              r=["ss"], w=["ss"])
        S.act(lambda e, t0=t0, t1=t1: e.sqrt(ss[:, t0:t1], ss[:, t0:t1]), r=["ss"], w=["ss"])
        S.dve(lambda e, t0=t0, t1=t1: e.reciprocal(rstd[:, t0:t1], ss[:, t0:t1]), r=["ss"], w=["rstd"])
        for t in range(t0, t1):
            S.dve(lambda e, t=t, t0=t0: e.scalar_tensor_tensor(kpad[:, t, :], ckst[:, t - t0, :], rstd[:, t:t + 1], gt[:],
                                                               ALU.mult, ALU.mult),
                  r=["ckst", "rstd", "gt"], w=["kpad"])
    for g0 in range(0, nkt, 8):
        g1 = min(nkt, g0 + 8)
        for t in range(g0, g1):
            S.pe(lambda e, t=t, g0=g0: e.transpose(tpk[:, t - g0, :], kpad[:, t, :], ident[:]), r=["kpad", "ident"], w=["tpk"])
        S.dve(lambda e, g0=g0, g1=g1: e.tensor_copy(cknT[:, g0 * 128:g1 * 128],
                                                    tpk[:, 0:g1 - g0, :].rearrange("p a b -> p (a b)")),
              r=["tpk"], w=["cknT"])

    chunks = []
    for c0 in range(0, NQL, 512):
        chunks.append((c0, min(512, NQL - c0), 0, nkt))
    if NQC:
        chunks.append((NQL, NQC, 0, 2))
    sti = 0
    oti = 0
    osti = 0
    def prologue(h):
        hb = h % 2
        for c0 in range(0, NK, 512):
            n = min(512, NK - c0)
            S.pe(lambda e, c0=c0, n=n: e.matmul(pk[0:64, 0:n], wb[:, h * 128:h * 128 + 64], cknT[:, c0:c0 + n],
                                                start=True, stop=True), r=["wb", "cknT"], w=["pk"])
            S.dve(lambda e, c0=c0, n=n: e.tensor_copy(kT[hb][0:64, c0:c0 + n], pk[0:64, 0:n]), r=["pk"], w=[f"kT{hb}"])
        for g0 in range(0, nkt, 8):
            g1 = min(nkt, g0 + 8)
            for t in range(g0, g1):
                S.pe(lambda e, t=t, g0=g0: e.matmul(pk[:, (t - g0) * 64:(t - g0 + 1) * 64], cknT[:, t * 128:(t + 1) * 128],
                                                    wb[:, h * 128 + 64:h * 128 + 128], start=True, stop=True),
                     r=["wb", "cknT"], w=["pk"])
            S.dve(lambda e, g0=g0, g1=g1: e.tensor_copy(
                vx[hb][:, g0:g1, 0:64], pk[:, 0:(g1 - g0) * 64].rearrange("p (a b) -> p a b", b=64)),
                r=["pk"], w=[f"vx{hb}"])
        S.dma(lambda e, h=h: e.dma_start(out=qh[:], in_=q[:, h * 96:(h + 1) * 96].rearrange("(t p) c -> p t c", p=128)),
              w=["qh"], group="qh")
        S.pool(lambda e: e.tensor_copy(qpad[:, :, 0:64], qh[:, :, 0:64]), r=["qh"], w=["qpad"])
        rope("pool", "dve", qh, csqt, tq, qpad, ["qh", "csqt"], "tq", "qpad", 64)
        for g0 in range(0, nqt, 8):
            g1 = min(nqt, g0 + 8)
            for t in range(g0, g1):
                S.pe(lambda e, t=t, g0=g0: e.transpose(tpk[:, t - g0, :], qpad[:, t, :], ident[:]),
                     r=["qpad", "ident"], w=["tpk"])
            S.dve(lambda e, g0=g0, g1=g1, hb=hb: e.tensor_copy(
                qT[hb][:, g0 * 128:g1 * 128], tpk[:, 0:g1 - g0, :].rearrange("p a b -> p (a b)")),
                r=["tpk"], w=[f"qT{hb}"])

    st = dict(sti=0, oti=0, osti=0)
    LAG = 2

    def mainloop(h):
        hb = h % 2
        its = []
        for (q0, qn, k0, k1) in chunks:
            op = st["oti"] % 2
            st["oti"] += 1
            for kt in range(k0, k1):
                its.append((q0, qn, k0, k1, kt, op))

        def emit_qk(n):
            q0, qn, k0, k1, kt, op = its[n]
            sp = st["sti"] % 3
            st["sti"] += 1
            its[n] = its[n] + (sp,)
            S.pe(lambda e: e.matmul(sTp[sp][:, 0:qn], kT[hb][:, kt * 128:(kt + 1) * 128], qT[hb][:, q0:q0 + qn],
                                    start=True, stop=True), r=[f"kT{hb}", f"qT{hb}"], w=[f"sTp{sp}"])
            S.act(lambda e: e.activation(pT[sp][:, 0:qn], sTp[sp][:, 0:qn], AF.Exp, scale=MLA_SCALE),
                  r=[f"sTp{sp}"], w=[f"pT{sp}"])

        def emit_pv(n):
            q0, qn, k0, k1, kt, op, sp = its[n]
            S.pe(lambda e: e.matmul(oTp[op][:, 0:qn], vx[hb][:, kt, :], pT[sp][:, 0:qn], start=(kt == k0), stop=(kt == k1 - 1)),
                 r=[f"vx{hb}", f"pT{sp}"], w=[f"oTp{op}"])
            if kt != k1 - 1:
                return
            S.dve(lambda e: e.tensor_copy(oTs[op][:, 0:qn], oTp[op][:, 0:qn]), r=[f"oTp{op}"], w=[f"oTs{op}"])
            for j in range(qn // 128):
                os_ = st["osti"] % 4
                st["osti"] += 1
                S.pe(lambda e, j=j: e.transpose(tpo[:], oTs[op][:, j * 128:(j + 1) * 128], identf[0:65, 0:65]),
                     r=[f"oTs{op}", "identf"], w=["tpo"])
                S.dve(lambda e, os_=os_: e.reciprocal(rc[os_][:], tpo[:, 64:65]), r=["tpo"], w=[f"rc{os_}"])
                S.dve(lambda e, os_=os_: e.tensor_scalar(ost[os_][:], tpo[:, 0:64], rc[os_][:, 0:1], None, ALU.mult),
                      r=["tpo", f"rc{os_}"], w=[f"ost{os_}"])
                r0 = q0 + j * 128
                S.dma(lambda e, os_=os_, r0=r0: e.dma_start(out=out[r0:r0 + 128, h * 64:(h + 1) * 64], in_=ost[os_][:]),
                      r=[f"ost{os_}"], w=[f"oo{os_}"], group=f"oo{os_}")

        nI = len(its)
        for n in range(nI):
            emit_qk(n)
            if n >= LAG:
                emit_pv(n - LAG)
            if n == nI // 2 and h + 1 < nheads:
                prologue(h + 1)
        for n in range(max(0, nI - LAG), nI):
            emit_pv(n)

    prologue(0)
    for h in range(nheads):
        mainloop(h)


def gla_consts2():
    s = np.arange(64)[:, None]
    t = np.arange(64)[None, :]
    out = np.zeros((2, 64, 320), np.float32)
    U = (s <= t).astype(np.float32)
    out[0, :, 0:64] = U - (s <= 31)
    out[0, :, 64:128] = U
    out[0, :, 128:192] = (s > t)
    out[0, :, 192:256] = U
    Ub = (s >= t).astype(np.float32)
    out[1, :, 0:64] = Ub - (s >= 32)
    out[1, :, 64:128] = Ub
    out[1, :, 128:192] = (s < t)
    out[1, :, 192:256] = Ub
    out[:, 0, 256:320] = 1.0
    return out


def emit_k4s(g, Z, OG, segloc, segall, w2, bb, cst_d, mask):
    S = g.S
    g.phase()
    NLC, NCC = 32, 4
    ZH = ["Zh0", "Zh1", "Zh2", "Zh3"]
    NCH = NLC + NCC
    identf, ident = load_ident(g)
    cst = g.sb([64, 2, 320], F32)
    maskb = g.sb([64, 2, 64], BF16)
    w2t = g.sb([16, 2, 256], F32)
    bbt = g.sb([1, 2, 256], F32)
    mk = g.sb([64, 16], F32)
    zh_off = g.off
    Zh = g.sb([64, NCH, 288], F32)
    vb = g.sb([64, NCH, 128], BF16)
    OLs = g.sb([64, NCH, 512], F32)
    qbP = g.sb([64, 8, NLC, 64], BF16)
    Sctx = g.sb([64, 8, 128], F32)
    SEG = g.sb([64, 8, 129], F32)
    Sib = g.sb([64, 8, 128], BF16)
    Wk = g.sb([64, 128], F32)
    cand = g.sb([64, 128], F32)
    diff = g.sb([64, 128], F32)
    NP = 7
    NPA, NPB = 5, 3
    mk_t = lambda shape, dt: [g.sb(shape, dt) for i in range(NP)]
    qTs = mk_t([64, 64], F32); kTs = mk_t([64, 64], F32); glTs = mk_t([16, 64], F32)
    Et = mk_t([64, 64], F32); Lt = mk_t([64, 64], F32); e12 = mk_t([64, 192], F32); e4 = mk_t([64, 64], F32)
    dec = mk_t([64, 1], F32)
    qe = mk_t([64, 64], BF16); ke = mk_t([64, 64], BF16); qb = mk_t([64, 64], BF16); kd = mk_t([64, 64], BF16)
    attm = mk_t([64, 64], BF16)
    Sf = [g.sb([64, 128], F32) for i in range(2)]
    Sb = [g.sb([64, 128], BF16) for i in range(2)]
    Pt = [g.sb([64, 1], F32) for i in range(2)]
    pA = [g.ps(i, [64, 512], F32) for i in range(NPA)]
    pB = [g.ps(NPA + i, [64, 512], F32) for i in range(NPB)]
    id64 = identf[0:64, 0:64]

    S.dma(lambda e: e.dma_start(out=cst[:], in_=cst_d.rearrange("d p f -> p d f")), w=["cst"], group="cst")
    S.dve(lambda e: e.tensor_copy(maskb[:], cst[:, :, 192:256]), r=["cst"], w=["maskb"])
    S.dma(lambda e: e.dma_start(out=w2t[:], in_=w2.rearrange("d r e -> r d e")), w=["w2t"], group="w2t")
    S.dma(lambda e: e.dma_start(out=bbt[:], in_=bb.rearrange("(o d) e -> o d e", o=1)), w=["bbt"], group="bbt")
    S.dma(lambda e: e.dma_start(out=mk[:], in_=mask[0:64, :]), w=["mk"], group="mk")
    Zv = Z.rearrange("(c p) f -> p c f", p=64)
    tasks = []
    ci = 0
    for h in range(4):
        first_of_head = True
        for d in range(2):
            hd = h * 2 + d
            for part in ("ctx", "lat"):
                lat = part == "lat"
                order = list(range(NLC)) if lat else list(range(NLC, NCH))
                if d == 1:
                    order = order[::-1]
                si = 0
                for idx, c in enumerate(order):
                    p = ci % NP
                    pa, pb = ci % NPA, ci % NPB
                    ci += 1
                    s0, s1 = si % 2, (si + 1) % 2
                    si += 1
                    tasks.append(dict(h=h, d=d, hd=hd, lat=lat, c=c, p=p, pa=pa, pb=pb, s0=s0, s1=s1, first=(idx == 0),
                                      last=(idx == len(order) - 1), head_start=first_of_head))
                    first_of_head = False

    def load_head(h):
        srcs = [(slice(h * 64, (h + 1) * 64), slice(0, 64)), (slice(256 + h * 64, 256 + (h + 1) * 64), slice(64, 128)),
                (slice(512 + h * 128, 512 + (h + 1) * 128), slice(128, 256)), (slice(1024, 1056), slice(256, 288))]
        for k, (sc, dc) in enumerate(srcs):
            S.dma(lambda e, sc=sc, dc=dc: e.dma_start(out=Zh[:, :, dc], in_=Zv[:, :, sc]), w=[f"Zh{k}"], group=f"Zh{k}")
        S.pool(lambda e: e.tensor_copy(vb[:], Zh[:, :, 128:256]), r=ZH, w=["vb"])

    def hdr(t):
        h, d, c, p, pa = t["h"], t["d"], t["c"], t["p"], t["pa"]
        return h, d, c, p, pa, f"pA{pa}"

    def stageA1(t):
        h, d, c, p, pa, A = hdr(t)
        gcol = 256 + 16 * d
        S.pe(lambda e: e.transpose(pA[pa][:, 320:384], Zh[:, c, 0:64], id64), r=ZH + ["identf"], w=[A])
        S.pe(lambda e: e.transpose(pA[pa][:, 384:448], Zh[:, c, 64:128], id64), r=ZH + ["identf"], w=[A])
        S.pe(lambda e: e.transpose(pA[pa][0:16, 448:512], Zh[:, c, gcol:gcol + 16], id64), r=ZH + ["identf"], w=[A])
        S.dve(lambda e: e.tensor_copy(glTs[p][:], pA[pa][0:16, 448:512]), r=[A], w=[f"glTs{p}"])
        S.dve(lambda e: e.tensor_scalar(qTs[p][:], pA[pa][:, 320:384], 0.125, None, ALU.mult), r=[A], w=[f"qTs{p}"])
        S.dve(lambda e: e.tensor_copy(kTs[p][:], pA[pa][:, 384:448]), r=[A], w=[f"kTs{p}"])

    def stageA2(t):
        h, d, c, p, pa, A = hdr(t)
        w2hd = w2t[:, d, h * 64:(h + 1) * 64]
        bbhd = bbt[0:1, d, h * 64:(h + 1) * 64]
        S.pe(lambda e: e.matmul(pA[pa][:, 0:64], glTs[p][:], w2hd, start=True, stop=False), r=[f"glTs{p}", "w2t"], w=[A])
        S.pe(lambda e: e.matmul(pA[pa][:, 0:64], cst[0:1, 0, 256:320], bbhd, start=False, stop=True), r=["cst", "bbt"], w=[A])
        S.act(lambda e: e.activation(Et[p][:], pA[pa][:, 0:64], AF.Exp, scale=-1.0), r=[A], w=[f"Et{p}"])
        S.act(lambda e: e.activation(Lt[p][:], Et[p][:], AF.Ln, bias=1.0), r=[f"Et{p}"], w=[f"Lt{p}"])

    def stageA3(t):
        h, d, c, p, pa, A = hdr(t)
        cD = lambda a, b: cst[:, d, a:b]
        deccol = 127 if d == 0 else 64
        S.pe(lambda e: e.matmul(pA[pa][:, 64:128], Lt[p][:], cD(0, 64), start=True, stop=True), r=[f"Lt{p}", "cst"], w=[A])
        S.pe(lambda e: e.matmul(pA[pa][:, 128:192], Lt[p][:], cD(64, 128), start=True, stop=True), r=[f"Lt{p}", "cst"], w=[A])
        S.pe(lambda e: e.matmul(pA[pa][:, 192:256], cD(128, 192), Lt[p][:], start=True, stop=True), r=[f"Lt{p}", "cst"], w=[A])
        S.act(lambda e: e.activation(e12[p][:, 0:128], pA[pa][:, 64:192], AF.Exp, scale=-1.0 / 16), r=[A], w=[f"e12{p}"])
        S.act(lambda e: e.activation(e12[p][:, 128:192], pA[pa][:, 64:128], AF.Exp, scale=1.0 / 16), r=[A], w=[f"e12{p}"])
        S.act(lambda e: e.activation(e4[p][:], pA[pa][:, 192:256], AF.Exp, scale=-1.0 / 16), r=[A], w=[f"e4{p}"])
        S.pool(lambda e: e.tensor_tensor(qe[p][:], qTs[p][:], e12[p][:, 0:64], ALU.mult), r=[f"qTs{p}", f"e12{p}"], w=[f"qe{p}"])
        S.pool(lambda e: e.tensor_tensor(ke[p][:], kTs[p][:], e12[p][:, 128:192], ALU.mult), r=[f"kTs{p}", f"e12{p}"], w=[f"ke{p}"])
        S.pool(lambda e: e.tensor_tensor(qb[p][:], qTs[p][:], e12[p][:, 64:128], ALU.mult), r=[f"qTs{p}", f"e12{p}"], w=[f"qb{p}"])
        S.pool(lambda e: e.tensor_tensor(kd[p][:], Zh[:, c, 64:128], e4[p][:], ALU.mult), r=ZH + [f"e4{p}"], w=[f"kd{p}"])

    def stageB1(t):
        d, p, pb = t["d"], t["p"], t["pb"]
        B = f"pB{pb}"
        S.pe(lambda e: e.matmul(pB[pb][:, 256:320], ke[p][:], qe[p][:], start=True, stop=True), r=[f"ke{p}", f"qe{p}"], w=[B])
        S.dve(lambda e: e.tensor_tensor(attm[p][:], pB[pb][:, 256:320], maskb[:, d, :], ALU.mult), r=[B, "maskb"], w=[f"attm{p}"])

    def stageB(t):
        h, d, hd, c, p, s0, s1, lat = t["h"], t["d"], t["hd"], t["c"], t["p"], t["s0"], t["s1"], t["lat"]
        pb = t["pb"]
        B = f"pB{pb}"
        dc = 127 if d == 0 else 64
        decp = e12[p][:, dc:dc + 1]
        if t["first"]:
            S.dve(lambda e: e.memset(Sf[0][:], 0.0), w=["Sf0"])
            S.pool(lambda e: e.memset(Sb[0][:], 0.0), w=["Sb0"])
            if lat:
                S.dve(lambda e: e.memset(Pt[0][:], 1.0), w=["P0"])
        if lat:
            S.dve(lambda e: e.scalar_tensor_tensor(qbP[:, hd, c, :], qTs[p][:], Pt[s0][:, 0:1], e12[p][:, 64:128],
                                                   ALU.mult, ALU.mult), r=[f"qTs{p}", f"P{s0}", f"e12{p}"], w=["qbP"])
            S.dve(lambda e: e.tensor_tensor(Pt[s1][:], Pt[s0][:], decp, ALU.mult), r=[f"P{s0}", f"e12{p}"], w=[f"P{s1}"])
        S.pe(lambda e: e.matmul(pB[pb][:, 0:128], attm[p][:], vb[:, c, :], start=True, stop=False), r=[f"attm{p}", "vb"], w=[B])
        S.pe(lambda e: e.matmul(pB[pb][:, 0:128], qb[p][:], Sb[s0][:], start=False, stop=True), r=[f"qb{p}", f"Sb{s0}"], w=[B])
        S.pe(lambda e: e.matmul(pB[pb][:, 128:256], kd[p][:], vb[:, c, :], start=True, stop=True), r=[f"kd{p}", "vb"], w=[B])
        S.dve(lambda e: e.scalar_tensor_tensor(Sf[s1][:], Sf[s0][:], decp, pB[pb][:, 128:256], ALU.mult, ALU.add),
              r=[f"Sf{s0}", f"e12{p}", B], w=[f"Sf{s1}"])
        S.pool(lambda e: e.tensor_copy(Sb[s1][:], Sf[s1][:]), r=[f"Sf{s1}"], w=[f"Sb{s1}"])
        ocols = slice(h * 128, (h + 1) * 128)
        if d == 0:
            S.dve(lambda e: e.tensor_copy(OLs[:, c, ocols], pB[pb][:, 0:128]), r=[B], w=[f"OL{c}"])
        else:
            S.dve(lambda e: e.tensor_tensor(OLs[:, c, ocols], pB[pb][:, 0:128], OLs[:, c, ocols], ALU.add),
                  r=[B, f"OL{c}"], w=[f"OL{c}"])
        if t["last"]:
            if lat:
                S.dve(lambda e: e.tensor_copy(SEG[:, hd, 0:128], Sf[s1][:]), r=[f"Sf{s1}"], w=["SEG"])
                S.dve(lambda e: e.tensor_copy(SEG[:, hd, 128:129], Pt[s1][:]), r=[f"P{s1}"], w=["SEG"])
            else:
                S.dve(lambda e: e.tensor_copy(Sctx[:, hd, :], Sf[s1][:]), r=[f"Sf{s1}"], w=["Sctx"])

    stages = [stageB, stageB1, stageA3, stageA2, stageA1]
    heads = {}
    for t in tasks:
        heads.setdefault(t["h"], []).append(t)
    for h in range(4):
        tl = heads[h]
        load_head(h)
        n = len(tl)
        for step in range(n + 4):
            for k, stg in enumerate(stages):
                idx = step - (4 - k)
                if 0 <= idx < n:
                    stg(tl[idx])
    S.dma(lambda e: e.dma_start(out=segloc.ap(), in_=SEG[:].rearrange("p a b -> p (a b)")), r=["SEG"], w=["segloc"],
          group="segio")
    allgather(g, segall, segloc, "segall", r=["segloc"])
    SEGa = g.arena[0:64, zh_off:zh_off + 4 * 8 * 129].rearrange("p (r a b) -> p r a b", r=4, a=8, b=129)
    S.dma(lambda e: e.dma_start(out=SEGa.rearrange("p r a b -> p r (a b)"),
                                in_=segall.ap().rearrange("(r p) f -> p r f", p=64)),
          r=["segall"], w=ZH + ["SEGa"], group="segio")
    for h in range(4):
        for d in range(2):
            hd = h * 2 + d
            S.dve(lambda e, hd=hd: e.tensor_copy(Wk[:], Sctx[:, hd, :]), r=["Sctx"], w=["Wk"])
            js = range(4) if d == 0 else range(3, -1, -1)
            for j in js:
                mcol = (4 if d == 0 else 8) + j
                S.dve(lambda e, j=j, hd=hd: e.scalar_tensor_tensor(cand[:], Wk[:], SEGa[:, j, hd, 128:129],
                                                                   SEGa[:, j, hd, 0:128], ALU.mult, ALU.add),
                      r=["Wk", "SEGa"], w=["cand"])
                S.dve(lambda e: e.tensor_tensor(diff[:], cand[:], Wk[:], ALU.subtract), r=["cand", "Wk"], w=["diff"])
                S.dve(lambda e, mcol=mcol: e.scalar_tensor_tensor(Wk[:], diff[:], mk[:, mcol:mcol + 1], Wk[:],
                                                                  ALU.mult, ALU.add), r=["diff", "mk", "Wk"], w=["Wk"])
            S.dve(lambda e, hd=hd: e.tensor_copy(Sib[:, hd, :], Wk[:]), r=["Wk"], w=["Sib"])
    for c in range(NLC):
        p = c % NPA
        A = f"pA{p}"
        for h in range(4):
            for d in range(2):
                hd = h * 2 + d
                S.pe(lambda e, p=p, h=h, d=d, hd=hd, c=c: e.matmul(pA[p][:, h * 128:(h + 1) * 128], qbP[:, hd, c, :],
                                                                    Sib[:, hd, :], start=(d == 0), stop=(d == 1)),
                     r=["qbP", "Sib"], w=[A])
        S.dve(lambda e, p=p, c=c: e.tensor_tensor(OLs[:, c, :], pA[p][:], OLs[:, c, :], ALU.add), r=[A, f"OL{c}"], w=[f"OL{c}"])
    OGv = OG.rearrange("(c p) f -> p c f", p=64)
    for k in range(4):
        cs = slice(k * 9, (k + 1) * 9)
        S.dma(lambda e, cs=cs: e.dma_start(out=OGv[:, cs, :], in_=OLs[:, cs, :]),
              r=[f"OL{c}" for c in range(k * 9, (k + 1) * 9)], w=[f"ogo{k}"], group=f"ogo{k}")


def emit_k5(g, T, kind, n_lat_tiles, x, w, vecs, z, out, og=None, oml=None, gng=None, fr_lat=None, fr_ctx=None,
            swT=None, sbT=None, mask=None, eps=1e-6):
    S = g.S
    g.phase()
    D = 1024
    nt = T // 128
    identf, ident = load_ident(g)
    wbf = g.sb([128, 8, D], BF16)
    wst = [g.sb([128, D], F32) for i in range(4)]
    vt = g.sb([128, 4, D], F32)
    for a in range(4):
        S.dma(lambda e, a=a: e.dma_start(out=vt[:, a, :], in_=vecs[a].partition_broadcast(128)), w=["vt"], group="modt")
    if kind == "even":
        gn = g.sb([128, 128], F32)
        S.dma(lambda e: e.dma_start(out=gn[:], in_=gng.partition_broadcast(128)), w=["gn"], group="gt")
    else:
        swf = g.sb([128, 4, 128], F32)
        swb = g.sb([128, 4, 128], BF16)
        sbt = g.sb([128, 4], F32)
        mk = g.sb([128, 16], F32)
        S.dma(lambda e: e.dma_start(out=swf[:], in_=swT.rearrange("g s t -> s g t")), w=["swf"], group="swf")
        S.dve(lambda e: e.tensor_copy(swb[:], swf[:]), r=["swf"], w=["swb"])
        S.dma(lambda e: e.dma_start(out=sbt[:], in_=sbT), w=["sbt"], group="sbt")
        S.dma(lambda e: e.dma_start(out=mk[:], in_=mask), w=["mk"], group="mk")
    for k in range(8):
        s = k % 4
        S.dma(lambda e, k=k, s=s: e.dma_start(out=wst[s][:], in_=w[k * 128:(k + 1) * 128, :]),
              w=[f"wst{s}"], group=f"wst{s}", eng=("sp", "act", "pool", "sp")[s])
        if k % 2 == 0:
            S.act(lambda e, k=k, s=s: e.copy(wbf[:, k, :], wst[s][:]), r=[f"wst{s}"], w=[f"wbf{k}"])
        else:
            S.dve(lambda e, k=k, s=s: e.tensor_copy(wbf[:, k, :], wst[s][:]), r=[f"wst{s}"], w=[f"wbf{k}"])
    NX = 2
    xt = [g.sb([128, D], F32) for i in range(4)]
    ab = [g.sb([128, D], BF16) for i in range(NX)]
    aT = [g.sb([128, 8, 128], BF16) for i in range(NX)]
    rr = [g.sb([128, D], F32) for i in range(NX)]
    ot = [g.sb([128, D], F32) for i in range(NX)]
    st = [g.sb([128, 8, 6], F32) for i in range(NX)]
    mv = [g.sb([128, 16], F32) for i in range(NX)]
    if kind == "even":
        i1 = [g.sb([128, 512], F32) for i in range(NX)]
        i2 = [g.sb([128, 512], F32) for i in range(NX)]
        i3 = [g.sb([128, 512], F32) for i in range(NX)]
        i4 = [g.sb([128, 512], F32) for i in range(NX)]
        i5 = [g.sb([128, 512], F32) for i in range(NX)]
    else:
        i1 = [g.sb([128, 512], F32) for i in range(NX)]
        fc4 = [g.sb([128, 4, 512], F32) for i in range(NX)]
        zz = [g.sb([128, 2048], F32) for i in range(NX)]
        vgb = [g.sb([128, 512], BF16) for i in range(NX)]
        svp = [g.ps(0, [128, 512], F32)]
    tp = [g.ps(1 + i, [128, 8, 128], BF16) for i in range(2)]
    yp = [g.ps(3 + i, [128, 512], F32) for i in range(4)]
    wkeys = [f"wbf{k}" for k in range(8)]

    def rstd_chain(s, col_var, col_out, ncol=1):
        S.dve(lambda e: e.tensor_scalar_add(mv[s][:, col_var:col_var + ncol], mv[s][:, col_var:col_var + ncol], eps),
              r=[f"mv{s}"], w=[f"mv{s}"])
        S.act(lambda e: e.sqrt(mv[s][:, col_var:col_var + ncol], mv[s][:, col_var:col_var + ncol]),
              r=[f"mv{s}"], w=[f"mv{s}"])
        S.dve(lambda e: e.reciprocal(mv[s][:, col_out:col_out + ncol], mv[s][:, col_var:col_var + ncol]),
              r=[f"mv{s}"], w=[f"mv{s}"])

    def emit_loads(i):
        s = i % NX
        rows = slice(i * 128, (i + 1) * 128)
        xs = i % 4
        S.dma(lambda e, xs=xs, rows=rows: e.dma_start(out=xt[xs][:], in_=x[rows, :]), w=[f"xt{xs}"], group=f"xt{xs}")
        if kind == "even":
            S.dma(lambda e, s=s, rows=rows: e.dma_start(out=i1[s][:], in_=og[rows, :]), w=[f"i1{s}"], group=f"i1{s}")
            S.dma(lambda e, s=s, rows=rows: e.dma_start(out=i3[s][:], in_=z[rows, 1056:1568]), w=[f"i3{s}"], group=f"i3{s}")
            S.dma(lambda e, s=s, rows=rows: e.dma_start(out=i4[s][:], in_=oml[rows, :]), w=[f"i4{s}"], group=f"i4{s}")
            S.dma(lambda e, s=s, rows=rows: e.dma_start(out=i5[s][:], in_=z[rows, 1984:2496]), w=[f"i5{s}"], group=f"i5{s}")
        else:
            if i < n_lat_tiles:
                for qq in range(4):
                    S.dma(lambda e, s=s, qq=qq, i=i: e.dma_start(
                        out=fc4[s][:, qq, :].rearrange("p (g c) -> p g c", g=4), in_=fr_lat(qq, i)),
                        w=[f"fc4{s}.{qq}"], group=f"fc4{s}")
            else:
                S.dma(lambda e, s=s, i=i: e.dma_start(out=i1[s][:].rearrange("p (g c) -> p g c", g=4), in_=fr_ctx(i)),
                      w=[f"i1{s}"], group=f"i1{s}")
            S.dma(lambda e, s=s, rows=rows: e.dma_start(out=zz[s][:], in_=z[rows, 512:2560]), w=[f"zz{s}"], group=f"zz{s}")

    def stageE(i):
        s = i % NX
        rows = slice(i * 128, (i + 1) * 128)
        if i + 1 < nt:
            emit_loads(i + 1)
        if kind == "even":
            S.pool(lambda e, s=s: e.tensor_tensor(i2[s][:], i1[s][:], i1[s][:], ALU.mult), r=[f"i1{s}"], w=[f"i2{s}"])
            S.dve(lambda e, s=s: e.reduce_sum(mv[s][:, 0:4], i2[s][:].rearrange("p (h d) -> p h d", h=4), AX.X),
                  r=[f"i2{s}"], w=[f"mv{s}"])
            S.dve(lambda e, s=s: e.tensor_scalar(mv[s][:, 0:4], mv[s][:, 0:4], 1.0 / 128, None, ALU.mult),
                  r=[f"mv{s}"], w=[f"mv{s}"])
            rstd_chain(s, 0, 4, 4)
            S.act(lambda e, s=s: e.activation(i3[s][:], i3[s][:], AF.Silu), r=[f"i3{s}"], w=[f"i3{s}"])
            S.act(lambda e, s=s: e.activation(i5[s][:], i5[s][:], AF.Silu), r=[f"i5{s}"], w=[f"i5{s}"])
            for h in range(4):
                hs = slice(h * 128, (h + 1) * 128)
                S.dve(lambda e, s=s, h=h, hs=hs: e.scalar_tensor_tensor(
                    i1[s][:, hs], i1[s][:, hs], mv[s][:, 4 + h:5 + h], gn[:], ALU.mult, ALU.mult),
                    r=[f"i1{s}", f"mv{s}", "gn"], w=[f"i1{s}"])
            S.pool(lambda e, s=s: e.tensor_tensor(ab[s][:, 0:512], i1[s][:], i3[s][:], ALU.mult),
                   r=[f"i1{s}", f"i3{s}"], w=[f"ab{s}"])
            S.pool(lambda e, s=s: e.tensor_tensor(ab[s][:, 512:1024], i4[s][:], i5[s][:], ALU.mult),
                   r=[f"i4{s}", f"i5{s}"], w=[f"ab{s}"])
        else:
            if i < n_lat_tiles:
                S.dve(lambda e, s=s: e.tensor_scalar(i1[s][:], fc4[s][:, 0, :], mk[:, 0:1], None, ALU.mult),
                      r=[f"fc4{s}.0", f"fc4{s}.3", "mk"], w=[f"i1{s}"])
                for qq in range(1, 4):
                    S.dve(lambda e, s=s, qq=qq: e.scalar_tensor_tensor(
                        i1[s][:], fc4[s][:, qq, :], mk[:, qq:qq + 1], i1[s][:], ALU.mult, ALU.add),
                        r=[f"fc4{s}.{qq}", f"fc4{s}.3", "mk", f"i1{s}"], w=[f"i1{s}"])
            S.act(lambda e, s=s: e.activation(zz[s][:, 0:512], zz[s][:, 0:512], AF.Silu), r=[f"zz{s}"], w=[f"zz{s}"])
            S.act(lambda e, s=s: e.activation(zz[s][:, 1536:2048], zz[s][:, 1536:2048], AF.Silu), r=[f"zz{s}"], w=[f"zz{s}"])
            S.act(lambda e, s=s: e.activation(zz[s][:, 512:1536], zz[s][:, 512:1536], AF.Gelu), r=[f"zz{s}"], w=[f"zz{s}"])
            S.pool(lambda e, s=s: e.tensor_tensor(ab[s][:, 0:512], i1[s][:], zz[s][:, 0:512], ALU.mult),
                   r=[f"i1{s}", f"zz{s}"], w=[f"ab{s}"])
            for g in range(4):
                S.dve(lambda e, s=s, g=g: e.bn_stats(st[s][:, g, :], zz[s][:, 1024 + g * 128:1024 + (g + 1) * 128]),
                      r=[f"zz{s}"], w=[f"st{s}"])
                S.dve(lambda e, s=s, g=g: e.bn_aggr(mv[s][:, 2 * g:2 * g + 2], st[s][:, g:g + 1, :]),
                      r=[f"st{s}"], w=[f"mv{s}"])
            for g in range(4):
                rstd_chain(s, 2 * g + 1, 8 + g, 1)
            for g in range(4):
                S.dve(lambda e, s=s, g=g: e.tensor_scalar(
                    vgb[s][:, g * 128:(g + 1) * 128], zz[s][:, 1024 + g * 128:1024 + (g + 1) * 128],
                    mv[s][:, 2 * g:2 * g + 1], mv[s][:, 8 + g:9 + g], ALU.subtract, ALU.mult),
                    r=[f"zz{s}", f"mv{s}"], w=[f"vgb{s}"])
            for g in range(4):
                S.pe(lambda e, s=s, g=g: e.matmul(svp[0][:, g * 128:(g + 1) * 128], swb[:, g, :],
                                                  vgb[s][:, g * 128:(g + 1) * 128], start=True, stop=True),
                     r=[f"vgb{s}", "swb"], w=["svp0"])
            for g in range(4):
                gs = slice(g * 128, (g + 1) * 128)
                S.dve(lambda e, s=s, g=g, gs=gs: e.scalar_tensor_tensor(
                    zz[s][:, 512 + g * 128:512 + (g + 1) * 128], svp[0][:, gs], sbt[:, g:g + 1],
                    zz[s][:, 512 + g * 128:512 + (g + 1) * 128], ALU.add, ALU.mult),
                    r=["svp0", "sbt", f"zz{s}"], w=[f"zz{s}"])
            S.pool(lambda e, s=s: e.tensor_tensor(ab[s][:, 512:1024], zz[s][:, 512:1024], zz[s][:, 1536:2048], ALU.mult),
                   r=[f"zz{s}"], w=[f"ab{s}"])

    def stageT(i):
        s = i % NX
        t = i % 2
        for k in range(8):
            S.pe(lambda e, s=s, t=t, k=k: e.transpose(tp[t][:, k, :], ab[s][:, k * 128:(k + 1) * 128], ident[:]),
                 r=[f"ab{s}", "ident"], w=[f"tp{t}"])
        S.act(lambda e, s=s, t=t: e.copy(aT[s][:], tp[t][:]), r=[f"tp{t}"], w=[f"aT{s}"])

    def stageM(i):
        s = i % NX
        rows = slice(i * 128, (i + 1) * 128)
        gi = 0 if i < n_lat_tiles else 1
        for c in range(2):
            p = (2 * i + c) % 4
            cs = slice(c * 512, (c + 1) * 512)
            for k in range(8):
                S.pe(lambda e, s=s, p=p, k=k, cs=cs: e.matmul(yp[p][:], aT[s][:, k, :], wbf[:, k, cs],
                                                             start=(k == 0), stop=(k == 7)),
                     r=[f"aT{s}", wkeys[k]], w=[f"yp{p}"])
            S.dve(lambda e, s=s, p=p, cs=cs, gi=gi: e.tensor_tensor(rr[s][:, cs], yp[p][:], vt[:, gi, cs], ALU.mult),
                  r=[f"yp{p}", "vt"], w=[f"rr{s}"])
        xs = i % 4
        S.dve(lambda e, s=s, xs=xs: e.scalar_tensor_tensor(rr[s][:], xt[xs][:], ALPHA, rr[s][:], ALU.mult, ALU.add),
               r=[f"xt{xs}", f"rr{s}"], w=[f"rr{s}"])
        for j in range(2):
            S.dve(lambda e, s=s, j=j: e.bn_stats(st[s][:, 4 + j, :], rr[s][:, j * 512:(j + 1) * 512]),
                  r=[f"rr{s}"], w=[f"st{s}"])
        S.dve(lambda e, s=s: e.bn_aggr(mv[s][:, 12:14], st[s][:, 4:6, :]), r=[f"st{s}"], w=[f"mv{s}"])
        rstd_chain(s, 13, 14, 1)
        S.dve(lambda e, s=s: e.tensor_scalar(rr[s][:], rr[s][:], mv[s][:, 12:13], mv[s][:, 14:15],
                                             ALU.subtract, ALU.mult), r=[f"rr{s}", f"mv{s}"], w=[f"rr{s}"])
        S.pool(lambda e, s=s: e.tensor_tensor(rr[s][:], rr[s][:], vt[:, 2, :], ALU.mult), r=[f"rr{s}", "vt"], w=[f"rr{s}"])
        S.pool(lambda e, s=s: e.tensor_tensor(ot[s][:], rr[s][:], vt[:, 3, :], ALU.add), r=[f"rr{s}", "vt"], w=[f"ot{s}"])
        S.dma(lambda e, s=s, rows=rows: e.dma_start(out=out[rows, :], in_=ot[s][:]), r=[f"ot{s}"], w=[f"oo{s}"],
              group=f"oo{s}", eng="act")


    emit_loads(0)
    stageE(0)
    if nt > 1:
        stageE(1)
    stageT(0)
    for i in range(nt):
        if i + 2 < nt:
            stageE(i + 2)
        if i + 1 < nt:
            stageT(i + 1)
        stageM(i)


def emit_k7(g, fall, fout, TW, FC, W3, TWc, mask, with_ctx=True):
    S = g.S
    g.phase()
    st = [g.sb([128, 4, 512], F32) for i in range(2)]
    tmp = [g.sb([128, 4, 128], F32) for i in range(2)]
    mk = g.sb([128, 16], F32)
    TWb = g.sb([128, 64, 256], BF16)
    fb = g.sb([128, 64, 128], BF16)
    Y = g.sb([128, 2, 64, 128], BF16)
    U = g.sb([128, 64, 256], BF16)
    FCb = g.sb([128, 512], BF16)
    W3b = g.sb([128, 256], BF16)
    frt = g.sb([128, 64, 128], F32)
    ps = [g.ps(i, [128, 512], F32) for i in range(4)]
    S.dma(lambda e: e.dma_start(out=mk[:], in_=mask), w=["mk"], group="mk")
    stf = lambda s: st[s][:].rearrange("p a b -> p (a b)")
    for i in range(8):
        s = i % 2
        S.dma(lambda e, i=i, s=s: e.dma_start(out=stf(s), in_=TW[:, i * 8:(i + 1) * 8, :].rearrange("p a b -> p (a b)")),
              w=[f"st{s}"], group=f"st{s}")
        S.add("dve" if i % 2 == 0 else "pool",
              lambda e, i=i, s=s: e.tensor_copy(TWb[:, i * 8:(i + 1) * 8, :].rearrange("p a b -> p (a b)"), stf(s)),
              r=[f"st{s}"], w=["TWb"])
    S.dma(lambda e: e.dma_start(out=stf(0)[:, 0:512], in_=FC), w=["st0"], group="st0")
    S.dve(lambda e: e.tensor_copy(FCb[:], stf(0)[:, 0:512]), r=["st0"], w=["FCb"])
    S.dma(lambda e: e.dma_start(out=stf(1)[:, 0:256], in_=W3), w=["st1"], group="st1")
    S.dve(lambda e: e.tensor_copy(W3b[:], stf(1)[:, 0:256]), r=["st1"], w=["W3b"])

    S.barrier()

    def select(s, t, dst):
        stk = [f"st{s}.{r}.{pp}" for r in range(4) for pp in range(4)]
        S.dve(lambda e: e.tensor_scalar(tmp[t][:], st[s][:, :, 0:128], mk[:, 0:1], None, ALU.mult),
              r=stk + ["mk"], w=[f"tmp{t}"])
        for gg in range(1, 3):
            S.dve(lambda e, gg=gg: e.scalar_tensor_tensor(tmp[t][:], st[s][:, :, gg * 128:(gg + 1) * 128],
                                                         mk[:, gg:gg + 1], tmp[t][:], ALU.mult, ALU.add),
                  r=stk + ["mk", f"tmp{t}"], w=[f"tmp{t}"])
        S.dve(lambda e: e.scalar_tensor_tensor(dst, st[s][:, :, 384:512], mk[:, 3:4], tmp[t][:], ALU.mult, ALU.add),
              r=stk + ["mk", f"tmp{t}"], w=["fb"])

    for j in range(16):
        s = j % 2
        for r in range(4):
            for pp in range(4):
                src = fall[pp][r * 512:(r + 1) * 512, :].rearrange("(a n) c -> a n c", n=64)[:, 4 * j:4 * j + 4, :]
                p0 = 32 * r + 8 * pp
                S.dma(lambda e, s=s, p0=p0, src=src: e.dma_start(out=st[s][p0:p0 + 8, :, :], in_=src),
                      w=[f"st{s}.{r}.{pp}"], group=f"st{s}")
        select(s, s, fb[:, 4 * j:4 * j + 4, :])
    pi = 0
    for n2 in range(0, 64, 2):
        p = pi % 4; pi += 1
        for d in range(2):
            S.pe(lambda e, p=p, n2=n2, d=d: e.matmul(ps[p][:, d * 256:(d + 1) * 256], fb[:, n2 + d, :], TWb[:, n2 + d, :],
                                                     start=True, stop=True), r=["fb", "TWb"], w=[f"ps{p}"])
        for d in range(2):
            src = lambda p=p, d=d: ps[p][:, d * 256:(d + 1) * 256].rearrange("p (r j a) -> p r j a", r=2, a=2)
            dst = lambda n2=n2, d=d: Y[:, :, :, 2 * (n2 + d):2 * (n2 + d) + 2]
            if d == 0:
                S.act(lambda e, src=src, dst=dst: e.copy(dst(), src()), r=[], w=["Y", f"ps{p}"])
            else:
                S.dve(lambda e, src=src, dst=dst: e.tensor_copy(dst(), src()), r=[], w=["Y", f"ps{p}"])
    for j in range(0, 64, 2):
        p = pi % 4; pi += 1
        for d in range(2):
            jj = j + d
            S.pe(lambda e, p=p, jj=jj, d=d: e.matmul(ps[p][:, d * 256:(d + 1) * 256], Y[:, 0, jj, :],
                                                     FCb[:, 0:256], start=True, stop=False), r=["Y", "FCb"], w=[f"ps{p}"])
            S.pe(lambda e, p=p, jj=jj, d=d: e.matmul(ps[p][:, d * 256:(d + 1) * 256], Y[:, 1, jj, :],
                                                     FCb[:, 256:512], start=False, stop=True), r=["Y", "FCb"], w=[f"ps{p}"])
        if (j // 2) % 2 == 0:
            S.act(lambda e, p=p, j=j: e.copy(U[:, j:j + 2, :].rearrange("p a b -> p (a b)"), ps[p][:]), r=[f"ps{p}"], w=["U"])
        else:
            S.dve(lambda e, p=p, j=j: e.tensor_copy(U[:, j:j + 2, :].rearrange("p a b -> p (a b)"), ps[p][:]), r=[f"ps{p}"], w=["U"])
    scale = 1.0 / 1024.0
    for j0 in range(0, 64, 4):
        p = pi % 4; pi += 1
        for d in range(4):
            jj = j0 + d
            S.pe(lambda e, p=p, jj=jj, d=d: e.matmul(ps[p][:, d * 128:(d + 1) * 128], W3b[:, 0:128], U[:, jj, 0:128],
                                                     start=True, stop=False), r=["U", "W3b"], w=[f"ps{p}"])
            S.pe(lambda e, p=p, jj=jj, d=d: e.matmul(ps[p][:, d * 128:(d + 1) * 128], W3b[:, 128:256], U[:, jj, 128:256],
                                                     start=False, stop=True), r=["U", "W3b"], w=[f"ps{p}"])
        S.dve(lambda e, p=p, j0=j0: e.tensor_scalar(frt[:, j0:j0 + 4, :].rearrange("p a b -> p (a b)"), ps[p][:],
                                                    scale, None, ALU.mult), r=[f"ps{p}"], w=["frt"])
    for q in range(4):
        fv = fout[q].rearrange("(k2 jj a) c -> a k2 jj c", jj=64, a=2)
        for a in range(2):
            S.dma(lambda e, a=a, q=q, fv=fv: e.dma_start(out=fv[a], in_=frt[a * 64 + 16 * q:a * 64 + 16 * (q + 1), :, :]),
                  r=["frt"], w=[f"fo{a}{q}"], group=f"fo{a}")
    if with_ctx:
        S.barrier()
        fcb = g.sb([128, 2, 128], BF16)
        TWcb = g.sb([128, 2, 512], BF16)
        Yc = g.sb([128, 512], BF16)
        oc = g.sb([128, 2, 128], F32)
        for t in range(2):
            S.dma(lambda e, t=t: e.dma_start(out=st[t][:, 0, :], in_=fall[4][t * 128:(t + 1) * 128, :]),
                  w=[f"st{t}"], group=f"st{t}")
            S.dve(lambda e, t=t: e.tensor_scalar(tmp[t][:, 0, :], st[t][:, 0, 0:128], mk[:, 0:1], None, ALU.mult),
                  r=[f"st{t}", "mk"], w=[f"tmp{t}"])
            for gg in range(1, 4):
                S.dve(lambda e, t=t, gg=gg: e.scalar_tensor_tensor(
                    tmp[t][:, 0, :], st[t][:, 0, gg * 128:(gg + 1) * 128], mk[:, gg:gg + 1], tmp[t][:, 0, :], ALU.mult, ALU.add),
                    r=[f"st{t}", "mk", f"tmp{t}"], w=[f"tmp{t}"])
            S.dve(lambda e, t=t: e.tensor_copy(fcb[:, t, :], tmp[t][:, 0, :]), r=[f"tmp{t}"], w=["fcb"])
        for t in range(2):
            S.dma(lambda e, t=t: e.dma_start(out=stf(t)[:, 0:512], in_=TWc[:, t, :]), w=[f"st{t}"], group=f"st{t}")
            S.dve(lambda e, t=t: e.tensor_copy(TWcb[:, t, :], stf(t)[:, 0:512]), r=[f"st{t}"], w=["TWcb"])
        p = pi % 4; pi += 1
        for t in range(2):
            S.pe(lambda e, p=p, t=t: e.matmul(ps[p][:], fcb[:, t, :], TWcb[:, t, :], start=(t == 0), stop=(t == 1)),
                 r=["fcb", "TWcb"], w=[f"ps{p}"])
        S.dve(lambda e, p=p: e.tensor_copy(Yc[:], ps[p][:]), r=[f"ps{p}"], w=["Yc"])
        p = pi % 4; pi += 1
        for kt in range(2):
            S.pe(lambda e, p=p, kt=kt: e.matmul(ps[p][:, kt * 128:(kt + 1) * 128], Yc[:, kt * 128:(kt + 1) * 128],
                                                FCb[:, 0:128], start=True, stop=False), r=["Yc", "FCb"], w=[f"ps{p}"])
            S.pe(lambda e, p=p, kt=kt: e.matmul(ps[p][:, kt * 128:(kt + 1) * 128], Yc[:, 256 + kt * 128:256 + (kt + 1) * 128],
                                                FCb[:, 256:384], start=False, stop=True), r=["Yc", "FCb"], w=[f"ps{p}"])
        S.dve(lambda e, p=p: e.tensor_scalar(oc[:].rearrange("p a b -> p (a b)"), ps[p][:, 0:256],
                                             1.0 / np.sqrt(256.0 * 128.0), None, ALU.mult), r=[f"ps{p}"], w=["oc"])
        S.dma(lambda e: e.dma_start(out=fout[4].rearrange("(t p) c -> p t c", p=128), in_=oc[:]),
              r=["oc"], w=["oco"], group="oco")


Q, L, SEQ, D = 2048, 256, 8192, 1024
T = Q + L
NCORES = 8


def build_fused(depth=4, stop_after=None):
    nc = bass.Bass(target_bir_lowering=False)
    g = G(nc)
    if stop_after is not None:
        g.max_phase = stop_after
    ext = lambda name, shape: nc.dram_tensor(name, list(shape), F32, kind="ExternalInput")
    xin = ext("xin", [T, D]); cin = ext("cin", [128, D])
    ada_w = ext("ada_w", [D, 3 * D]); ada_b = ext("ada_b", [3 * D])
    plg = ext("post_ln_g", [4, D]); plb = ext("post_ln_b", [4, D])
    ewi = ext("even_w_in", [2, D, 2496]); ewo = ext("even_w_out", [2, D, D])
    owi = ext("odd_w_in", [2, D, 2560]); owo = ext("odd_w_out", [2, D, D])
    gw2 = ext("gla_w2", [2, 2, 16, 256]); gb = ext("gla_b", [2, 2, 256]); gng = ext("gla_norm_g", [2, 128])
    qng = ext("mla_q_norm_g", [2, 256]); wuq = ext("mla_w_uq", [2, 256, 768])
    kng = ext("mla_kv_norm_g", [2, 128]); wukv = ext("mla_w_ukv", [2, 128, 1024])
    swT = ext("sgu_wT", [2, 4, 128, 128]); sbT = ext("sgu_bT", [2, 128, 4])
    csq = ext("csq", [T, 32]); csk = ext("csk", [SEQ + L, 32])
    ident_d = ext("ident", [128, 128]); g.ident_d = ident_d.ap()
    gcst = ext("gcst", [2, 64, 320]); mask = ext("mask", [128, 16])
    TW = ext("TW", [128, 64, 256]); FC = ext("FC", [128, 512]); W3 = ext("W3", [128, 256]); TWc = ext("TWc", [128, 2, 512])
    xout = nc.dram_tensor("xout", [Q, D], F32, kind="ExternalOutput")
    ML = g.dram("ML", [128, 3 * D]); MS = g.dram("MS", [2, 3 * D]); MALL = g.dram("MALL", [8, 3 * D])
    X = [g.dram(f"X{i}", [T, D]) for i in range(2)]
    Z = g.dram("Z", [T, 2560])
    QP = g.dram("QP", [T, 768])
    KSIN = [g.dram(f"KSIN{p}", [Q // 2, 160]) for p in range(2)]
    KSG = [g.dram(f"KSG{p}", [4 * Q // 2, 160]) for p in range(2)]
    KSA = g.dram("KSA", [SEQ + L, 160])
    OML = g.dram("OML", [T, 512]); OG = g.dram("OG", [T, 512])
    SEGL = g.dram("SEGL", [64, 8 * 129]); SEGA = g.dram("SEGA", [256, 8 * 129])
    FPR = [512, 512, 512, 512, 256]
    FIN = [g.dram(f"FIN{p}", [FPR[p], 512]) for p in range(5)]
    FALL = [g.dram(f"FALL{p}", [4 * FPR[p], 512]) for p in range(5)]
    OPR = [2048, 2048, 2048, 2048, 256]
    FOUT = [g.dram(f"FOUT{p}", [OPR[p], 128]) for p in range(5)]
    FOALL = [g.dram(f"FOALL{p}", [4 * OPR[p], 128]) for p in range(5)]
    S = g.S
    rows = lambda i: slice(i * 128, (i + 1) * 128)
    HN = 3 * D // 2
    for hh in range(2):
        cs = slice(hh * HN, (hh + 1) * HN)
        emit_k1(g, lambda i: cin.ap()[rows(i), :], 1, D, HN, ada_w.ap()[:, cs],
                lambda i, cs=cs: ML.ap()[rows(i), cs], "silu", bias=ada_b.ap()[cs])
    g.phase()
    S.dma(lambda e: e.dma_start(out=MS.ap(), in_=ML.ap()[0:2, :]), w=["ms"], group="dc0")
    allgather(g, MALL, MS, "mall", r=["ms"])
    xcur = xin
    for l in range(depth):
        li = l // 2
        even = l % 2 == 0
        last = l == depth - 1
        Ml = MALL.ap()
        r0, r1 = 2 * l, 2 * l + 1
        mods = (Ml[r0, 0:D], Ml[r0, D:2 * D], Ml[r1, 0:D], Ml[r1, D:2 * D])
        vecs = (Ml[r0, 2 * D:3 * D], Ml[r1, 2 * D:3 * D], plg.ap()[l], plb.ap()[l])
        N = 2496 if even else 2560
        Zl = Z.ap()[:, 0:N]
        xap = xcur.ap()
        emit_k1(g, lambda i, xap=xap: xap[rows(i), :], T // 128, D, N, (ewi if even else owi).ap()[li],
                lambda i, Zl=Zl: Zl[rows(i), :], "ln", n_lat_tiles=Q // 128, mods=mods)
        Tk = Q if last else T
        xn = xout if last else X[l % 2]
        if even:
            emit_k1(g, lambda i, Zl=Zl: Zl[rows(i), 1568:1824], T // 128, 256, 768, wuq.ap()[li],
                    lambda i: QP.ap()[rows(i), :], "rms", gvec=qng.ap()[li])
            g.phase()
            S.dma(lambda e, Zl=Zl: e.dma_start(out=KSA.ap()[0:L, :], in_=Zl[Q:T, 1824:1984]), w=["ksa0"], group="dc1")
            HQ = Q // 2
            for p in range(2):
                S.dma(lambda e, Zl=Zl, p=p: e.dma_start(out=KSIN[p].ap(), in_=Zl[HQ * p:HQ * (p + 1), 1824:1984]),
                      w=[f"ksin{p}"], group=f"dc0{p}")
                allgather(g, KSG[p], KSIN[p], f"ksg{p}", r=[f"ksin{p}"])
                dst = KSA.ap()[L:L + SEQ, :].rearrange("(r h n) c -> h r n c", r=4, h=2)[p]
                S.dma(lambda e, p=p, dst=dst: e.dma_start(out=dst, in_=KSG[p].ap().rearrange("(r n) c -> r n c", r=4)),
                      r=[f"ksg{p}"], w=[f"ksa1{p}"], group=f"dc2{p}")
            emit_k3(g, Q, L, SEQ + L, QP.ap(), KSA.ap(), kng.ap()[li], wukv.ap()[li], csq.ap(), csk.ap(), OML.ap())
            emit_k4s(g, Zl, OG.ap(), SEGL, SEGA, gw2.ap()[li], gb.ap()[li], gcst.ap(), mask.ap())
            emit_k5(g, Tk, "even", Q // 128, xap, ewo.ap()[li], vecs, Zl, xn.ap(), og=OG.ap(), oml=OML.ap(),
                    gng=gng.ap()[li])
        else:
            g.phase()
            r0 = 0
            for p in range(5):
                S.dma(lambda e, Zl=Zl, p=p, r0=r0: e.dma_start(out=FIN[p].ap(), in_=Zl[r0:r0 + FPR[p], 0:512]),
                      w=[f"fin{p}"], group=f"dcf{p}")
                allgather(g, FALL[p], FIN[p], f"fall{p}", r=[f"fin{p}"])
                r0 += FPR[p]
            emit_k7(g, [a.ap() for a in FALL], [a.ap() for a in FOUT], TW.ap(), FC.ap(), W3.ap(), TWc.ap(), mask.ap())
            g.phase()
            for p in range(5):
                allgather(g, FOALL[p], FOUT[p], f"foall{p}")
            fo = [a.ap().rearrange("(g r) c -> r g c", g=4) for a in FOALL]
            emit_k5(g, Tk, "odd", Q // 128, xap, owo.ap()[li], vecs, Zl, xn.ap(),
                    fr_lat=lambda qq, i, fo=fo: fo[qq][128 * i:128 * (i + 1), :, :],
                    fr_ctx=lambda i, fo=fo: fo[4][128 * (i - Q // 128):128 * (i - Q // 128 + 1), :, :],
                    swT=swT.ap()[li], sbT=sbT.ap()[li], mask=mask.ap())
        xcur = xn
    g.finish()
    return nc


def make_inputs(x, c, ctx, c_ctx, ada_w, ada_b, post_ln_g, post_ln_b, even_w_in, gla_w2, gla_b, gla_norm_g,
                mla_q_norm_g, mla_w_uq, mla_kv_norm_g, mla_w_ukv, even_w_out, odd_w_in, sgu_w, sgu_b, odd_w_out,
                rope_tables, fnet_consts):
    f32 = np.float32
    cc = lambda a: np.ascontiguousarray(a, dtype=f32)
    shared = dict(post_ln_g=cc(post_ln_g), post_ln_b=cc(post_ln_b),
                  even_w_in=cc(even_w_in), even_w_out=cc(even_w_out), odd_w_in=cc(odd_w_in), odd_w_out=cc(odd_w_out),
                  gla_w2=cc(gla_w2), gla_b=cc(gla_b), gla_norm_g=cc(gla_norm_g), mla_q_norm_g=cc(mla_q_norm_g),
                  mla_w_uq=cc(mla_w_uq), mla_kv_norm_g=cc(mla_kv_norm_g), mla_w_ukv=cc(mla_w_ukv),
                  sgu_wT=cc(np.transpose(sgu_w, (0, 1, 3, 2))), sgu_bT=cc(np.transpose(sgu_b, (0, 2, 1))),
                  csk=rope_tables(np.concatenate([-np.ones(L, int), np.arange(SEQ)])),
                  ident=np.eye(128, dtype=f32), gcst=gla_consts2(), **fnet_consts())
    maps = []
    for j in range(NCORES):
        b, i = j // 4, j % 4
        m = dict(shared)
        m["xin"] = cc(np.concatenate([x[b, Q * i:Q * (i + 1)], ctx[b]], 0))
        cin = np.zeros((128, D), f32)
        cin[0] = c[b]; cin[1] = c_ctx
        m["cin"] = cin
        m["ada_w"] = cc(ada_w[i])
        m["ada_b"] = cc(ada_b[i])
        m["csq"] = rope_tables(np.concatenate([np.arange(Q) + Q * i, -np.ones(L, int)]))
        mk = np.zeros((128, 16), f32)
        mk[:, i] = 1.0
        for jj in range(4):
            mk[:, 4 + jj] = 1.0 if jj < i else 0.0
            mk[:, 8 + jj] = 1.0 if jj > i else 0.0
        m["mask"] = mk
        maps.append(m)
    return maps

def rope_tables(pos):
    pos = np.asarray(pos)
    row = (pos // 64).astype(np.float32)
    col = (pos % 64).astype(np.float32)
    inv = (10000.0 ** (-np.arange(8, dtype=np.float32) / 8)).astype(np.float32)
    ang = np.concatenate([row[:, None] * inv, col[:, None] * inv], -1).astype(np.float32)
    c = np.cos(ang).astype(np.float32)
    s = np.sin(ang).astype(np.float32)
    ident = pos < 0
    c[ident] = 1.0
    s[ident] = 0.0
    return np.concatenate([c, s], -1).astype(np.float32)


def fnet_consts():
    n1 = np.arange(128)[:, None, None].astype(np.float64)
    n2 = np.arange(64)[None, :, None].astype(np.float64)
    k1 = np.arange(128)[None, None, :].astype(np.float64)
    ang = 2 * np.pi * k1 * (64 * n1 + n2) / 8192.0
    TW = np.concatenate([np.cos(ang), -np.sin(ang)], -1).astype(np.float32)
    c = np.arange(128)[:, None].astype(np.float64)
    cp = np.arange(128)[None, :].astype(np.float64)
    a = 2 * np.pi * c * cp / 128.0
    Cc, Sc = np.cos(a), np.sin(a)
    FC = np.concatenate([Cc, -Sc, Sc, Cc], -1).astype(np.float32)
    W3 = np.zeros((64, 2, 2, 2, 64), np.float64)
    n2v = np.arange(64)[:, None]
    k2v = np.arange(64)[None, :]
    a3 = 2 * np.pi * n2v * k2v / 64.0
    for aa in range(2):
        W3[:, aa, 0, aa, :] = np.cos(a3)
        W3[:, aa, 1, aa, :] = np.sin(a3)
    W3 = W3.reshape(128, 256).astype(np.float32)
    n = np.arange(256)[:, None].astype(np.float64)
    k = np.arange(256)[None, :].astype(np.float64)
    ac = 2 * np.pi * n * k / 256.0
    TWc = np.concatenate([np.cos(ac), -np.sin(ac)], -1).reshape(2, 128, 512).transpose(1, 0, 2)
    TWc = np.ascontiguousarray(TWc).astype(np.float32)
    return dict(TW=TW, FC=FC, W3=W3, TWc=TWc)


_NC = {}


def kernel(x, c, ctx, c_ctx, ada_w, ada_b, post_ln_g, post_ln_b, even_w_in, gla_w2, gla_b, gla_norm_g,
           mla_q_norm_g, mla_w_uq, mla_kv_norm_g, mla_w_ukv, even_w_out, odd_w_in, sgu_w, sgu_b, odd_w_out):
    if "nc" not in _NC:
        _NC["nc"] = build_fused(4)
    maps = make_inputs(np.asarray(x), np.asarray(c), np.asarray(ctx), np.asarray(c_ctx), np.asarray(ada_w),
                       np.asarray(ada_b), np.asarray(post_ln_g), np.asarray(post_ln_b), np.asarray(even_w_in),
                       np.asarray(gla_w2), np.asarray(gla_b), np.asarray(gla_norm_g), np.asarray(mla_q_norm_g),
                       np.asarray(mla_w_uq), np.asarray(mla_kv_norm_g), np.asarray(mla_w_ukv), np.asarray(even_w_out),
                       np.asarray(odd_w_in), np.asarray(sgu_w), np.asarray(sgu_b), np.asarray(odd_w_out),
                       rope_tables, fnet_consts)
    res = run_bass_kernel_spmd(_NC["nc"], maps, core_ids=list(range(NCORES)))
    out = np.empty((2, SEQ, D), np.float32)
    for j in range(NCORES):
        out[j // 4, (j % 4) * Q:(j % 4 + 1) * Q] = res.results[j]["xout"]
    return out
```

```python
import contextlib
import numpy as np
import concourse.bass as bass
import concourse.mybir as mybir
from concourse.bass_utils import run_bass_kernel_spmd

F32 = mybir.dt.float32
BF16 = mybir.dt.bfloat16
AF = mybir.ActivationFunctionType
ALU = mybir.AluOpType
AX = mybir.AxisListType
ALPHA = 8 ** 0.25
MLA_SCALE = 96 ** -0.5
ARENA_F32 = 52900


class Sched:
    def __init__(self, nc):
        self.nc = nc
        self.ops = []
        self.last_w = {}
        self.readers = {}
        self.stack = contextlib.ExitStack()
        self.bar = set()
        self.pending = {}

    def barrier(self):
        last = {}
        for i, op in enumerate(self.ops):
            k = ("dma", op["dma"]) if op["dma"] is not None else ("eng", op["eng"])
            last[k] = i
        self.bar = set(last.values())
        self.pending = {e: True for e in ["pe", "act", "dve", "pool", "sp"]}
        self.last_w = {}
        self.readers = {}

    muted = False

    def add(self, eng, fn, r=(), w=(), dma=None, inc=16):
        if self.muted:
            return -1
        idx = len(self.ops)
        deps = set()
        for k in r:
            if k in self.last_w:
                deps.add(self.last_w[k])
        for k in w:
            if k in self.last_w:
                deps.add(self.last_w[k])
            for x in self.readers.get(k, ()):
                deps.add(x)
        if self.pending.get(eng):
            deps |= self.bar
            self.pending[eng] = False
        deps.discard(idx)
        self.ops.append(dict(eng=eng, fn=fn, deps=deps, dma=dma, inc=inc))
        for k in r:
            self.readers.setdefault(k, []).append(idx)
        for k in w:
            self.last_w[k] = idx
            self.readers[k] = []
        return idx

    def pe(self, fn, r=(), w=()):
        return self.add("pe", fn, r, w)

    def act(self, fn, r=(), w=()):
        return self.add("act", fn, r, w)

    def dve(self, fn, r=(), w=()):
        return self.add("dve", fn, r, w)

    def pool(self, fn, r=(), w=()):
        return self.add("pool", fn, r, w)

    def dma(self, fn, r=(), w=(), group=None, eng="sp", inc=16):
        assert group is not None
        return self.add(eng, fn, r, w, dma=group, inc=inc)

    def emit(self):
        nc = self.nc
        ops = self.ops
        n = len(ops)
        needs_signal = [False] * n
        for i, op in enumerate(ops):
            keep = set()
            for d in op["deps"]:
                dop = ops[d]
                if dop["dma"] is None and dop["eng"] == op["eng"] and op["eng"] == "pe":
                    continue
                keep.add(d)
                needs_signal[d] = True
            op["deps"] = keep
        engs = ["pe", "act", "dve", "pool", "sp"]
        sems = {e: self.stack.enter_context(nc.semaphore(f"s_{e}")) for e in engs}
        cnt = {e: 0 for e in engs}
        groups = {}
        gcnt = {}
        for i, op in enumerate(ops):
            if op["dma"] is not None:
                g = op["dma"]
                if g not in groups:
                    groups[g] = self.stack.enter_context(nc.semaphore(f"d_{len(groups)}"))
                    gcnt[g] = 0
                op["sem"] = groups[g]
                gcnt[g] += op["inc"] * getattr(op["fn"], "ndma", 1)
                op["val"] = gcnt[g]
            elif needs_signal[i]:
                cnt[op["eng"]] += 1
                op["sem"] = sems[op["eng"]]
                op["val"] = cnt[op["eng"]]
        print("sched: ops", n, "dma groups", len(groups), "sem counts", cnt, flush=True)
        final = dict((g, (groups[g], gcnt[g])) for g in groups)

        def stream(ename):
            def body(eng):
                known = {}
                for i, op in enumerate(ops):
                    if op["eng"] != ename:
                        continue
                    need = {}
                    for d in op["deps"]:
                        dop = ops[d]
                        s, v = dop["sem"], dop["val"]
                        if v > need.get(id(s), (None, 0))[1]:
                            need[id(s)] = (s, v)
                    for sid, (s, v) in sorted(need.items(), key=lambda kv: kv[1][1]):
                        if known.get(sid, 0) < v:
                            eng.wait_ge(s, v)
                            known[sid] = v
                    ins = op["fn"](eng)
                    if op["dma"] is not None:
                        if not isinstance(ins, (list, tuple)):
                            ins = [ins]
                        assert len(ins) == getattr(op["fn"], "ndma", 1)
                        for x in ins:
                            x.then_inc(op["sem"], op["inc"])
                    elif needs_signal[i]:
                        ins.then_inc(op["sem"], 1)
                if ename == "sp":
                    for g, (s, v) in final.items():
                        if known.get(id(s), 0) < v:
                            eng.wait_ge(s, v)
            return body

        with nc.Block() as block:
            block.tensor(stream("pe"))
            block.scalar(stream("act"))
            block.vector(stream("dve"))
            block.gpsimd(stream("pool"))
            block.sync(stream("sp"))

    def close(self):
        self.stack.close()


class G:
    def __init__(self, nc):
        self.nc = nc
        self.S = Sched(nc)
        self.arena = self.S.stack.enter_context(nc.sbuf_tensor("arena", [128, ARENA_F32], F32))
        self.banks = [self.S.stack.enter_context(nc.psum_tensor(f"bank{i}", [128, 512], F32)) for i in range(8)]
        self.off = 0
        self.nd = 0
        self.ncc = 0

    nphase = 0
    max_phase = 10 ** 9

    def phase(self):
        self.nphase += 1
        if self.nphase > self.max_phase:
            self.S.muted = True
        self.S.barrier()
        self.off = 0

    def sb(self, shape, dtype=F32, name=None):
        P = shape[0]
        n = int(np.prod(shape[1:]))
        words = n if dtype == F32 else (n + 1) // 2
        o = self.off
        self.off += words
        assert self.off <= ARENA_F32, ("SBUF arena overflow", self.off)
        ap = self.arena[0:P, o:o + words]
        if dtype != F32:
            ap = ap.bitcast(dtype)[:, 0:n]
        if len(shape) == 3:
            ap = ap.rearrange("p (a b) -> p a b", a=shape[1], b=shape[2])
        elif len(shape) == 4:
            ap = ap.rearrange("p (a b c) -> p a b c", a=shape[1], b=shape[2], c=shape[3])
        return ap

    def ps(self, bank, shape, dtype=F32):
        P = shape[0]
        n = int(np.prod(shape[1:]))
        ap = self.banks[bank][0:P, :]
        if dtype != F32:
            ap = ap.bitcast(dtype)
        ap = ap[:, 0:n]
        if len(shape) == 3:
            ap = ap.rearrange("p (a b) -> p a b", a=shape[1], b=shape[2])
        return ap

    def dram(self, name, shape, kind="Internal"):
        return self.nc.dram_tensor(name, list(shape), F32, kind=kind)

    def finish(self):
        self.S.emit()
        self.S.close()


def load_ident(g, tag="id"):
    S = g.S
    identf = g.sb([128, 128], F32)
    ident = g.sb([128, 128], BF16)
    S.dma(lambda e: e.dma_start(out=identf[:], in_=g.ident_d), w=["identf"], group="identf")
    S.dve(lambda e: e.tensor_copy(ident[:], identf[:]), r=["identf"], w=["ident"])
    return identf, ident


def dcopy(g, dst, src, name, grp="dcopy"):
    g.S.dma(lambda e: e.dma_start(out=dst, in_=src), w=[name], group=grp)


def allgather(g, dst, src, name, r=()):
    groups = [[0, 1, 2, 3], [4, 5, 6, 7]]
    g.S.dma(lambda e: e.collective_compute("AllGather", ALU.bypass, replica_groups=groups,
                                            ins=[src.ap().opt()], outs=[dst.ap().opt()]),
            r=list(r), w=[name], group=f"cc{g.ncc % 3}", eng="pool", inc=1)
    g.ncc += 1


def emit_k1(g, xsrc, nt, K, N, w, z, mode, n_lat_tiles=None, mods=None, gvec=None, bias=None, eps=1e-6):
    S = g.S
    g.phase()
    kc = K // 128
    nch = (N + 511) // 512
    identf, ident = load_ident(g)
    wbf = g.sb([128, kc, N], BF16)
    NW = 4 if kc > 2 else 2
    wst = [g.sb([128, N], F32) for i in range(NW)]
    if mode == "ln":
        modt = g.sb([128, 4, K], F32)
        for a in range(4):
            S.dma(lambda e, a=a: e.dma_start(out=modt[:, a, :], in_=mods[a].partition_broadcast(128)),
                  w=["modt"], group="modt")
        for a in (1, 3):
            S.dve(lambda e, a=a: e.tensor_scalar_add(modt[:, a, :], modt[:, a, :], 1.0), r=["modt"], w=["modt"])
    elif mode == "rms":
        gt = g.sb([128, K], F32)
        S.dma(lambda e: e.dma_start(out=gt[:], in_=gvec.partition_broadcast(128)), w=["gt"], group="gt")
    else:
        bt = g.sb([128, N], F32)
        S.dma(lambda e: e.dma_start(out=bt[:], in_=bias.partition_broadcast(128)), w=["bt"], group="gt")
    for k in range(kc):
        s = k % NW
        S.dma(lambda e, k=k, s=s: e.dma_start(out=wst[s][:], in_=w[k * 128:(k + 1) * 128, :]),
              w=[f"wst{s}"], group=f"wst{s}", eng=("sp", "act", "pool", "sp")[s])
        if k % 2 == 0:
            S.act(lambda e, k=k, s=s: e.copy(wbf[:, k, :], wst[s][:]), r=[f"wst{s}"], w=[f"wbf{k}"])
        else:
            S.dve(lambda e, k=k, s=s: e.tensor_copy(wbf[:, k, :], wst[s][:]), r=[f"wst{s}"], w=[f"wbf{k}"])
    NX = 2
    xt = [g.sb([128, K], F32) for i in range(NX)]
    xn = [g.sb([128, K], F32) for i in range(NX)]
    hb = [g.sb([128, K], BF16) for i in range(NX)]
    hT = [g.sb([128, kc, 128], BF16) for i in range(NX)]
    st = [g.sb([128, 8, 6], F32) for i in range(NX)]
    mv = [g.sb([128, 4], F32) for i in range(NX)]
    zt = [g.sb([128, N], F32) for i in range(NX)]
    tp = [g.ps(i, [128, kc, 128], BF16) for i in range(2)]
    zp = [g.ps(2 + i, [128, 512], F32) for i in range(4)]
    wkeys = [f"wbf{k}" for k in range(kc)]
    def emit_load(i):
        s = i % NX
        S.dma(lambda e, i=i, s=s: e.dma_start(out=xt[s][:], in_=xsrc(i)), w=[f"xt{s}"], group=f"xt{s}")

    zst = dict(zpi=0)

    def stageE(i):
        s = i % NX
        if i + 1 < nt:
            emit_load(i + 1)
        if mode == "ln":
            nsub = max(1, K // 512)
            fs = K // nsub
            for j in range(nsub):
                S.dve(lambda e, s=s, j=j, fs=fs: e.bn_stats(st[s][:, j, :], xt[s][:, j * fs:(j + 1) * fs]),
                      r=[f"xt{s}"], w=[f"st{s}"])
            S.dve(lambda e, s=s, nsub=nsub: e.bn_aggr(mv[s][:, 0:2], st[s][:, 0:nsub, :]), r=[f"st{s}"], w=[f"mv{s}"])
            S.dve(lambda e, s=s: e.tensor_scalar_add(mv[s][:, 3:4], mv[s][:, 1:2], eps), r=[f"mv{s}"], w=[f"mv{s}"])
            S.act(lambda e, s=s: e.sqrt(mv[s][:, 3:4], mv[s][:, 3:4]), r=[f"mv{s}"], w=[f"mv{s}"])
            S.dve(lambda e, s=s: e.reciprocal(mv[s][:, 2:3], mv[s][:, 3:4]), r=[f"mv{s}"], w=[f"mv{s}"])
            S.dve(lambda e, s=s: e.tensor_scalar(xn[s][:], xt[s][:], mv[s][:, 0:1], mv[s][:, 2:3],
                                                 ALU.subtract, ALU.mult),
                  r=[f"xt{s}", f"mv{s}"], w=[f"xn{s}"])
            a = 0 if (n_lat_tiles is None or i < n_lat_tiles) else 2
            S.pool(lambda e, s=s, a=a: e.tensor_tensor(xn[s][:], xn[s][:], modt[:, a + 1, :], ALU.mult),
                   r=[f"xn{s}", "modt"], w=[f"xn{s}"])
            S.pool(lambda e, s=s, a=a: e.tensor_tensor(hb[s][:], xn[s][:], modt[:, a, :], ALU.add),
                   r=[f"xn{s}", "modt"], w=[f"hb{s}"])
        elif mode == "silu":
            S.act(lambda e, s=s: e.activation(hb[s][:], xt[s][:], AF.Silu), r=[f"xt{s}"], w=[f"hb{s}"])
        else:
            S.act(lambda e, s=s: e.activation(xn[s][:], xt[s][:], AF.Square, accum_out=mv[s][:, 0:1]),
                  r=[f"xt{s}"], w=[f"xn{s}", f"mv{s}"])
            S.dve(lambda e, s=s: e.tensor_scalar(mv[s][:, 1:2], mv[s][:, 0:1], 1.0 / K, eps, ALU.mult, ALU.add),
                  r=[f"mv{s}"], w=[f"mv{s}"])
            S.act(lambda e, s=s: e.sqrt(mv[s][:, 3:4], mv[s][:, 1:2]), r=[f"mv{s}"], w=[f"mv{s}"])
            S.dve(lambda e, s=s: e.reciprocal(mv[s][:, 2:3], mv[s][:, 3:4]), r=[f"mv{s}"], w=[f"mv{s}"])
            S.dve(lambda e, s=s: e.scalar_tensor_tensor(hb[s][:], xt[s][:], mv[s][:, 2:3], gt[:],
                                                        ALU.mult, ALU.mult),
                  r=[f"xt{s}", f"mv{s}", "gt"], w=[f"hb{s}"])

    def stageT(i):
        s = i % NX
        t = i % 2
        for k in range(kc):
            S.pe(lambda e, s=s, t=t, k=k: e.transpose(tp[t][:, k, :], hb[s][:, k * 128:(k + 1) * 128], ident[:]),
                 r=[f"hb{s}", "ident"], w=[f"tp{t}"])
        S.act(lambda e, s=s, t=t: e.copy(hT[s][:], tp[t][:]), r=[f"tp{t}"], w=[f"hT{s}"])

    def stageM(i):
        s = i % NX
        for c in range(nch):
            c0, c1 = c * 512, min(N, (c + 1) * 512)
            p = zst["zpi"] % 4
            zst["zpi"] += 1
            for k in range(kc):
                S.pe(lambda e, s=s, p=p, k=k, c0=c0, c1=c1: e.matmul(
                    zp[p][:, 0:c1 - c0], hT[s][:, k, :], wbf[:, k, c0:c1], start=(k == 0), stop=(k == kc - 1)),
                    r=[f"hT{s}", wkeys[k]], w=[f"zp{p}"])
            if mode == "silu":
                S.dve(lambda e, s=s, p=p, c0=c0, c1=c1: e.tensor_tensor(zt[s][:, c0:c1], zp[p][:, 0:c1 - c0], bt[:, c0:c1], ALU.add),
                      r=[f"zp{p}", "bt"], w=[f"zt{s}.{c}"])
            elif c % 2 == 0:
                S.dve(lambda e, s=s, p=p, c0=c0, c1=c1: e.tensor_copy(zt[s][:, c0:c1], zp[p][:, 0:c1 - c0]),
                      r=[f"zp{p}"], w=[f"zt{s}.{c}"])
            else:
                S.act(lambda e, s=s, p=p, c0=c0, c1=c1: e.copy(zt[s][:, c0:c1], zp[p][:, 0:c1 - c0]),
                      r=[f"zp{p}"], w=[f"zt{s}.{c}"])
            S.dma(lambda e, i=i, s=s, c0=c0, c1=c1: e.dma_start(out=z(i)[:, c0:c1], in_=zt[s][:, c0:c1]),
                  r=[f"zt{s}.{c}"], w=[f"zout{s}.{c}"], group=f"zo{s}.{c}", eng=("act" if c % 2 == 0 else "sp"))


    emit_load(0)
    stageE(0)
    if nt > 1:
        stageE(1)
    stageT(0)
    for i in range(nt):
        if i + 2 < nt:
            stageE(i + 2)
        if i + 1 < nt:
            stageT(i + 1)
        stageM(i)


def emit_k3(g, NQL, NQC, NK, q, ksa, kvg, wukv, csq, csk, out, nheads=8, eps=1e-6):
    kr = ksa[:, 128:160]
    S = g.S
    g.phase()
    NQ = NQL + NQC
    nqt = NQ // 128
    nkt = NK // 128
    identf, ident = load_ident(g)
    KH = (nkt + 1) // 2
    ckst = g.sb([128, KH, 128], F32)
    cknT = g.sb([128, nkt * 128], BF16)
    wst = g.sb([128, 1024], F32)
    wb = g.sb([128, 1024], BF16)
    gt = g.sb([128, 128], F32)
    ss = g.sb([128, nkt], F32)
    rstd = g.sb([128, nkt], F32)
    junk = g.sb([128, 128], F32)
    pk = g.ps(7, [128, 512], F32)
    kpad = g.sb([128, nkt, 128], BF16)
    vx = [g.sb([128, nkt, 65], BF16) for i in range(2)]
    kT = [g.sb([128, nkt * 128], BF16) for i in range(2)]
    qT = [g.sb([128, nqt * 128], BF16) for i in range(2)]
    qh = g.sb([128, nqt, 96], F32)
    qpad = g.sb([128, nqt, 128], BF16)
    krl = g.sb([128, nkt, 32], F32)
    cskt = g.sb([128, nkt, 32], F32)
    csqt = g.sb([128, nqt, 32], F32)
    tk = [g.sb([128, nkt, 16], F32) for i in range(2)]
    tq = [g.sb([128, nqt, 16], F32) for i in range(2)]
    pT = [g.sb([128, 512], BF16) for i in range(3)]
    oTs = [g.sb([65, 512], F32) for i in range(2)]
    ost = [g.sb([128, 64], F32) for i in range(4)]
    rc = [g.sb([128, 1], F32) for i in range(4)]
    sTp = [g.ps(i, [128, 512], F32) for i in range(3)]
    oTp = [g.ps(3 + i, [65, 512], F32) for i in range(2)]
    tpk = g.ps(5, [128, 8, 128], BF16)
    tpo = g.ps(6, [128, 65], F32)

    S.pool(lambda e: e.memset(kpad[:], 0.0), w=["kpad"])
    S.pool(lambda e: e.memset(qpad[:], 0.0), w=["qpad"])
    for i in range(2):
        S.pool(lambda e, i=i: e.memset(vx[i][:], 1.0), w=[f"vx{i}"])
    S.dma(lambda e: e.dma_start(out=krl[:], in_=kr.rearrange("(t p) c -> p t c", p=128)), w=["krl"], group="krl")
    S.dma(lambda e: e.dma_start(out=cskt[:], in_=csk.rearrange("(t p) c -> p t c", p=128)), w=["cskt"], group="cskt")
    S.dma(lambda e: e.dma_start(out=csqt[:], in_=csq.rearrange("(t p) c -> p t c", p=128)), w=["csqt"], group="csqt")

    def rope(eng_a, eng_b, src, cs, tmp, dst, keys_r, key_tmp, key_dst, xo):
        x1 = lambda: src[:, :, xo:xo + 16]
        x2 = lambda: src[:, :, xo + 16:xo + 32]
        c = lambda: cs[:, :, 0:16]
        sn = lambda: cs[:, :, 16:32]
        S.add(eng_a, lambda e: e.tensor_tensor(tmp[0][:], x1(), c(), ALU.mult), r=keys_r, w=[key_tmp + "0"])
        S.add(eng_b, lambda e: e.tensor_tensor(tmp[1][:], x2(), sn(), ALU.mult), r=keys_r, w=[key_tmp + "1"])
        S.add(eng_a, lambda e: e.tensor_tensor(dst[:, :, 64:80], tmp[0][:], tmp[1][:], ALU.subtract),
              r=[key_tmp + "0", key_tmp + "1"], w=[key_dst])
        S.add(eng_a, lambda e: e.tensor_tensor(tmp[0][:], x1(), sn(), ALU.mult), r=keys_r, w=[key_tmp + "0"])
        S.add(eng_b, lambda e: e.tensor_tensor(tmp[1][:], x2(), c(), ALU.mult), r=keys_r, w=[key_tmp + "1"])
        S.add(eng_a, lambda e: e.tensor_tensor(dst[:, :, 96:112], tmp[0][:], tmp[1][:], ALU.add),
              r=[key_tmp + "0", key_tmp + "1"], w=[key_dst])

    rope("dve", "pool", krl, cskt, tk, kpad, ["krl", "cskt"], "tk", "kpad", 0)

    for g0 in range(0, nkt, 8):
        g1 = min(nkt, g0 + 8)
        for t in range(g0, g1):
            S.pe(lambda e, t=t, g0=g0: e.transpose(tpk[:, t - g0, :], kpad[:, t, :], ident[:]), r=["kpad", "ident"], w=["tpk"])
        for i in range(2):
            S.dve(lambda e, g0=g0, g1=g1, i=i: e.tensor_copy(
                kT[i][64:128, g0 * 128:g1 * 128], tpk[64:128, 0:g1 - g0, :].rearrange("p a b -> p (a b)")),
                r=["tpk"], w=[f"kT{i}"])
    S.dma(lambda e: e.dma_start(out=gt[:], in_=kvg.partition_broadcast(128)), w=["gt"], group="gt")
    S.dma(lambda e: e.dma_start(out=wst[:], in_=wukv), w=["wst"], group="wst0")
    S.dve(lambda e: e.tensor_copy(wb[:], wst[:]), r=["wst"], w=["wb"])
    for half in range(2):
        t0 = half * KH
        t1 = min(nkt, t0 + KH)
        S.dma(lambda e, t0=t0, t1=t1: e.dma_start(
            out=ckst[:, 0:t1 - t0, :], in_=ksa[t0 * 128:t1 * 128, 0:128].rearrange("(t p) c -> p t c", p=128)),
            w=["ckst"], group="kvh")
        for t in range(t0, t1):
            S.act(lambda e, t=t, t0=t0: e.activation(junk[:], ckst[:, t - t0, :], AF.Square, accum_out=ss[:, t:t + 1]),
                  r=["ckst"], w=["junk", "ss"])
        S.dve(lambda e, t0=t0, t1=t1: e.tensor_scalar(ss[:, t0:t1], ss[:, t0:t1], 1.0 / 128, eps, ALU.mult, ALU.add),
              r=["ss"], w=["ss"])
        S.act(lambda e, t0=t0, t1=t1: e.sqrt(ss[:, t0:t1], ss[:, t0:t1]), r=["ss"], w=["ss"])
        S.dve(lambda e, t0=t0, t1=t1: e.reciprocal(rstd[:, t0:t1], ss[:, t0:t1]), r=["ss"], w=["rstd"])
        for t in range(t0, t1):
            S.dve(lambda e, t=t, t0=t0: e.scalar_tensor_tensor(kpad[:, t, :], ckst[:, t - t0, :], rstd[:, t:t + 1], gt[:],
                                                               ALU.mult, ALU.mult),
                  r=["ckst", "rstd", "gt"], w=["kpad"])
    for g0 in range(0, nkt, 8):
        g1 = min(nkt, g0 + 8)
        for t in range(g0, g1):
            S.pe(lambda e, t=t, g0=g0: e.transpose(tpk[:, t - g0, :], kpad[:, t, :], ident[:]), r=["kpad", "ident"], w=["tpk"])
        S.dve(lambda e, g0=g0, g1=g1: e.tensor_copy(cknT[:, g0 * 128:g1 * 128],
                                                    tpk[:, 0:g1 - g0, :].rearrange("p a b -> p (a b)")),
              r=["tpk"], w=["cknT"])

    chunks = []
    for c0 in range(0, NQL, 512):
        chunks.append((c0, min(512, NQL - c0), 0, nkt))
    if NQC:
        chunks.append((NQL, NQC, 0, 2))
    sti = 0
    oti = 0
    osti = 0
    def prologue(h):
        hb = h % 2
        for c0 in range(0, NK, 512):
            n = min(512, NK - c0)
            S.pe(lambda e, c0=c0, n=n: e.matmul(pk[0:64, 0:n], wb[:, h * 128:h * 128 + 64], cknT[:, c0:c0 + n],
                                                start=True, stop=True), r=["wb", "cknT"], w=["pk"])
            S.dve(lambda e, c0=c0, n=n: e.tensor_copy(kT[hb][0:64, c0:c0 + n], pk[0:64, 0:n]), r=["pk"], w=[f"kT{hb}"])
        for g0 in range(0, nkt, 8):
            g1 = min(nkt, g0 + 8)
            for t in range(g0, g1):
                S.pe(lambda e, t=t, g0=g0: e.matmul(pk[:, (t - g0) * 64:(t - g0 + 1) * 64], cknT[:, t * 128:(t + 1) * 128],
                                                    wb[:, h * 128 + 64:h * 128 + 128], start=True, stop=True),
                     r=["wb", "cknT"], w=["pk"])
            S.dve(lambda e, g0=g0, g1=g1: e.tensor_copy(
                vx[hb][:, g0:g1, 0:64], pk[:, 0:(g1 - g0) * 64].rearrange("p (a b) -> p a b", b=64)),
                r=["pk"], w=[f"vx{hb}"])
        S.dma(lambda e, h=h: e.dma_start(out=qh[:], in_=q[:, h * 96:(h + 1) * 96].rearrange("(t p) c -> p t c", p=128)),
              w=["qh"], group="qh")
        S.pool(lambda e: e.tensor_copy(qpad[:, :, 0:64], qh[:, :, 0:64]), r=["qh"], w=["qpad"])
        rope("pool", "dve", qh, csqt, tq, qpad, ["qh", "csqt"], "tq", "qpad", 64)
        for g0 in range(0, nqt, 8):
            g1 = min(nqt, g0 + 8)
            for t in range(g0, g1):
                S.pe(lambda e, t=t, g0=g0: e.transpose(tpk[:, t - g0, :], qpad[:, t, :], ident[:]),
                     r=["qpad", "ident"], w=["tpk"])
            S.dve(lambda e, g0=g0, g1=g1, hb=hb: e.tensor_copy(
                qT[hb][:, g0 * 128:g1 * 128], tpk[:, 0:g1 - g0, :].rearrange("p a b -> p (a b)")),
                r=["tpk"], w=[f"qT{hb}"])

    st = dict(sti=0, oti=0, osti=0)
    LAG = 2

    def mainloop(h):
        hb = h % 2
        its = []
        for (q0, qn, k0, k1) in chunks:
            op = st["oti"] % 2
            st["oti"] += 1
            for kt in range(k0, k1):
                its.append((q0, qn, k0, k1, kt, op))

        def emit_qk(n):
            q0, qn, k0, k1, kt, op = its[n]
            sp = st["sti"] % 3
            st["sti"] += 1
            its[n] = its[n] + (sp,)
            S.pe(lambda e: e.matmul(sTp[sp][:, 0:qn], kT[hb][:, kt * 128:(kt + 1) * 128], qT[hb][:, q0:q0 + qn],
                                    start=True, stop=True), r=[f"kT{hb}", f"qT{hb}"], w=[f"sTp{sp}"])
            S.act(lambda e: e.activation(pT[sp][:, 0:qn], sTp[sp][:, 0:qn], AF.Exp, scale=MLA_SCALE),
                  r=[f"sTp{sp}"], w=[f"pT{sp}"])

        def emit_pv(n):
            q0, qn, k0, k1, kt, op, sp = its[n]
            S.pe(lambda e: e.matmul(oTp[op][:, 0:qn], vx[hb][:, kt, :], pT[sp][:, 0:qn], start=(kt == k0), stop=(kt == k1 - 1)),
                 r=[f"vx{hb}", f"pT{sp}"], w=[f"oTp{op}"])
            if kt != k1 - 1:
                return
            S.dve(lambda e: e.tensor_copy(oTs[op][:, 0:qn], oTp[op][:, 0:qn]), r=[f"oTp{op}"], w=[f"oTs{op}"])
            for j in range(qn // 128):
                os_ = st["osti"] % 4
                st["osti"] += 1
                S.pe(lambda e, j=j: e.transpose(tpo[:], oTs[op][:, j * 128:(j + 1) * 128], identf[0:65, 0:65]),
                     r=[f"oTs{op}", "identf"], w=["tpo"])
                S.dve(lambda e, os_=os_: e.reciprocal(rc[os_][:], tpo[:, 64:65]), r=["tpo"], w=[f"rc{os_}"])
                S.dve(lambda e, os_=os_: e.tensor_scalar(ost[os_][:], tpo[:, 0:64], rc[os_][:, 0:1], None, ALU.mult),
                      r=["tpo", f"rc{os_}"], w=[f"ost{os_}"])
                r0 = q0 + j * 128
                S.dma(lambda e, os_=os_, r0=r0: e.dma_start(out=out[r0:r0 + 128, h * 64:(h + 1) * 64], in_=ost[os_][:]),
                      r=[f"ost{os_}"], w=[f"oo{os_}"], group=f"oo{os_}")

        nI = len(its)
        for n in range(nI):
            emit_qk(n)
            if n >= LAG:
                emit_pv(n - LAG)
            if n == nI // 2 and h + 1 < nheads:
                prologue(h + 1)
        for n in range(max(0, nI - LAG), nI):
            emit_pv(n)

    prologue(0)
    for h in range(nheads):
        mainloop(h)


def gla_consts2():
    s = np.arange(64)[:, None]
    t = np.arange(64)[None, :]
    out = np.zeros((2, 64, 320), np.float32)
    U = (s <= t).astype(np.float32)
    out[0, :, 0:64] = U - (s <= 31)
    out[0, :, 64:128] = U
    out[0, :, 128:192] = (s > t)
    out[0, :, 192:256] = U
    Ub = (s >= t).astype(np.float32)
    out[1, :, 0:64] = Ub - (s >= 32)
    out[1, :, 64:128] = Ub
    out[1, :, 128:192] = (s < t)
    out[1, :, 192:256] = Ub
    out[:, 0, 256:320] = 1.0
    return out


def emit_k4s(g, Z, OG, segloc, segall, w2, bb, cst_d, mask):
    S = g.S
    g.phase()
    NLC, NCC = 32, 4
    ZH = ["Zh0", "Zh1", "Zh2", "Zh3"]
    NCH = NLC + NCC
    identf, ident = load_ident(g)
    cst = g.sb([64, 2, 320], F32)
    maskb = g.sb([64, 2, 64], BF16)
    w2t = g.sb([16, 2, 256], F32)
    bbt = g.sb([1, 2, 256], F32)
    mk = g.sb([64, 16], F32)
    zh_off = g.off
    Zh = g.sb([64, NCH, 288], F32)
    vb = g.sb([64, NCH, 128], BF16)
    OLs = g.sb([64, NCH, 512], F32)
    qbP = g.sb([64, 8, NLC, 64], BF16)
    Sctx = g.sb([64, 8, 128], F32)
    SEG = g.sb([64, 8, 129], F32)
    Sib = g.sb([64, 8, 128], BF16)
    Wk = g.sb([64, 128], F32)
    cand = g.sb([64, 128], F32)
    diff = g.sb([64, 128], F32)
    NP = 7
    NPA, NPB = 5, 3
    mk_t = lambda shape, dt: [g.sb(shape, dt) for i in range(NP)]
    qTs = mk_t([64, 64], F32); kTs = mk_t([64, 64], F32); glTs = mk_t([16, 64], F32)
    Et = mk_t([64, 64], F32); Lt = mk_t([64, 64], F32); e12 = mk_t([64, 192], F32); e4 = mk_t([64, 64], F32)
    dec = mk_t([64, 1], F32)
    qe = mk_t([64, 64], BF16); ke = mk_t([64, 64], BF16); qb = mk_t([64, 64], BF16); kd = mk_t([64, 64], BF16)
    attm = mk_t([64, 64], BF16)
    Sf = [g.sb([64, 128], F32) for i in range(2)]
    Sb = [g.sb([64, 128], BF16) for i in range(2)]
    Pt = [g.sb([64, 1], F32) for i in range(2)]
    pA = [g.ps(i, [64, 512], F32) for i in range(NPA)]
    pB = [g.ps(NPA + i, [64, 512], F32) for i in range(NPB)]
    id64 = identf[0:64, 0:64]

    S.dma(lambda e: e.dma_start(out=cst[:], in_=cst_d.rearrange("d p f -> p d f")), w=["cst"], group="cst")
    S.dve(lambda e: e.tensor_copy(maskb[:], cst[:, :, 192:256]), r=["cst"], w=["maskb"])
    S.dma(lambda e: e.dma_start(out=w2t[:], in_=w2.rearrange("d r e -> r d e")), w=["w2t"], group="w2t")
    S.dma(lambda e: e.dma_start(out=bbt[:], in_=bb.rearrange("(o d) e -> o d e", o=1)), w=["bbt"], group="bbt")
    S.dma(lambda e: e.dma_start(out=mk[:], in_=mask[0:64, :]), w=["mk"], group="mk")
    Zv = Z.rearrange("(c p) f -> p c f", p=64)
    tasks = []
    ci = 0
    for h in range(4):
        first_of_head = True
        for d in range(2):
            hd = h * 2 + d
            for part in ("ctx", "lat"):
                lat = part == "lat"
                order = list(range(NLC)) if lat else list(range(NLC, NCH))
                if d == 1:
                    order = order[::-1]
                si = 0
                for idx, c in enumerate(order):
                    p = ci % NP
                    pa, pb = ci % NPA, ci % NPB
                    ci += 1
                    s0, s1 = si % 2, (si + 1) % 2
                    si += 1
                    tasks.append(dict(h=h, d=d, hd=hd, lat=lat, c=c, p=p, pa=pa, pb=pb, s0=s0, s1=s1, first=(idx == 0),
                                      last=(idx == len(order) - 1), head_start=first_of_head))
                    first_of_head = False

    def load_head(h):
        srcs = [(slice(h * 64, (h + 1) * 64), slice(0, 64)), (slice(256 + h * 64, 256 + (h + 1) * 64), slice(64, 128)),
                (slice(512 + h * 128, 512 + (h + 1) * 128), slice(128, 256)), (slice(1024, 1056), slice(256, 288))]
        for k, (sc, dc) in enumerate(srcs):
            S.dma(lambda e, sc=sc, dc=dc: e.dma_start(out=Zh[:, :, dc], in_=Zv[:, :, sc]), w=[f"Zh{k}"], group=f"Zh{k}")
        S.pool(lambda e: e.tensor_copy(vb[:], Zh[:, :, 128:256]), r=ZH, w=["vb"])

    def hdr(t):
        h, d, c, p, pa = t["h"], t["d"], t["c"], t["p"], t["pa"]
        return h, d, c, p, pa, f"pA{pa}"

    def stageA1(t):
        h, d, c, p, pa, A = hdr(t)
        gcol = 256 + 16 * d
        S.pe(lambda e: e.transpose(pA[pa][:, 320:384], Zh[:, c, 0:64], id64), r=ZH + ["identf"], w=[A])
        S.pe(lambda e: e.transpose(pA[pa][:, 384:448], Zh[:, c, 64:128], id64), r=ZH + ["identf"], w=[A])
        S.pe(lambda e: e.transpose(pA[pa][0:16, 448:512], Zh[:, c, gcol:gcol + 16], id64), r=ZH + ["identf"], w=[A])
        S.dve(lambda e: e.tensor_copy(glTs[p][:], pA[pa][0:16, 448:512]), r=[A], w=[f"glTs{p}"])
        S.dve(lambda e: e.tensor_scalar(qTs[p][:], pA[pa][:, 320:384], 0.125, None, ALU.mult), r=[A], w=[f"qTs{p}"])
        S.dve(lambda e: e.tensor_copy(kTs[p][:], pA[pa][:, 384:448]), r=[A], w=[f"kTs{p}"])

    def stageA2(t):
        h, d, c, p, pa, A = hdr(t)
        w2hd = w2t[:, d, h * 64:(h + 1) * 64]
        bbhd = bbt[0:1, d, h * 64:(h + 1) * 64]
        S.pe(lambda e: e.matmul(pA[pa][:, 0:64], glTs[p][:], w2hd, start=True, stop=False), r=[f"glTs{p}", "w2t"], w=[A])
        S.pe(lambda e: e.matmul(pA[pa][:, 0:64], cst[0:1, 0, 256:320], bbhd, start=False, stop=True), r=["cst", "bbt"], w=[A])
        S.act(lambda e: e.activation(Et[p][:], pA[pa][:, 0:64], AF.Exp, scale=-1.0), r=[A], w=[f"Et{p}"])
        S.act(lambda e: e.activation(Lt[p][:], Et[p][:], AF.Ln, bias=1.0), r=[f"Et{p}"], w=[f"Lt{p}"])

    def stageA3(t):
        h, d, c, p, pa, A = hdr(t)
        cD = lambda a, b: cst[:, d, a:b]
        deccol = 127 if d == 0 else 64
        S.pe(lambda e: e.matmul(pA[pa][:, 64:128], Lt[p][:], cD(0, 64), start=True, stop=True), r=[f"Lt{p}", "cst"], w=[A])
        S.pe(lambda e: e.matmul(pA[pa][:, 128:192], Lt[p][:], cD(64, 128), start=True, stop=True), r=[f"Lt{p}", "cst"], w=[A])
        S.pe(lambda e: e.matmul(pA[pa][:, 192:256], cD(128, 192), Lt[p][:], start=True, stop=True), r=[f"Lt{p}", "cst"], w=[A])
        S.act(lambda e: e.activation(e12[p][:, 0:128], pA[pa][:, 64:192], AF.Exp, scale=-1.0 / 16), r=[A], w=[f"e12{p}"])
        S.act(lambda e: e.activation(e12[p][:, 128:192], pA[pa][:, 64:128], AF.Exp, scale=1.0 / 16), r=[A], w=[f"e12{p}"])
        S.act(lambda e: e.activation(e4[p][:], pA[pa][:, 192:256], AF.Exp, scale=-1.0 / 16), r=[A], w=[f"e4{p}"])
        S.pool(lambda e: e.tensor_tensor(qe[p][:], qTs[p][:], e12[p][:, 0:64], ALU.mult), r=[f"qTs{p}", f"e12{p}"], w=[f"qe{p}"])
        S.pool(lambda e: e.tensor_tensor(ke[p][:], kTs[p][:], e12[p][:, 128:192], ALU.mult), r=[f"kTs{p}", f"e12{p}"], w=[f"ke{p}"])
        S.pool(lambda e: e.tensor_tensor(qb[p][:], qTs[p][:], e12[p][:, 64:128], ALU.mult), r=[f"qTs{p}", f"e12{p}"], w=[f"qb{p}"])
        S.pool(lambda e: e.tensor_tensor(kd[p][:], Zh[:, c, 64:128], e4[p][:], ALU.mult), r=ZH + [f"e4{p}"], w=[f"kd{p}"])

    def stageB1(t):
        d, p, pb = t["d"], t["p"], t["pb"]
        B = f"pB{pb}"
        S.pe(lambda e: e.matmul(pB[pb][:, 256:320], ke[p][:], qe[p][:], start=True, stop=True), r=[f"ke{p}", f"qe{p}"], w=[B])
        S.dve(lambda e: e.tensor_tensor(attm[p][:], pB[pb][:, 256:320], maskb[:, d, :], ALU.mult), r=[B, "maskb"], w=[f"attm{p}"])

    def stageB(t):
        h, d, hd, c, p, s0, s1, lat = t["h"], t["d"], t["hd"], t["c"], t["p"], t["s0"], t["s1"], t["lat"]
        pb = t["pb"]
        B = f"pB{pb}"
        dc = 127 if d == 0 else 64
        decp = e12[p][:, dc:dc + 1]
        if t["first"]:
            S.dve(lambda e: e.memset(Sf[0][:], 0.0), w=["Sf0"])
            S.pool(lambda e: e.memset(Sb[0][:], 0.0), w=["Sb0"])
            if lat:
                S.dve(lambda e: e.memset(Pt[0][:], 1.0), w=["P0"])
        if lat:
            S.dve(lambda e: e.scalar_tensor_tensor(qbP[:, hd, c, :], qTs[p][:], Pt[s0][:, 0:1], e12[p][:, 64:128],
                                                   ALU.mult, ALU.mult), r=[f"qTs{p}", f"P{s0}", f"e12{p}"], w=["qbP"])
            S.dve(lambda e: e.tensor_tensor(Pt[s1][:], Pt[s0][:], decp, ALU.mult), r=[f"P{s0}", f"e12{p}"], w=[f"P{s1}"])
        S.pe(lambda e: e.matmul(pB[pb][:, 0:128], attm[p][:], vb[:, c, :], start=True, stop=False), r=[f"attm{p}", "vb"], w=[B])
        S.pe(lambda e: e.matmul(pB[pb][:, 0:128], qb[p][:], Sb[s0][:], start=False, stop=True), r=[f"qb{p}", f"Sb{s0}"], w=[B])
        S.pe(lambda e: e.matmul(pB[pb][:, 128:256], kd[p][:], vb[:, c, :], start=True, stop=True), r=[f"kd{p}", "vb"], w=[B])
        S.dve(lambda e: e.scalar_tensor_tensor(Sf[s1][:], Sf[s0][:], decp, pB[pb][:, 128:256], ALU.mult, ALU.add),
              r=[f"Sf{s0}", f"e12{p}", B], w=[f"Sf{s1}"])
        S.pool(lambda e: e.tensor_copy(Sb[s1][:], Sf[s1][:]), r=[f"Sf{s1}"], w=[f"Sb{s1}"])
        ocols = slice(h * 128, (h + 1) * 128)
        if d == 0:
            S.dve(lambda e: e.tensor_copy(OLs[:, c, ocols], pB[pb][:, 0:128]), r=[B], w=[f"OL{c}"])
        else:
            S.dve(lambda e: e.tensor_tensor(OLs[:, c, ocols], pB[pb][:, 0:128], OLs[:, c, ocols], ALU.add),
                  r=[B, f"OL{c}"], w=[f"OL{c}"])
        if t["last"]:
            if lat:
                S.dve(lambda e: e.tensor_copy(SEG[:, hd, 0:128], Sf[s1][:]), r=[f"Sf{s1}"], w=["SEG"])
                S.dve(lambda e: e.tensor_copy(SEG[:, hd, 128:129], Pt[s1][:]), r=[f"P{s1}"], w=["SEG"])
            else:
                S.dve(lambda e: e.tensor_copy(Sctx[:, hd, :], Sf[s1][:]), r=[f"Sf{s1}"], w=["Sctx"])

    stages = [stageB, stageB1, stageA3, stageA2, stageA1]
    heads = {}
    for t in tasks:
        heads.setdefault(t["h"], []).append(t)
    for h in range(4):
        tl = heads[h]
        load_head(h)
        n = len(tl)
        for step in range(n + 4):
            for k, stg in enumerate(stages):
                idx = step - (4 - k)
                if 0 <= idx < n:
                    stg(tl[idx])
    S.dma(lambda e: e.dma_start(out=segloc.ap(), in_=SEG[:].rearrange("p a b -> p (a b)")), r=["SEG"], w=["segloc"],
          group="segio")
    allgather(g, segall, segloc, "segall", r=["segloc"])
    SEGa = g.arena[0:64, zh_off:zh_off + 4 * 8 * 129].rearrange("p (r a b) -> p r a b", r=4, a=8, b=129)
    S.dma(lambda e: e.dma_start(out=SEGa.rearrange("p r a b -> p r (a b)"),
                                in_=segall.ap().rearrange("(r p) f -> p r f", p=64)),
          r=["segall"], w=ZH + ["SEGa"], group="segio")
    for h in range(4):
        for d in range(2):
            hd = h * 2 + d
            S.dve(lambda e, hd=hd: e.tensor_copy(Wk[:], Sctx[:, hd, :]), r=["Sctx"], w=["Wk"])
            js = range(4) if d == 0 else range(3, -1, -1)
            for j in js:
                mcol = (4 if d == 0 else 8) + j
                S.dve(lambda e, j=j, hd=hd: e.scalar_tensor_tensor(cand[:], Wk[:], SEGa[:, j, hd, 128:129],
                                                                   SEGa[:, j, hd, 0:128], ALU.mult, ALU.add),
                      r=["Wk", "SEGa"], w=["cand"])
                S.dve(lambda e: e.tensor_tensor(diff[:], cand[:], Wk[:], ALU.subtract), r=["cand", "Wk"], w=["diff"])
                S.dve(lambda e, mcol=mcol: e.scalar_tensor_tensor(Wk[:], diff[:], mk[:, mcol:mcol + 1], Wk[:],
                                                                  ALU.mult, ALU.add), r=["diff", "mk", "Wk"], w=["Wk"])
            S.dve(lambda e, hd=hd: e.tensor_copy(Sib[:, hd, :], Wk[:]), r=["Wk"], w=["Sib"])
    for c in range(NLC):
        p = c % NPA
        A = f"pA{p}"
        for h in range(4):
            for d in range(2):
                hd = h * 2 + d
                S.pe(lambda e, p=p, h=h, d=d, hd=hd, c=c: e.matmul(pA[p][:, h * 128:(h + 1) * 128], qbP[:, hd, c, :],
                                                                    Sib[:, hd, :], start=(d == 0), stop=(d == 1)),
                     r=["qbP", "Sib"], w=[A])
        S.dve(lambda e, p=p, c=c: e.tensor_tensor(OLs[:, c, :], pA[p][:], OLs[:, c, :], ALU.add), r=[A, f"OL{c}"], w=[f"OL{c}"])
    OGv = OG.rearrange("(c p) f -> p c f", p=64)
    for k in range(4):
        cs = slice(k * 9, (k + 1) * 9)
        S.dma(lambda e, cs=cs: e.dma_start(out=OGv[:, cs, :], in_=OLs[:, cs, :]),
              r=[f"OL{c}" for c in range(k * 9, (k + 1) * 9)], w=[f"ogo{k}"], group=f"ogo{k}")


def emit_k5(g, T, kind, n_lat_tiles, x, w, vecs, z, out, og=None, oml=None, gng=None, fr_lat=None, fr_ctx=None,
            swT=None, sbT=None, mask=None, eps=1e-6):
    S = g.S
    g.phase()
    D = 1024
    nt = T // 128
    identf, ident = load_ident(g)
    wbf = g.sb([128, 8, D], BF16)
    wst = [g.sb([128, D], F32) for i in range(4)]
    vt = g.sb([128, 4, D], F32)
    for a in range(4):
        S.dma(lambda e, a=a: e.dma_start(out=vt[:, a, :], in_=vecs[a].partition_broadcast(128)), w=["vt"], group="modt")
    if kind == "even":
        gn = g.sb([128, 128], F32)
        S.dma(lambda e: e.dma_start(out=gn[:], in_=gng.partition_broadcast(128)), w=["gn"], group="gt")
    else:
        swf = g.sb([128, 4, 128], F32)
        swb = g.sb([128, 4, 128], BF16)
        sbt = g.sb([128, 4], F32)
        mk = g.sb([128, 16], F32)
        S.dma(lambda e: e.dma_start(out=swf[:], in_=swT.rearrange("g s t -> s g t")), w=["swf"], group="swf")
        S.dve(lambda e: e.tensor_copy(swb[:], swf[:]), r=["swf"], w=["swb"])
        S.dma(lambda e: e.dma_start(out=sbt[:], in_=sbT), w=["sbt"], group="sbt")
        S.dma(lambda e: e.dma_start(out=mk[:], in_=mask), w=["mk"], group="mk")
    for k in range(8):
        s = k % 4
        S.dma(lambda e, k=k, s=s: e.dma_start(out=wst[s][:], in_=w[k * 128:(k + 1) * 128, :]),
              w=[f"wst{s}"], group=f"wst{s}", eng=("sp", "act", "pool", "sp")[s])
        if k % 2 == 0:
            S.act(lambda e, k=k, s=s: e.copy(wbf[:, k, :], wst[s][:]), r=[f"wst{s}"], w=[f"wbf{k}"])
        else:
            S.dve(lambda e, k=k, s=s: e.tensor_copy(wbf[:, k, :], wst[s][:]), r=[f"wst{s}"], w=[f"wbf{k}"])
    NX = 2
    xt = [g.sb([128, D], F32) for i in range(4)]
    ab = [g.sb([128, D], BF16) for i in range(NX)]
    aT = [g.sb([128, 8, 128], BF16) for i in range(NX)]
    rr = [g.sb([128, D], F32) for i in range(NX)]
    ot = [g.sb([128, D], F32) for i in range(NX)]
    st = [g.sb([128, 8, 6], F32) for i in range(NX)]
    mv = [g.sb([128, 16], F32) for i in range(NX)]
    if kind == "even":
        i1 = [g.sb([128, 512], F32) for i in range(NX)]
        i2 = [g.sb([128, 512], F32) for i in range(NX)]
        i3 = [g.sb([128, 512], F32) for i in range(NX)]
        i4 = [g.sb([128, 512], F32) for i in range(NX)]
        i5 = [g.sb([128, 512], F32) for i in range(NX)]
    else:
        i1 = [g.sb([128, 512], F32) for i in range(NX)]
        fc4 = [g.sb([128, 4, 512], F32) for i in range(NX)]
        zz = [g.sb([128, 2048], F32) for i in range(NX)]
        vgb = [g.sb([128, 512], BF16) for i in range(NX)]
        svp = [g.ps(0, [128, 512], F32)]
    tp = [g.ps(1 + i, [128, 8, 128], BF16) for i in range(2)]
    yp = [g.ps(3 + i, [128, 512], F32) for i in range(4)]
    wkeys = [f"wbf{k}" for k in range(8)]

    def rstd_chain(s, col_var, col_out, ncol=1):
        S.dve(lambda e: e.tensor_scalar_add(mv[s][:, col_var:col_var + ncol], mv[s][:, col_var:col_var + ncol], eps),
              r=[f"mv{s}"], w=[f"mv{s}"])
        S.act(lambda e: e.sqrt(mv[s][:, col_var:col_var + ncol], mv[s][:, col_var:col_var + ncol]),
              r=[f"mv{s}"], w=[f"mv{s}"])
        S.dve(lambda e: e.reciprocal(mv[s][:, col_out:col_out + ncol], mv[s][:, col_var:col_var + ncol]),
              r=[f"mv{s}"], w=[f"mv{s}"])

    def emit_loads(i):
        s = i % NX
        rows = slice(i * 128, (i + 1) * 128)
        xs = i % 4
        S.dma(lambda e, xs=xs, rows=rows: e.dma_start(out=xt[xs][:], in_=x[rows, :]), w=[f"xt{xs}"], group=f"xt{xs}")
        if kind == "even":
            S.dma(lambda e, s=s, rows=rows: e.dma_start(out=i1[s][:], in_=og[rows, :]), w=[f"i1{s}"], group=f"i1{s}")
            S.dma(lambda e, s=s, rows=rows: e.dma_start(out=i3[s][:], in_=z[rows, 1056:1568]), w=[f"i3{s}"], group=f"i3{s}")
            S.dma(lambda e, s=s, rows=rows: e.dma_start(out=i4[s][:], in_=oml[rows, :]), w=[f"i4{s}"], group=f"i4{s}")
            S.dma(lambda e, s=s, rows=rows: e.dma_start(out=i5[s][:], in_=z[rows, 1984:2496]), w=[f"i5{s}"], group=f"i5{s}")
        else:
            if i < n_lat_tiles:
                for qq in range(4):
                    S.dma(lambda e, s=s, qq=qq, i=i: e.dma_start(
                        out=fc4[s][:, qq, :].rearrange("p (g c) -> p g c", g=4), in_=fr_lat(qq, i)),
                        w=[f"fc4{s}.{qq}"], group=f"fc4{s}")
            else:
                S.dma(lambda e, s=s, i=i: e.dma_start(out=i1[s][:].rearrange("p (g c) -> p g c", g=4), in_=fr_ctx(i)),
                      w=[f"i1{s}"], group=f"i1{s}")
            S.dma(lambda e, s=s, rows=rows: e.dma_start(out=zz[s][:], in_=z[rows, 512:2560]), w=[f"zz{s}"], group=f"zz{s}")

    def stageE(i):
        s = i % NX
        rows = slice(i * 128, (i + 1) * 128)
        if i + 1 < nt:
            emit_loads(i + 1)
        if kind == "even":
            S.pool(lambda e, s=s: e.tensor_tensor(i2[s][:], i1[s][:], i1[s][:], ALU.mult), r=[f"i1{s}"], w=[f"i2{s}"])
            S.dve(lambda e, s=s: e.reduce_sum(mv[s][:, 0:4], i2[s][:].rearrange("p (h d) -> p h d", h=4), AX.X),
                  r=[f"i2{s}"], w=[f"mv{s}"])
            S.dve(lambda e, s=s: e.tensor_scalar(mv[s][:, 0:4], mv[s][:, 0:4], 1.0 / 128, None, ALU.mult),
                  r=[f"mv{s}"], w=[f"mv{s}"])
            rstd_chain(s, 0, 4, 4)
            S.act(lambda e, s=s: e.activation(i3[s][:], i3[s][:], AF.Silu), r=[f"i3{s}"], w=[f"i3{s}"])
            S.act(lambda e, s=s: e.activation(i5[s][:], i5[s][:], AF.Silu), r=[f"i5{s}"], w=[f"i5{s}"])
            for h in range(4):
                hs = slice(h * 128, (h + 1) * 128)
                S.dve(lambda e, s=s, h=h, hs=hs: e.scalar_tensor_tensor(
                    i1[s][:, hs], i1[s][:, hs], mv[s][:, 4 + h:5 + h], gn[:], ALU.mult, ALU.mult),
                    r=[f"i1{s}", f"mv{s}", "gn"], w=[f"i1{s}"])
            S.pool(lambda e, s=s: e.tensor_tensor(ab[s][:, 0:512], i1[s][:], i3[s][:], ALU.mult),
                   r=[f"i1{s}", f"i3{s}"], w=[f"ab{s}"])
            S.pool(lambda e, s=s: e.tensor_tensor(ab[s][:, 512:1024], i4[s][:], i5[s][:], ALU.mult),
                   r=[f"i4{s}", f"i5{s}"], w=[f"ab{s}"])
        else:
            if i < n_lat_tiles:
                S.dve(lambda e, s=s: e.tensor_scalar(i1[s][:], fc4[s][:, 0, :], mk[:, 0:1], None, ALU.mult),
                      r=[f"fc4{s}.0", f"fc4{s}.3", "mk"], w=[f"i1{s}"])
                for qq in range(1, 4):
                    S.dve(lambda e, s=s, qq=qq: e.scalar_tensor_tensor(
                        i1[s][:], fc4[s][:, qq, :], mk[:, qq:qq + 1], i1[s][:], ALU.mult, ALU.add),
                        r=[f"fc4{s}.{qq}", f"fc4{s}.3", "mk", f"i1{s}"], w=[f"i1{s}"])
            S.act(lambda e, s=s: e.activation(zz[s][:, 0:512], zz[s][:, 0:512], AF.Silu), r=[f"zz{s}"], w=[f"zz{s}"])
            S.act(lambda e, s=s: e.activation(zz[s][:, 1536:2048], zz[s][:, 1536:2048], AF.Silu), r=[f"zz{s}"], w=[f"zz{s}"])
            S.act(lambda e, s=s: e.activation(zz[s][:, 512:1536], zz[s][:, 512:1536], AF.Gelu), r=[f"zz{s}"], w=[f"zz{s}"])
            S.pool(lambda e, s=s: e.tensor_tensor(ab[s][:, 0:512], i1[s][:], zz[s][:, 0:512], ALU.mult),
                   r=[f"i1{s}", f"zz{s}"], w=[f"ab{s}"])
            for g in range(4):
                S.dve(lambda e, s=s, g=g: e.bn_stats(st[s][:, g, :], zz[s][:, 1024 + g * 128:1024 + (g + 1) * 128]),
                      r=[f"zz{s}"], w=[f"st{s}"])
                S.dve(lambda e, s=s, g=g: e.bn_aggr(mv[s][:, 2 * g:2 * g + 2], st[s][:, g:g + 1, :]),
                      r=[f"st{s}"], w=[f"mv{s}"])
            for g in range(4):
                rstd_chain(s, 2 * g + 1, 8 + g, 1)
            for g in range(4):
                S.dve(lambda e, s=s, g=g: e.tensor_scalar(
                    vgb[s][:, g * 128:(g + 1) * 128], zz[s][:, 1024 + g * 128:1024 + (g + 1) * 128],
                    mv[s][:, 2 * g:2 * g + 1], mv[s][:, 8 + g:9 + g], ALU.subtract, ALU.mult),
                    r=[f"zz{s}", f"mv{s}"], w=[f"vgb{s}"])
            for g in range(4):
                S.pe(lambda e, s=s, g=g: e.matmul(svp[0][:, g * 128:(g + 1) * 128], swb[:, g, :],
                                                  vgb[s][:, g * 128:(g + 1) * 128], start=True, stop=True),
                     r=[f"vgb{s}", "swb"], w=["svp0"])
            for g in range(4):
                gs = slice(g * 128, (g + 1) * 128)
                S.dve(lambda e, s=s, g=g, gs=gs: e.scalar_tensor_tensor(
                    zz[s][:, 512 + g * 128:512 + (g + 1) * 128], svp[0][:, gs], sbt[:, g:g + 1],
                    zz[s][:, 512 + g * 128:512 + (g + 1) * 128], ALU.add, ALU.mult),
                    r=["svp0", "sbt", f"zz{s}"], w=[f"zz{s}"])
            S.pool(lambda e, s=s: e.tensor_tensor(ab[s][:, 512:1024], zz[s][:, 512:1024], zz[s][:, 1536:2048], ALU.mult),
                   r=[f"zz{s}"], w=[f"ab{s}"])

    def stageT(i):
        s = i % NX
        t = i % 2
        for k in range(8):
            S.pe(lambda e, s=s, t=t, k=k: e.transpose(tp[t][:, k, :], ab[s][:, k * 128:(k + 1) * 128], ident[:]),
                 r=[f"ab{s}", "ident"], w=[f"tp{t}"])
        S.act(lambda e, s=s, t=t: e.copy(aT[s][:], tp[t][:]), r=[f"tp{t}"], w=[f"aT{s}"])

    def stageM(i):
        s = i % NX
        rows = slice(i * 128, (i + 1) * 128)
        gi = 0 if i < n_lat_tiles else 1
        for c in range(2):
            p = (2 * i + c) % 4
            cs = slice(c * 512, (c + 1) * 512)
            for k in range(8):
                S.pe(lambda e, s=s, p=p, k=k, cs=cs: e.matmul(yp[p][:], aT[s][:, k, :], wbf[:, k, cs],
                                                             start=(k == 0), stop=(k == 7)),
                     r=[f"aT{s}", wkeys[k]], w=[f"yp{p}"])
            S.dve(lambda e, s=s, p=p, cs=cs, gi=gi: e.tensor_tensor(rr[s][:, cs], yp[p][:], vt[:, gi, cs], ALU.mult),
                  r=[f"yp{p}", "vt"], w=[f"rr{s}"])
        xs = i % 4
        S.dve(lambda e, s=s, xs=xs: e.scalar_tensor_tensor(rr[s][:], xt[xs][:], ALPHA, rr[s][:], ALU.mult, ALU.add),
               r=[f"xt{xs}", f"rr{s}"], w=[f"rr{s}"])
        for j in range(2):
            S.dve(lambda e, s=s, j=j: e.bn_stats(st[s][:, 4 + j, :], rr[s][:, j * 512:(j + 1) * 512]),
                  r=[f"rr{s}"], w=[f"st{s}"])
        S.dve(lambda e, s=s: e.bn_aggr(mv[s][:, 12:14], st[s][:, 4:6, :]), r=[f"st{s}"], w=[f"mv{s}"])
        rstd_chain(s, 13, 14, 1)
        S.dve(lambda e, s=s: e.tensor_scalar(rr[s][:], rr[s][:], mv[s][:, 12:13], mv[s][:, 14:15],
                                             ALU.subtract, ALU.mult), r=[f"rr{s}", f"mv{s}"], w=[f"rr{s}"])
        S.pool(lambda e, s=s: e.tensor_tensor(rr[s][:], rr[s][:], vt[:, 2, :], ALU.mult), r=[f"rr{s}", "vt"], w=[f"rr{s}"])
        S.pool(lambda e, s=s: e.tensor_tensor(ot[s][:], rr[s][:], vt[:, 3, :], ALU.add), r=[f"rr{s}", "vt"], w=[f"ot{s}"])
        S.dma(lambda e, s=s, rows=rows: e.dma_start(out=out[rows, :], in_=ot[s][:]), r=[f"ot{s}"], w=[f"oo{s}"],
              group=f"oo{s}", eng="act")


    emit_loads(0)
    stageE(0)
    if nt > 1:
        stageE(1)
    stageT(0)
    for i in range(nt):
        if i + 2 < nt:
            stageE(i + 2)
        if i + 1 < nt:
            stageT(i + 1)
        stageM(i)


def emit_k7(g, fall, fout, TW, FC, W3, TWc, mask, with_ctx=True):
    S = g.S
    g.phase()
    st = [g.sb([128, 4, 512], F32) for i in range(2)]
    tmp = [g.sb([128, 4, 128], F32) for i in range(2)]
    mk = g.sb([128, 16], F32)
    TWb = g.sb([128, 64, 256], BF16)
    fb = g.sb([128, 64, 128], BF16)
    Y = g.sb([128, 2, 64, 128], BF16)
    U = g.sb([128, 64, 256], BF16)
    FCb = g.sb([128, 512], BF16)
    W3b = g.sb([128, 256], BF16)
    frt = g.sb([128, 64, 128], F32)
    ps = [g.ps(i, [128, 512], F32) for i in range(4)]
    S.dma(lambda e: e.dma_start(out=mk[:], in_=mask), w=["mk"], group="mk")
    stf = lambda s: st[s][:].rearrange("p a b -> p (a b)")
    for i in range(8):
        s = i % 2
        S.dma(lambda e, i=i, s=s: e.dma_start(out=stf(s), in_=TW[:, i * 8:(i + 1) * 8, :].rearrange("p a b -> p (a b)")),
              w=[f"st{s}"], group=f"st{s}")
        S.add("dve" if i % 2 == 0 else "pool",
              lambda e, i=i, s=s: e.tensor_copy(TWb[:, i * 8:(i + 1) * 8, :].rearrange("p a b -> p (a b)"), stf(s)),
              r=[f"st{s}"], w=["TWb"])
    S.dma(lambda e: e.dma_start(out=stf(0)[:, 0:512], in_=FC), w=["st0"], group="st0")
    S.dve(lambda e: e.tensor_copy(FCb[:], stf(0)[:, 0:512]), r=["st0"], w=["FCb"])
    S.dma(lambda e: e.dma_start(out=stf(1)[:, 0:256], in_=W3), w=["st1"], group="st1")
    S.dve(lambda e: e.tensor_copy(W3b[:], stf(1)[:, 0:256]), r=["st1"], w=["W3b"])

    S.barrier()

    def select(s, t, dst):
        stk = [f"st{s}.{r}.{pp}" for r in range(4) for pp in range(4)]
        S.dve(lambda e: e.tensor_scalar(tmp[t][:], st[s][:, :, 0:128], mk[:, 0:1], None, ALU.mult),
              r=stk + ["mk"], w=[f"tmp{t}"])
        for gg in range(1, 3):
            S.dve(lambda e, gg=gg: e.scalar_tensor_tensor(tmp[t][:], st[s][:, :, gg * 128:(gg + 1) * 128],
                                                         mk[:, gg:gg + 1], tmp[t][:], ALU.mult, ALU.add),
                  r=stk + ["mk", f"tmp{t}"], w=[f"tmp{t}"])
        S.dve(lambda e: e.scalar_tensor_tensor(dst, st[s][:, :, 384:512], mk[:, 3:4], tmp[t][:], ALU.mult, ALU.add),
              r=stk + ["mk", f"tmp{t}"], w=["fb"])

    for j in range(16):
        s = j % 2
        for r in range(4):
            for pp in range(4):
                src = fall[pp][r * 512:(r + 1) * 512, :].rearrange("(a n) c -> a n c", n=64)[:, 4 * j:4 * j + 4, :]
                p0 = 32 * r + 8 * pp
                S.dma(lambda e, s=s, p0=p0, src=src: e.dma_start(out=st[s][p0:p0 + 8, :, :], in_=src),
                      w=[f"st{s}.{r}.{pp}"], group=f"st{s}")
        select(s, s, fb[:, 4 * j:4 * j + 4, :])
    pi = 0
    for n2 in range(0, 64, 2):
        p = pi % 4; pi += 1
        for d in range(2):
            S.pe(lambda e, p=p, n2=n2, d=d: e.matmul(ps[p][:, d * 256:(d + 1) * 256], fb[:, n2 + d, :], TWb[:, n2 + d, :],
                                                     start=True, stop=True), r=["fb", "TWb"], w=[f"ps{p}"])
        for d in range(2):
            src = lambda p=p, d=d: ps[p][:, d * 256:(d + 1) * 256].rearrange("p (r j a) -> p r j a", r=2, a=2)
            dst = lambda n2=n2, d=d: Y[:, :, :, 2 * (n2 + d):2 * (n2 + d) + 2]
            if d == 0:
                S.act(lambda e, src=src, dst=dst: e.copy(dst(), src()), r=[], w=["Y", f"ps{p}"])
            else:
                S.dve(lambda e, src=src, dst=dst: e.tensor_copy(dst(), src()), r=[], w=["Y", f"ps{p}"])
    for j in range(0, 64, 2):
        p = pi % 4; pi += 1
        for d in range(2):
            jj = j + d
            S.pe(lambda e, p=p, jj=jj, d=d: e.matmul(ps[p][:, d * 256:(d + 1) * 256], Y[:, 0, jj, :],
                                                     FCb[:, 0:256], start=True, stop=False), r=["Y", "FCb"], w=[f"ps{p}"])
            S.pe(lambda e, p=p, jj=jj, d=d: e.matmul(ps[p][:, d * 256:(d + 1) * 256], Y[:, 1, jj, :],
                                                     FCb[:, 256:512], start=False, stop=True), r=["Y", "FCb"], w=[f"ps{p}"])
        if (j // 2) % 2 == 0:
            S.act(lambda e, p=p, j=j: e.copy(U[:, j:j + 2, :].rearrange("p a b -> p (a b)"), ps[p][:]), r=[f"ps{p}"], w=["U"])
        else:
            S.dve(lambda e, p=p, j=j: e.tensor_copy(U[:, j:j + 2, :].rearrange("p a b -> p (a b)"), ps[p][:]), r=[f"ps{p}"], w=["U"])
    scale = 1.0 / 1024.0
    for j0 in range(0, 64, 4):
        p = pi % 4; pi += 1
        for d in range(4):
            jj = j0 + d
            S.pe(lambda e, p=p, jj=jj, d=d: e.matmul(ps[p][:, d * 128:(d + 1) * 128], W3b[:, 0:128], U[:, jj, 0:128],
                                                     start=True, stop=False), r=["U", "W3b"], w=[f"ps{p}"])
            S.pe(lambda e, p=p, jj=jj, d=d: e.matmul(ps[p][:, d * 128:(d + 1) * 128], W3b[:, 128:256], U[:, jj, 128:256],
                                                     start=False, stop=True), r=["U", "W3b"], w=[f"ps{p}"])
        S.dve(lambda e, p=p, j0=j0: e.tensor_scalar(frt[:, j0:j0 + 4, :].rearrange("p a b -> p (a b)"), ps[p][:],
                                                    scale, None, ALU.mult), r=[f"ps{p}"], w=["frt"])
    for q in range(4):
        fv = fout[q].rearrange("(k2 jj a) c -> a k2 jj c", jj=64, a=2)
        for a in range(2):
            S.dma(lambda e, a=a, q=q, fv=fv: e.dma_start(out=fv[a], in_=frt[a * 64 + 16 * q:a * 64 + 16 * (q + 1), :, :]),
                  r=["frt"], w=[f"fo{a}{q}"], group=f"fo{a}")
    if with_ctx:
        S.barrier()
        fcb = g.sb([128, 2, 128], BF16)
        TWcb = g.sb([128, 2, 512], BF16)
        Yc = g.sb([128, 512], BF16)
        oc = g.sb([128, 2, 128], F32)
        for t in range(2):
            S.dma(lambda e, t=t: e.dma_start(out=st[t][:, 0, :], in_=fall[4][t * 128:(t + 1) * 128, :]),
                  w=[f"st{t}"], group=f"st{t}")
            S.dve(lambda e, t=t: e.tensor_scalar(tmp[t][:, 0, :], st[t][:, 0, 0:128], mk[:, 0:1], None, ALU.mult),
                  r=[f"st{t}", "mk"], w=[f"tmp{t}"])
            for gg in range(1, 4):
                S.dve(lambda e, t=t, gg=gg: e.scalar_tensor_tensor(
                    tmp[t][:, 0, :], st[t][:, 0, gg * 128:(gg + 1) * 128], mk[:, gg:gg + 1], tmp[t][:, 0, :], ALU.mult, ALU.add),
                    r=[f"st{t}", "mk", f"tmp{t}"], w=[f"tmp{t}"])
            S.dve(lambda e, t=t: e.tensor_copy(fcb[:, t, :], tmp[t][:, 0, :]), r=[f"tmp{t}"], w=["fcb"])
        for t in range(2):
            S.dma(lambda e, t=t: e.dma_start(out=stf(t)[:, 0:512], in_=TWc[:, t, :]), w=[f"st{t}"], group=f"st{t}")
            S.dve(lambda e, t=t: e.tensor_copy(TWcb[:, t, :], stf(t)[:, 0:512]), r=[f"st{t}"], w=["TWcb"])
        p = pi % 4; pi += 1
        for t in range(2):
            S.pe(lambda e, p=p, t=t: e.matmul(ps[p][:], fcb[:, t, :], TWcb[:, t, :], start=(t == 0), stop=(t == 1)),
                 r=["fcb", "TWcb"], w=[f"ps{p}"])
        S.dve(lambda e, p=p: e.tensor_copy(Yc[:], ps[p][:]), r=[f"ps{p}"], w=["Yc"])
        p = pi % 4; pi += 1
        for kt in range(2):
            S.pe(lambda e, p=p, kt=kt: e.matmul(ps[p][:, kt * 128:(kt + 1) * 128], Yc[:, kt * 128:(kt + 1) * 128],
                                                FCb[:, 0:128], start=True, stop=False), r=["Yc", "FCb"], w=[f"ps{p}"])
            S.pe(lambda e, p=p, kt=kt: e.matmul(ps[p][:, kt * 128:(kt + 1) * 128], Yc[:, 256 + kt * 128:256 + (kt + 1) * 128],
                                                FCb[:, 256:384], start=False, stop=True), r=["Yc", "FCb"], w=[f"ps{p}"])
        S.dve(lambda e, p=p: e.tensor_scalar(oc[:].rearrange("p a b -> p (a b)"), ps[p][:, 0:256],
                                             1.0 / np.sqrt(256.0 * 128.0), None, ALU.mult), r=[f"ps{p}"], w=["oc"])
        S.dma(lambda e: e.dma_start(out=fout[4].rearrange("(t p) c -> p t c", p=128), in_=oc[:]),
              r=["oc"], w=["oco"], group="oco")


Q, L, SEQ, D = 2048, 256, 8192, 1024
T = Q + L
NCORES = 8


def build_fused(depth=4, stop_after=None):
    nc = bass.Bass(target_bir_lowering=False)
    g = G(nc)
    if stop_after is not None:
        g.max_phase = stop_after
    ext = lambda name, shape: nc.dram_tensor(name, list(shape), F32, kind="ExternalInput")
    xin = ext("xin", [T, D]); cin = ext("cin", [128, D])
    ada_w = ext("ada_w", [D, 3 * D]); ada_b = ext("ada_b", [3 * D])
    plg = ext("post_ln_g", [4, D]); plb = ext("post_ln_b", [4, D])
    ewi = ext("even_w_in", [2, D, 2496]); ewo = ext("even_w_out", [2, D, D])
    owi = ext("odd_w_in", [2, D, 2560]); owo = ext("odd_w_out", [2, D, D])
    gw2 = ext("gla_w2", [2, 2, 16, 256]); gb = ext("gla_b", [2, 2, 256]); gng = ext("gla_norm_g", [2, 128])
    qng = ext("mla_q_norm_g", [2, 256]); wuq = ext("mla_w_uq", [2, 256, 768])
    kng = ext("mla_kv_norm_g", [2, 128]); wukv = ext("mla_w_ukv", [2, 128, 1024])
    swT = ext("sgu_wT", [2, 4, 128, 128]); sbT = ext("sgu_bT", [2, 128, 4])
    csq = ext("csq", [T, 32]); csk = ext("csk", [SEQ + L, 32])
    ident_d = ext("ident", [128, 128]); g.ident_d = ident_d.ap()
    gcst = ext("gcst", [2, 64, 320]); mask = ext("mask", [128, 16])
    TW = ext("TW", [128, 64, 256]); FC = ext("FC", [128, 512]); W3 = ext("W3", [128, 256]); TWc = ext("TWc", [128, 2, 512])
    xout = nc.dram_tensor("xout", [Q, D], F32, kind="ExternalOutput")
    ML = g.dram("ML", [128, 3 * D]); MS = g.dram("MS", [2, 3 * D]); MALL = g.dram("MALL", [8, 3 * D])
    X = [g.dram(f"X{i}", [T, D]) for i in range(2)]
    Z = g.dram("Z", [T, 2560])
    QP = g.dram("QP", [T, 768])
    KSIN = [g.dram(f"KSIN{p}", [Q // 2, 160]) for p in range(2)]
    KSG = [g.dram(f"KSG{p}", [4 * Q // 2, 160]) for p in range(2)]
    KSA = g.dram("KSA", [SEQ + L, 160])
    OML = g.dram("OML", [T, 512]); OG = g.dram("OG", [T, 512])
    SEGL = g.dram("SEGL", [64, 8 * 129]); SEGA = g.dram("SEGA", [256, 8 * 129])
    FPR = [512, 512, 512, 512, 256]
    FIN = [g.dram(f"FIN{p}", [FPR[p], 512]) for p in range(5)]
    FALL = [g.dram(f"FALL{p}", [4 * FPR[p], 512]) for p in range(5)]
    OPR = [2048, 2048, 2048, 2048, 256]
    FOUT = [g.dram(f"FOUT{p}", [OPR[p], 128]) for p in range(5)]
    FOALL = [g.dram(f"FOALL{p}", [4 * OPR[p], 128]) for p in range(5)]
    S = g.S
    rows = lambda i: slice(i * 128, (i + 1) * 128)
    HN = 3 * D // 2
    for hh in range(2):
        cs = slice(hh * HN, (hh + 1) * HN)
        emit_k1(g, lambda i: cin.ap()[rows(i), :], 1, D, HN, ada_w.ap()[:, cs],
                lambda i, cs=cs: ML.ap()[rows(i), cs], "silu", bias=ada_b.ap()[cs])
    g.phase()
    S.dma(lambda e: e.dma_start(out=MS.ap(), in_=ML.ap()[0:2, :]), w=["ms"], group="dc0")
    allgather(g, MALL, MS, "mall", r=["ms"])
    xcur = xin
    for l in range(depth):
        li = l // 2
        even = l % 2 == 0
        last = l == depth - 1
        Ml = MALL.ap()
        r0, r1 = 2 * l, 2 * l + 1
        mods = (Ml[r0, 0:D], Ml[r0, D:2 * D], Ml[r1, 0:D], Ml[r1, D:2 * D])
        vecs = (Ml[r0, 2 * D:3 * D], Ml[r1, 2 * D:3 * D], plg.ap()[l], plb.ap()[l])
        N = 2496 if even else 2560
        Zl = Z.ap()[:, 0:N]
        xap = xcur.ap()
        emit_k1(g, lambda i, xap=xap: xap[rows(i), :], T // 128, D, N, (ewi if even else owi).ap()[li],
                lambda i, Zl=Zl: Zl[rows(i), :], "ln", n_lat_tiles=Q // 128, mods=mods)
        Tk = Q if last else T
        xn = xout if last else X[l % 2]
        if even:
            emit_k1(g, lambda i, Zl=Zl: Zl[rows(i), 1568:1824], T // 128, 256, 768, wuq.ap()[li],
                    lambda i: QP.ap()[rows(i), :], "rms", gvec=qng.ap()[li])
            g.phase()
            S.dma(lambda e, Zl=Zl: e.dma_start(out=KSA.ap()[0:L, :], in_=Zl[Q:T, 1824:1984]), w=["ksa0"], group="dc1")
            HQ = Q // 2
            for p in range(2):
                S.dma(lambda e, Zl=Zl, p=p: e.dma_start(out=KSIN[p].ap(), in_=Zl[HQ * p:HQ * (p + 1), 1824:1984]),
                      w=[f"ksin{p}"], group=f"dc0{p}")
                allgather(g, KSG[p], KSIN[p], f"ksg{p}", r=[f"ksin{p}"])
                dst = KSA.ap()[L:L + SEQ, :].rearrange("(r h n) c -> h r n c", r=4, h=2)[p]
                S.dma(lambda e, p=p, dst=dst: e.dma_start(out=dst, in_=KSG[p].ap().rearrange("(r n) c -> r n c", r=4)),
                      r=[f"ksg{p}"], w=[f"ksa1{p}"], group=f"dc2{p}")
            emit_k3(g, Q, L, SEQ + L, QP.ap(), KSA.ap(), kng.ap()[li], wukv.ap()[li], csq.ap(), csk.ap(), OML.ap())
            emit_k4s(g, Zl, OG.ap(), SEGL, SEGA, gw2.ap()[li], gb.ap()[li], gcst.ap(), mask.ap())
            emit_k5(g, Tk, "even", Q // 128, xap, ewo.ap()[li], vecs, Zl, xn.ap(), og=OG.ap(), oml=OML.ap(),
                    gng=gng.ap()[li])
        else:
            g.phase()
            r0 = 0
            for p in range(5):
                S.dma(lambda e, Zl=Zl, p=p, r0=r0: e.dma_start(out=FIN[p].ap(), in_=Zl[r0:r0 + FPR[p], 0:512]),
                      w=[f"fin{p}"], group=f"dcf{p}")
                allgather(g, FALL[p], FIN[p], f"fall{p}", r=[f"fin{p}"])
                r0 += FPR[p]
            emit_k7(g, [a.ap() for a in FALL], [a.ap() for a in FOUT], TW.ap(), FC.ap(), W3.ap(), TWc.ap(), mask.ap())
            g.phase()
            for p in range(5):
                allgather(g, FOALL[p], FOUT[p], f"foall{p}")
            fo = [a.ap().rearrange("(g r) c -> r g c", g=4) for a in FOALL]
            emit_k5(g, Tk, "odd", Q // 128, xap, owo.ap()[li], vecs, Zl, xn.ap(),
                    fr_lat=lambda qq, i, fo=fo: fo[qq][128 * i:128 * (i + 1), :, :],
                    fr_ctx=lambda i, fo=fo: fo[4][128 * (i - Q // 128):128 * (i - Q // 128 + 1), :, :],
                    swT=swT.ap()[li], sbT=sbT.ap()[li], mask=mask.ap())
        xcur = xn
    g.finish()
    return nc


def make_inputs(x, c, ctx, c_ctx, ada_w, ada_b, post_ln_g, post_ln_b, even_w_in, gla_w2, gla_b, gla_norm_g,
                mla_q_norm_g, mla_w_uq, mla_kv_norm_g, mla_w_ukv, even_w_out, odd_w_in, sgu_w, sgu_b, odd_w_out,
                rope_tables, fnet_consts):
    f32 = np.float32
    cc = lambda a: np.ascontiguousarray(a, dtype=f32)
    shared = dict(post_ln_g=cc(post_ln_g), post_ln_b=cc(post_ln_b),
                  even_w_in=cc(even_w_in), even_w_out=cc(even_w_out), odd_w_in=cc(odd_w_in), odd_w_out=cc(odd_w_out),
                  gla_w2=cc(gla_w2), gla_b=cc(gla_b), gla_norm_g=cc(gla_norm_g), mla_q_norm_g=cc(mla_q_norm_g),
                  mla_w_uq=cc(mla_w_uq), mla_kv_norm_g=cc(mla_kv_norm_g), mla_w_ukv=cc(mla_w_ukv),
                  sgu_wT=cc(np.transpose(sgu_w, (0, 1, 3, 2))), sgu_bT=cc(np.transpose(sgu_b, (0, 2, 1))),
                  csk=rope_tables(np.concatenate([-np.ones(L, int), np.arange(SEQ)])),
                  ident=np.eye(128, dtype=f32), gcst=gla_consts2(), **fnet_consts())
    maps = []
    for j in range(NCORES):
        b, i = j // 4, j % 4
        m = dict(shared)
        m["xin"] = cc(np.concatenate([x[b, Q * i:Q * (i + 1)], ctx[b]], 0))
        cin = np.zeros((128, D), f32)
        cin[0] = c[b]; cin[1] = c_ctx
        m["cin"] = cin
        m["ada_w"] = cc(ada_w[i])
        m["ada_b"] = cc(ada_b[i])
        m["csq"] = rope_tables(np.concatenate([np.arange(Q) + Q * i, -np.ones(L, int)]))
        mk = np.zeros((128, 16), f32)
        mk[:, i] = 1.0
        for jj in range(4):
            mk[:, 4 + jj] = 1.0 if jj < i else 0.0
            mk[:, 8 + jj] = 1.0 if jj > i else 0.0
        m["mask"] = mk
        maps.append(m)
    return maps

def rope_tables(pos):
    pos = np.asarray(pos)
    row = (pos // 64).astype(np.float32)
    col = (pos % 64).astype(np.float32)
    inv = (10000.0 ** (-np.arange(8, dtype=np.float32) / 8)).astype(np.float32)
    ang = np.concatenate([row[:, None] * inv, col[:, None] * inv], -1).astype(np.float32)
    c = np.cos(ang).astype(np.float32)
    s = np.sin(ang).astype(np.float32)
    ident = pos < 0
    c[ident] = 1.0
    s[ident] = 0.0
    return np.concatenate([c, s], -1).astype(np.float32)


def fnet_consts():
    n1 = np.arange(128)[:, None, None].astype(np.float64)
    n2 = np.arange(64)[None, :, None].astype(np.float64)
    k1 = np.arange(128)[None, None, :].astype(np.float64)
    ang = 2 * np.pi * k1 * (64 * n1 + n2) / 8192.0
    TW = np.concatenate([np.cos(ang), -np.sin(ang)], -1).astype(np.float32)
    c = np.arange(128)[:, None].astype(np.float64)
    cp = np.arange(128)[None, :].astype(np.float64)
    a = 2 * np.pi * c * cp / 128.0
    Cc, Sc = np.cos(a), np.sin(a)
    FC = np.concatenate([Cc, -Sc, Sc, Cc], -1).astype(np.float32)
    W3 = np.zeros((64, 2, 2, 2, 64), np.float64)
    n2v = np.arange(64)[:, None]
    k2v = np.arange(64)[None, :]
    a3 = 2 * np.pi * n2v * k2v / 64.0
    for aa in range(2):
        W3[:, aa, 0, aa, :] = np.cos(a3)
        W3[:, aa, 1, aa, :] = np.sin(a3)
    W3 = W3.reshape(128, 256).astype(np.float32)
    n = np.arange(256)[:, None].astype(np.float64)
    k = np.arange(256)[None, :].astype(np.float64)
    ac = 2 * np.pi * n * k / 256.0
    TWc = np.concatenate([np.cos(ac), -np.sin(ac)], -1).reshape(2, 128, 512).transpose(1, 0, 2)
    TWc = np.ascontiguousarray(TWc).astype(np.float32)
    return dict(TW=TW, FC=FC, W3=W3, TWc=TWc)


_NC = {}


def kernel(x, c, ctx, c_ctx, ada_w, ada_b, post_ln_g, post_ln_b, even_w_in, gla_w2, gla_b, gla_norm_g,
           mla_q_norm_g, mla_w_uq, mla_kv_norm_g, mla_w_ukv, even_w_out, odd_w_in, sgu_w, sgu_b, odd_w_out):
    if "nc" not in _NC:
        _NC["nc"] = build_fused(4)
    maps = make_inputs(np.asarray(x), np.asarray(c), np.asarray(ctx), np.asarray(c_ctx), np.asarray(ada_w),
                       np.asarray(ada_b), np.asarray(post_ln_g), np.asarray(post_ln_b), np.asarray(even_w_in),
                       np.asarray(gla_w2), np.asarray(gla_b), np.asarray(gla_norm_g), np.asarray(mla_q_norm_g),
                       np.asarray(mla_w_uq), np.asarray(mla_kv_norm_g), np.asarray(mla_w_ukv), np.asarray(even_w_out),
                       np.asarray(odd_w_in), np.asarray(sgu_w), np.asarray(sgu_b), np.asarray(odd_w_out),
                       rope_tables, fnet_consts)
    res = run_bass_kernel_spmd(_NC["nc"], maps, core_ids=list(range(NCORES)))
    out = np.empty((2, SEQ, D), np.float32)
    for j in range(NCORES):
        out[j // 4, (j % 4) * Q:(j % 4 + 1) * Q] = res.results[j]["xout"]
    return out
```
